# Optimizing a Trainium2 kernel written in Bass

```python
import math, functools
import jax, jax.numpy as jnp
from jax import lax
import numpy as np

D_MODEL = 1024
BATCH = 16
SEQ = 2048
DEPTH = 1
DEC_BATCH = 32
DEC_SEQ = 64
PAST_LEN = 1024

CHUNK = 64
N_META = 16
Q_BLOCK = 128
EPS = 1e-6
NEG_INF = -1e30
A_HEADS = 8
A_HEAD_DIM = 64
A_V_DIM = 2 * A_HEAD_DIM
A_QK_WIDTH = A_HEADS * 2 * A_HEAD_DIM
A_V_WIDTH = A_HEADS * A_V_DIM
A_SCALE = A_HEAD_DIM ** -0.5
N_BUCKETS = 32
MAX_DISTANCE = 128
B_HEADS = 8
B_HEAD_DIM = 128
B_WIDTH = B_HEADS * B_HEAD_DIM
B_SCALE = B_HEAD_DIM ** -0.5
CONV_WIDTH = 4
GDN_BLOCK = 64
D_FF = 4 * D_MODEL
OFF_KA = A_QK_WIDTH
OFF_VA = OFF_KA + A_QK_WIDTH
OFF_B = OFF_VA + A_V_WIDTH
OFF_Z = OFF_B + 3 * B_WIDTH
OFF_BETA = OFF_Z + B_WIDTH
OFF_ALPHA = OFF_BETA + B_HEADS
OFF_GA = OFF_ALPHA + B_HEADS
OFF_GB = OFF_GA + D_MODEL
N_IN = OFF_GB + D_MODEL
SPLITS = (OFF_KA, OFF_VA, OFF_B, OFF_Z, OFF_BETA, OFF_ALPHA, OFF_GA, OFF_GB)

kernel_name = "hybrid_diffattn_gdn_stream_step"


def _rmsnorm(x, w):
    xf = x.astype(jnp.float32)
    y = xf * lax.rsqrt(jnp.mean(xf * xf, axis=-1, keepdims=True) + EPS)
    return (y * w.astype(jnp.float32)).astype(x.dtype)


def _l2norm(x):
    xf = x.astype(jnp.float32)
    return xf * lax.rsqrt(jnp.sum(xf * xf, axis=-1, keepdims=True) + EPS)


def _lambda_init(layer):
    return 0.8 - 0.6 * math.exp(-0.3 * layer)


def _rel_bias(q_pos, k_pos, table):
    rel = k_pos[None, :] - q_pos[:, None]
    half = N_BUCKETS // 2
    exact = half // 2
    n = jnp.abs(rel)
    large = exact + (jnp.log(jnp.maximum(n, 1).astype(jnp.float32) / exact)
                     / math.log(MAX_DISTANCE / exact) * (half - exact)).astype(jnp.int32)
    large = jnp.minimum(large, half - 1)
    bucket = jnp.where(rel > 0, half, 0) + jnp.where(n < exact, n, large)
    return jnp.transpose(jnp.take(table, bucket, axis=0).astype(jnp.float32), (2, 0, 1))


def _diff_core(q, k, v, bias, mask, lam):
    s = jnp.einsum("bqhmd,bkhmd->bmhqk", q, k).astype(jnp.float32) * A_SCALE + bias
    if mask is not None:
        s = jnp.where(mask, s, NEG_INF)
    p = jax.nn.softmax(s, axis=-1)
    w = p[:, 0] - lam * p[:, 1]
    return jnp.einsum("bhqk,bkhe->bqhe", w.astype(v.dtype), v)


def _key_end(last_tok, total):
    if last_tok < N_META:
        return N_META
    c = (last_tok - N_META) // CHUNK + 1
    return min(total, N_META + c * CHUNK)


def _prompt_attend(q, k, v, lam, rel_table):
    total = q.shape[1]
    pos = jnp.arange(total)
    cid = jnp.where(pos < N_META, 0, 1 + (pos - N_META) // CHUNK)
    outs = []
    for s in range(0, total, Q_BLOCK):
        e = min(s + Q_BLOCK, total)
        kend = _key_end(e - 1, total)
        bias = _rel_bias(pos[s:e], pos[:kend], rel_table)
        mask = cid[s:e, None] >= cid[None, :kend]
        outs.append(_diff_core(q[:, s:e], k[:, :kend], v[:, :kend], bias, mask, lam))
    return jnp.concatenate(outs, axis=1)


def _sample_attend(q, k, v, lam, rel_table, k_past, v_past):
    bsz, past = k_past.shape[0], k_past.shape[1]
    t = q.shape[1]
    k_all = jnp.concatenate([k_past.reshape(bsz, past, A_HEADS, 2, A_HEAD_DIM).astype(k.dtype), k], axis=1)
    v_all = jnp.concatenate([v_past.astype(v.dtype), v], axis=1)
    pos_k = jnp.arange(past + t)
    bias = _rel_bias(pos_k[past:], pos_k, rel_table)
    return _diff_core(q, k_all, v_all, bias, None, lam)


def _causal_conv(x, w):
    return lax.conv_general_dilated(x, w[:, None, :].astype(x.dtype), window_strides=(1,), padding="VALID",
                                    dimension_numbers=("NWC", "WIO", "NWC"),
                                    feature_group_count=x.shape[-1])


def _gated_delta(q, k, v, g, beta, s0):
    bsz, t = q.shape[0], q.shape[1]
    c = GDN_BLOCK
    n = -(-t // c)
    pad = n * c - t

    def prep(a):
        a = jnp.pad(a, [(0, 0), (0, pad)] + [(0, 0)] * (a.ndim - 2))
        a = a.reshape((bsz, n, c) + a.shape[2:])
        return jnp.moveaxis(a, [1, 2], [0, 3])

    q, k, v, g, beta = prep(q), prep(k), prep(v), prep(g), prep(beta)
    g = jnp.cumsum(g, axis=-1)
    tri_incl = jnp.tril(jnp.ones((c, c), dtype=bool))
    tri_strict = jnp.tril(jnp.ones((c, c), dtype=bool), -1)
    decay = jnp.exp(jnp.where(tri_incl, g[..., :, None] - g[..., None, :], -jnp.inf))
    kb = k * beta[..., None]
    lmat = jnp.where(tri_strict, jnp.einsum("...id,...jd->...ij", kb, k) * decay, 0.0)
    amat = lmat + jnp.eye(c, dtype=lmat.dtype)
    u = lax.linalg.triangular_solve(amat, v * beta[..., None], left_side=True, lower=True, unit_diagonal=True)
    w = lax.linalg.triangular_solve(amat, kb * jnp.exp(g)[..., None], left_side=True, lower=True,
                                    unit_diagonal=True)

    def step(s, xs):
        qc, kc, uc, wc, gc, dc = xs
        v_new = uc - jnp.einsum("bhcd,bhde->bhce", wc, s)
        intra = jnp.einsum("bhid,bhjd->bhij", qc, kc) * dc
        o = jnp.einsum("bhcd,bhde->bhce", qc * jnp.exp(gc)[..., None], s) + jnp.einsum("bhij,bhje->bhie", intra, v_new)
        glast = gc[..., -1]
        s = s * jnp.exp(glast)[..., None, None] + jnp.einsum(
            "bhcd,bhce->bhde", kc * jnp.exp(glast[..., None] - gc)[..., None], v_new)
        return s, o

    s_fin, o = lax.scan(step, s0, (q, k, u, w, g, decay))
    o = jnp.moveaxis(o, [0, 3], [1, 2])
    o = o.reshape((bsz, n * c) + o.shape[3:])[:, :t]
    return o, s_fin


def _layer(x, attend, conv_past, ssm0, lam_init, p):
    f32 = jnp.float32
    bsz, t, _ = x.shape
    h = _rmsnorm(x, p["norm1"])
    proj = h @ p["w_in"]
    qa, ka, va, qkv_b, z_b, beta_b, alpha_b, gate_a, gate_b = jnp.split(proj, SPLITS, axis=-1)

    qa = _rmsnorm(qa.reshape(bsz, t, A_HEADS, 2, A_HEAD_DIM), p["q_norm"])
    ka = _rmsnorm(ka.reshape(bsz, t, A_HEADS, 2, A_HEAD_DIM), p["k_norm"])
    va = va.reshape(bsz, t, A_HEADS, A_V_DIM)
    lam = (jnp.exp(jnp.sum(p["lambda_q1"].astype(f32) * p["lambda_k1"].astype(f32)))
           - jnp.exp(jnp.sum(p["lambda_q2"].astype(f32) * p["lambda_k2"].astype(f32))) + lam_init)
    o_a = attend(qa, ka, va, lam)
    o_a = (_rmsnorm(o_a, p["sub_norm"]) * (1.0 - lam_init)).reshape(bsz, t, A_V_WIDTH)

    conv_in = jnp.concatenate([conv_past.astype(qkv_b.dtype), qkv_b], axis=1)
    new_conv = conv_in[:, conv_in.shape[1] - (CONV_WIDTH - 1):]
    qkv = jax.nn.silu(_causal_conv(conv_in, p["conv_w"]).astype(f32)).reshape(bsz, t, 3, B_HEADS, B_HEAD_DIM)
    qb = _l2norm(qkv[:, :, 0]) * B_SCALE
    kb = _l2norm(qkv[:, :, 1])
    vb = qkv[:, :, 2]
    beta = jax.nn.sigmoid(beta_b.astype(f32))
    g = -jnp.exp(p["A_log"].astype(f32)) * jax.nn.softplus(alpha_b.astype(f32) + p["dt_bias"].astype(f32))
    o_b, s_new = _gated_delta(qb, kb, vb, g, beta, ssm0.astype(f32))
    z = jax.nn.silu(z_b.astype(f32)).reshape(bsz, t, B_HEADS, B_HEAD_DIM)
    o_b = (_rmsnorm(o_b, p["gdn_norm"]) * z).reshape(bsz, t, B_WIDTH).astype(x.dtype)

    y_a = o_a @ p["w_br_a"]
    y_b = o_b @ p["w_br_b"]
    mix = jax.nn.sigmoid(gate_a + p["b_gate"][0]) * y_a + jax.nn.sigmoid(gate_b + p["b_gate"][1]) * y_b
    x = x + mix @ p["w_out"]

    h2 = _rmsnorm(x, p["norm2"])
    x = x + jnp.square(jax.nn.relu(h2 @ p["w_up"])) @ p["w_down"]
    k_rows = ka.reshape(bsz, t, A_HEADS, 2 * A_HEAD_DIM)
    return x, k_rows, va, s_new.astype(ssm0.dtype), new_conv


def setup_inputs(seed: int = 0) -> dict:
    key = jax.random.key(seed)
    ks = jax.random.split(key, 32)
    f32 = jnp.float32

    def nrm(k, shape, scale):
        return jax.random.normal(k, shape, f32) * scale

    dt = jnp.exp(jax.random.uniform(ks[20], (DEPTH, B_HEADS), f32, math.log(1e-3), math.log(1e-1)))
    return {
        "x_prompt": nrm(ks[0], (BATCH, SEQ, D_MODEL), 1.0),
        "x_sample": nrm(ks[1], (DEC_BATCH, DEC_SEQ, D_MODEL), 1.0),
        "cache_attn_k": nrm(ks[2], (DEPTH, DEC_BATCH, PAST_LEN, A_HEADS, 2 * A_HEAD_DIM), 1.0),
        "cache_attn_v": nrm(ks[3], (DEPTH, DEC_BATCH, PAST_LEN, A_HEADS, A_V_DIM), 1.0),
        "state_gdn": nrm(ks[4], (DEPTH, DEC_BATCH, B_HEADS, B_HEAD_DIM, B_HEAD_DIM), 0.05),
        "state_conv": nrm(ks[5], (DEPTH, DEC_BATCH, CONV_WIDTH - 1, 3 * B_WIDTH), 1.0),
        "meta_tokens": nrm(ks[6], (N_META, D_MODEL), 1.0),
        "rel_bias": nrm(ks[7], (N_BUCKETS, A_HEADS), 0.1),
        "norm1": 1.0 + nrm(ks[8], (DEPTH, D_MODEL), 0.02),
        "w_in": nrm(ks[9], (DEPTH, D_MODEL, N_IN), D_MODEL ** -0.5),
        "b_gate": nrm(ks[10], (DEPTH, 2, D_MODEL), 0.02),
        "q_norm": 1.0 + nrm(ks[11], (DEPTH, A_HEAD_DIM), 0.02),
        "k_norm": 1.0 + nrm(ks[12], (DEPTH, A_HEAD_DIM), 0.02),
        "lambda_q1": nrm(ks[13], (DEPTH, A_HEAD_DIM), 0.1),
        "lambda_k1": nrm(ks[14], (DEPTH, A_HEAD_DIM), 0.1),
        "lambda_q2": nrm(ks[15], (DEPTH, A_HEAD_DIM), 0.1),
        "lambda_k2": nrm(ks[16], (DEPTH, A_HEAD_DIM), 0.1),
        "sub_norm": 1.0 + nrm(ks[17], (DEPTH, A_V_DIM), 0.02),
        "conv_w": nrm(ks[18], (DEPTH, CONV_WIDTH, 3 * B_WIDTH), CONV_WIDTH ** -0.5),
        "A_log": jnp.log(jax.random.uniform(ks[19], (DEPTH, B_HEADS), f32, 1.0, 16.0)),
        "dt_bias": dt + jnp.log(-jnp.expm1(-dt)),
        "gdn_norm": 1.0 + nrm(ks[21], (DEPTH, B_HEAD_DIM), 0.02),
        "w_br_a": nrm(ks[22], (DEPTH, A_V_WIDTH, D_MODEL), A_V_WIDTH ** -0.5),
        "w_br_b": nrm(ks[23], (DEPTH, B_WIDTH, D_MODEL), B_WIDTH ** -0.5),
        "w_out": nrm(ks[24], (DEPTH, D_MODEL, D_MODEL), D_MODEL ** -0.5),
        "norm2": 1.0 + nrm(ks[25], (DEPTH, D_MODEL), 0.02),
        "w_up": nrm(ks[26], (DEPTH, D_MODEL, D_FF), D_MODEL ** -0.5),
        "w_down": nrm(ks[27], (DEPTH, D_FF, D_MODEL), D_FF ** -0.5),
    }


def reference(x_prompt, x_sample, cache_attn_k, cache_attn_v, state_gdn, state_conv, meta_tokens, rel_bias,
              norm1, w_in, b_gate, q_norm, k_norm, lambda_q1, lambda_k1, lambda_q2, lambda_k2, sub_norm,
              conv_w, A_log, dt_bias, gdn_norm, w_br_a, w_br_b, w_out, norm2, w_up, w_down):
    bp = x_prompt.shape[0]
    bs = x_sample.shape[0]
    xp = jnp.concatenate([jnp.broadcast_to(meta_tokens[None].astype(x_prompt.dtype), (bp, N_META, D_MODEL)),
                          x_prompt], axis=1)
    xs = x_sample
    kp_l, vp_l, sp_l, cp_l, ks_l, vs_l, ss_l, cs_l = [], [], [], [], [], [], [], []
    for l in range(DEPTH):
        p = {"norm1": norm1[l], "w_in": w_in[l], "b_gate": b_gate[l], "q_norm": q_norm[l], "k_norm": k_norm[l],
             "lambda_q1": lambda_q1[l], "lambda_k1": lambda_k1[l], "lambda_q2": lambda_q2[l],
             "lambda_k2": lambda_k2[l], "sub_norm": sub_norm[l], "conv_w": conv_w[l], "A_log": A_log[l],
             "dt_bias": dt_bias[l], "gdn_norm": gdn_norm[l], "w_br_a": w_br_a[l], "w_br_b": w_br_b[l],
             "w_out": w_out[l], "norm2": norm2[l], "w_up": w_up[l], "w_down": w_down[l]}
        lam_init = _lambda_init(l)
        xp, kp, vp, sp, cp = _layer(
            xp, functools.partial(_prompt_attend, rel_table=rel_bias),
            jnp.zeros((bp, CONV_WIDTH - 1, 3 * B_WIDTH), state_conv.dtype),
            jnp.zeros((bp, B_HEADS, B_HEAD_DIM, B_HEAD_DIM), state_gdn.dtype), lam_init, p)
        xs, ks_, vs_, ss, cs = _layer(
            xs, functools.partial(_sample_attend, rel_table=rel_bias, k_past=cache_attn_k[l], v_past=cache_attn_v[l]),
            state_conv[l], state_gdn[l], lam_init, p)
        kp_l.append(kp); vp_l.append(vp); sp_l.append(sp); cp_l.append(cp)
        ks_l.append(ks_); vs_l.append(vs_); ss_l.append(ss); cs_l.append(cs)
    y_prompt = xp[:, N_META:]
    y_sample = xs
    return (y_prompt, y_sample, jnp.stack(kp_l), jnp.stack(vp_l), jnp.stack(sp_l), jnp.stack(cp_l),
            jnp.stack(ks_l), jnp.stack(vs_l), jnp.stack(ss_l), jnp.stack(cs_l))
```

```python
import contextlib
import math
from collections import defaultdict

import numpy as np
import concourse.bass as bass
import concourse.mybir as mybir
from concourse.bass_utils import run_bass_kernel_spmd

F32 = mybir.dt.float32
BF16 = mybir.dt.bfloat16
I32 = mybir.dt.int32
ALU = mybir.AluOpType
AF = mybir.ActivationFunctionType
AX = mybir.AxisListType

SEM_LIMIT = 1000
DMA_SEMS = 24
NEG = -30000.0


class V:
    __slots__ = ("ap", "key", "lo", "hi", "page", "track")

    def __init__(self, ap, key, lo, hi, page, track=True):
        self.ap = ap
        self.key = key
        self.lo = lo
        self.hi = hi
        self.page = page
        self.track = track

    def with_ap(self, ap):
        return V(ap, self.key, self.lo, self.hi, self.page, self.track)


class T:
    def __init__(self, ap, name, shape, dram=False, esize=4, base_off=0, page=2048, track=True, whole=False):
        self.whole = whole
        self.ap = ap
        self.name = name
        self.shape = list(shape)
        self.dram = dram
        self.esize = esize
        self.base_off = base_off
        self.page = page
        self.track = track
        fs = self.shape if dram else self.shape[1:]
        st = []
        acc = 1
        for s in reversed(fs):
            st.append(acc)
            acc *= s
        self.fstrides = list(reversed(st))

    def __getitem__(self, key):
        if not isinstance(key, tuple):
            key = (key,)
        ap = self.ap[key]
        fs = self.shape if self.dram else self.shape[1:]
        k2 = list(key) if self.dram else list(key[1:])
        while len(k2) < len(fs):
            k2.append(slice(None))
        lo = 0
        hi = 0
        for k, s, st in zip(k2, fs, self.fstrides):
            if isinstance(k, slice):
                a = 0 if k.start is None else k.start
                b = s if k.stop is None else k.stop
            else:
                a = k
                b = k + 1
            lo += a * st
            hi += (b - 1) * st
        hi += 1
        if self.whole:
            return V(ap, self.name, 0, self.page, self.page, self.track)
        return V(ap, self.name, self.base_off + lo * self.esize, self.base_off + hi * self.esize, self.page, self.track)

    def full(self):
        return self[tuple(slice(None) for _ in self.shape)]


class Op:
    __slots__ = ("eng", "fn", "deps", "id", "is_dma", "has_dependents", "sig", "dma_sem", "dma_val")

    def __init__(self, eng, fn, is_dma):
        self.eng = eng
        self.fn = fn
        self.deps = set()
        self.is_dma = is_dma
        self.has_dependents = False
        self.sig = None
        self.dma_sem = None
        self.dma_val = None


ENGS = ("pe", "act", "dve", "pool", "sp")


class Prog:
    def __init__(self, nc):
        self.nc = nc
        self.ops = []
        self.hist = defaultdict(list)

    def op(self, eng, fn, reads=(), writes=(), dma=False):
        o = Op(eng, fn, dma)
        o.id = len(self.ops)
        self.ops.append(o)
        deps = o.deps
        tag = eng + ("_dma" if dma else "")
        hist = self.hist
        for v in reads:
            if not v.track:
                continue
            lo, hi = v.lo, v.hi
            for pg in range(lo // v.page, (hi - 1) // v.page + 1):
                for rec in hist[(v.key, pg)]:
                    if rec[2] == "W" and rec[0] < hi and lo < rec[1]:
                        if rec[4] == "pe" and tag == "pe":
                            continue
                        deps.add(rec[3])
        for v in writes:
            if not v.track:
                continue
            lo, hi = v.lo, v.hi
            for pg in range(lo // v.page, (hi - 1) // v.page + 1):
                h = hist[(v.key, pg)]
                keep = []
                for rec in h:
                    if rec[0] < hi and lo < rec[1]:
                        if not (rec[4] == "pe" and tag == "pe"):
                            deps.add(rec[3])
                        if lo <= rec[0] and rec[1] <= hi:
                            continue
                    keep.append(rec)
                keep.append([lo, hi, "W", o.id, tag])
                hist[(v.key, pg)] = keep
        for v in reads:
            if not v.track:
                continue
            lo, hi = v.lo, v.hi
            for pg in range(lo // v.page, (hi - 1) // v.page + 1):
                h = hist[(v.key, pg)]
                found = False
                if not dma:
                    for r in h:
                        if r[2] == "R" and r[0] == lo and r[1] == hi and r[4] == tag:
                            r[3] = o.id
                            found = True
                            break
                if not found:
                    h.append([lo, hi, "R", o.id, tag])
        deps.discard(o.id)
        return o

    def dma(self, eng, out, in_, **kw):
        def fn(e):
            return e.dma_start(out=out.ap, in_=in_.ap, **kw)
        return self.op(eng, fn, reads=[in_], writes=[out], dma=True)

    def emit(self):
        nc = self.nc
        ops = self.ops
        for o in ops:
            for d in o.deps:
                ops[d].has_dependents = True
        cnt = {e: 0 for e in ENGS}
        dma_cnt = {e: 0 for e in ENGS}
        for o in ops:
            if o.is_dma:
                i = dma_cnt[o.eng]
                dma_cnt[o.eng] += 1
                o.dma_sem = (o.eng, i % DMA_SEMS)
                o.dma_val = 16 * (i // DMA_SEMS + 1)
            elif o.has_dependents:
                cnt[o.eng] += 1
                o.sig = cnt[o.eng]
        n_epochs = {e: (cnt[e] + SEM_LIMIT - 1) // SEM_LIMIT for e in ENGS}
        stack = contextlib.ExitStack()
        sems = {}
        for e in ENGS:
            for ep in range(max(1, n_epochs[e])):
                sems[(e, ep)] = stack.enter_context(nc.semaphore(f"s_{e}_{ep}"))
            if dma_cnt[e]:
                for i in range(DMA_SEMS):
                    sems[("dma", e, i)] = stack.enter_context(nc.semaphore(f"d_{e}_{i}"))
        per_eng = {e: [] for e in ENGS}
        for o in ops:
            per_eng[o.eng].append(o)
        waited = {e: {} for e in ENGS}
        waits = {}
        for o in ops:
            w = {}
            for d in o.deps:
                p = ops[d]
                if p.is_dma:
                    key = ("dma",) + p.dma_sem
                    val = p.dma_val
                else:
                    key = ("c", p.eng)
                    val = p.sig
                if val > w.get(key, 0):
                    w[key] = val
            if o.is_dma:
                key = ("dma",) + o.dma_sem
                if o.dma_val > 16 and o.dma_val - 16 > w.get(key, 0):
                    w[key] = o.dma_val - 16
            wl = []
            wd = waited[o.eng]
            for key, val in w.items():
                if wd.get(key, 0) >= val:
                    continue
                wd[key] = val
                wl.append((key, val))
            waits[o.id] = wl
        final = {}
        for e in ENGS:
            if dma_cnt[e]:
                fl = []
                for i in range(DMA_SEMS):
                    n = len(range(i, dma_cnt[e], DMA_SEMS))
                    if n:
                        fl.append((("dma", e, i), 16 * n))
                final[e] = fl

        def sem_of(key, val):
            if key[0] == "dma":
                return sems[("dma", key[1], key[2])], val
            e = key[1]
            ep = (val - 1) // SEM_LIMIT
            return sems[(e, ep)], val - ep * SEM_LIMIT

        def run(engname, engobj):
            for o in per_eng[engname]:
                for key, val in waits[o.id]:
                    s, v = sem_of(key, val)
                    engobj.wait_ge(s, v)
                ins = o.fn(engobj)
                if o.is_dma:
                    ins.then_inc(sems[("dma",) + o.dma_sem], 16)
                elif o.sig is not None:
                    s, v = sem_of(("c", engname), o.sig)
                    ins.then_inc(s, 1)
            for key, val in final.get(engname, []):
                s, v = sem_of(key, val)
                engobj.wait_ge(s, v)

        with nc.Block() as block:
            @block.sync
            def _(e):
                run("sp", e)

            @block.scalar
            def _(e):
                run("act", e)

            @block.vector
            def _(e):
                run("dve", e)

            @block.gpsimd
            def _(e):
                run("pool", e)

            @block.tensor
            def _(e):
                run("pe", e)
        stack.close()
        self.stats = {e: len(per_eng[e]) for e in ENGS}


D = 1024
SEQ = 2048
NMETA = 16
TP = NMETA + SEQ
PAST = 1024
DSEQ = 64
NIN = 9232
OFF_KA, OFF_VA, OFF_B, OFF_Z, OFF_BETA, OFF_ALPHA, OFF_GA, OFF_GB = 1024, 2048, 3072, 6144, 7168, 7176, 7184, 8208
DFF = 4096
EPS = 1e-6
LAM_INIT = 0.8 - 0.6 * math.exp(-0.3 * 0)
A_SCALE = 0.125
B_SCALE = 128 ** -0.5
NPS = 2
NSS = 4
GW = 1024
GL = 1152
GOFF = 384

B_Q, B_K, B_V, B_H, B_BA, B_M, B_WO, B_WU, B_WD, NBLK = 0, 2, 4, 6, 14, 15, 23, 25, 33, 41


def _consts():
    c = {}
    c["identf"] = np.eye(128, dtype=np.float32)
    p = np.arange(64)[:, None]
    f = np.arange(64)[None, :]
    def rep(m):
        return np.ascontiguousarray(np.broadcast_to(m[:, None, :], (64, 8, 64)).reshape(64, 512)).astype(np.float32)
    c["negU_incl"] = rep(np.where(f >= p, 0.0, NEG))
    c["negU_strict"] = rep(np.where(f > p, 0.0, NEG))
    c["negL_strict"] = rep(np.where(f < p, 0.0, NEG))
    c["identrep"] = rep(np.eye(64))
    bm = np.zeros((8, 8, 64), np.float32)
    for h in range(8):
        bm[h, h, :] = 1.0
    c["blockmask"] = bm.reshape(8, 512)
    pp = np.arange(128)[:, None]
    cc = np.arange(GW)[None, :] - GOFF
    c["maskG"] = np.where(np.floor_divide(cc, 64) >= np.floor_divide(pp, 64), 0.0, NEG).astype(np.float32)
    lo = [0, 1, 2, 3, 4, 5, 6, 7, 8, 12, 16, 23, 32, 46, 64, 91]
    hi = lo[1:] + [10 ** 9]
    oh = np.zeros((32, GL), np.float32)
    for i in range(GL):
        rel = 511 - i
        n = abs(rel)
        b = 0
        for k in range(16):
            if lo[k] <= n < hi[k]:
                b = k
        if rel > 0:
            b += 16
        oh[b, i] = 1.0
    c["onehot"] = oh
    return c


def build_program(stage=99, dbg=False):
    nc = bass.Bass("TRN2", target_bir_lowering=False)
    P = Prog(nc)
    es = contextlib.ExitStack()

    def dram(name, shape, kind, dt=F32, page=1 << 20, track=False):
        ap = nc.dram_tensor(name, shape, dt, kind=kind).ap()
        return T(ap, name, shape, dram=True, esize=(2 if dt == BF16 else 4), page=page, track=track)

    def din(name, shape):
        return dram(name, shape, "ExternalInput")

    def dout(name, shape):
        return dram(name, shape, "ExternalOutput")

    xp = din("xp", [NPS, SEQ, D])
    xs = din("xs", [NSS * DSEQ, D])
    ck = din("ck", [NSS, PAST, D])
    cv = din("cv", [NSS, PAST, D])
    sg = din("sg", [NSS, 8, 128, 128])
    sc = din("sc", [NSS, 3, 3072])
    meta = din("meta", [NMETA, D])
    relb = din("relb", [32, 8])
    norm1 = din("norm1", [D])
    w_in = din("w_in", [D, NIN])
    b_gate = din("b_gate", [2 * D])
    q_norm = din("q_norm", [64])
    k_norm = din("k_norm", [64])
    lq1 = din("lq1", [64]); lk1 = din("lk1", [64]); lq2 = din("lq2", [64]); lk2 = din("lk2", [64])
    sub_norm = din("sub_norm", [128])
    conv_w = din("conv_w", [4 * 3072])
    a_log = din("a_log", [8])
    dt_bias = din("dt_bias", [8])
    gdn_norm = din("gdn_norm", [128])
    w_bra = din("w_bra", [D, D]); w_brb = din("w_brb", [D, D]); w_out = din("w_out", [D, D])
    norm2 = din("norm2", [D])
    w_up = din("w_up", [D, DFF]); w_down = din("w_down", [DFF, D])
    cst = {k: din("c_" + k, list(v.shape)) for k, v in _consts().items()}
    yp = dout("yp", [NPS, SEQ, D]); ys = dout("ys", [NSS * DSEQ, D])
    kp = dout("kp", [NPS, TP, D]); vp = dout("vp", [NPS, TP, D])
    gp = dout("gp", [NPS, 8, 128, 128]); cpo = dout("cpo", [NPS, 3, 3072])
    kso = dout("kso", [NSS * DSEQ, D]); vso = dout("vso", [NSS * DSEQ, D])
    gso = dout("gso", [NSS, 8, 128, 128]); cso = dout("cso", [NSS, 3, 3072])
    WS = dram("WS", [NBLK, 128, 4096], "Internal", BF16, page=1 << 20, track=True)
    KTd = dram("KTd", [NPS, 8, 128, TP], "Internal", BF16, page=1 << 16, track=True)
    Vd = dram("Vd", [NPS, TP, D], "Internal", BF16, page=1 << 16, track=True)
    KTs = dram("KTs", [NSS, 8, 128, PAST + DSEQ], "Internal", BF16, page=1 << 16, track=True)
    FD = dram("FD", [8, 128, GL], "Internal", F32, page=1 << 16, track=True)
    Vsd = dram("Vsd", [NSS, DSEQ, D], "Internal", BF16, page=1 << 16, track=True)
    Vcd = dram("Vcd", [NSS, PAST, D], "Internal", BF16, page=1 << 16, track=True)

    ARENA = 206 * 1024
    arena_h = es.enter_context(nc.sbuf_tensor("arena", [128, ARENA // 4], F32))
    top = [0]

    def alloc(shape, dt=F32):
        n = int(np.prod(shape))
        esz = 2 if dt == BF16 else 4
        nb = (n * esz + 31) // 32 * 32
        off = top[0]
        top[0] += nb
        assert top[0] <= ARENA, ("arena overflow", top[0])
        ap = arena_h[:, off // 4:(off + nb) // 4]
        if dt != F32:
            ap = ap.bitcast(dt)
        ap = ap[:, 0:n]
        if len(shape) == 2:
            ap = ap.rearrange("p (a b) -> p a b", a=shape[0])
        elif len(shape) == 3:
            ap = ap.rearrange("p (a b c) -> p a b c", a=shape[0], b=shape[1])
        return T(ap, "arena", [128] + list(shape), esize=esz, base_off=off)

    psb = []
    psb16 = []
    for i in range(8):
        h = es.enter_context(nc.psum_tensor(f"ps{i}", [128, 512], F32))
        psb.append(T(h, f"ps{i}", [128, 512], page=4096, whole=True))
        psb16.append(T(h[:, :].bitcast(BF16), f"ps{i}", [128, 1024], esize=2, page=4096, whole=True))
    rot = [0]

    def pbank(pool=8):
        i = rot[0] % pool
        rot[0] += 1
        return i

    def rw(*vs):
        return [v for v in vs if isinstance(v, V)]

    def A_(x):
        return x.ap if isinstance(x, V) else x

    def mm(out, lhsT, rhs, start=True, stop=True):
        P.op("pe", lambda e: e.matmul(out=out.ap, lhsT=lhsT.ap, rhs=rhs.ap, start=start, stop=stop), reads=[lhsT, rhs], writes=[out])

    def tr(out, in_, ident):
        P.op("pe", lambda e: e.transpose(out=out.ap, in_=in_.ap, identity=ident.ap), reads=[in_, ident], writes=[out])

    def act(out, in_, func, bias=None, scale=None, accum=None):
        kw = {}
        if bias is not None:
            kw["bias"] = A_(bias)
        if scale is not None:
            kw["scale"] = A_(scale)
        if accum is not None:
            kw["accum_out"] = accum.ap
        P.op("act", lambda e: e.activation(out=out.ap, in_=in_.ap, func=func, **kw), reads=rw(in_, bias, scale), writes=rw(out, accum))

    def tt(out, a, b, op, eng="dve"):
        P.op(eng, lambda e: e.tensor_tensor(out=out.ap, in0=a.ap, in1=b.ap, op=op), reads=[a, b], writes=[out])

    def ts(out, a, s1, op0, s2=None, op1=None, eng="dve"):
        if op1 is None:
            P.op(eng, lambda e: e.tensor_scalar(out=out.ap, in0=a.ap, scalar1=A_(s1), scalar2=0.0, op0=op0, op1=ALU.add), reads=rw(a, s1), writes=[out])
        else:
            P.op(eng, lambda e: e.tensor_scalar(out=out.ap, in0=a.ap, scalar1=A_(s1), scalar2=A_(s2), op0=op0, op1=op1), reads=rw(a, s1, s2), writes=[out])

    def stt(out, a, s, b, op0, op1, eng="dve"):
        P.op(eng, lambda e: e.scalar_tensor_tensor(out=out.ap, in0=a.ap, scalar=A_(s), in1=b.ap, op0=op0, op1=op1), reads=rw(a, s, b), writes=[out])

    def cp(out, in_, eng="dve"):
        if eng == "act":
            P.op("act", lambda e: e.copy(out=out.ap, in_=in_.ap), reads=[in_], writes=[out])
        else:
            P.op(eng, lambda e: e.tensor_copy(out=out.ap, in_=in_.ap), reads=[in_], writes=[out])

    def red(out, in_, op=ALU.add):
        P.op("dve", lambda e: e.tensor_reduce(out=out.ap, in_=in_.ap, axis=AX.X, op=op), reads=[in_], writes=[out])

    def recip(out, in_):
        P.op("dve", lambda e: e.reciprocal(out=out.ap, in_=in_.ap), reads=[in_], writes=[out])

    def memset(v, val, eng="dve"):
        P.op(eng, lambda e: e.memset(v.ap, val), writes=[v])

    def rsqrt(out, in_, scale, tmp):
        act(tmp, in_, AF.Sqrt, bias=EPS, scale=scale)
        recip(out, tmp)

    def bc3(v, shape):
        return v.with_ap(v.ap.unsqueeze(2).to_broadcast(shape))

    def dap(t, off, pat):
        return t.full().with_ap(bass.AP(t.ap.tensor, off, pat))

    def ws_k8(b):
        return WS.ap[b].rearrange("p (k c) -> p k c", k=8)

    def cast_piece(b, off, w, src, c0):
        dst = WS[b].with_ap(ws_k8(b)[:, :, off:off + w])
        s = src.full().with_ap(src.ap[:, c0:c0 + w].rearrange("(k p) c -> p k c", p=128))
        P.dma("pool", dst, s)

    def cast_block(b):
        if B_Q <= b < B_K:
            cast_piece(b, 0, 512, w_in, (b - B_Q) * 512)
        elif B_K <= b < B_V:
            cast_piece(b, 0, 512, w_in, OFF_KA + (b - B_K) * 512)
        elif B_V <= b < B_H:
            cast_piece(b, 0, 512, w_in, OFF_VA + (b - B_V) * 512)
        elif b == B_BA:
            cast_piece(B_BA, 0, 16, w_in, OFF_BETA)
        elif B_H <= b < B_BA:
            h = b - B_H
            for j in range(3):
                cast_piece(b, j * 128, 128, w_in, OFF_B + j * 1024 + h * 128)
            cast_piece(b, 384, 128, w_in, OFF_Z + h * 128)
        elif B_M <= b < B_WO:
            oc = b - B_M
            cast_piece(b, 0, 128, w_in, OFF_GA + oc * 128)
            cast_piece(b, 128, 128, w_in, OFF_GB + oc * 128)
            cast_piece(b, 256, 128, w_bra, oc * 128)
            cast_piece(b, 384, 128, w_brb, oc * 128)
        elif B_WO <= b < B_WU:
            cast_piece(b, 0, 512, w_out, (b - B_WO) * 512)
        elif B_WU <= b < B_WD:
            cast_piece(b, 0, 512, w_up, (b - B_WU) * 512)
        else:
            oc = b - B_WD
            dst = WS[b].with_ap(WS.ap[b].rearrange("p (f c) -> p f c", f=32))
            s_ = w_down.full().with_ap(w_down.ap[:, oc * 128:(oc + 1) * 128].rearrange("(f p) c -> p f c", p=128))
            P.dma("pool", dst, s_)

    cast_order = ([B_K, B_K + 1, B_V, B_V + 1, B_BA] + [B_H + h for h in range(8)] + [B_Q, B_Q + 1]
                  + [B_M + i for i in range(8)] + [B_WO, B_WO + 1] + [B_WU + i for i in range(8)] + [B_WD + i for i in range(8)])
    cast_done = [0]

    def cast_upto(n):
        while cast_done[0] < min(n, len(cast_order)):
            cast_block(cast_order[cast_done[0]])
            cast_done[0] += 1
    cast_upto(4)

    identf = alloc([128]); P.dma("sp", identf.full(), cst["identf"].full())
    identb = alloc([128], BF16); cp(identb.full(), identf.full())
    ones_bf = alloc([128], BF16); memset(ones_bf.full(), 1.0)
    ones8 = alloc([128]); memset(ones8.full(), 1.0)
    negU_incl = alloc([8, 64]); P.dma("sp", negU_incl[0:64], cst["negU_incl"].full().with_ap(cst["negU_incl"].ap.rearrange("p (a b) -> p a b", a=8)))
    negU_strict = alloc([8, 64]); P.dma("sp", negU_strict[0:64], cst["negU_strict"].full().with_ap(cst["negU_strict"].ap.rearrange("p (a b) -> p a b", a=8)))
    negL_strict = alloc([8, 64]); P.dma("sp", negL_strict[0:64], cst["negL_strict"].full().with_ap(cst["negL_strict"].ap.rearrange("p (a b) -> p a b", a=8)))
    identrep = alloc([8, 64]); P.dma("sp", identrep[0:64], cst["identrep"].full().with_ap(cst["identrep"].ap.rearrange("p (a b) -> p a b", a=8)))
    blockmask = alloc([8, 64]); P.dma("sp", blockmask[0:8], cst["blockmask"].full().with_ap(cst["blockmask"].ap.rearrange("p (a b) -> p a b", a=8)))
    norm1_bc = alloc([D]); P.dma("sp", norm1_bc.full(), dap(norm1, 0, [[0, 128], [1, D]]))
    norm2_bc = alloc([D]); P.dma("sp", norm2_bc.full(), dap(norm2, 0, [[0, 128], [1, D]]))
    qn_bc = alloc([8, 64]); P.dma("sp", qn_bc.full(), dap(q_norm, 0, [[0, 128], [0, 8], [1, 64]]))
    kn_bc = alloc([8, 64]); P.dma("sp", kn_bc.full(), dap(k_norm, 0, [[0, 128], [0, 8], [1, 64]]))
    small = alloc([64])
    P.dma("sp", small[:, 0:1], dap(gdn_norm, 0, [[1, 128], [1, 1]]))
    P.dma("sp", small[:, 1:2], dap(sub_norm, 0, [[1, 128], [1, 1]]))
    ts(small[:, 1:2], small[:, 1:2], 1.0 - LAM_INIT, ALU.mult)
    P.dma("sp", small[0:8, 3:4], dap(dt_bias, 0, [[1, 8], [1, 1]]))
    P.dma("sp", small[0:8, 5:6], dap(a_log, 0, [[1, 8], [1, 1]]))
    act(small[0:8, 4:5], small[0:8, 5:6], AF.Exp)
    ts(small[0:8, 4:5], small[0:8, 4:5], -1.0, ALU.mult)
    lam4 = alloc([4, 64])
    for i, t in enumerate((lq1, lk1, lq2, lk2)):
        P.dma("sp", lam4[:, i, :], dap(t, 0, [[0, 128], [1, 64]]))
    tt(lam4[:, 0, :], lam4[:, 0, :], lam4[:, 1, :], ALU.mult)
    tt(lam4[:, 2, :], lam4[:, 2, :], lam4[:, 3, :], ALU.mult)
    red(small[:, 6:7], lam4[:, 0, :]); red(small[:, 7:8], lam4[:, 2, :])
    act(small[:, 6:8], small[:, 6:8], AF.Exp)
    tt(small[:, 8:9], small[:, 7:8], small[:, 6:7], ALU.subtract)
    ts(small[:, 2:3], small[:, 8:9], -LAM_INIT, ALU.add)
    b15 = alloc([8]); P.dma("sp", b15.full(), dap(relb, 15 * 8, [[0, 128], [1, 8]]))
    rowsA = alloc([128]); P.dma("sp", rowsA[0:16, :], b_gate.full().with_ap(b_gate.ap.rearrange("(t p) -> t p", p=128)))
    rowsC = alloc([128]); P.dma("sp", rowsC[0:96, :], conv_w.full().with_ap(conv_w.ap.rearrange("(t p) -> t p", p=128)))
    bgT = alloc([16])
    cwT = alloc([96])
    pb = pbank()
    tr(psb[pb][:, 0:16], rowsA[0:16, :], identf[0:16, 0:16])
    tr(psb[pb][:, 16:112], rowsC[0:96, :], identf[0:96, 0:96])
    cp(bgT.full(), psb[pb][:, 0:16])
    cp(cwT.full(), psb[pb][:, 16:112])
    G = alloc([8, GW], BF16)
    m0 = top[0]
    onehot = alloc([GL]); P.dma("sp", onehot[0:32, :], cst["onehot"].full())
    tab = alloc([8]); P.dma("sp", tab[0:32, :], relb.full())
    tabrep = alloc([8, 128])
    cp(tabrep[0:32], bc3(tab[0:32, :], [32, 8, 128]))
    maskG = alloc([GW]); P.dma("sp", maskG.full(), cst["maskG"].full())
    frep = alloc([GL])
    gsk = alloc([GW])
    for h in range(8):
        for j in range(3):
            pb = pbank()
            mm(psb[pb][:, 0:384], tabrep[0:32, h, :], onehot[0:32, j * 384:(j + 1) * 384])
            ts(frep[:, j * 384:(j + 1) * 384], psb[pb][:, 0:384], 1.0 / A_SCALE, ALU.mult)
        P.dma("sp", FD[h], frep.full())
        P.dma("sp", gsk.full(), FD[h].with_ap(bass.AP(FD.ap.tensor, h * 128 * GL + 127, [[GL - 1, 128], [1, GW]])))
        tt(G[:, h, :], gsk.full(), maskG.full(), ALU.add)
    top[0] = m0

    S_meta = alloc([8, 128])
    ctx_meta = alloc([24, 3])
    KTm = alloc([8, 16], BF16)
    Vm = alloc([1024], BF16)
    S_cur = alloc([8, 128])
    S_bf = alloc([8, 128], BF16)
    ctx_cur = alloc([24, 4, 3])
    NSLOT = 3
    wring = [alloc([4096], BF16) for _ in range(NSLOT)]
    wcnt = [0]

    def wload(b):
        cast_upto(cast_order.index(b) + 4)
        slot = wring[wcnt[0] % NSLOT]
        wcnt[0] += 1
        if b == B_BA:
            P.dma("sp", wk8(slot)[:, :, 0:16], WS[b].with_ap(ws_k8(b)[:, :, 0:16]))
        else:
            P.dma("sp", slot.full(), WS[b])
        return slot

    def wk8(slot):
        return T(slot.ap.rearrange("p (k c) -> p k c", k=8), "arena", [128, 8, 512], esize=2, base_off=slot.base_off)

    def wf32(slot):
        return T(slot.ap.rearrange("p (f c) -> p f c", f=32), "arena", [128, 32, 128], esize=2, base_off=slot.base_off)

    base_top = top[0]

    def run_tile(kind, s=0, t=0):
        top[0] = base_top
        if kind == "meta":
            NT, ST, nst, nseg, L, C = 16, 16, 1, 1, 16, 16
        elif kind == "prompt":
            NT, ST, nst, nseg, L, C = 512, 128, 4, 1, 512, 64
        else:
            NT, ST, nst, nseg, L, C = 256, 128, 2, 4, 64, 64
        nch = NT // C
        xtok = alloc([nst, D])
        xnT = alloc([8, NT], BF16)
        sstat = alloc([32])

        for st in range(nst):
            if kind == "meta":
                src = meta.full()
            elif kind == "prompt":
                src = xp[s, t * 512 + st * 128: t * 512 + (st + 1) * 128, :]
            else:
                src = xs[st * 128:(st + 1) * 128, :]
            P.dma("sp", xtok[0:ST, st, :], src)

        def norm_T(norm_bc):
            m = top[0]
            junk = alloc([D])
            xnb = alloc([D], BF16)
            for st in range(nst):
                memset(sstat[0:ST, st:st + 1], 0.0)
                act(junk[0:ST, :], xtok[0:ST, st, :], AF.Square, accum=sstat[0:ST, st:st + 1])
                rsqrt(sstat[0:ST, 8 + st:9 + st], sstat[0:ST, st:st + 1], 1.0 / D, sstat[0:ST, 16 + st:17 + st])
                stt(xnb[0:ST, :], xtok[0:ST, st, :], sstat[0:ST, 8 + st:9 + st], norm_bc[0:ST, :], ALU.mult, ALU.mult)
                pb = pbank()
                for kc in range(8):
                    tr(psb16[pb][:, kc * ST:(kc + 1) * ST], xnb[0:ST, kc * 128:(kc + 1) * 128], identb[0:ST, 0:ST])
                src = psb16[pb][:, 0:8 * ST]
                cp(xnT[:, :, st * ST:(st + 1) * ST], src.with_ap(src.ap.rearrange("p (k t) -> p k t", k=8)), eng="act")
            top[0] = m

        norm_T(norm1_bc)

        oaT = alloc([8, NT], BF16) if kind != "meta" else None
        vnew_s = alloc([4, D], BF16) if kind == "sample" else None
        m_attn = top[0]
        qT = alloc([8, NT], BF16) if kind != "meta" else None

        def proj_tok(blk_id, half, which):
            slot = wk8(wload(blk_id))
            m = top[0]
            sq = alloc([512]); t1 = alloc([512]); kn = alloc([512]); kb = alloc([512], BF16)
            for st in range(nst):
                pb = pbank()
                for kc in range(8):
                    mm(psb[pb][0:ST, :], xnT[:, kc, st * ST:(st + 1) * ST], slot[:, kc, :], start=(kc == 0), stop=(kc == 7))
                ps = psb[pb]
                cols = slice(half * 512, (half + 1) * 512)
                if which in ("q", "k"):
                    act(sq[0:ST, :], ps[0:ST, :], AF.Square)
                    sqv = sq[0:ST, :]
                    red(sstat[0:ST, 24:32], sqv.with_ap(sqv.ap.rearrange("p (a b) -> p a b", a=8)))
                    rsqrt(sstat[0:ST, 24:32], sstat[0:ST, 24:32], 1.0 / 64, sstat[0:ST, 24:32])
                    psv = ps[0:ST, :]
                    t1v = t1[0:ST, :]
                    tt(t1v.with_ap(t1v.ap.rearrange("p (a b) -> p a b", a=8)), psv.with_ap(psv.ap.rearrange("p (a b) -> p a b", a=8)),
                       bc3(sstat[0:ST, 24:32], [ST, 8, 64]), ALU.mult)
                    wbc = (qn_bc if which == "q" else kn_bc)[0:ST]
                    if which == "k":
                        tt(kn[0:ST, :], t1v, wbc.with_ap(wbc.ap.rearrange("p a b -> p (a b)")), ALU.mult)
                        cp(kb[0:ST, :], kn[0:ST, :], eng="pool")
                    else:
                        tt(kb[0:ST, :], t1v, wbc.with_ap(wbc.ap.rearrange("p a b -> p (a b)")), ALU.mult)
                    pb2 = pbank()
                    for hh in range(4):
                        tr(psb16[pb2][:, hh * ST:(hh + 1) * ST], kb[0:ST, hh * 128:(hh + 1) * 128], identb[0:ST, 0:ST])
                    src = psb16[pb2][:, 0:4 * ST]
                    srcv = src.with_ap(src.ap.rearrange("p (k t) -> p k t", k=4))
                    if which == "q":
                        cp(qT[:, half * 4:(half + 1) * 4, st * ST:(st + 1) * ST], srcv, eng="act")
                    else:
                        ktile = alloc([4, ST], BF16)
                        cp(ktile.full(), srcv, eng="act")
                        if kind == "meta":
                            cp(KTm[:, half * 4:(half + 1) * 4, :], ktile.full(), eng="pool")
                            for s2 in range(NPS):
                                P.dma("pool", kp[s2, 0:16, cols], kn[0:ST, :])
                        elif kind == "prompt":
                            tok0 = NMETA + t * 512 + st * 128
                            P.dma("pool", kp[s, tok0:tok0 + 128, cols], kn[0:ST, :])
                            P.dma("sp", KTd[s, half * 4:(half + 1) * 4, :, tok0:tok0 + 128].with_ap(
                                KTd.ap[s, half * 4:(half + 1) * 4, :, tok0:tok0 + 128].rearrange("h p t -> p h t")), ktile.full())
                        else:
                            P.dma("pool", kso[st * 128:(st + 1) * 128, cols], kn[0:ST, :])
                            for q2 in range(2):
                                sq_ = st * 2 + q2
                                P.dma("sp", KTs[sq_, half * 4:(half + 1) * 4, :, PAST:PAST + 64].with_ap(
                                    KTs.ap[sq_, half * 4:(half + 1) * 4, :, PAST:PAST + 64].rearrange("h p t -> p h t")),
                                    ktile[:, :, q2 * 64:(q2 + 1) * 64])
                else:
                    cp(kn[0:ST, :], ps[0:ST, :], eng="act")
                    if kind == "meta":
                        cp(Vm[0:ST, cols], kn[0:ST, :], eng="pool")
                        for s2 in range(NPS):
                            P.dma("pool", vp[s2, 0:16, cols], kn[0:ST, :])
                    elif kind == "prompt":
                        tok0 = NMETA + t * 512 + st * 128
                        cp(kb[0:ST, :], kn[0:ST, :], eng="pool")
                        P.dma("pool", vp[s, tok0:tok0 + 128, cols], kn[0:ST, :])
                        P.dma("sp", Vd[s, tok0:tok0 + 128, cols], kb[0:ST, :])
                    else:
                        P.dma("pool", vso[st * 128:(st + 1) * 128, cols], kn[0:ST, :])
                        cp(kb[0:ST, :], kn[0:ST, :], eng="pool")
                        for q2 in range(2):
                            P.dma("sp", Vsd[st * 2 + q2, :, cols], kb[q2 * 64:(q2 + 1) * 64, :])
            top[0] = m

        if kind != "meta":
            proj_tok(B_Q, 0, "q"); proj_tok(B_Q + 1, 1, "q")
        proj_tok(B_K, 0, "k"); proj_tok(B_K + 1, 1, "k")
        proj_tok(B_V, 0, "v"); proj_tok(B_V + 1, 1, "v")
        if stage < 2:
            return
        if kind != "meta":
            m = top[0]
            NQ = 512 if kind == "prompt" else 64
            nkeys = (NMETA + (t + 1) * 512) if kind == "prompt" else (PAST + 64)
            ktb = [alloc([TP], BF16) for _ in range(2)]
            vtb = [alloc([17, 128], BF16) for _ in range(2)]
            pT = [alloc([512], BF16) for _ in range(4)]
            o1 = alloc([512]); o2 = alloc([512]); rr = alloc([512]); osq = alloc([512], BF16)
            pcount = [0]
            segs = [0] if kind == "prompt" else list(range(4))
            hl = [(sg_, h) for sg_ in segs for h in range(8)]

            def load_kv(i):
                sg_, h = hl[i]
                kt = ktb[i % 2]; vt = vtb[i % 2]
                if kind == "prompt":
                    P.dma("sp", kt[:, 16:nkeys], KTd[s, h, :, 16:nkeys])
                    for g4 in range(t + 1):
                        r0 = NMETA + g4 * 512
                        P.dma("sp", vt[:, g4 * 4:(g4 + 1) * 4, :], Vd[s, r0:r0 + 512, h * 128:(h + 1) * 128].with_ap(
                            Vd.ap[s, r0:r0 + 512, h * 128:(h + 1) * 128].rearrange("(c p) e -> p c e", p=128)))
                else:
                    P.dma("sp", kt[:, 0:nkeys], KTs[sg_, h, :, 0:nkeys])
                    for g4 in range(2):
                        P.dma("sp", vt[:, g4 * 4:(g4 + 1) * 4, :], Vcd[sg_, g4 * 512:(g4 + 1) * 512, h * 128:(h + 1) * 128].with_ap(
                            Vcd.ap[sg_, g4 * 512:(g4 + 1) * 512, h * 128:(h + 1) * 128].rearrange("(c p) e -> p c e", p=128)))
            if kind == "sample":
                memset(vnew_s[64:128], 0.0)
                for kt_ in ktb:
                    memset(kt_[:, PAST + 64:PAST + 128], 0.0)
                for sq_ in range(NSS):
                    P.dma("sp", vnew_s[0:64, sq_, :], Vsd[sq_])
            load_kv(0)
            for i, (sg_, h) in enumerate(hl):
                if i + 1 < len(hl):
                    load_kv(i + 1)
                kt = ktb[i % 2]; vt = vtb[i % 2]
                q0c = sg_ * 64 if kind == "sample" else 0
                blocks = []
                if kind == "prompt":
                    blocks.append((KTm[:, h, :], Vm[0:16, h * 128:(h + 1) * 128], 16, (GOFF + 16) if t == 0 else None))
                    for kc in range((t + 1) * 4):
                        delta = kc * 128 - t * 512
                        win = (GOFF - delta) if delta >= -128 else None
                        blocks.append((kt[:, 16 + kc * 128:16 + (kc + 1) * 128], vt[:, kc, :], 128, win))
                else:
                    for kc in range(8):
                        win = (GOFF + 128) if kc == 7 else None
                        blocks.append((kt[:, kc * 128:(kc + 1) * 128], vt[:, kc, :], 128, win))
                    blocks.append((kt[:, PAST:PAST + 128], vnew_s[:, sg_, h * 128:(h + 1) * 128], 128, GOFF))
                import os as _os
                _sk = _os.environ.get("ATT_SKIP", "")
                if kind == "sample" and _sk:
                    nb_ = []
                    for bi_, blk in enumerate(blocks):
                        typ = "new" if bi_ == 8 else ("win7" if bi_ == 7 else "far")
                        if typ not in _sk:
                            nb_.append(blk)
                    blocks = nb_
                nb = len(blocks)
                for bi, (kv, vv, nk, win) in enumerate(blocks):
                    for mp in range(2):
                        pbS = pbank(4)
                        S = psb[pbS][0:nk, 0:NQ]
                        mm(S, V(kv.ap[mp * 64:(mp + 1) * 64, :], kv.key, kv.lo, kv.hi, kv.page), qT[mp * 64:(mp + 1) * 64, h, q0c:q0c + NQ],
                           start=True, stop=(win is None))
                        if win is not None:
                            mm(S, identb[0:nk, 0:nk], G[0:nk, h, win:win + NQ], start=False, stop=True)
                        pt = pT[pcount[0] % 4]; pcount[0] += 1
                        if win is None:
                            act(pt[0:nk, 0:NQ], S, AF.Exp, bias=b15[0:nk, h:h + 1], scale=A_SCALE)
                        else:
                            act(pt[0:nk, 0:NQ], S, AF.Exp, scale=A_SCALE)
                        mm(psb[4 + mp][:, 0:NQ], vv, pt[0:nk, 0:NQ], start=(bi == 0), stop=(bi == nb - 1))
                        mm(psb[6 + mp][:, 0:NQ], ones_bf[0:nk, :], pt[0:nk, 0:NQ], start=(bi == 0), stop=(bi == nb - 1))
                recip(rr[:, 0:NQ], psb[6][:, 0:NQ])
                tt(o1[:, 0:NQ], psb[4][:, 0:NQ], rr[:, 0:NQ], ALU.mult)
                recip(rr[:, 0:NQ], psb[7][:, 0:NQ])
                tt(o2[:, 0:NQ], psb[5][:, 0:NQ], rr[:, 0:NQ], ALU.mult)
                stt(o1[:, 0:NQ], o2[:, 0:NQ], small[:, 2:3], o1[:, 0:NQ], ALU.mult, ALU.add)
                act(osq[:, 0:NQ], o1[:, 0:NQ], AF.Square)
                pbn = pbank(4)
                mm(psb[pbn][:, 0:NQ], ones_bf.full(), osq[:, 0:NQ])
                rsqrt(rr[:, 0:NQ], psb[pbn][:, 0:NQ], 1.0 / 128, o2[:, 0:NQ])
                stt(oaT[:, h, q0c:q0c + NQ], o1[:, 0:NQ], small[:, 1:2], rr[:, 0:NQ], ALU.mult, ALU.mult)
            top[0] = m
        top[0] = m_attn
        if dbg and kind == "prompt" and s == 0 and t == 0:
            P.dma("pool", dbg_oa.full(), oaT.full())
        if stage < 3:
            return
        obT = alloc([8, NT], BF16) if kind != "meta" else None
        m_gdn = top[0]
        qg = alloc([8, NT], BF16); kg = alloc([8, NT], BF16); vg = alloc([8, NT], BF16)
        sz = alloc([8, NT], BF16) if kind != "meta" else None
        og = alloc([8, NT], BF16) if kind != "meta" else None
        cb = alloc([8, NT])
        slotBA = wk8(wload(B_BA))
        pb = pbank()
        for kc in range(8):
            mm(psb[pb][0:8, 0:NT], slotBA[:, kc, 0:8], xnT[:, kc, :], start=(kc == 0), stop=(kc == 7))
        pb2 = pbank()
        for kc in range(8):
            mm(psb[pb2][0:8, 0:NT], slotBA[:, kc, 8:16], xnT[:, kc, :], start=(kc == 0), stop=(kc == 7))
        act(cb[0:8, 4, :], psb[pb][0:8, 0:NT], AF.Sigmoid)
        act(cb[0:8, 6, :], psb[pb][0:8, 0:NT], AF.Exp, scale=-1.0)
        act(cb[0:8, 6, :], cb[0:8, 6, :], AF.Ln, bias=1.0)
        act(cb[0:8, 7, :], psb[pb2][0:8, 0:NT], AF.Exp, bias=small[0:8, 3:4])
        act(cb[0:8, 7, :], cb[0:8, 7, :], AF.Ln, bias=1.0)
        ts(cb[0:8, 0, :], cb[0:8, 7, :], small[0:8, 4:5], ALU.mult)
        a_, b_ = 0, 7
        sh = 1
        while sh < C:
            av = cb[0:8, a_, :]; bv = cb[0:8, b_, :]
            a3 = av.with_ap(av.ap.rearrange("p (c l) -> p c l", l=C)); b3 = bv.with_ap(bv.ap.rearrange("p (c l) -> p c l", l=C))
            cp(V(b3.ap[:, :, 0:sh], bv.key, bv.lo, bv.hi, bv.page), V(a3.ap[:, :, 0:sh], av.key, av.lo, av.hi, av.page))
            tt(V(b3.ap[:, :, sh:C], bv.key, bv.lo, bv.hi, bv.page), V(a3.ap[:, :, sh:C], av.key, av.lo, av.hi, av.page),
               V(a3.ap[:, :, 0:C - sh], av.key, av.lo, av.hi, av.page), ALU.add)
            a_, b_ = b_, a_
            sh *= 2
        if a_ != 0:
            cp(cb[0:8, 0, :], cb[0:8, a_, :])
        tt(cb[0:8, 1, :], cb[0:8, 0, :], cb[0:8, 6, :], ALU.subtract)
        act(cb[0:8, 2, :], cb[0:8, 0, :], AF.Exp)
        gv = cb[0:8, 0, :]
        g3 = gv.with_ap(gv.ap.rearrange("p (c l) -> p c l", l=C))
        kdv = cb[0:8, 3, :]
        kd3 = kdv.with_ap(kdv.ap.rearrange("p (c l) -> p c l", l=C))
        tt(kd3, V(g3.ap[:, :, C - 1:C].to_broadcast([8, nch, C]), gv.key, gv.lo, gv.hi, gv.page), g3, ALU.subtract)
        act(cb[0:8, 3, :], cb[0:8, 3, :], AF.Exp)
        tt(cb[0:8, 5, :], cb[0:8, 4, :], cb[0:8, 2, :], ALU.mult)

        m_conv = top[0]
        cin = [alloc([nseg, L + 3]) for _ in range(3)]
        cacc = alloc([nseg, L])
        csq = alloc([NT], BF16)
        crn = alloc([NT])
        if kind == "sample":
            scrow = alloc([3072])
            for sg_ in range(4):
                P.dma("sp", scrow[0:3, :], sc[sg_])
                for g6 in range(6):
                    pb = pbank()
                    for c4 in range(4):
                        cid_ = g6 * 4 + c4
                        tr(psb[pb][:, c4 * 3:(c4 + 1) * 3], scrow[0:3, cid_ * 128:(cid_ + 1) * 128], identf[0:3, 0:3])
                    pv_ = psb[pb][:, 0:12]
                    cp(ctx_cur[:, g6 * 4:(g6 + 1) * 4, sg_, :], pv_.with_ap(pv_.ap.rearrange("p (c w) -> p c w", c=4)))
        for h in range(8):
            slot = wk8(wload(B_H + h))
            for j in range(4):
                pb = pbank()
                for kc in range(8):
                    mm(psb[pb][:, 0:NT], slot[:, kc, j * 128:(j + 1) * 128], xnT[:, kc, :], start=(kc == 0), stop=(kc == 7))
                ps = psb[pb][:, 0:NT]
                if j == 3:
                    if kind != "meta":
                        act(sz[:, h, :], ps, AF.Silu)
                    continue
                cid = j * 8 + h
                ci = cin[j]
                if kind == "meta":
                    memset(ci[:, :, 0:3], 0.0, eng="pool")
                elif kind == "prompt":
                    cp(ci[:, 0, 0:3], (ctx_meta[:, cid, :] if t == 0 else ctx_cur[:, cid, 0, :]), eng="pool")
                else:
                    cp(ci[:, :, 0:3], ctx_cur[:, cid, 0:4, :], eng="pool")
                cp(ci[:, :, 3:3 + L], ps.with_ap(ps.ap.rearrange("p (s l) -> p s l", s=nseg)), eng="act")
                if kind == "meta":
                    cp(ctx_meta[:, cid, :], ci[:, 0, L:L + 3], eng="pool")
                else:
                    cp(ctx_cur[:, cid, 0:nseg, :], ci[:, :, L:L + 3], eng="pool")
                ts(cacc.full(), ci[:, :, 0:L], cwT[:, cid:cid + 1], ALU.mult)
                for w in range(1, 4):
                    stt(cacc.full(), ci[:, :, w:w + L], cwT[:, w * 24 + cid:w * 24 + cid + 1], cacc.full(), ALU.mult, ALU.add)
                cflat = cacc.full().with_ap(cacc.ap.rearrange("p s l -> p (s l)"))
                if j == 2:
                    act(vg[:, h, :], cflat, AF.Silu)
                else:
                    act(cflat, cflat, AF.Silu)
                    act(csq.full(), cflat, AF.Square)
                    pbn = pbank()
                    mm(psb[pbn][:, 0:NT], ones_bf.full(), csq.full())
                    act(crn.full(), psb[pbn][:, 0:NT], AF.Sqrt, bias=EPS, scale=1.0)
                    recip(crn.full(), crn.full())
                    if j == 0:
                        stt(qg[:, h, :], cflat, B_SCALE, crn.full(), ALU.mult, ALU.mult)
                    else:
                        tt(kg[:, h, :], cflat, crn.full(), ALU.mult)
        if (kind == "prompt" and t == 3) or kind == "sample":
            tls = [alloc([512]) for _ in range(2)]
            for sg_ in range(nseg):
                for g6 in range(6):
                    tl = tls[g6 % 2]
                    pb = pbank()
                    for c4 in range(4):
                        tr(psb[pb][0:3, c4 * 128:(c4 + 1) * 128], ctx_cur[:, g6 * 4 + c4, sg_, :], identf.full())
                    cp(tl[0:3, :], psb[pb][0:3, :], eng="act")
                    dst_ = cpo[s, :, g6 * 512:(g6 + 1) * 512] if kind == "prompt" else cso[sg_, :, g6 * 512:(g6 + 1) * 512]
                    P.dma("pool", dst_, tl[0:3, :])

        top[0] = m_conv
        nlev = {64: 5, 16: 3}[C]
        if C == 64:
            nU_i, nU_s, nL_s, idr, bmk = negU_incl[0:C], negU_strict[0:C], negL_strict[0:C], identrep[0:C], blockmask[0:8]
        else:
            cm = []
            for src_, np_ in ((negU_incl, C), (negU_strict, C), (negL_strict, C), (identrep, C), (blockmask, 8)):
                d_ = alloc([8, C])
                cp(d_[0:np_], src_[0:np_, :, 0:C])
                cm.append(d_[0:np_])
            nU_i, nU_s, nL_s, idr, bmk = cm
        gd = [alloc([8, C]) for _ in range(4)]
        ngc = alloc([C])
        tokc = alloc([32])
        DTi = alloc([8, C], BF16); NDT = alloc([8, C], BF16); NTD = alloc([8, C], BF16)
        Pm = [alloc([8, C], BF16) for _ in range(2)]; PmT = [alloc([8, C], BF16) for _ in range(2)]
        Rm = [alloc([8, C], BF16) for _ in range(2)]
        MT = alloc([8, C], BF16); qgc = alloc([8, C], BF16); nwT = alloc([8, C], BF16)
        bv_ = alloc([8, 128], BF16); kbg = alloc([8, 128], BF16); kdc = alloc([8, 128], BF16); vnw = alloc([8, 128], BF16)
        egl = alloc([8])

        def fl(v):
            return v.with_ap(v.ap.rearrange("p a b -> p (a b)"))

        for ci_ in range(nch):
            sgi = ci_ if kind == "sample" else 0
            cs = slice(ci_ * C, (ci_ + 1) * C)
            W8 = 8 * C
            if kind == "meta":
                if ci_ == 0:
                    memset(S_cur.full(), 0.0); memset(S_bf.full(), 0.0)
            elif kind == "prompt":
                if ci_ == 0 and t == 0:
                    cp(S_cur.full(), S_meta.full()); cp(S_bf.full(), S_meta.full(), eng="act")
            else:
                P.dma("sp", S_cur.full(), sg[sgi].with_ap(sg.ap[sgi].rearrange("h d e -> d h e")))
                cp(S_bf.full(), S_cur.full(), eng="act")
            for k_, row in enumerate((0, 1, 2)):
                src = cb[0:8, row, cs]
                tt(gd[k_][0:8], bmk, V(src.ap.unsqueeze(1).to_broadcast([8, 8, C]), src.key, src.lo, src.hi, src.page), ALU.mult, eng="pool")
            ts(ngc[0:8, 0:C], cb[0:8, 0, cs], -1.0, ALU.mult, eng="pool")
            pbt = pbank()
            for k_, row in enumerate((4, 5, 3)):
                tr(psb[pbt][0:C, k_ * 8:(k_ + 1) * 8], cb[0:8, row, cs], identf[0:8, 0:8])
            cp(tokc[0:C, 0:24], psb[pbt][0:C, 0:24])
            pbx = pbank()
            X = psb[pbx][0:C, 0:W8]
            mm(X, ones8[0:8, 0:C], fl(gd[0][0:8]), start=True, stop=False)
            mm(X, ngc[0:8, 0:C], fl(bmk), start=False, stop=False)
            mm(X, identf[0:C, 0:C], fl(nU_i), start=False, stop=True)
            act(fl(DTi[0:C]), X, AF.Exp)
            pbx = pbank()
            X = psb[pbx][0:C, 0:W8]
            mm(X, ones8[0:8, 0:C], fl(gd[1][0:8]), start=True, stop=False)
            mm(X, ngc[0:8, 0:C], fl(bmk), start=False, stop=False)
            mm(X, identf[0:C, 0:C], fl(nU_s), start=False, stop=True)
            act(fl(NDT[0:C]), X, AF.Exp)
            ts(gd[3][0:8], gd[0][0:8], -1.0, ALU.mult, eng="pool")
            pbx = pbank()
            X = psb[pbx][0:C, 0:W8]
            mm(X, ones8[0:8, 0:C], fl(gd[3][0:8]), start=True, stop=False)
            mm(X, cb[0:8, 1, cs], fl(bmk), start=False, stop=False)
            mm(X, identf[0:C, 0:C], fl(nL_s), start=False, stop=True)
            act(fl(NTD[0:C]), X, AF.Exp)
            pbe = pbank()
            mm(psb[pbe][:, 0:W8], ones8[0:8, :], fl(gd[2][0:8]))
            pe_v = psb[pbe][:, 0:W8]
            pe3 = pe_v.with_ap(pe_v.ap.rearrange("p (h c) -> p h c", h=8))
            tt(qgc[:, :, 0:C], qg[:, :, cs], pe3, ALU.mult)
            cp(egl.full(), V(pe3.ap[:, :, C - 1], pe_v.key, pe_v.lo, pe_v.hi, pe_v.page))
            pbk = pbank(); pbq = pbank()
            for h in range(8):
                mm(psb[pbk][0:C, h * C:(h + 1) * C], kg[:, h, cs], kg[:, h, cs])
            for h in range(8):
                mm(psb[pbq][0:C, h * C:(h + 1) * C], kg[:, h, cs], qg[:, h, cs])
            stt(fl(Pm[0][0:C]), psb[pbk][0:C, 0:W8], -1.0, fl(NDT[0:C]), ALU.mult, ALU.mult)
            stt(fl(PmT[0][0:C]), psb[pbk][0:C, 0:W8], -1.0, fl(NTD[0:C]), ALU.mult, ALU.mult)
            tt(fl(MT[0:C]), psb[pbq][0:C, 0:W8], fl(DTi[0:C]), ALU.mult)
            tt(fl(Rm[0][0:C]), fl(Pm[0][0:C]), fl(idr), ALU.add, eng="pool")
            cur = 0
            for lv in range(1, nlev + 1):
                nxt = 1 - cur
                pbp = pbank(); pbpt = pbank()
                for h in range(8):
                    mm(psb[pbpt][0:C, h * C:(h + 1) * C], Pm[cur][0:C, h, :], PmT[cur][0:C, h, :])
                if lv < nlev:
                    for h in range(8):
                        mm(psb[pbp][0:C, h * C:(h + 1) * C], PmT[cur][0:C, h, :], Pm[cur][0:C, h, :])
                cp(fl(PmT[nxt][0:C]), psb[pbpt][0:C, 0:W8], eng="act")
                if lv < nlev:
                    cp(fl(Pm[nxt][0:C]), psb[pbp][0:C, 0:W8])
                pbr = pbank()
                for h in range(8):
                    mm(psb[pbr][0:C, h * C:(h + 1) * C], PmT[nxt][0:C, h, :], Rm[cur][0:C, h, :])
                tt(fl(Rm[nxt][0:C]), psb[pbr][0:C, 0:W8], fl(Rm[cur][0:C]), ALU.add)
                cur = nxt
            TT = Rm[cur]
            pbk = pbank(); pbv = pbank()
            for h in range(8):
                tr(psb16[pbk][0:C, h * 128:(h + 1) * 128], kg[:, h, cs], identb.full())
            for h in range(8):
                tr(psb16[pbv][0:C, h * 128:(h + 1) * 128], vg[:, h, cs], identb.full())
            kt3 = psb16[pbk][0:C, :]; kt3 = kt3.with_ap(kt3.ap.rearrange("p (h d) -> p h d", h=8))
            vt3 = psb16[pbv][0:C, :]; vt3 = vt3.with_ap(vt3.ap.rearrange("p (h d) -> p h d", h=8))
            tt(bv_[0:C], vt3, bc3(tokc[0:C, 0:8], [C, 8, 128]), ALU.mult)
            tt(kbg[0:C], kt3, bc3(tokc[0:C, 8:16], [C, 8, 128]), ALU.mult)
            tt(kdc[0:C], kt3, bc3(tokc[0:C, 16:24], [C, 8, 128]), ALU.mult)
            pbw = pbank()
            for h in range(8):
                mm(psb[pbw][:, h * C:(h + 1) * C], kbg[0:C, h, :], TT[0:C, h, :])
            ts(fl(nwT[:, :, 0:C]), psb[pbw][:, 0:W8], -1.0, ALU.mult)
            pv0 = pbank(); pv1 = pbank()
            for h in range(8):
                o = psb[pv0 if h < 4 else pv1][0:C, (h % 4) * 128:(h % 4 + 1) * 128]
                mm(o, TT[0:C, h, :], bv_[0:C, h, :], start=True, stop=False)
                mm(o, nwT[:, h, 0:C], S_bf[:, h, :], start=False, stop=True)
            cp(fl(vnw[0:C, 0:4, :]), psb[pv0][0:C, :], eng="act")
            cp(fl(vnw[0:C, 4:8, :]), psb[pv1][0:C, :])
            if kind != "meta":
                pbo = pbank()
                for h in range(8):
                    o = psb[pbo][:, h * C:(h + 1) * C]
                    mm(o, S_bf[:, h, :], qgc[:, h, 0:C], start=True, stop=False)
                    mm(o, vnw[0:C, h, :], MT[0:C, h, :], start=False, stop=True)
                po = psb[pbo][:, 0:W8]
                cp(og[:, :, cs], po.with_ap(po.ap.rearrange("p (h c) -> p h c", h=8)), eng="act")
            ps0 = pbank(); ps1 = pbank()
            for h in range(8):
                mm(psb[ps0 if h < 4 else ps1][:, (h % 4) * 128:(h % 4 + 1) * 128], kdc[0:C, h, :], vnw[0:C, h, :])
            tt(S_cur.full(), S_cur.full(), bc3(egl.full(), [128, 8, 128]), ALU.mult)
            tt(fl(S_cur[:, 0:4, :]), fl(S_cur[:, 0:4, :]), psb[ps0].full(), ALU.add)
            tt(fl(S_cur[:, 4:8, :]), fl(S_cur[:, 4:8, :]), psb[ps1].full(), ALU.add)
            cp(S_bf.full(), S_cur.full(), eng="act")
            if kind == "sample":
                P.dma("pool", gso[sgi].with_ap(gso.ap[sgi].rearrange("h d e -> d h e")), S_cur.full())
        if kind == "meta":
            cp(S_meta.full(), S_cur.full())
            return
        if kind == "prompt" and t == 3:
            P.dma("pool", gp[s].with_ap(gp.ap[s].rearrange("h d e -> d h e")), S_cur.full())
        top[0] = m_conv
        gsq = alloc([NT], BF16); grn = alloc([NT]); gt = alloc([NT])
        for h in range(8):
            act(gsq.full(), og[:, h, :], AF.Square)
            pbn = pbank()
            mm(psb[pbn][:, 0:NT], ones_bf.full(), gsq.full())
            rsqrt(grn.full(), psb[pbn][:, 0:NT], 1.0 / 128, gt.full())
            stt(gt.full(), og[:, h, :], small[:, 0:1], grn.full(), ALU.mult, ALU.mult)
            tt(obT[:, h, :], gt.full(), sz[:, h, :], ALU.mult, eng="pool")
        if dbg and kind == "prompt" and s == 0 and t == 0:
            P.dma("pool", dbg_ob.full(), obT.full())
        top[0] = m_gdn
        if stage < 4:
            return
        mixT = alloc([8, NT], BF16)
        sga = alloc([NT]); sgb = alloc([NT]); tmp = alloc([NT])
        for oc in range(8):
            slot = wk8(wload(B_M + oc))
            pa = pbank(); pb_ = pbank(); pya = pbank(); pyb = pbank()
            for kc in range(8):
                mm(psb[pa][:, 0:NT], slot[:, kc, 0:128], xnT[:, kc, :], start=(kc == 0), stop=(kc == 7))
            for kc in range(8):
                mm(psb[pb_][:, 0:NT], slot[:, kc, 128:256], xnT[:, kc, :], start=(kc == 0), stop=(kc == 7))
            for kc in range(8):
                mm(psb[pya][:, 0:NT], slot[:, kc, 256:384], oaT[:, kc, :], start=(kc == 0), stop=(kc == 7))
            for kc in range(8):
                mm(psb[pyb][:, 0:NT], slot[:, kc, 384:512], obT[:, kc, :], start=(kc == 0), stop=(kc == 7))
            act(sga.full(), psb[pa][:, 0:NT], AF.Sigmoid, bias=bgT[:, oc:oc + 1])
            act(sgb.full(), psb[pb_][:, 0:NT], AF.Sigmoid, bias=bgT[:, 8 + oc:9 + oc])
            tt(tmp.full(), psb[pya][:, 0:NT], sga.full(), ALU.mult)
            tt(sgb.full(), psb[pyb][:, 0:NT], sgb.full(), ALU.mult)
            tt(mixT[:, oc, :], tmp.full(), sgb.full(), ALU.add, eng="pool")
        for half in range(2):
            slot = wk8(wload(B_WO + half))
            for st in range(nst):
                pb = pbank()
                for kc in range(8):
                    mm(psb[pb][0:ST, :], mixT[:, kc, st * ST:(st + 1) * ST], slot[:, kc, :], start=(kc == 0), stop=(kc == 7))
                tt(xtok[0:ST, st, half * 512:(half + 1) * 512], xtok[0:ST, st, half * 512:(half + 1) * 512], psb[pb][0:ST, :], ALU.add)
        if stage < 5:
            return
        norm_T(norm2_bc)
        uT = alloc([32, NT], BF16)
        rl = [alloc([NT]) for _ in range(2)]
        for j in range(8):
            slot = wk8(wload(B_WU + j))
            for c4 in range(4):
                fc = j * 4 + c4
                pb = pbank()
                for kc in range(8):
                    mm(psb[pb][:, 0:NT], slot[:, kc, c4 * 128:(c4 + 1) * 128], xnT[:, kc, :], start=(kc == 0), stop=(kc == 7))
                r = rl[fc % 2]
                act(r.full(), psb[pb][:, 0:NT], AF.Relu)
                tt(uT[:, fc, :], r.full(), r.full(), ALU.mult, eng=("pool" if fc % 2 else "dve"))
        for oc in range(8):
            slot = wf32(wload(B_WD + oc))
            pb = pbank()
            for st in range(nst):
                for fc in range(32):
                    mm(psb[pb][0:ST, st * 128:(st + 1) * 128], uT[:, fc, st * ST:(st + 1) * ST], slot[:, fc, :], start=(fc == 0), stop=(fc == 31))
            pv = psb[pb][0:ST, 0:nst * 128]
            tt(xtok[0:ST, :, oc * 128:(oc + 1) * 128], xtok[0:ST, :, oc * 128:(oc + 1) * 128],
               pv.with_ap(pv.ap.rearrange("p (s c) -> p s c", s=nst)), ALU.add)
        for st in range(nst):
            if kind == "prompt":
                P.dma("pool", yp[s, t * 512 + st * 128: t * 512 + (st + 1) * 128, :], xtok[0:ST, st, :])
            else:
                P.dma("pool", ys[st * 128:(st + 1) * 128, :], xtok[0:ST, st, :])

    def cache_k_prep():
        top[0] = base_top
        ckf = [alloc([D]) for _ in range(2)]
        ckb = [alloc([D], BF16) for _ in range(2)]
        ckt = [alloc([8, 128], BF16) for _ in range(2)]
        i = 0
        for sq_ in range(NSS):
            P.dma("pool", Vcd[sq_], cv[sq_])
        for sq_ in range(NSS):
            for c in range(8):
                f = ckf[i % 2]; b = ckb[i % 2]; kt_ = ckt[i % 2]
                P.dma("sp", f.full(), ck[sq_, c * 128:(c + 1) * 128, :])
                cp(b.full(), f.full(), eng=("pool" if i % 2 else "dve"))
                pb = pbank()
                for h in range(8):
                    tr(psb16[pb][:, h * 128:(h + 1) * 128], b[:, h * 128:(h + 1) * 128], identb.full())
                src = psb16[pb].full()
                cp(kt_.full(), src.with_ap(src.ap.rearrange("p (h t) -> p h t", h=8)), eng="act")
                P.dma("sp", KTs[sq_, :, :, c * 128:(c + 1) * 128].with_ap(KTs.ap[sq_, :, :, c * 128:(c + 1) * 128].rearrange("h p t -> p h t")), kt_.full())
                i += 1

    if dbg:
        dbg_oa = dram("dbg_oa2", [128, 8, 512], "ExternalOutput", BF16)
        dbg_ob = dram("dbg_ob2", [128, 8, 512], "ExternalOutput", BF16)

    import os
    sel = os.environ.get("KTILES", "msp")
    run_tile("meta")
    if "s" in sel:
        cache_k_prep()
        run_tile("sample")
    if "p" in sel:
        for s in range(NPS):
            for t in range(4):
                run_tile("prompt", s, t)
    elif "q" in sel:
        run_tile("prompt", 0, 0)
    P.emit()
    es.close()
    return nc, P


_CACHE = {}


def kernel(x_prompt, x_sample, cache_attn_k, cache_attn_v, state_gdn, state_conv, meta_tokens, rel_bias,
           norm1, w_in, b_gate, q_norm, k_norm, lambda_q1, lambda_k1, lambda_q2, lambda_k2, sub_norm,
           conv_w, A_log, dt_bias, gdn_norm, w_br_a, w_br_b, w_out, norm2, w_up, w_down, _stage=99, _cores=8, _dbg=False):
    f = lambda a: np.ascontiguousarray(np.asarray(a, dtype=np.float32))
    key = (_stage, _dbg)
    if key not in _CACHE:
        _CACHE[key] = build_program(_stage, _dbg)
    nc, P = _CACHE[key]
    consts = _consts()
    shared = {
        "meta": f(meta_tokens), "relb": f(rel_bias), "norm1": f(norm1).reshape(-1), "w_in": f(w_in)[0],
        "b_gate": f(b_gate).reshape(-1), "q_norm": f(q_norm).reshape(-1), "k_norm": f(k_norm).reshape(-1),
        "lq1": f(lambda_q1).reshape(-1), "lk1": f(lambda_k1).reshape(-1), "lq2": f(lambda_q2).reshape(-1), "lk2": f(lambda_k2).reshape(-1),
        "sub_norm": f(sub_norm).reshape(-1), "conv_w": f(conv_w).reshape(-1), "a_log": f(A_log).reshape(-1),
        "dt_bias": f(dt_bias).reshape(-1), "gdn_norm": f(gdn_norm).reshape(-1), "w_bra": f(w_br_a)[0], "w_brb": f(w_br_b)[0],
        "w_out": f(w_out)[0], "norm2": f(norm2).reshape(-1), "w_up": f(w_up)[0], "w_down": f(w_down)[0],
    }
    for k, v in consts.items():
        shared["c_" + k] = v
    xp = f(x_prompt); xs = f(x_sample)
    ck = f(cache_attn_k)[0].reshape(32, PAST, D); cv = f(cache_attn_v)[0].reshape(32, PAST, D)
    sg = f(state_gdn)[0]; sc = f(state_conv)[0]
    in_maps = []
    for c in range(_cores):
        m = dict(shared)
        m["xp"] = xp[c * NPS:(c + 1) * NPS]
        m["xs"] = xs[c * NSS:(c + 1) * NSS].reshape(NSS * DSEQ, D)
        m["ck"] = ck[c * NSS:(c + 1) * NSS]
        m["cv"] = cv[c * NSS:(c + 1) * NSS]
        m["sg"] = sg[c * NSS:(c + 1) * NSS]
        m["sc"] = sc[c * NSS:(c + 1) * NSS]
        in_maps.append(m)
    res = run_bass_kernel_spmd(nc, in_maps, core_ids=list(range(_cores)))
    R = res.results
    cat = lambda k: np.concatenate([np.asarray(r[k], dtype=np.float32) for r in R], axis=0)
    nb = _cores * NPS
    ns = _cores * NSS
    outs = (
        cat("yp"),
        cat("ys").reshape(ns, DSEQ, D),
        cat("kp").reshape(1, nb, TP, 8, 128),
        cat("vp").reshape(1, nb, TP, 8, 128),
        cat("gp").reshape(1, nb, 8, 128, 128),
        cat("cpo").reshape(1, nb, 3, 3072),
        cat("kso").reshape(1, ns, DSEQ, 8, 128),
        cat("vso").reshape(1, ns, DSEQ, 8, 128),
        cat("gso").reshape(1, ns, 8, 128, 128),
        cat("cso").reshape(1, ns, 3, 3072),
    )
    if _dbg:
        return outs, R
    return outs
```

```python
import contextlib
import math
import os
from collections import defaultdict

import numpy as np
import concourse.bass as bass
import concourse.mybir as mybir
from concourse.bass_utils import run_bass_kernel_spmd

F32 = mybir.dt.float32
BF16 = mybir.dt.bfloat16
I32 = mybir.dt.int32
ALU = mybir.AluOpType
AF = mybir.ActivationFunctionType
AX = mybir.AxisListType

SEM_LIMIT = 1000
DMA_SEMS = 24
NEG = -30000.0


class V:
    __slots__ = ("ap", "key", "lo", "hi", "page", "track")

    def __init__(self, ap, key, lo, hi, page, track=True):
        self.ap = ap
        self.key = key
        self.lo = lo
        self.hi = hi
        self.page = page
        self.track = track

    def with_ap(self, ap):
        return V(ap, self.key, self.lo, self.hi, self.page, self.track)


class T:
    def __init__(self, ap, name, shape, dram=False, esize=4, base_off=0, page=2048, track=True, whole=False):
        self.whole = whole
        self.ap = ap
        self.name = name
        self.shape = list(shape)
        self.dram = dram
        self.esize = esize
        self.base_off = base_off
        self.page = page
        self.track = track
        fs = self.shape if dram else self.shape[1:]
        st = []
        acc = 1
        for s in reversed(fs):
            st.append(acc)
            acc *= s
        self.fstrides = list(reversed(st))

    def __getitem__(self, key):
        if not isinstance(key, tuple):
            key = (key,)
        ap = self.ap[key]
        fs = self.shape if self.dram else self.shape[1:]
        k2 = list(key) if self.dram else list(key[1:])
        while len(k2) < len(fs):
            k2.append(slice(None))
        lo = 0
        hi = 0
        for k, s, st in zip(k2, fs, self.fstrides):
            if isinstance(k, slice):
                a = 0 if k.start is None else k.start
                b = s if k.stop is None else k.stop
            else:
                a = k
                b = k + 1
            lo += a * st
            hi += (b - 1) * st
        hi += 1
        if self.whole:
            return V(ap, self.name, 0, self.page, self.page, self.track)
        return V(ap, self.name, self.base_off + lo * self.esize, self.base_off + hi * self.esize, self.page, self.track)

    def full(self):
        return self[tuple(slice(None) for _ in self.shape)]


class Op:
    __slots__ = ("eng", "fn", "deps", "id", "is_dma", "has_dependents", "sig", "dma_sem", "dma_val", "phase")

    def __init__(self, eng, fn, is_dma):
        self.eng = eng
        self.fn = fn
        self.deps = set()
        self.is_dma = is_dma
        self.has_dependents = False
        self.sig = None
        self.dma_sem = None
        self.dma_val = None


ENGS = ("pe", "act", "dve", "pool", "sp")


class Prog:
    def __init__(self, nc):
        self.nc = nc
        self.ops = []
        self.hist = defaultdict(list)

    def op(self, eng, fn, reads=(), writes=(), dma=False):
        o = Op(eng, fn, dma)
        o.phase = getattr(self, "phase", "")
        o.id = len(self.ops)
        self.ops.append(o)
        deps = o.deps
        tag = eng + ("_dma" if dma else "")
        hist = self.hist
        for v in reads:
            if not v.track:
                continue
            lo, hi = v.lo, v.hi
            for pg in range(lo // v.page, (hi - 1) // v.page + 1):
                for rec in hist[(v.key, pg)]:
                    if rec[2] == "W" and rec[0] < hi and lo < rec[1]:
                        if rec[4] == "pe" and tag == "pe":
                            continue
                        deps.add(rec[3])
                    elif rec[2] == "R" and v.key.startswith("ps") and rec[4] != tag:
                        deps.add(rec[3])
        for v in writes:
            if not v.track:
                continue
            lo, hi = v.lo, v.hi
            for pg in range(lo // v.page, (hi - 1) // v.page + 1):
                h = hist[(v.key, pg)]
                keep = []
                for rec in h:
                    if rec[0] < hi and lo < rec[1]:
                        if not (rec[4] == "pe" and tag == "pe"):
                            deps.add(rec[3])
                        if lo <= rec[0] and rec[1] <= hi:
                            continue
                    keep.append(rec)
                keep.append([lo, hi, "W", o.id, tag])
                hist[(v.key, pg)] = keep
        for v in reads:
            if not v.track:
                continue
            lo, hi = v.lo, v.hi
            for pg in range(lo // v.page, (hi - 1) // v.page + 1):
                h = hist[(v.key, pg)]
                found = False
                if not dma:
                    for r in h:
                        if r[2] == "R" and r[0] == lo and r[1] == hi and r[4] == tag:
                            r[3] = o.id
                            found = True
                            break
                if not found:
                    h.append([lo, hi, "R", o.id, tag])
        deps.discard(o.id)
        return o

    def dma(self, eng, out, in_, **kw):
        def fn(e):
            return e.dma_start(out=out.ap, in_=in_.ap, **kw)
        return self.op(eng, fn, reads=[in_], writes=[out], dma=True)

    def emit(self):
        nc = self.nc
        ops = self.ops
        for o in ops:
            for d in o.deps:
                ops[d].has_dependents = True
        cnt = {e: 0 for e in ENGS}
        dma_cnt = {e: 0 for e in ENGS}
        for o in ops:
            if o.is_dma:
                i = dma_cnt[o.eng]
                dma_cnt[o.eng] += 1
                o.dma_sem = (o.eng, i % DMA_SEMS)
                o.dma_val = 16 * (i // DMA_SEMS + 1)
            elif o.has_dependents:
                cnt[o.eng] += 1
                o.sig = cnt[o.eng]
        n_epochs = {e: (cnt[e] + SEM_LIMIT - 1) // SEM_LIMIT for e in ENGS}
        stack = contextlib.ExitStack()
        sems = {}
        for e in ENGS:
            for ep in range(max(1, n_epochs[e])):
                sems[(e, ep)] = stack.enter_context(nc.semaphore(f"s_{e}_{ep}"))
            if dma_cnt[e]:
                for i in range(DMA_SEMS):
                    sems[("dma", e, i)] = stack.enter_context(nc.semaphore(f"d_{e}_{i}"))
        per_eng = {e: [] for e in ENGS}
        for o in ops:
            per_eng[o.eng].append(o)
        waited = {e: {} for e in ENGS}
        waits = {}
        for o in ops:
            w = {}
            for d in o.deps:
                p = ops[d]
                if p.is_dma:
                    key = ("dma",) + p.dma_sem
                    val = p.dma_val
                else:
                    key = ("c", p.eng)
                    val = p.sig
                if val > w.get(key, 0):
                    w[key] = val
            if o.is_dma:
                key = ("dma",) + o.dma_sem
                if o.dma_val > 16 and o.dma_val - 16 > w.get(key, 0):
                    w[key] = o.dma_val - 16
            wl = []
            wd = waited[o.eng]
            for key, val in w.items():
                if wd.get(key, 0) >= val:
                    continue
                wd[key] = val
                wl.append((key, val))
            waits[o.id] = wl
        final = {}
        for e in ENGS:
            if dma_cnt[e]:
                fl = []
                for i in range(DMA_SEMS):
                    n = len(range(i, dma_cnt[e], DMA_SEMS))
                    if n:
                        fl.append((("dma", e, i), 16 * n))
                final[e] = fl

        def sem_of(key, val):
            if key[0] == "dma":
                return sems[("dma", key[1], key[2])], val
            e = key[1]
            ep = (val - 1) // SEM_LIMIT
            return sems[(e, ep)], val - ep * SEM_LIMIT

        def run(engname, engobj):
            for o in per_eng[engname]:
                for key, val in waits[o.id]:
                    s, v = sem_of(key, val)
                    engobj.wait_ge(s, v)
                ins = o.fn(engobj)
                if o.is_dma:
                    ins.then_inc(sems[("dma",) + o.dma_sem], 16)
                elif o.sig is not None:
                    s, v = sem_of(("c", engname), o.sig)
                    ins.then_inc(s, 1)
            for key, val in final.get(engname, []):
                s, v = sem_of(key, val)
                engobj.wait_ge(s, v)

        with nc.Block() as block:
            @block.sync
            def _(e):
                run("sp", e)

            @block.scalar
            def _(e):
                run("act", e)

            @block.vector
            def _(e):
                run("dve", e)

            @block.gpsimd
            def _(e):
                run("pool", e)

            @block.tensor
            def _(e):
                run("pe", e)
        stack.close()
        self.stats = {e: len(per_eng[e]) for e in ENGS}


D = 1024
SEQ = 2048
NMETA = 16
TP = NMETA + SEQ
PAST = 1024
DSEQ = 64
NIN = 9232
OFF_KA, OFF_VA, OFF_B, OFF_Z, OFF_BETA, OFF_ALPHA, OFF_GA, OFF_GB = 1024, 2048, 3072, 6144, 7168, 7176, 7184, 8208
DFF = 4096
EPS = 1e-6
LAM_INIT = 0.8 - 0.6 * math.exp(-0.3 * 0)
A_SCALE = 0.125
B_SCALE = 128 ** -0.5
NPS = 2
NSS = 4
GW = 1024
GL = 1152
GOFF = 384

B_Q, B_K, B_V, B_H, B_BA, B_M, B_WO, B_WU, B_WD, NBLK = 0, 2, 4, 6, 14, 15, 23, 25, 33, 41


def _consts():
    c = {}
    c["identf"] = np.eye(128, dtype=np.float32)
    p = np.arange(64)[:, None]
    f = np.arange(64)[None, :]
    def rep(m):
        return np.ascontiguousarray(np.broadcast_to(m[:, None, :], (64, 8, 64)).reshape(64, 512)).astype(np.float32)
    c["negU_incl"] = rep(np.where(f >= p, 0.0, NEG))
    c["negU_strict"] = rep(np.where(f > p, 0.0, NEG))
    c["negL_strict"] = rep(np.where(f < p, 0.0, NEG))
    c["identrep"] = rep(np.eye(64))
    bm = np.zeros((8, 8, 64), np.float32)
    for h in range(8):
        bm[h, h, :] = 1.0
    c["blockmask"] = bm.reshape(8, 512)
    pp = np.arange(128)[:, None]
    cc = np.arange(GW)[None, :] - GOFF
    c["maskG"] = np.where(np.floor_divide(cc, 64) >= np.floor_divide(pp, 64), 0.0, NEG).astype(np.float32)
    lo = [0, 1, 2, 3, 4, 5, 6, 7, 8, 12, 16, 23, 32, 46, 64, 91]
    hi = lo[1:] + [10 ** 9]
    oh = np.zeros((32, GL), np.float32)
    for i in range(GL):
        rel = 511 - i
        n = abs(rel)
        b = 0
        for k in range(16):
            if lo[k] <= n < hi[k]:
                b = k
        if rel > 0:
            b += 16
        oh[b, i] = 1.0
    c["onehot"] = oh
    return c


def build_program(stage=99, dbg=False):
    nc = bass.Bass("TRN2", target_bir_lowering=False)
    P = Prog(nc)
    es = contextlib.ExitStack()

    def dram(name, shape, kind, dt=F32, page=1 << 20, track=False):
        ap = nc.dram_tensor(name, shape, dt, kind=kind).ap()
        return T(ap, name, shape, dram=True, esize=(2 if dt == BF16 else 4), page=page, track=track)

    def din(name, shape):
        return dram(name, shape, "ExternalInput")

    def dout(name, shape):
        return dram(name, shape, "ExternalOutput")

    xp = din("xp", [NPS, SEQ, D])
    xs = din("xs", [NSS * DSEQ, D])
    ck = din("ck", [NSS, PAST, D])
    cv = din("cv", [NSS, PAST, D])
    sg = din("sg", [NSS, 8, 128, 128])
    sc = din("sc", [NSS, 3, 3072])
    meta = din("meta", [NMETA, D])
    relb = din("relb", [32, 8])
    norm1 = din("norm1", [D])
    w_in = din("w_in", [D, NIN])
    b_gate = din("b_gate", [2 * D])
    q_norm = din("q_norm", [64])
    k_norm = din("k_norm", [64])
    lq1 = din("lq1", [64]); lk1 = din("lk1", [64]); lq2 = din("lq2", [64]); lk2 = din("lk2", [64])
    sub_norm = din("sub_norm", [128])
    conv_w = din("conv_w", [4 * 3072])
    a_log = din("a_log", [8])
    dt_bias = din("dt_bias", [8])
    gdn_norm = din("gdn_norm", [128])
    w_bra = din("w_bra", [D, D]); w_brb = din("w_brb", [D, D]); w_out = din("w_out", [D, D])
    norm2 = din("norm2", [D])
    w_up = din("w_up", [D, DFF]); w_down = din("w_down", [DFF, D])
    cst = {k: din("c_" + k, list(v.shape)) for k, v in _consts().items()}
    yp = dout("yp", [NPS, SEQ, D]); ys = dout("ys", [NSS * DSEQ, D])
    kp = dout("kp", [NPS, TP, D]); vp = dout("vp", [NPS, TP, D])
    gp = dout("gp", [NPS, 8, 128, 128]); cpo = dout("cpo", [NPS, 3, 3072])
    kso = dout("kso", [NSS * DSEQ, D]); vso = dout("vso", [NSS * DSEQ, D])
    gso = dout("gso", [NSS, 8, 128, 128]); cso = dout("cso", [NSS, 3, 3072])
    WS = dram("WS", [NBLK, 128, 4096], "Internal", BF16, page=1 << 20, track=True)
    KTd = dram("KTd", [NPS, 8, 128, TP], "Internal", BF16, page=1 << 16, track=True)
    Vd = dram("Vd", [NPS, TP, D], "Internal", BF16, page=1 << 16, track=True)
    KTs = dram("KTs", [NSS, 8, 128, PAST + DSEQ], "Internal", BF16, page=1 << 16, track=True)
    FD = dram("FD", [8, 128, GL], "Internal", F32, page=1 << 16, track=True)
    Vsd = dram("Vsd", [NSS, DSEQ, D], "Internal", BF16, page=1 << 16, track=True)
    Vcd = dram("Vcd", [NSS, PAST, D], "Internal", BF16, page=1 << 16, track=True)

    ARENA = 206 * 1024
    arena_h = es.enter_context(nc.sbuf_tensor("arena", [128, ARENA // 4], F32))
    top = [0]

    def alloc(shape, dt=F32):
        n = int(np.prod(shape))
        esz = 2 if dt == BF16 else 4
        nb = (n * esz + 31) // 32 * 32
        off = top[0]
        top[0] += nb
        assert top[0] <= ARENA, ("arena overflow", top[0])
        ap = arena_h[:, off // 4:(off + nb) // 4]
        if dt != F32:
            ap = ap.bitcast(dt)
        ap = ap[:, 0:n]
        if len(shape) == 2:
            ap = ap.rearrange("p (a b) -> p a b", a=shape[0])
        elif len(shape) == 3:
            ap = ap.rearrange("p (a b c) -> p a b c", a=shape[0], b=shape[1])
        return T(ap, "arena", [128] + list(shape), esize=esz, base_off=off)

    psb = []
    psb16 = []
    for i in range(8):
        h = es.enter_context(nc.psum_tensor(f"ps{i}", [128, 512], F32))
        psb.append(T(h, f"ps{i}", [128, 512], page=4096, whole=True))
        psb16.append(T(h[:, :].bitcast(BF16), f"ps{i}", [128, 1024], esize=2, page=4096, whole=True))
    rot = [0]

    def pbank(pool=8):
        i = rot[0] % pool
        rot[0] += 1
        return i

    def rw(*vs):
        return [v for v in vs if isinstance(v, V)]

    def A_(x):
        return x.ap if isinstance(x, V) else x

    def mm(out, lhsT, rhs, start=True, stop=True):
        P.op("pe", lambda e: e.matmul(out=out.ap, lhsT=lhsT.ap, rhs=rhs.ap, start=start, stop=stop), reads=[lhsT, rhs], writes=[out])

    def tr(out, in_, ident):
        P.op("pe", lambda e: e.transpose(out=out.ap, in_=in_.ap, identity=ident.ap), reads=[in_, ident], writes=[out])

    def act(out, in_, func, bias=None, scale=None, accum=None):
        kw = {}
        if bias is not None:
            kw["bias"] = A_(bias)
        if scale is not None:
            kw["scale"] = A_(scale)
        if accum is not None:
            kw["accum_out"] = accum.ap
        P.op("act", lambda e: e.activation(out=out.ap, in_=in_.ap, func=func, **kw), reads=rw(in_, bias, scale), writes=rw(out, accum))

    def tt(out, a, b, op, eng="dve"):
        P.op(eng, lambda e: e.tensor_tensor(out=out.ap, in0=a.ap, in1=b.ap, op=op), reads=[a, b], writes=[out])

    def ts(out, a, s1, op0, s2=None, op1=None, eng="dve"):
        if op1 is None:
            P.op(eng, lambda e: e.tensor_scalar(out=out.ap, in0=a.ap, scalar1=A_(s1), scalar2=0.0, op0=op0, op1=ALU.add), reads=rw(a, s1), writes=[out])
        else:
            P.op(eng, lambda e: e.tensor_scalar(out=out.ap, in0=a.ap, scalar1=A_(s1), scalar2=A_(s2), op0=op0, op1=op1), reads=rw(a, s1, s2), writes=[out])

    def stt(out, a, s, b, op0, op1, eng="dve"):
        P.op(eng, lambda e: e.scalar_tensor_tensor(out=out.ap, in0=a.ap, scalar=A_(s), in1=b.ap, op0=op0, op1=op1), reads=rw(a, s, b), writes=[out])

    def cp(out, in_, eng="dve"):
        if eng == "act":
            P.op("act", lambda e: e.copy(out=out.ap, in_=in_.ap), reads=[in_], writes=[out])
        else:
            P.op(eng, lambda e: e.tensor_copy(out=out.ap, in_=in_.ap), reads=[in_], writes=[out])

    def red(out, in_, op=ALU.add):
        P.op("dve", lambda e: e.tensor_reduce(out=out.ap, in_=in_.ap, axis=AX.X, op=op), reads=[in_], writes=[out])

    def recip(out, in_):
        act(out, in_, AF.Ln)
        act(out, out, AF.Exp, scale=-1.0)

    def memset(v, val, eng="dve"):
        P.op(eng, lambda e: e.memset(v.ap, val), writes=[v])

    def rsqrt(out, in_, scale, tmp):
        act(tmp, in_, AF.Ln, bias=EPS, scale=scale)
        act(out, tmp, AF.Exp, scale=-0.5)

    def bc3(v, shape):
        return v.with_ap(v.ap.unsqueeze(2).to_broadcast(shape))

    deferred = []

    def defer(fn, delay=1):
        deferred.append([delay, fn])

    def group_issued():
        run_now = []
        keep = []
        for d in deferred:
            d[0] -= 1
            (run_now if d[0] <= 0 else keep).append(d)
        deferred[:] = keep
        for d in run_now:
            d[1]()

    def flush_deferred():
        while deferred:
            group_issued()

    def dap(t, off, pat):
        return t.full().with_ap(bass.AP(t.ap.tensor, off, pat))

    def ws_k8(b):
        return WS.ap[b].rearrange("p (k c) -> p k c", k=8)

    def cast_piece(b, off, w, src, c0):
        dst = WS[b].with_ap(ws_k8(b)[:, :, off:off + w])
        s = src.full().with_ap(src.ap[:, c0:c0 + w].rearrange("(k p) c -> p k c", p=128))
        P.dma("pool", dst, s)

    def cast_block(b):
        if B_Q <= b < B_K:
            cast_piece(b, 0, 512, w_in, (b - B_Q) * 512)
        elif B_K <= b < B_V:
            cast_piece(b, 0, 512, w_in, OFF_KA + (b - B_K) * 512)
        elif B_V <= b < B_H:
            cast_piece(b, 0, 512, w_in, OFF_VA + (b - B_V) * 512)
        elif b == B_BA:
            cast_piece(B_BA, 0, 16, w_in, OFF_BETA)
        elif B_H <= b < B_BA:
            h = b - B_H
            for j in range(3):
                cast_piece(b, j * 128, 128, w_in, OFF_B + j * 1024 + h * 128)
            cast_piece(b, 384, 128, w_in, OFF_Z + h * 128)
        elif B_M <= b < B_WO:
            oc = b - B_M
            cast_piece(b, 0, 128, w_in, OFF_GA + oc * 128)
            cast_piece(b, 128, 128, w_in, OFF_GB + oc * 128)
            cast_piece(b, 256, 128, w_bra, oc * 128)
            cast_piece(b, 384, 128, w_brb, oc * 128)
        elif B_WO <= b < B_WU:
            cast_piece(b, 0, 512, w_out, (b - B_WO) * 512)
        elif B_WU <= b < B_WD:
            cast_piece(b, 0, 512, w_up, (b - B_WU) * 512)
        else:
            oc = b - B_WD
            dst = WS[b].with_ap(WS.ap[b].rearrange("p (f c) -> p f c", f=32))
            s_ = w_down.full().with_ap(w_down.ap[:, oc * 128:(oc + 1) * 128].rearrange("(f p) c -> p f c", p=128))
            P.dma("pool", dst, s_)

    cast_order = ([B_K, B_K + 1, B_V, B_V + 1, B_BA] + [B_H + h for h in range(8)] + [B_Q, B_Q + 1]
                  + [B_M + i for i in range(8)] + [B_WO, B_WO + 1] + [B_WU + i for i in range(8)] + [B_WD + i for i in range(8)])
    cast_done = [0]

    def cast_upto(n):
        while cast_done[0] < min(n, len(cast_order)):
            cast_block(cast_order[cast_done[0]])
            cast_done[0] += 1
    cast_upto(4)

    identf = alloc([128]); P.dma("sp", identf.full(), cst["identf"].full())
    identb = alloc([128], BF16); cp(identb.full(), identf.full())
    ones_bf = alloc([128], BF16); memset(ones_bf.full(), 1.0)
    negU_incl = alloc([8, 64], BF16)
    negU_strict = alloc([8, 64], BF16)
    negL_strict = alloc([8, 64], BF16)
    identrep = alloc([8, 64], BF16)
    blockmask = alloc([8, 64], BF16)
    qn_bc = alloc([8, 64]); P.dma("sp", qn_bc.full(), dap(q_norm, 0, [[0, 128], [0, 8], [1, 64]]))
    kn_bc = alloc([8, 64]); P.dma("sp", kn_bc.full(), dap(k_norm, 0, [[0, 128], [0, 8], [1, 64]]))
    small = alloc([64])
    P.dma("sp", small[:, 0:1], dap(gdn_norm, 0, [[1, 128], [1, 1]]))
    P.dma("sp", small[:, 1:2], dap(sub_norm, 0, [[1, 128], [1, 1]]))
    ts(small[:, 1:2], small[:, 1:2], 1.0 - LAM_INIT, ALU.mult)
    P.dma("sp", small[0:8, 3:4], dap(dt_bias, 0, [[1, 8], [1, 1]]))
    P.dma("sp", small[0:8, 5:6], dap(a_log, 0, [[1, 8], [1, 1]]))
    act(small[0:8, 4:5], small[0:8, 5:6], AF.Exp)
    ts(small[0:8, 4:5], small[0:8, 4:5], -1.0, ALU.mult)
    b15 = alloc([8]); P.dma("sp", b15.full(), dap(relb, 15 * 8, [[0, 128], [1, 8]]))
    rowsA = alloc([128]); P.dma("sp", rowsA[0:16, :], b_gate.full().with_ap(b_gate.ap.rearrange("(t p) -> t p", p=128)))
    rowsC = alloc([128]); P.dma("sp", rowsC[0:96, :], conv_w.full().with_ap(conv_w.ap.rearrange("(t p) -> t p", p=128)))
    bgT = alloc([16])
    cwT = alloc([96])
    pb = pbank()
    tr(psb[pb][:, 0:16], rowsA[0:16, :], identf[0:16, 0:16])
    tr(psb[pb][:, 16:112], rowsC[0:96, :], identf[0:96, 0:96])
    cp(bgT.full(), psb[pb][:, 0:16])
    cp(cwT.full(), psb[pb][:, 16:112])
    G = alloc([8, GW], BF16)
    m0 = top[0]
    lam4 = alloc([4, 64])
    for i, t in enumerate((lq1, lk1, lq2, lk2)):
        P.dma("sp", lam4[:, i, :], dap(t, 0, [[0, 128], [1, 64]]))
    tt(lam4[:, 0, :], lam4[:, 0, :], lam4[:, 1, :], ALU.mult)
    tt(lam4[:, 2, :], lam4[:, 2, :], lam4[:, 3, :], ALU.mult)
    red(small[:, 6:7], lam4[:, 0, :]); red(small[:, 7:8], lam4[:, 2, :])
    act(small[:, 6:8], small[:, 6:8], AF.Exp)
    tt(small[:, 8:9], small[:, 7:8], small[:, 6:7], ALU.subtract)
    ts(small[:, 2:3], small[:, 8:9], -LAM_INIT, ALU.add)
    for nm_, dst_, rows_ in (("negU_incl", negU_incl, 64), ("negU_strict", negU_strict, 64), ("negL_strict", negL_strict, 64),
                             ("identrep", identrep, 64), ("blockmask", blockmask, 8)):
        stg_ = alloc([8, 64])
        P.dma("sp", stg_[0:rows_], cst[nm_].full().with_ap(cst[nm_].ap.rearrange("p (a b) -> p a b", a=8)))
        cp(dst_[0:rows_], stg_[0:rows_])
    onehot = alloc([GL]); P.dma("sp", onehot[0:32, :], cst["onehot"].full())
    tab = alloc([8]); P.dma("sp", tab[0:32, :], relb.full())
    tabrep = alloc([8, 128])
    cp(tabrep[0:32], bc3(tab[0:32, :], [32, 8, 128]))
    maskG = alloc([GW]); P.dma("sp", maskG.full(), cst["maskG"].full())
    frep = alloc([GL])
    gsk = alloc([GW])
    for h in range(8):
        for j in range(3):
            pb = pbank()
            mm(psb[pb][:, 0:384], tabrep[0:32, h, :], onehot[0:32, j * 384:(j + 1) * 384])
            ts(frep[:, j * 384:(j + 1) * 384], psb[pb][:, 0:384], 1.0 / A_SCALE, ALU.mult)
        P.dma("sp", FD[h], frep.full())
        P.dma("sp", gsk.full(), FD[h].with_ap(bass.AP(FD.ap.tensor, h * 128 * GL + 127, [[GL - 1, 128], [1, GW]])))
        tt(G[:, h, :], gsk.full(), maskG.full(), ALU.add)
    top[0] = m0

    S_meta = alloc([8, 128])
    ctx_meta = alloc([24, 3])
    KTm = alloc([8, 16], BF16)
    Vm = alloc([1024], BF16)
    S_cur = alloc([8, 128])
    S_bf = alloc([8, 128], BF16)
    ctx_cur = alloc([24, 4, 3])
    NSLOT = 3
    wring = [alloc([4096], BF16) for _ in range(NSLOT)]
    wcnt = [0]

    def wload(b):
        cast_upto(cast_order.index(b) + 4)
        slot = wring[wcnt[0] % NSLOT]
        wcnt[0] += 1
        if b == B_BA:
            P.dma("sp", wk8(slot)[:, :, 0:16], WS[b].with_ap(ws_k8(b)[:, :, 0:16]))
        else:
            P.dma("sp", slot.full(), WS[b])
        return slot

    def wk8(slot):
        return T(slot.ap.rearrange("p (k c) -> p k c", k=8), "arena", [128, 8, 512], esize=2, base_off=slot.base_off)

    def wf32(slot):
        return T(slot.ap.rearrange("p (f c) -> p f c", f=32), "arena", [128, 32, 128], esize=2, base_off=slot.base_off)

    base_top = top[0]

    def run_tile(kind, s=0, t=0):
        top[0] = base_top
        if kind == "meta":
            NT, ST, nst, nseg, L, C = 16, 16, 1, 1, 16, 16
        elif kind == "prompt":
            NT, ST, nst, nseg, L, C = 512, 128, 4, 1, 512, 64
        else:
            NT, ST, nst, nseg, L, C = 256, 128, 2, 4, 64, 64
        nch = NT // C
        xtok = alloc([nst, D])
        xnT = alloc([8, NT], BF16)
        sstat = alloc([32])

        for st in range(nst):
            if kind == "meta":
                src = meta.full()
            elif kind == "prompt":
                src = xp[s, t * 512 + st * 128: t * 512 + (st + 1) * 128, :]
            else:
                src = xs[st * 128:(st + 1) * 128, :]
            P.dma("sp", xtok[0:ST, st, :], src)

        def norm_T(norm_dram):
            m = top[0]
            norm_bc = alloc([D])
            P.dma("sp", norm_bc.full(), dap(norm_dram, 0, [[0, 128], [1, D]]))
            junk = alloc([D])
            xnb = alloc([D], BF16)
            for st in range(nst):
                memset(sstat[0:ST, st:st + 1], 0.0)
                act(junk[0:ST, :], xtok[0:ST, st, :], AF.Square, accum=sstat[0:ST, st:st + 1])
                rsqrt(sstat[0:ST, 8 + st:9 + st], sstat[0:ST, st:st + 1], 1.0 / D, sstat[0:ST, 16 + st:17 + st])
                stt(xnb[0:ST, :], xtok[0:ST, st, :], sstat[0:ST, 8 + st:9 + st], norm_bc[0:ST, :], ALU.mult, ALU.mult)
                pb = pbank()
                for kc in range(8):
                    tr(psb16[pb][:, kc * ST:(kc + 1) * ST], xnb[0:ST, kc * 128:(kc + 1) * 128], identb[0:ST, 0:ST])
                src = psb16[pb][:, 0:8 * ST]
                cp(xnT[:, :, st * ST:(st + 1) * ST], src.with_ap(src.ap.rearrange("p (k t) -> p k t", k=8)), eng="act")
            top[0] = m

        if stage < 0:
            return
        P.phase = kind + ":norm1"
        norm_T(norm1)
        if stage < 1:
            flush_deferred()
            return
        P.phase = kind + ":qkvproj"

        oaT = alloc([8, NT], BF16) if kind != "meta" else None
        vnew_s = alloc([4, D], BF16) if kind == "sample" else None
        m_attn = top[0]
        qT = alloc([8, NT], BF16) if kind != "meta" else None

        NB_ = 3
        sqs = [alloc([512]) for _ in range(NB_)]
        t1s = [alloc([512]) for _ in range(NB_)]
        kns = [alloc([512]) for _ in range(NB_)]
        kbs = [alloc([512], BF16) for _ in range(NB_)]
        ktiles = [alloc([4, ST], BF16) for _ in range(NB_)]
        rsts = [alloc([16]) for _ in range(NB_)]
        ptc = [0]

        def post2(which, half, cols, st, i):
            kb = kbs[i]; kn = kns[i]; ktile = ktiles[i]
            pb2 = pbank()
            for hh in range(4):
                tr(psb16[pb2][:, hh * ST:(hh + 1) * ST], kb[0:ST, hh * 128:(hh + 1) * 128], identb[0:ST, 0:ST])
            src = psb16[pb2][:, 0:4 * ST]
            srcv = src.with_ap(src.ap.rearrange("p (k t) -> p k t", k=4))
            if which == "q":
                cp(qT[:, half * 4:(half + 1) * 4, st * ST:(st + 1) * ST], srcv, eng="act")
                return
            cp(ktile.full(), srcv, eng="act")
            if kind == "meta":
                cp(KTm[:, half * 4:(half + 1) * 4, :], ktile.full(), eng="pool")
            elif kind == "prompt":
                tok0 = NMETA + t * 512 + st * 128
                P.dma("sp", KTd[s, half * 4:(half + 1) * 4, :, tok0:tok0 + 128].with_ap(
                    KTd.ap[s, half * 4:(half + 1) * 4, :, tok0:tok0 + 128].rearrange("h p t -> p h t")), ktile.full())
            else:
                for q2 in range(2):
                    sq_ = st * 2 + q2
                    P.dma("sp", KTs[sq_, half * 4:(half + 1) * 4, :, PAST:PAST + 64].with_ap(
                        KTs.ap[sq_, half * 4:(half + 1) * 4, :, PAST:PAST + 64].rearrange("h p t -> p h t")),
                        ktile[:, :, q2 * 64:(q2 + 1) * 64])

        def proj_tok(blk_id, half, which):
            slot = wk8(wload(blk_id))
            cols = slice(half * 512, (half + 1) * 512)
            for st in range(nst):
                pb = pbank()
                for kc in range(8):
                    mm(psb[pb][0:ST, :], xnT[:, kc, st * ST:(st + 1) * ST], slot[:, kc, :], start=(kc == 0), stop=(kc == 7))
                group_issued()
                i = ptc[0] % NB_
                ptc[0] += 1
                ps = psb[pb]
                sq = sqs[i]; t1 = t1s[i]; kn = kns[i]; kb = kbs[i]; rs = rsts[i]
                PTS = int(os.environ.get("PT_STOP", "9"))
                if PTS <= 1 or (which == "v" and os.environ.get("PT_VSKIP", "0") == "1"):
                    cp(sq[0:ST, :], ps[0:ST, :])
                    continue
                if which in ("q", "k"):
                    act(sq[0:ST, :], ps[0:ST, :], AF.Square)
                    sqv = sq[0:ST, :]
                    red(rs[0:ST, 0:8], sqv.with_ap(sqv.ap.rearrange("p (a b) -> p a b", a=8)))
                    PRS = int(os.environ.get("PT_RS", "2"))
                    if PRS == 2:
                        rsqrt(rs[0:ST, 0:8], rs[0:ST, 0:8], 1.0 / 64, rs[0:ST, 8:16])
                    elif PRS == 1:
                        act(rs[0:ST, 8:16], rs[0:ST, 0:8], AF.Sqrt, bias=EPS, scale=1.0 / 64)
                        P.op("dve", lambda e, rs=rs: e.reciprocal(out=rs[0:ST, 0:8].ap, in_=rs[0:ST, 8:16].ap), reads=[rs[0:ST, 8:16]], writes=[rs[0:ST, 0:8]])
                    if PTS <= 2:
                        continue
                    psv = ps[0:ST, :]
                    t1v = t1[0:ST, :]
                    tt(t1v.with_ap(t1v.ap.rearrange("p (a b) -> p a b", a=8)), psv.with_ap(psv.ap.rearrange("p (a b) -> p a b", a=8)),
                       bc3(rs[0:ST, 0:8], [ST, 8, 64]), ALU.mult)
                    wbc = (qn_bc if which == "q" else kn_bc)[0:ST]
                    wflat = wbc.with_ap(wbc.ap.rearrange("p a b -> p (a b)"))
                    tt(kb[0:ST, :], t1v, wflat, ALU.mult)
                    if which == "k":
                        tt(kn[0:ST, :], t1v, wflat, ALU.mult, eng=("pool" if os.environ.get("PT_POOLMUL", "1") == "1" else "dve"))
                        if kind == "meta":
                            for s2 in range(NPS):
                                P.dma("pool", kp[s2, 0:16, cols], kn[0:ST, :])
                        elif kind == "prompt":
                            tok0 = NMETA + t * 512 + st * 128
                            P.dma("pool", kp[s, tok0:tok0 + 128, cols], kn[0:ST, :])
                        else:
                            P.dma("pool", kso[st * 128:(st + 1) * 128, cols], kn[0:ST, :])
                    if PTS >= 4:
                        defer(lambda which=which, half=half, cols=cols, st=st, i=i: post2(which, half, cols, st, i))
                else:
                    cp(kn[0:ST, :], ps[0:ST, :], eng="act")
                    if kind == "meta":
                        if os.environ.get("PT_VCP", "1") == "1":
                            cp(Vm[0:ST, cols], ps[0:ST, :])
                        else:
                            cp(Vm[0:ST, cols], kn[0:ST, :], eng="pool")
                        for s2 in range(NPS):
                            P.dma("pool", vp[s2, 0:16, cols], kn[0:ST, :])
                    elif kind == "prompt":
                        tok0 = NMETA + t * 512 + st * 128
                        cp(kb[0:ST, :], ps[0:ST, :])
                        P.dma("pool", vp[s, tok0:tok0 + 128, cols], kn[0:ST, :])
                        P.dma("sp", Vd[s, tok0:tok0 + 128, cols], kb[0:ST, :])
                    else:
                        P.dma("pool", vso[st * 128:(st + 1) * 128, cols], kn[0:ST, :])
                        cp(kb[0:ST, :], ps[0:ST, :])
                        for q2 in range(2):
                            P.dma("sp", Vsd[st * 2 + q2, :, cols], kb[q2 * 64:(q2 + 1) * 64, :])

        if kind != "meta":
            proj_tok(B_Q, 0, "q"); proj_tok(B_Q + 1, 1, "q")
        proj_tok(B_K, 0, "k"); proj_tok(B_K + 1, 1, "k")
        proj_tok(B_V, 0, "v"); proj_tok(B_V + 1, 1, "v")
        flush_deferred()
        if stage < 2:
            return
        P.phase = kind + ":attn"
        if kind != "meta":
            m = top[0]
            NQ = 512 if kind == "prompt" else 64
            nkeys = (NMETA + (t + 1) * 512) if kind == "prompt" else (PAST + 64)
            ktb = [alloc([TP], BF16) for _ in range(2)]
            vtb = [alloc([17, 128], BF16) for _ in range(2)]
            pT = [alloc([512], BF16) for _ in range(4)]
            o1 = alloc([512]); o2 = alloc([512]); rr = alloc([512]); osq = alloc([512], BF16)
            pcount = [0]
            segs = [0] if kind == "prompt" else list(range(4))
            hl = [(sg_, h) for sg_ in segs for h in range(8)]

            def load_kv(i):
                sg_, h = hl[i]
                kt = ktb[i % 2]; vt = vtb[i % 2]
                if kind == "prompt":
                    P.dma("sp", kt[:, 16:nkeys], KTd[s, h, :, 16:nkeys])
                    for g4 in range(t + 1):
                        r0 = NMETA + g4 * 512
                        P.dma("sp", vt[:, g4 * 4:(g4 + 1) * 4, :], Vd[s, r0:r0 + 512, h * 128:(h + 1) * 128].with_ap(
                            Vd.ap[s, r0:r0 + 512, h * 128:(h + 1) * 128].rearrange("(c p) e -> p c e", p=128)))
                else:
                    P.dma("sp", kt[:, 0:nkeys], KTs[sg_, h, :, 0:nkeys])
                    for g4 in range(2):
                        P.dma("sp", vt[:, g4 * 4:(g4 + 1) * 4, :], Vcd[sg_, g4 * 512:(g4 + 1) * 512, h * 128:(h + 1) * 128].with_ap(
                            Vcd.ap[sg_, g4 * 512:(g4 + 1) * 512, h * 128:(h + 1) * 128].rearrange("(c p) e -> p c e", p=128)))
            if kind == "sample":
                memset(vnew_s[64:128], 0.0)
                for kt_ in ktb:
                    memset(kt_[:, PAST + 64:PAST + 128], 0.0)
                for sq_ in range(NSS):
                    P.dma("sp", vnew_s[0:64, sq_, :], Vsd[sq_])
            load_kv(0)
            for i, (sg_, h) in enumerate(hl):
                if i + 1 < len(hl):
                    load_kv(i + 1)
                kt = ktb[i % 2]; vt = vtb[i % 2]
                q0c = sg_ * 64 if kind == "sample" else 0
                blocks = []
                if kind == "prompt":
                    blocks.append((KTm[:, h, :], Vm[0:16, h * 128:(h + 1) * 128], 16, (GOFF + 16) if t == 0 else None))
                    for kc in range((t + 1) * 4):
                        delta = kc * 128 - t * 512
                        win = (GOFF - delta) if delta >= -128 else None
                        blocks.append((kt[:, 16 + kc * 128:16 + (kc + 1) * 128], vt[:, kc, :], 128, win))
                else:
                    for kc in range(8):
                        win = (GOFF + 128) if kc == 7 else None
                        blocks.append((kt[:, kc * 128:(kc + 1) * 128], vt[:, kc, :], 128, win))
                    blocks.append((kt[:, PAST:PAST + 128], vnew_s[:, sg_, h * 128:(h + 1) * 128], 128, GOFF))
                import os as _os
                _sk = _os.environ.get("ATT_SKIP", "")
                if kind == "sample" and _sk:
                    nb_ = []
                    for bi_, blk in enumerate(blocks):
                        typ = "new" if bi_ == 8 else ("win7" if bi_ == 7 else "far")
                        if typ not in _sk:
                            nb_.append(blk)
                    blocks = nb_
                nb = len(blocks)
                for bi, (kv, vv, nk, win) in enumerate(blocks):
                    for mp in range(2):
                        pbS = pbank(4)
                        S = psb[pbS][0:nk, 0:NQ]
                        mm(S, V(kv.ap[mp * 64:(mp + 1) * 64, :], kv.key, kv.lo, kv.hi, kv.page), qT[mp * 64:(mp + 1) * 64, h, q0c:q0c + NQ],
                           start=True, stop=(win is None))
                        if win is not None:
                            mm(S, identb[0:nk, 0:nk], G[0:nk, h, win:win + NQ], start=False, stop=True)
                        pt = pT[pcount[0] % 4]; pcount[0] += 1
                        if win is None:
                            act(pt[0:nk, 0:NQ], S, AF.Exp, bias=b15[0:nk, h:h + 1], scale=A_SCALE)
                        else:
                            act(pt[0:nk, 0:NQ], S, AF.Exp, scale=A_SCALE)
                        mm(psb[4 + mp][:, 0:NQ], vv, pt[0:nk, 0:NQ], start=(bi == 0), stop=(bi == nb - 1))
                        mm(psb[6 + mp][:, 0:NQ], ones_bf[0:nk, :], pt[0:nk, 0:NQ], start=(bi == 0), stop=(bi == nb - 1))
                recip(rr[:, 0:NQ], psb[6][:, 0:NQ])
                tt(o1[:, 0:NQ], psb[4][:, 0:NQ], rr[:, 0:NQ], ALU.mult)
                recip(rr[:, 0:NQ], psb[7][:, 0:NQ])
                tt(o2[:, 0:NQ], psb[5][:, 0:NQ], rr[:, 0:NQ], ALU.mult)
                stt(o1[:, 0:NQ], o2[:, 0:NQ], small[:, 2:3], o1[:, 0:NQ], ALU.mult, ALU.add)
                act(osq[:, 0:NQ], o1[:, 0:NQ], AF.Square)
                pbn = pbank(4)
                mm(psb[pbn][:, 0:NQ], ones_bf.full(), osq[:, 0:NQ])
                rsqrt(rr[:, 0:NQ], psb[pbn][:, 0:NQ], 1.0 / 128, o2[:, 0:NQ])
                stt(oaT[:, h, q0c:q0c + NQ], o1[:, 0:NQ], small[:, 1:2], rr[:, 0:NQ], ALU.mult, ALU.mult)
            top[0] = m
        top[0] = m_attn
        if dbg and kind == "prompt" and s == 0 and t == 0:
            P.dma("pool", dbg_oa.full(), oaT.full())
        if stage < 3:
            return
        P.phase = kind + ":gdnproj"
        obT = alloc([8, NT], BF16) if kind != "meta" else None
        m_gdn = top[0]
        qg = alloc([8, NT], BF16); kg = alloc([8, NT], BF16); vg = alloc([8, NT], BF16)
        sz = alloc([8, NT], BF16) if kind != "meta" else None
        og = alloc([8, NT], BF16) if kind != "meta" else None
        cb = alloc([8, NT])
        slotBA = wk8(wload(B_BA))
        pb = pbank()
        for kc in range(8):
            mm(psb[pb][0:8, 0:NT], slotBA[:, kc, 0:8], xnT[:, kc, :], start=(kc == 0), stop=(kc == 7))
        pb2 = pbank()
        for kc in range(8):
            mm(psb[pb2][0:8, 0:NT], slotBA[:, kc, 8:16], xnT[:, kc, :], start=(kc == 0), stop=(kc == 7))
        act(cb[0:8, 4, :], psb[pb][0:8, 0:NT], AF.Sigmoid)
        act(cb[0:8, 6, :], psb[pb][0:8, 0:NT], AF.Exp, scale=-1.0)
        act(cb[0:8, 6, :], cb[0:8, 6, :], AF.Ln, bias=1.0)
        act(cb[0:8, 7, :], psb[pb2][0:8, 0:NT], AF.Exp, bias=small[0:8, 3:4])
        act(cb[0:8, 7, :], cb[0:8, 7, :], AF.Ln, bias=1.0)
        ts(cb[0:8, 0, :], cb[0:8, 7, :], small[0:8, 4:5], ALU.mult)
        a_, b_ = 0, 7
        sh = 1
        while sh < C:
            av = cb[0:8, a_, :]; bv = cb[0:8, b_, :]
            a3 = av.with_ap(av.ap.rearrange("p (c l) -> p c l", l=C)); b3 = bv.with_ap(bv.ap.rearrange("p (c l) -> p c l", l=C))
            cp(V(b3.ap[:, :, 0:sh], bv.key, bv.lo, bv.hi, bv.page), V(a3.ap[:, :, 0:sh], av.key, av.lo, av.hi, av.page))
            tt(V(b3.ap[:, :, sh:C], bv.key, bv.lo, bv.hi, bv.page), V(a3.ap[:, :, sh:C], av.key, av.lo, av.hi, av.page),
               V(a3.ap[:, :, 0:C - sh], av.key, av.lo, av.hi, av.page), ALU.add)
            a_, b_ = b_, a_
            sh *= 2
        if a_ != 0:
            cp(cb[0:8, 0, :], cb[0:8, a_, :])
        tt(cb[0:8, 1, :], cb[0:8, 0, :], cb[0:8, 6, :], ALU.subtract)
        act(cb[0:8, 2, :], cb[0:8, 0, :], AF.Exp)
        gv = cb[0:8, 0, :]
        g3 = gv.with_ap(gv.ap.rearrange("p (c l) -> p c l", l=C))
        kdv = cb[0:8, 3, :]
        kd3 = kdv.with_ap(kdv.ap.rearrange("p (c l) -> p c l", l=C))
        tt(kd3, V(g3.ap[:, :, C - 1:C].to_broadcast([8, nch, C]), gv.key, gv.lo, gv.hi, gv.page), g3, ALU.subtract)
        act(cb[0:8, 3, :], cb[0:8, 3, :], AF.Exp)
        tt(cb[0:8, 5, :], cb[0:8, 4, :], cb[0:8, 2, :], ALU.mult)
        cbb = alloc([8, NT], BF16)
        for q_, row in enumerate((0, 1, 2)):
            cp(cbb[0:8, 2 * q_, :], cb[0:8, row, :])
            tt(cb[0:8, 6, :], cb[0:8, row, :], cbb[0:8, 2 * q_, :], ALU.subtract)
            cp(cbb[0:8, 2 * q_ + 1, :], cb[0:8, 6, :])
        ts(cbb[0:8, 6, :], cbb[0:8, 0, :], -1.0, ALU.mult)
        ts(cbb[0:8, 7, :], cbb[0:8, 1, :], -1.0, ALU.mult)

        m_conv = top[0]
        cin = [alloc([nseg, L + 3]) for _ in range(3)]
        cacc = alloc([nseg, L])
        csq = alloc([NT], BF16)
        crn = alloc([NT])
        if kind == "sample":
            scrow = alloc([3072])
            for sg_ in range(4):
                P.dma("sp", scrow[0:3, :], sc[sg_])
                for g6 in range(6):
                    pb = pbank()
                    for c4 in range(4):
                        cid_ = g6 * 4 + c4
                        tr(psb[pb][:, c4 * 3:(c4 + 1) * 3], scrow[0:3, cid_ * 128:(cid_ + 1) * 128], identf[0:3, 0:3])
                    pv_ = psb[pb][:, 0:12]
                    cp(ctx_cur[:, g6 * 4:(g6 + 1) * 4, sg_, :], pv_.with_ap(pv_.ap.rearrange("p (c w) -> p c w", c=4)))
        caccs = [cacc] + [alloc([nseg, L]) for _ in range(2)]
        ctmp = alloc([nseg, L])
        csqs = [csq] + [alloc([NT], BF16) for _ in range(2)]
        crns = [crn, alloc([NT])]
        gcnt = [0]

        def l2norm_finish(h, j, i):
            cflat = caccs[i].full().with_ap(caccs[i].ap.rearrange("p s l -> p (s l)"))
            crn_ = crns[(h * 2 + j) % 2]
            pbn = pbank()
            mm(psb[pbn][:, 0:NT], ones_bf.full(), csqs[i].full())
            act(crn_.full(), psb[pbn][:, 0:NT], AF.Ln, bias=EPS, scale=1.0)
            act(crn_.full(), crn_.full(), AF.Exp, scale=-0.5)
            if j == 0:
                stt(qg[:, h, :], cflat, B_SCALE, crn_.full(), ALU.mult, ALU.mult)
            else:
                tt(kg[:, h, :], cflat, crn_.full(), ALU.mult)

        for h in range(8):
            slot = wk8(wload(B_H + h))
            for j in range(4):
                pb = pbank()
                for kc in range(8):
                    mm(psb[pb][:, 0:NT], slot[:, kc, j * 128:(j + 1) * 128], xnT[:, kc, :], start=(kc == 0), stop=(kc == 7))
                group_issued()
                ps = psb[pb][:, 0:NT]
                if j == 3:
                    if kind != "meta":
                        act(sz[:, h, :], ps, AF.Silu)
                    continue
                cid = j * 8 + h
                ci = cin[j]
                ce = "pool" if j == 2 else "dve"
                if kind == "meta":
                    memset(ci[:, :, 0:3], 0.0, eng="pool")
                elif kind == "prompt":
                    cp(ci[:, 0, 0:3], (ctx_meta[:, cid, :] if t == 0 else ctx_cur[:, cid, 0, :]), eng="pool")
                else:
                    cp(ci[:, :, 0:3], ctx_cur[:, cid, 0:4, :], eng="pool")
                cp(ci[:, :, 3:3 + L], ps.with_ap(ps.ap.rearrange("p (s l) -> p s l", s=nseg)), eng="act")
                if kind == "meta":
                    cp(ctx_meta[:, cid, :], ci[:, 0, L:L + 3], eng="pool")
                else:
                    cp(ctx_cur[:, cid, 0:nseg, :], ci[:, :, L:L + 3], eng="pool")
                i = gcnt[0] % 3
                gcnt[0] += 1
                ca = caccs[i]
                ts(ca.full(), ci[:, :, 0:L], cwT[:, cid:cid + 1], ALU.mult, eng=ce)
                for w in range(1, 4):
                    if ce == "dve":
                        stt(ca.full(), ci[:, :, w:w + L], cwT[:, w * 24 + cid:w * 24 + cid + 1], ca.full(), ALU.mult, ALU.add)
                    else:
                        ts(ctmp.full(), ci[:, :, w:w + L], cwT[:, w * 24 + cid:w * 24 + cid + 1], ALU.mult, eng="pool")
                        tt(ca.full(), ca.full(), ctmp.full(), ALU.add, eng="pool")
                cflat = ca.full().with_ap(ca.ap.rearrange("p s l -> p (s l)"))
                if j == 2:
                    act(vg[:, h, :], cflat, AF.Silu)
                else:
                    act(cflat, cflat, AF.Silu)
                    act(csqs[i].full(), cflat, AF.Square)
                    defer(lambda h=h, j=j, i=i: l2norm_finish(h, j, i), delay=2)
        flush_deferred()
        if (kind == "prompt" and t == 3) or kind == "sample":
            tls = [alloc([512]) for _ in range(2)]
            for sg_ in range(nseg):
                for g6 in range(6):
                    tl = tls[g6 % 2]
                    pb = pbank()
                    for c4 in range(4):
                        tr(psb[pb][0:3, c4 * 128:(c4 + 1) * 128], ctx_cur[:, g6 * 4 + c4, sg_, :], identf.full())
                    cp(tl[0:3, :], psb[pb][0:3, :], eng="act")
                    dst_ = cpo[s, :, g6 * 512:(g6 + 1) * 512] if kind == "prompt" else cso[sg_, :, g6 * 512:(g6 + 1) * 512]
                    P.dma("pool", dst_, tl[0:3, :])

        P.phase = kind + ":gdnchunk"
        top[0] = m_conv
        nlev = {64: 5, 16: 3}[C]
        if C == 64:
            nU_i, nU_s, nL_s, idr, bmk = negU_incl[0:C], negU_strict[0:C], negL_strict[0:C], identrep[0:C], blockmask[0:8]
        else:
            cm = []
            for src_, np_ in ((negU_incl, C), (negU_strict, C), (negL_strict, C), (identrep, C), (blockmask, 8)):
                d_ = alloc([8, C], BF16)
                cp(d_[0:np_], src_[0:np_, :, 0:C])
                cm.append(d_[0:np_])
            nU_i, nU_s, nL_s, idr, bmk = cm
        gdb = [alloc([8, C], BF16) for _ in range(8)]
        tokc = alloc([32])
        DTi = alloc([8, C], BF16); NDT = alloc([8, C], BF16); NTD = alloc([8, C], BF16)
        Pm = [alloc([8, C], BF16) for _ in range(2)]; PmT = [alloc([8, C], BF16) for _ in range(2)]
        Rm = [alloc([8, C], BF16) for _ in range(2)]
        MT = alloc([8, C], BF16); qgc = alloc([8, C], BF16); nwT = alloc([8, C], BF16)
        bv_ = alloc([8, 128], BF16); kbg = alloc([8, 128], BF16); kdc = alloc([8, 128], BF16); vnw = alloc([8, 128], BF16)
        egl = alloc([8])

        def fl(v):
            return v.with_ap(v.ap.rearrange("p a b -> p (a b)"))

        for ci_ in range(nch):
            sgi = ci_ if kind == "sample" else 0
            cs = slice(ci_ * C, (ci_ + 1) * C)
            W8 = 8 * C
            if kind == "meta":
                if ci_ == 0:
                    memset(S_cur.full(), 0.0); memset(S_bf.full(), 0.0)
            elif kind == "prompt":
                if ci_ == 0 and t == 0:
                    cp(S_cur.full(), S_meta.full()); cp(S_bf.full(), S_meta.full(), eng="act")
            else:
                P.dma("sp", S_cur.full(), sg[sgi].with_ap(sg.ap[sgi].rearrange("h d e -> d h e")))
                cp(S_bf.full(), S_cur.full(), eng="act")
            for k_ in range(8):
                src = cbb[0:8, k_, cs]
                tt(gdb[k_][0:8], bmk, V(src.ap.unsqueeze(1).to_broadcast([8, 8, C]), src.key, src.lo, src.hi, src.page), ALU.mult, eng="pool")
            pbt = pbank()
            for k_, row in enumerate((4, 5, 3)):
                tr(psb[pbt][0:C, k_ * 8:(k_ + 1) * 8], cb[0:8, row, cs], identf[0:8, 0:8])
            cp(tokc[0:C, 0:24], psb[pbt][0:C, 0:24])
            on8 = ones_bf[0:8, 0:C]

            def xmat(diag_hi, diag_lo, col_hi, col_lo, mask):
                pbx = pbank()
                X = psb[pbx][0:C, 0:W8]
                mm(X, on8, fl(gdb[diag_hi][0:8]), start=True, stop=False)
                mm(X, on8, fl(gdb[diag_lo][0:8]), start=False, stop=False)
                mm(X, cbb[0:8, col_hi, cs], fl(bmk), start=False, stop=False)
                mm(X, cbb[0:8, col_lo, cs], fl(bmk), start=False, stop=False)
                mm(X, identb[0:C, 0:C], fl(mask), start=False, stop=True)
                return X
            act(fl(DTi[0:C]), xmat(0, 1, 6, 7, nU_i), AF.Exp)
            act(fl(NDT[0:C]), xmat(2, 3, 6, 7, nU_s), AF.Exp)
            act(fl(NTD[0:C]), xmat(6, 7, 2, 3, nL_s), AF.Exp)
            pbe = pbank()
            mm(psb[pbe][:, 0:W8], ones_bf[0:8, :], fl(gdb[4][0:8]), start=True, stop=False)
            mm(psb[pbe][:, 0:W8], ones_bf[0:8, :], fl(gdb[5][0:8]), start=False, stop=True)
            pe_v = psb[pbe][:, 0:W8]
            pe3 = pe_v.with_ap(pe_v.ap.rearrange("p (h c) -> p h c", h=8))
            tt(qgc[:, :, 0:C], qg[:, :, cs], pe3, ALU.mult)
            cp(egl.full(), V(pe3.ap[:, :, C - 1], pe_v.key, pe_v.lo, pe_v.hi, pe_v.page))
            pbk = pbank(); pbq = pbank()
            for h in range(8):
                mm(psb[pbk][0:C, h * C:(h + 1) * C], kg[:, h, cs], kg[:, h, cs])
            for h in range(8):
                mm(psb[pbq][0:C, h * C:(h + 1) * C], kg[:, h, cs], qg[:, h, cs])
            stt(fl(Pm[0][0:C]), psb[pbk][0:C, 0:W8], -1.0, fl(NDT[0:C]), ALU.mult, ALU.mult)
            stt(fl(PmT[0][0:C]), psb[pbk][0:C, 0:W8], -1.0, fl(NTD[0:C]), ALU.mult, ALU.mult)
            tt(fl(MT[0:C]), psb[pbq][0:C, 0:W8], fl(DTi[0:C]), ALU.mult)
            tt(fl(Rm[0][0:C]), fl(Pm[0][0:C]), fl(idr), ALU.add, eng="pool")
            cur = 0
            for lv in range(1, nlev + 1):
                nxt = 1 - cur
                pbp = pbank(); pbpt = pbank()
                for h in range(8):
                    mm(psb[pbpt][0:C, h * C:(h + 1) * C], Pm[cur][0:C, h, :], PmT[cur][0:C, h, :])
                if lv < nlev:
                    for h in range(8):
                        mm(psb[pbp][0:C, h * C:(h + 1) * C], PmT[cur][0:C, h, :], Pm[cur][0:C, h, :])
                cp(fl(PmT[nxt][0:C]), psb[pbpt][0:C, 0:W8], eng="act")
                if lv < nlev:
                    cp(fl(Pm[nxt][0:C]), psb[pbp][0:C, 0:W8])
                pbr = pbank()
                for h in range(8):
                    mm(psb[pbr][0:C, h * C:(h + 1) * C], PmT[nxt][0:C, h, :], Rm[cur][0:C, h, :])
                tt(fl(Rm[nxt][0:C]), psb[pbr][0:C, 0:W8], fl(Rm[cur][0:C]), ALU.add)
                cur = nxt
            TT = Rm[cur]
            pbk = pbank(); pbv = pbank()
            for h in range(8):
                tr(psb16[pbk][0:C, h * 128:(h + 1) * 128], kg[:, h, cs], identb.full())
            for h in range(8):
                tr(psb16[pbv][0:C, h * 128:(h + 1) * 128], vg[:, h, cs], identb.full())
            kt3 = psb16[pbk][0:C, :]; kt3 = kt3.with_ap(kt3.ap.rearrange("p (h d) -> p h d", h=8))
            vt3 = psb16[pbv][0:C, :]; vt3 = vt3.with_ap(vt3.ap.rearrange("p (h d) -> p h d", h=8))
            tt(bv_[0:C], vt3, bc3(tokc[0:C, 0:8], [C, 8, 128]), ALU.mult)
            tt(kbg[0:C], kt3, bc3(tokc[0:C, 8:16], [C, 8, 128]), ALU.mult)
            tt(kdc[0:C], kt3, bc3(tokc[0:C, 16:24], [C, 8, 128]), ALU.mult)
            pbw = pbank()
            for h in range(8):
                mm(psb[pbw][:, h * C:(h + 1) * C], kbg[0:C, h, :], TT[0:C, h, :])
            ts(fl(nwT[:, :, 0:C]), psb[pbw][:, 0:W8], -1.0, ALU.mult)
            pv0 = pbank(); pv1 = pbank()
            for h in range(8):
                o = psb[pv0 if h < 4 else pv1][0:C, (h % 4) * 128:(h % 4 + 1) * 128]
                mm(o, TT[0:C, h, :], bv_[0:C, h, :], start=True, stop=False)
                mm(o, nwT[:, h, 0:C], S_bf[:, h, :], start=False, stop=True)
            cp(fl(vnw[0:C, 0:4, :]), psb[pv0][0:C, :], eng="act")
            cp(fl(vnw[0:C, 4:8, :]), psb[pv1][0:C, :])
            if kind != "meta":
                pbo = pbank()
                for h in range(8):
                    o = psb[pbo][:, h * C:(h + 1) * C]
                    mm(o, S_bf[:, h, :], qgc[:, h, 0:C], start=True, stop=False)
                    mm(o, vnw[0:C, h, :], MT[0:C, h, :], start=False, stop=True)
                po = psb[pbo][:, 0:W8]
                cp(og[:, :, cs], po.with_ap(po.ap.rearrange("p (h c) -> p h c", h=8)), eng="act")
            ps0 = pbank(); ps1 = pbank()
            for h in range(8):
                mm(psb[ps0 if h < 4 else ps1][:, (h % 4) * 128:(h % 4 + 1) * 128], kdc[0:C, h, :], vnw[0:C, h, :])
            tt(S_cur.full(), S_cur.full(), bc3(egl.full(), [128, 8, 128]), ALU.mult)
            tt(fl(S_cur[:, 0:4, :]), fl(S_cur[:, 0:4, :]), psb[ps0].full(), ALU.add)
            tt(fl(S_cur[:, 4:8, :]), fl(S_cur[:, 4:8, :]), psb[ps1].full(), ALU.add)
            cp(S_bf.full(), S_cur.full(), eng="act")
            if kind == "sample":
                P.dma("pool", gso[sgi].with_ap(gso.ap[sgi].rearrange("h d e -> d h e")), S_cur.full())
        if kind == "meta":
            cp(S_meta.full(), S_cur.full())
            return
        if kind == "prompt" and t == 3:
            P.dma("pool", gp[s].with_ap(gp.ap[s].rearrange("h d e -> d h e")), S_cur.full())
        P.phase = kind + ":gdnnorm"
        top[0] = m_conv
        gsq = alloc([NT], BF16); grn = alloc([NT]); gt = alloc([NT])
        for h in range(8):
            act(gsq.full(), og[:, h, :], AF.Square)
            pbn = pbank()
            mm(psb[pbn][:, 0:NT], ones_bf.full(), gsq.full())
            rsqrt(grn.full(), psb[pbn][:, 0:NT], 1.0 / 128, gt.full())
            stt(gt.full(), og[:, h, :], small[:, 0:1], grn.full(), ALU.mult, ALU.mult)
            tt(obT[:, h, :], gt.full(), sz[:, h, :], ALU.mult, eng="pool")
        if dbg and kind == "prompt" and s == 0 and t == 0:
            P.dma("pool", dbg_ob.full(), obT.full())
        top[0] = m_gdn
        if stage < 4:
            return
        P.phase = kind + ":merge"
        mixT = alloc([8, NT], BF16)
        sga = alloc([NT]); sgb = alloc([NT]); tmp = alloc([NT])
        for oc in range(8):
            slot = wk8(wload(B_M + oc))
            pa = pbank(); pb_ = pbank(); pya = pbank(); pyb = pbank()
            for kc in range(8):
                mm(psb[pa][:, 0:NT], slot[:, kc, 0:128], xnT[:, kc, :], start=(kc == 0), stop=(kc == 7))
            for kc in range(8):
                mm(psb[pb_][:, 0:NT], slot[:, kc, 128:256], xnT[:, kc, :], start=(kc == 0), stop=(kc == 7))
            for kc in range(8):
                mm(psb[pya][:, 0:NT], slot[:, kc, 256:384], oaT[:, kc, :], start=(kc == 0), stop=(kc == 7))
            for kc in range(8):
                mm(psb[pyb][:, 0:NT], slot[:, kc, 384:512], obT[:, kc, :], start=(kc == 0), stop=(kc == 7))
            act(sga.full(), psb[pa][:, 0:NT], AF.Sigmoid, bias=bgT[:, oc:oc + 1])
            act(sgb.full(), psb[pb_][:, 0:NT], AF.Sigmoid, bias=bgT[:, 8 + oc:9 + oc])
            tt(tmp.full(), psb[pya][:, 0:NT], sga.full(), ALU.mult)
            tt(sgb.full(), psb[pyb][:, 0:NT], sgb.full(), ALU.mult)
            tt(mixT[:, oc, :], tmp.full(), sgb.full(), ALU.add, eng="pool")
        for half in range(2):
            slot = wk8(wload(B_WO + half))
            for st in range(nst):
                pb = pbank()
                for kc in range(8):
                    mm(psb[pb][0:ST, :], mixT[:, kc, st * ST:(st + 1) * ST], slot[:, kc, :], start=(kc == 0), stop=(kc == 7))
                tt(xtok[0:ST, st, half * 512:(half + 1) * 512], xtok[0:ST, st, half * 512:(half + 1) * 512], psb[pb][0:ST, :], ALU.add)
        if stage < 5:
            return
        P.phase = kind + ":ffn"
        norm_T(norm2)
        uT = alloc([32, NT], BF16)
        rl = [alloc([NT]) for _ in range(2)]
        for j in range(8):
            slot = wk8(wload(B_WU + j))
            for c4 in range(4):
                fc = j * 4 + c4
                pb = pbank()
                for kc in range(8):
                    mm(psb[pb][:, 0:NT], slot[:, kc, c4 * 128:(c4 + 1) * 128], xnT[:, kc, :], start=(kc == 0), stop=(kc == 7))
                r = rl[fc % 2]
                act(r.full(), psb[pb][:, 0:NT], AF.Relu)
                tt(uT[:, fc, :], r.full(), r.full(), ALU.mult, eng=("pool" if fc % 2 else "dve"))
        for oc in range(8):
            slot = wf32(wload(B_WD + oc))
            pb = pbank()
            for st in range(nst):
                for fc in range(32):
                    mm(psb[pb][0:ST, st * 128:(st + 1) * 128], uT[:, fc, st * ST:(st + 1) * ST], slot[:, fc, :], start=(fc == 0), stop=(fc == 31))
            pv = psb[pb][0:ST, 0:nst * 128]
            tt(xtok[0:ST, :, oc * 128:(oc + 1) * 128], xtok[0:ST, :, oc * 128:(oc + 1) * 128],
               pv.with_ap(pv.ap.rearrange("p (s c) -> p s c", s=nst)), ALU.add)
        for st in range(nst):
            if kind == "prompt":
                P.dma("pool", yp[s, t * 512 + st * 128: t * 512 + (st + 1) * 128, :], xtok[0:ST, st, :])
            else:
                P.dma("pool", ys[st * 128:(st + 1) * 128, :], xtok[0:ST, st, :])

    def cache_k_prep():
        P.phase = "cachek"
        top[0] = base_top
        ckf = [alloc([D]) for _ in range(2)]
        ckb = [alloc([D], BF16) for _ in range(2)]
        ckt = [alloc([8, 128], BF16) for _ in range(2)]
        i = 0
        for sq_ in range(NSS):
            P.dma("pool", Vcd[sq_], cv[sq_])
        for sq_ in range(NSS):
            for c in range(8):
                f = ckf[i % 2]; b = ckb[i % 2]; kt_ = ckt[i % 2]
                P.dma("sp", f.full(), ck[sq_, c * 128:(c + 1) * 128, :])
                cp(b.full(), f.full(), eng=("pool" if i % 2 else "dve"))
                pb = pbank()
                for h in range(8):
                    tr(psb16[pb][:, h * 128:(h + 1) * 128], b[:, h * 128:(h + 1) * 128], identb.full())
                src = psb16[pb].full()
                cp(kt_.full(), src.with_ap(src.ap.rearrange("p (h t) -> p h t", h=8)), eng="act")
                P.dma("sp", KTs[sq_, :, :, c * 128:(c + 1) * 128].with_ap(KTs.ap[sq_, :, :, c * 128:(c + 1) * 128].rearrange("h p t -> p h t")), kt_.full())
                i += 1

    if dbg:
        dbg_oa = dram("dbg_oa2", [128, 8, 512], "ExternalOutput", BF16)
        dbg_ob = dram("dbg_ob2", [128, 8, 512], "ExternalOutput", BF16)

    import os
    sel = os.environ.get("KTILES", "msp")
    if "s" in sel:
        cache_k_prep()
    run_tile("meta")
    if "s" in sel:
        run_tile("sample")
    if "p" in sel:
        for s in range(NPS):
            for t in range(4):
                run_tile("prompt", s, t)
    elif "q" in sel:
        run_tile("prompt", 0, 0)
    P.emit()
    es.close()
    return nc, P


_CACHE = {}


def kernel(x_prompt, x_sample, cache_attn_k, cache_attn_v, state_gdn, state_conv, meta_tokens, rel_bias,
           norm1, w_in, b_gate, q_norm, k_norm, lambda_q1, lambda_k1, lambda_q2, lambda_k2, sub_norm,
           conv_w, A_log, dt_bias, gdn_norm, w_br_a, w_br_b, w_out, norm2, w_up, w_down, _stage=99, _cores=8, _dbg=False):
    f = lambda a: np.ascontiguousarray(np.asarray(a, dtype=np.float32))
    key = (_stage, _dbg)
    if key not in _CACHE:
        _CACHE[key] = build_program(_stage, _dbg)
    nc, P = _CACHE[key]
    consts = _consts()
    shared = {
        "meta": f(meta_tokens), "relb": f(rel_bias), "norm1": f(norm1).reshape(-1), "w_in": f(w_in)[0],
        "b_gate": f(b_gate).reshape(-1), "q_norm": f(q_norm).reshape(-1), "k_norm": f(k_norm).reshape(-1),
        "lq1": f(lambda_q1).reshape(-1), "lk1": f(lambda_k1).reshape(-1), "lq2": f(lambda_q2).reshape(-1), "lk2": f(lambda_k2).reshape(-1),
        "sub_norm": f(sub_norm).reshape(-1), "conv_w": f(conv_w).reshape(-1), "a_log": f(A_log).reshape(-1),
        "dt_bias": f(dt_bias).reshape(-1), "gdn_norm": f(gdn_norm).reshape(-1), "w_bra": f(w_br_a)[0], "w_brb": f(w_br_b)[0],
        "w_out": f(w_out)[0], "norm2": f(norm2).reshape(-1), "w_up": f(w_up)[0], "w_down": f(w_down)[0],
    }
    for k, v in consts.items():
        shared["c_" + k] = v
    xp = f(x_prompt); xs = f(x_sample)
    ck = f(cache_attn_k)[0].reshape(32, PAST, D); cv = f(cache_attn_v)[0].reshape(32, PAST, D)
    sg = f(state_gdn)[0]; sc = f(state_conv)[0]
    in_maps = []
    for c in range(_cores):
        m = dict(shared)
        m["xp"] = xp[c * NPS:(c + 1) * NPS]
        m["xs"] = xs[c * NSS:(c + 1) * NSS].reshape(NSS * DSEQ, D)
        m["ck"] = ck[c * NSS:(c + 1) * NSS]
        m["cv"] = cv[c * NSS:(c + 1) * NSS]
        m["sg"] = sg[c * NSS:(c + 1) * NSS]
        m["sc"] = sc[c * NSS:(c + 1) * NSS]
        in_maps.append(m)
    res = run_bass_kernel_spmd(nc, in_maps, core_ids=list(range(_cores)))
    R = res.results
    cat = lambda k: np.concatenate([np.asarray(r[k], dtype=np.float32) for r in R], axis=0)
    nb = _cores * NPS
    ns = _cores * NSS
    outs = (
        cat("yp"),
        cat("ys").reshape(ns, DSEQ, D),
        cat("kp").reshape(1, nb, TP, 8, 128),
        cat("vp").reshape(1, nb, TP, 8, 128),
        cat("gp").reshape(1, nb, 8, 128, 128),
        cat("cpo").reshape(1, nb, 3, 3072),
        cat("kso").reshape(1, ns, DSEQ, 8, 128),
        cat("vso").reshape(1, ns, DSEQ, 8, 128),
        cat("gso").reshape(1, ns, 8, 128, 128),
        cat("cso").reshape(1, ns, 3, 3072),
    )
    if _dbg:
        return outs, R
    return outs
```

```python
import contextlib
import math
import os
from collections import defaultdict

import numpy as np
import concourse.bass as bass
import concourse.mybir as mybir
from concourse.bass_utils import run_bass_kernel_spmd

F32 = mybir.dt.float32
BF16 = mybir.dt.bfloat16
I32 = mybir.dt.int32
ALU = mybir.AluOpType
AF = mybir.ActivationFunctionType
AX = mybir.AxisListType

SEM_LIMIT = 1000
DMA_SEMS = 24
NEG = -30000.0


class V:
    __slots__ = ("ap", "key", "lo", "hi", "page", "track")

    def __init__(self, ap, key, lo, hi, page, track=True):
        self.ap = ap
        self.key = key
        self.lo = lo
        self.hi = hi
        self.page = page
        self.track = track

    def with_ap(self, ap):
        return V(ap, self.key, self.lo, self.hi, self.page, self.track)


class T:
    def __init__(self, ap, name, shape, dram=False, esize=4, base_off=0, page=2048, track=True, whole=False):
        self.whole = whole
        self.ap = ap
        self.name = name
        self.shape = list(shape)
        self.dram = dram
        self.esize = esize
        self.base_off = base_off
        self.page = page
        self.track = track
        fs = self.shape if dram else self.shape[1:]
        st = []
        acc = 1
        for s in reversed(fs):
            st.append(acc)
            acc *= s
        self.fstrides = list(reversed(st))

    def __getitem__(self, key):
        if not isinstance(key, tuple):
            key = (key,)
        ap = self.ap[key]
        fs = self.shape if self.dram else self.shape[1:]
        k2 = list(key) if self.dram else list(key[1:])
        while len(k2) < len(fs):
            k2.append(slice(None))
        lo = 0
        hi = 0
        for k, s, st in zip(k2, fs, self.fstrides):
            if isinstance(k, slice):
                a = 0 if k.start is None else k.start
                b = s if k.stop is None else k.stop
            else:
                a = k
                b = k + 1
            lo += a * st
            hi += (b - 1) * st
        hi += 1
        if self.whole:
            return V(ap, self.name, 0, self.page, self.page, self.track)
        return V(ap, self.name, self.base_off + lo * self.esize, self.base_off + hi * self.esize, self.page, self.track)

    def full(self):
        return self[tuple(slice(None) for _ in self.shape)]


class Op:
    __slots__ = ("eng", "fn", "deps", "id", "is_dma", "has_dependents", "sig", "dma_sem", "dma_val", "phase")

    def __init__(self, eng, fn, is_dma):
        self.eng = eng
        self.fn = fn
        self.deps = set()
        self.is_dma = is_dma
        self.has_dependents = False
        self.sig = None
        self.dma_sem = None
        self.dma_val = None


ENGS = ("pe", "act", "dve", "pool", "sp")


class Prog:
    def __init__(self, nc):
        self.nc = nc
        self.ops = []
        self.hist = defaultdict(list)

    def op(self, eng, fn, reads=(), writes=(), dma=False):
        o = Op(eng, fn, dma)
        o.phase = getattr(self, "phase", "")
        o.id = len(self.ops)
        self.ops.append(o)
        deps = o.deps
        tag = eng + ("_dma" if dma else "")
        hist = self.hist
        for v in reads:
            if not v.track:
                continue
            lo, hi = v.lo, v.hi
            for pg in range(lo // v.page, (hi - 1) // v.page + 1):
                for rec in hist[(v.key, pg)]:
                    if rec[2] == "W" and rec[0] < hi and lo < rec[1]:
                        if rec[4] == "pe" and tag == "pe":
                            continue
                        deps.add(rec[3])
                    elif rec[2] == "R" and v.key.startswith("ps") and rec[4] != tag:
                        deps.add(rec[3])
        for v in writes:
            if not v.track:
                continue
            lo, hi = v.lo, v.hi
            for pg in range(lo // v.page, (hi - 1) // v.page + 1):
                h = hist[(v.key, pg)]
                keep = []
                for rec in h:
                    if rec[0] < hi and lo < rec[1]:
                        if not (rec[4] == "pe" and tag == "pe"):
                            deps.add(rec[3])
                        if lo <= rec[0] and rec[1] <= hi:
                            continue
                    keep.append(rec)
                keep.append([lo, hi, "W", o.id, tag])
                hist[(v.key, pg)] = keep
        for v in reads:
            if not v.track:
                continue
            lo, hi = v.lo, v.hi
            for pg in range(lo // v.page, (hi - 1) // v.page + 1):
                h = hist[(v.key, pg)]
                found = False
                if not dma:
                    for r in h:
                        if r[2] == "R" and r[0] == lo and r[1] == hi and r[4] == tag:
                            r[3] = o.id
                            found = True
                            break
                if not found:
                    h.append([lo, hi, "R", o.id, tag])
        deps.discard(o.id)
        return o

    def dma(self, eng, out, in_, **kw):
        def fn(e):
            return e.dma_start(out=out.ap, in_=in_.ap, **kw)
        return self.op(eng, fn, reads=[in_], writes=[out], dma=True)

    def emit(self):
        nc = self.nc
        ops = self.ops
        for o in ops:
            for d in o.deps:
                ops[d].has_dependents = True
        cnt = {e: 0 for e in ENGS}
        dma_cnt = {e: 0 for e in ENGS}
        for o in ops:
            if o.is_dma:
                i = dma_cnt[o.eng]
                dma_cnt[o.eng] += 1
                o.dma_sem = (o.eng, i % DMA_SEMS)
                o.dma_val = 16 * (i // DMA_SEMS + 1)
            elif o.has_dependents:
                cnt[o.eng] += 1
                o.sig = cnt[o.eng]
        n_epochs = {e: (cnt[e] + SEM_LIMIT - 1) // SEM_LIMIT for e in ENGS}
        stack = contextlib.ExitStack()
        sems = {}
        for e in ENGS:
            for ep in range(max(1, n_epochs[e])):
                sems[(e, ep)] = stack.enter_context(nc.semaphore(f"s_{e}_{ep}"))
            if dma_cnt[e]:
                for i in range(DMA_SEMS):
                    sems[("dma", e, i)] = stack.enter_context(nc.semaphore(f"d_{e}_{i}"))
        per_eng = {e: [] for e in ENGS}
        for o in ops:
            per_eng[o.eng].append(o)
        waited = {e: {} for e in ENGS}
        waits = {}
        for o in ops:
            w = {}
            for d in o.deps:
                p = ops[d]
                if p.is_dma:
                    key = ("dma",) + p.dma_sem
                    val = p.dma_val
                else:
                    key = ("c", p.eng)
                    val = p.sig
                if val > w.get(key, 0):
                    w[key] = val
            if o.is_dma:
                key = ("dma",) + o.dma_sem
                if o.dma_val > 16 and o.dma_val - 16 > w.get(key, 0):
                    w[key] = o.dma_val - 16
            wl = []
            wd = waited[o.eng]
            for key, val in w.items():
                if wd.get(key, 0) >= val:
                    continue
                wd[key] = val
                wl.append((key, val))
            waits[o.id] = wl
        final = {}
        for e in ENGS:
            if dma_cnt[e]:
                fl = []
                for i in range(DMA_SEMS):
                    n = len(range(i, dma_cnt[e], DMA_SEMS))
                    if n:
                        fl.append((("dma", e, i), 16 * n))
                final[e] = fl

        def sem_of(key, val):
            if key[0] == "dma":
                return sems[("dma", key[1], key[2])], val
            e = key[1]
            ep = (val - 1) // SEM_LIMIT
            return sems[(e, ep)], val - ep * SEM_LIMIT

        def run(engname, engobj):
            for o in per_eng[engname]:
                for key, val in waits[o.id]:
                    s, v = sem_of(key, val)
                    engobj.wait_ge(s, v)
                ins = o.fn(engobj)
                if o.is_dma:
                    ins.then_inc(sems[("dma",) + o.dma_sem], 16)
                elif o.sig is not None:
                    s, v = sem_of(("c", engname), o.sig)
                    ins.then_inc(s, 1)
            for key, val in final.get(engname, []):
                s, v = sem_of(key, val)
                engobj.wait_ge(s, v)

        with nc.Block() as block:
            @block.sync
            def _(e):
                run("sp", e)

            @block.scalar
            def _(e):
                run("act", e)

            @block.vector
            def _(e):
                run("dve", e)

            @block.gpsimd
            def _(e):
                run("pool", e)

            @block.tensor
            def _(e):
                run("pe", e)
        stack.close()
        self.stats = {e: len(per_eng[e]) for e in ENGS}


D = 1024
SEQ = 2048
NMETA = 16
TP = NMETA + SEQ
PAST = 1024
DSEQ = 64
NIN = 9232
OFF_KA, OFF_VA, OFF_B, OFF_Z, OFF_BETA, OFF_ALPHA, OFF_GA, OFF_GB = 1024, 2048, 3072, 6144, 7168, 7176, 7184, 8208
DFF = 4096
EPS = 1e-6
LAM_INIT = 0.8 - 0.6 * math.exp(-0.3 * 0)
A_SCALE = 0.125
B_SCALE = 128 ** -0.5
NPS = 2
NSS = 4
GW = 1024
GL = 1152
GOFF = 384

B_Q, B_K, B_V, B_H, B_BA, B_M, B_WO, B_WU, B_WD, NBLK = 0, 2, 4, 6, 14, 15, 23, 25, 33, 41


def _consts():
    c = {}
    c["identf"] = np.eye(128, dtype=np.float32)
    p = np.arange(64)[:, None]
    f = np.arange(64)[None, :]
    def rep(m):
        return np.ascontiguousarray(np.broadcast_to(m[:, None, :], (64, 8, 64)).reshape(64, 512)).astype(np.float32)
    c["negU_incl"] = rep(np.where(f >= p, 0.0, NEG))
    c["negU_strict"] = rep(np.where(f > p, 0.0, NEG))
    c["negL_strict"] = rep(np.where(f < p, 0.0, NEG))
    c["identrep"] = rep(np.eye(64))
    bm = np.zeros((8, 8, 64), np.float32)
    for h in range(8):
        bm[h, h, :] = 1.0
    c["blockmask"] = bm.reshape(8, 512)
    pp = np.arange(128)[:, None]
    cc = np.arange(GW)[None, :] - GOFF
    c["maskG"] = np.where(np.floor_divide(cc, 64) >= np.floor_divide(pp, 64), 0.0, NEG).astype(np.float32)
    lo = [0, 1, 2, 3, 4, 5, 6, 7, 8, 12, 16, 23, 32, 46, 64, 91]
    hi = lo[1:] + [10 ** 9]
    oh = np.zeros((32, GL), np.float32)
    for i in range(GL):
        rel = 511 - i
        n = abs(rel)
        b = 0
        for k in range(16):
            if lo[k] <= n < hi[k]:
                b = k
        if rel > 0:
            b += 16
        oh[b, i] = 1.0
    c["onehot"] = oh
    return c


def build_program(stage=99, dbg=False):
    nc = bass.Bass("TRN2", target_bir_lowering=False)
    P = Prog(nc)
    es = contextlib.ExitStack()

    def dram(name, shape, kind, dt=F32, page=1 << 20, track=False):
        ap = nc.dram_tensor(name, shape, dt, kind=kind).ap()
        return T(ap, name, shape, dram=True, esize=(2 if dt == BF16 else 4), page=page, track=track)

    def din(name, shape):
        return dram(name, shape, "ExternalInput")

    def dout(name, shape):
        return dram(name, shape, "ExternalOutput")

    xp = din("xp", [NPS, SEQ, D])
    xs = din("xs", [NSS * DSEQ, D])
    ck = din("ck", [NSS, PAST, D])
    cv = din("cv", [NSS, PAST, D])
    sg = din("sg", [NSS, 8, 128, 128])
    sc = din("sc", [NSS, 3, 3072])
    meta = din("meta", [NMETA, D])
    relb = din("relb", [32, 8])
    norm1 = din("norm1", [D])
    w_in = din("w_in", [D, NIN])
    b_gate = din("b_gate", [2 * D])
    q_norm = din("q_norm", [64])
    k_norm = din("k_norm", [64])
    lq1 = din("lq1", [64]); lk1 = din("lk1", [64]); lq2 = din("lq2", [64]); lk2 = din("lk2", [64])
    sub_norm = din("sub_norm", [128])
    conv_w = din("conv_w", [4 * 3072])
    a_log = din("a_log", [8])
    dt_bias = din("dt_bias", [8])
    gdn_norm = din("gdn_norm", [128])
    w_bra = din("w_bra", [D, D]); w_brb = din("w_brb", [D, D]); w_out = din("w_out", [D, D])
    norm2 = din("norm2", [D])
    w_up = din("w_up", [D, DFF]); w_down = din("w_down", [DFF, D])
    cst = {k: din("c_" + k, list(v.shape)) for k, v in _consts().items()}
    yp = dout("yp", [NPS, SEQ, D]); ys = dout("ys", [NSS * DSEQ, D])
    kp = dout("kp", [NPS, TP, D]); vp = dout("vp", [NPS, TP, D])
    gp = dout("gp", [NPS, 8, 128, 128]); cpo = dout("cpo", [NPS, 3, 3072])
    kso = dout("kso", [NSS * DSEQ, D]); vso = dout("vso", [NSS * DSEQ, D])
    gso = dout("gso", [NSS, 8, 128, 128]); cso = dout("cso", [NSS, 3, 3072])
    WS = dram("WS", [NBLK, 128, 4096], "Internal", BF16, page=1 << 20, track=True)
    KTd = dram("KTd", [NPS, 8, 128, TP], "Internal", BF16, page=1 << 16, track=True)
    Vd = dram("Vd", [NPS, TP, D], "Internal", BF16, page=1 << 16, track=True)
    KTs = dram("KTs", [NSS, 8, 128, PAST + DSEQ], "Internal", BF16, page=1 << 16, track=True)
    FD = dram("FD", [8, 128, GL], "Internal", F32, page=1 << 16, track=True)
    Vsd = dram("Vsd", [NSS, DSEQ, D], "Internal", BF16, page=1 << 16, track=True)
    Vcd = dram("Vcd", [NSS, PAST, D], "Internal", BF16, page=1 << 16, track=True)

    ARENA = 206 * 1024
    arena_h = es.enter_context(nc.sbuf_tensor("arena", [128, ARENA // 4], F32))
    top = [0]

    def alloc(shape, dt=F32):
        n = int(np.prod(shape))
        esz = 2 if dt == BF16 else 4
        nb = (n * esz + 31) // 32 * 32
        off = top[0]
        top[0] += nb
        assert top[0] <= ARENA, ("arena overflow", top[0])
        ap = arena_h[:, off // 4:(off + nb) // 4]
        if dt != F32:
            ap = ap.bitcast(dt)
        ap = ap[:, 0:n]
        if len(shape) == 2:
            ap = ap.rearrange("p (a b) -> p a b", a=shape[0])
        elif len(shape) == 3:
            ap = ap.rearrange("p (a b c) -> p a b c", a=shape[0], b=shape[1])
        return T(ap, "arena", [128] + list(shape), esize=esz, base_off=off)

    psb = []
    psb16 = []
    for i in range(8):
        h = es.enter_context(nc.psum_tensor(f"ps{i}", [128, 512], F32))
        psb.append(T(h, f"ps{i}", [128, 512], page=4096, whole=True))
        psb16.append(T(h[:, :].bitcast(BF16), f"ps{i}", [128, 1024], esize=2, page=4096, whole=True))
    rot = [0]

    def pbank(pool=8):
        i = rot[0] % pool
        rot[0] += 1
        return i

    def rw(*vs):
        return [v for v in vs if isinstance(v, V)]

    def A_(x):
        return x.ap if isinstance(x, V) else x

    def mm(out, lhsT, rhs, start=True, stop=True):
        P.op("pe", lambda e: e.matmul(out=out.ap, lhsT=lhsT.ap, rhs=rhs.ap, start=start, stop=stop), reads=[lhsT, rhs], writes=[out])

    def tr(out, in_, ident):
        P.op("pe", lambda e: e.transpose(out=out.ap, in_=in_.ap, identity=ident.ap), reads=[in_, ident], writes=[out])

    def act(out, in_, func, bias=None, scale=None, accum=None):
        kw = {}
        if bias is not None:
            kw["bias"] = A_(bias)
        if scale is not None:
            kw["scale"] = A_(scale)
        if accum is not None:
            kw["accum_out"] = accum.ap
        P.op("act", lambda e: e.activation(out=out.ap, in_=in_.ap, func=func, **kw), reads=rw(in_, bias, scale), writes=rw(out, accum))

    def tt(out, a, b, op, eng="dve"):
        P.op(eng, lambda e: e.tensor_tensor(out=out.ap, in0=a.ap, in1=b.ap, op=op), reads=[a, b], writes=[out])

    def ts(out, a, s1, op0, s2=None, op1=None, eng="dve"):
        if op1 is None:
            P.op(eng, lambda e: e.tensor_scalar(out=out.ap, in0=a.ap, scalar1=A_(s1), scalar2=0.0, op0=op0, op1=ALU.add), reads=rw(a, s1), writes=[out])
        else:
            P.op(eng, lambda e: e.tensor_scalar(out=out.ap, in0=a.ap, scalar1=A_(s1), scalar2=A_(s2), op0=op0, op1=op1), reads=rw(a, s1, s2), writes=[out])

    def stt(out, a, s, b, op0, op1, eng="dve"):
        P.op(eng, lambda e: e.scalar_tensor_tensor(out=out.ap, in0=a.ap, scalar=A_(s), in1=b.ap, op0=op0, op1=op1), reads=rw(a, s, b), writes=[out])

    def cp(out, in_, eng="dve"):
        if eng == "act":
            P.op("act", lambda e: e.copy(out=out.ap, in_=in_.ap), reads=[in_], writes=[out])
        else:
            P.op(eng, lambda e: e.tensor_copy(out=out.ap, in_=in_.ap), reads=[in_], writes=[out])

    def red(out, in_, op=ALU.add):
        P.op("dve", lambda e: e.tensor_reduce(out=out.ap, in_=in_.ap, axis=AX.X, op=op), reads=[in_], writes=[out])

    def recip(out, in_):
        act(out, in_, AF.Ln)
        act(out, out, AF.Exp, scale=-1.0)

    def memset(v, val, eng="dve"):
        P.op(eng, lambda e: e.memset(v.ap, val), writes=[v])

    def rsqrt(out, in_, scale, tmp):
        act(tmp, in_, AF.Ln, bias=EPS, scale=scale)
        act(out, tmp, AF.Exp, scale=-0.5)

    def bc3(v, shape):
        return v.with_ap(v.ap.unsqueeze(2).to_broadcast(shape))

    deferred = []

    def defer(fn, delay=1):
        deferred.append([delay, fn])

    def group_issued():
        run_now = []
        keep = []
        for d in deferred:
            d[0] -= 1
            (run_now if d[0] <= 0 else keep).append(d)
        deferred[:] = keep
        for d in run_now:
            d[1]()

    def flush_deferred():
        while deferred:
            group_issued()

    def dap(t, off, pat):
        return t.full().with_ap(bass.AP(t.ap.tensor, off, pat))

    def ws_k8(b):
        return WS.ap[b].rearrange("p (k c) -> p k c", k=8)

    def cast_piece(b, off, w, src, c0):
        dst = WS[b].with_ap(ws_k8(b)[:, :, off:off + w])
        s = src.full().with_ap(src.ap[:, c0:c0 + w].rearrange("(k p) c -> p k c", p=128))
        P.dma("pool", dst, s)

    def cast_block(b):
        if B_Q <= b < B_K:
            cast_piece(b, 0, 512, w_in, (b - B_Q) * 512)
        elif B_K <= b < B_V:
            cast_piece(b, 0, 512, w_in, OFF_KA + (b - B_K) * 512)
        elif B_V <= b < B_H:
            cast_piece(b, 0, 512, w_in, OFF_VA + (b - B_V) * 512)
        elif b == B_BA:
            cast_piece(B_BA, 0, 16, w_in, OFF_BETA)
        elif B_H <= b < B_BA:
            h = b - B_H
            for j in range(3):
                cast_piece(b, j * 128, 128, w_in, OFF_B + j * 1024 + h * 128)
            cast_piece(b, 384, 128, w_in, OFF_Z + h * 128)
        elif B_M <= b < B_WO:
            oc = b - B_M
            cast_piece(b, 0, 128, w_in, OFF_GA + oc * 128)
            cast_piece(b, 128, 128, w_in, OFF_GB + oc * 128)
            cast_piece(b, 256, 128, w_bra, oc * 128)
            cast_piece(b, 384, 128, w_brb, oc * 128)
        elif B_WO <= b < B_WU:
            cast_piece(b, 0, 512, w_out, (b - B_WO) * 512)
        elif B_WU <= b < B_WD:
            cast_piece(b, 0, 512, w_up, (b - B_WU) * 512)
        else:
            oc = b - B_WD
            dst = WS[b].with_ap(WS.ap[b].rearrange("p (f c) -> p f c", f=32))
            s_ = w_down.full().with_ap(w_down.ap[:, oc * 128:(oc + 1) * 128].rearrange("(f p) c -> p f c", p=128))
            P.dma("pool", dst, s_)

    cast_order = ([B_K, B_K + 1, B_V, B_V + 1, B_BA] + [B_H + h for h in range(8)] + [B_Q, B_Q + 1]
                  + [B_M + i for i in range(8)] + [B_WO, B_WO + 1] + [B_WU + i for i in range(8)] + [B_WD + i for i in range(8)])
    cast_done = [0]

    def cast_upto(n):
        while cast_done[0] < min(n, len(cast_order)):
            cast_block(cast_order[cast_done[0]])
            cast_done[0] += 1
    cast_upto(4)

    identf = alloc([128]); P.dma("sp", identf.full(), cst["identf"].full())
    identb = alloc([128], BF16); cp(identb.full(), identf.full())
    ones_bf = alloc([128], BF16); memset(ones_bf.full(), 1.0)
    negU_incl = alloc([8, 64], BF16)
    negU_strict = alloc([8, 64], BF16)
    negL_strict = alloc([8, 64], BF16)
    identrep = alloc([8, 64], BF16)
    blockmask = alloc([8, 64], BF16)
    qn_bc = alloc([8, 64]); P.dma("sp", qn_bc.full(), dap(q_norm, 0, [[0, 128], [0, 8], [1, 64]]))
    kn_bc = alloc([8, 64]); P.dma("sp", kn_bc.full(), dap(k_norm, 0, [[0, 128], [0, 8], [1, 64]]))
    small = alloc([64])
    P.dma("sp", small[:, 0:1], dap(gdn_norm, 0, [[1, 128], [1, 1]]))
    P.dma("sp", small[:, 1:2], dap(sub_norm, 0, [[1, 128], [1, 1]]))
    ts(small[:, 1:2], small[:, 1:2], 1.0 - LAM_INIT, ALU.mult)
    P.dma("sp", small[0:8, 3:4], dap(dt_bias, 0, [[1, 8], [1, 1]]))
    P.dma("sp", small[0:8, 5:6], dap(a_log, 0, [[1, 8], [1, 1]]))
    act(small[0:8, 4:5], small[0:8, 5:6], AF.Exp)
    ts(small[0:8, 4:5], small[0:8, 4:5], -1.0, ALU.mult)
    b15 = alloc([8]); P.dma("sp", b15.full(), dap(relb, 15 * 8, [[0, 128], [1, 8]]))
    rowsA = alloc([128]); P.dma("sp", rowsA[0:16, :], b_gate.full().with_ap(b_gate.ap.rearrange("(t p) -> t p", p=128)))
    rowsC = alloc([128]); P.dma("sp", rowsC[0:96, :], conv_w.full().with_ap(conv_w.ap.rearrange("(t p) -> t p", p=128)))
    bgT = alloc([16])
    cwT = alloc([96])
    pb = pbank()
    tr(psb[pb][:, 0:16], rowsA[0:16, :], identf[0:16, 0:16])
    tr(psb[pb][:, 16:112], rowsC[0:96, :], identf[0:96, 0:96])
    cp(bgT.full(), psb[pb][:, 0:16])
    cp(cwT.full(), psb[pb][:, 16:112])
    G = alloc([8, GW], BF16)
    m0 = top[0]
    lam4 = alloc([4, 64])
    for i, t in enumerate((lq1, lk1, lq2, lk2)):
        P.dma("sp", lam4[:, i, :], dap(t, 0, [[0, 128], [1, 64]]))
    tt(lam4[:, 0, :], lam4[:, 0, :], lam4[:, 1, :], ALU.mult)
    tt(lam4[:, 2, :], lam4[:, 2, :], lam4[:, 3, :], ALU.mult)
    red(small[:, 6:7], lam4[:, 0, :]); red(small[:, 7:8], lam4[:, 2, :])
    act(small[:, 6:8], small[:, 6:8], AF.Exp)
    tt(small[:, 8:9], small[:, 7:8], small[:, 6:7], ALU.subtract)
    ts(small[:, 2:3], small[:, 8:9], -LAM_INIT, ALU.add)
    for nm_, dst_, rows_ in (("negU_incl", negU_incl, 64), ("negU_strict", negU_strict, 64), ("negL_strict", negL_strict, 64),
                             ("identrep", identrep, 64), ("blockmask", blockmask, 8)):
        stg_ = alloc([8, 64])
        P.dma("sp", stg_[0:rows_], cst[nm_].full().with_ap(cst[nm_].ap.rearrange("p (a b) -> p a b", a=8)))
        cp(dst_[0:rows_], stg_[0:rows_])
    onehot = alloc([GL]); P.dma("sp", onehot[0:32, :], cst["onehot"].full())
    tab = alloc([8]); P.dma("sp", tab[0:32, :], relb.full())
    tabrep = alloc([8, 128])
    cp(tabrep[0:32], bc3(tab[0:32, :], [32, 8, 128]))
    maskG = alloc([GW]); P.dma("sp", maskG.full(), cst["maskG"].full())
    frep = alloc([GL])
    gsk = alloc([GW])
    for h in range(8):
        for j in range(3):
            pb = pbank()
            mm(psb[pb][:, 0:384], tabrep[0:32, h, :], onehot[0:32, j * 384:(j + 1) * 384])
            ts(frep[:, j * 384:(j + 1) * 384], psb[pb][:, 0:384], 1.0 / A_SCALE, ALU.mult)
        P.dma("sp", FD[h], frep.full())
        P.dma("sp", gsk.full(), FD[h].with_ap(bass.AP(FD.ap.tensor, h * 128 * GL + 127, [[GL - 1, 128], [1, GW]])))
        tt(G[:, h, :], gsk.full(), maskG.full(), ALU.add)
    top[0] = m0

    S_meta = alloc([8, 128])
    ctx_meta = alloc([24, 3])
    KTm = alloc([8, 16], BF16)
    Vm = alloc([1024], BF16)
    S_cur = alloc([8, 128])
    S_bf = alloc([8, 128], BF16)
    ctx_cur = alloc([24, 4, 3])
    NSLOT = 3
    wring = [alloc([4096], BF16) for _ in range(NSLOT)]
    wcnt = [0]

    def wload(b):
        cast_upto(cast_order.index(b) + 4)
        slot = wring[wcnt[0] % NSLOT]
        wcnt[0] += 1
        if b == B_BA:
            P.dma("sp", wk8(slot)[:, :, 0:16], WS[b].with_ap(ws_k8(b)[:, :, 0:16]))
        else:
            P.dma("sp", slot.full(), WS[b])
        return slot

    def wk8(slot):
        return T(slot.ap.rearrange("p (k c) -> p k c", k=8), "arena", [128, 8, 512], esize=2, base_off=slot.base_off)

    def wf32(slot):
        return T(slot.ap.rearrange("p (f c) -> p f c", f=32), "arena", [128, 32, 128], esize=2, base_off=slot.base_off)

    base_top = top[0]

    def run_tile(kind, s=0, t=0):
        top[0] = base_top
        if kind == "meta":
            NT, ST, nst, nseg, L, C = 16, 16, 1, 1, 16, 16
        elif kind == "prompt":
            NT, ST, nst, nseg, L, C = 512, 128, 4, 1, 512, 64
        else:
            NT, ST, nst, nseg, L, C = 256, 128, 2, 4, 64, 64
        nch = NT // C
        xtok = alloc([nst, D])
        xnT = alloc([8, NT], BF16)
        sstat = alloc([32])

        for st in range(nst):
            if kind == "meta":
                src = meta.full()
            elif kind == "prompt":
                src = xp[s, t * 512 + st * 128: t * 512 + (st + 1) * 128, :]
            else:
                src = xs[st * 128:(st + 1) * 128, :]
            P.dma("sp", xtok[0:ST, st, :], src)

        def norm_T(norm_dram):
            m = top[0]
            norm_bc = alloc([D])
            P.dma("sp", norm_bc.full(), dap(norm_dram, 0, [[0, 128], [1, D]]))
            junk = alloc([D])
            xnb = alloc([D], BF16)
            for st in range(nst):
                memset(sstat[0:ST, st:st + 1], 0.0)
                act(junk[0:ST, :], xtok[0:ST, st, :], AF.Square, accum=sstat[0:ST, st:st + 1])
                rsqrt(sstat[0:ST, 8 + st:9 + st], sstat[0:ST, st:st + 1], 1.0 / D, sstat[0:ST, 16 + st:17 + st])
                stt(xnb[0:ST, :], xtok[0:ST, st, :], sstat[0:ST, 8 + st:9 + st], norm_bc[0:ST, :], ALU.mult, ALU.mult)
                pb = pbank()
                for kc in range(8):
                    tr(psb16[pb][:, kc * ST:(kc + 1) * ST], xnb[0:ST, kc * 128:(kc + 1) * 128], identb[0:ST, 0:ST])
                src = psb16[pb][:, 0:8 * ST]
                cp(xnT[:, :, st * ST:(st + 1) * ST], src.with_ap(src.ap.rearrange("p (k t) -> p k t", k=8)), eng="act")
            top[0] = m

        if stage < 0:
            return
        P.phase = kind + ":norm1"
        norm_T(norm1)
        if stage < 1:
            flush_deferred()
            return
        P.phase = kind + ":qkvproj"

        oaT = alloc([8, NT], BF16) if kind != "meta" else None
        vnew_s = alloc([4, D], BF16) if kind == "sample" else None
        m_attn = top[0]
        qT = alloc([8, NT], BF16) if kind != "meta" else None

        NB_ = 4
        sqs = [alloc([512]) for _ in range(NB_)]
        t1s = [alloc([512]) for _ in range(NB_)]
        kns = [alloc([512]) for _ in range(NB_)]
        kbs = [alloc([512], BF16) for _ in range(NB_)]
        ktiles = [alloc([4, ST], BF16) for _ in range(NB_)]
        rsts = [alloc([16]) for _ in range(NB_)]
        ptc = [0]

        def post2(which, half, cols, st, i):
            kb = kbs[i]; kn = kns[i]; ktile = ktiles[i]
            pb2 = pbank()
            for hh in range(4):
                tr(psb16[pb2][:, hh * ST:(hh + 1) * ST], kb[0:ST, hh * 128:(hh + 1) * 128], identb[0:ST, 0:ST])
            src = psb16[pb2][:, 0:4 * ST]
            srcv = src.with_ap(src.ap.rearrange("p (k t) -> p k t", k=4))
            if which == "q":
                cp(qT[:, half * 4:(half + 1) * 4, st * ST:(st + 1) * ST], srcv, eng="act")
                return
            cp(ktile.full(), srcv, eng="act")
            if kind == "meta":
                cp(KTm[:, half * 4:(half + 1) * 4, :], ktile.full(), eng="pool")
            elif kind == "prompt":
                tok0 = NMETA + t * 512 + st * 128
                P.dma("sp", KTd[s, half * 4:(half + 1) * 4, :, tok0:tok0 + 128].with_ap(
                    KTd.ap[s, half * 4:(half + 1) * 4, :, tok0:tok0 + 128].rearrange("h p t -> p h t")), ktile.full())
            else:
                for q2 in range(2):
                    sq_ = st * 2 + q2
                    P.dma("sp", KTs[sq_, half * 4:(half + 1) * 4, :, PAST:PAST + 64].with_ap(
                        KTs.ap[sq_, half * 4:(half + 1) * 4, :, PAST:PAST + 64].rearrange("h p t -> p h t")),
                        ktile[:, :, q2 * 64:(q2 + 1) * 64])

        def proj_tok(blk_id, half, which):
            slot = wk8(wload(blk_id))
            cols = slice(half * 512, (half + 1) * 512)
            for st in range(nst):
                pb = pbank()
                for kc in range(8):
                    mm(psb[pb][0:ST, :], xnT[:, kc, st * ST:(st + 1) * ST], slot[:, kc, :], start=(kc == 0), stop=(kc == 7))
                group_issued()
                i = ptc[0] % NB_
                ptc[0] += 1
                ps = psb[pb]
                sq = sqs[i]; t1 = t1s[i]; kn = kns[i]; kb = kbs[i]; rs = rsts[i]
                PTS = int(os.environ.get("PT_STOP", "9"))
                if PTS <= 1 or (which == "v" and os.environ.get("PT_VSKIP", "0") == "1"):
                    cp(sq[0:ST, :], ps[0:ST, :])
                    continue
                if which in ("q", "k"):
                    act(sq[0:ST, :], ps[0:ST, :], AF.Square)
                    sqv = sq[0:ST, :]
                    red(rs[0:ST, 0:8], sqv.with_ap(sqv.ap.rearrange("p (a b) -> p a b", a=8)))
                    PRS = int(os.environ.get("PT_RS", "2"))
                    if PRS == 2:
                        rsqrt(rs[0:ST, 0:8], rs[0:ST, 0:8], 1.0 / 64, rs[0:ST, 8:16])
                    elif PRS == 1:
                        act(rs[0:ST, 8:16], rs[0:ST, 0:8], AF.Sqrt, bias=EPS, scale=1.0 / 64)
                        P.op("dve", lambda e, rs=rs: e.reciprocal(out=rs[0:ST, 0:8].ap, in_=rs[0:ST, 8:16].ap), reads=[rs[0:ST, 8:16]], writes=[rs[0:ST, 0:8]])
                    if PTS <= 2:
                        continue
                    psv = ps[0:ST, :]
                    t1v = t1[0:ST, :]
                    tt(t1v.with_ap(t1v.ap.rearrange("p (a b) -> p a b", a=8)), psv.with_ap(psv.ap.rearrange("p (a b) -> p a b", a=8)),
                       bc3(rs[0:ST, 0:8], [ST, 8, 64]), ALU.mult)
                    wbc = (qn_bc if which == "q" else kn_bc)[0:ST]
                    wflat = wbc.with_ap(wbc.ap.rearrange("p a b -> p (a b)"))
                    tt(kb[0:ST, :], t1v, wflat, ALU.mult)
                    if which == "k":
                        tt(kn[0:ST, :], t1v, wflat, ALU.mult, eng=("pool" if os.environ.get("PT_POOLMUL", "1") == "1" else "dve"))
                        if kind == "meta":
                            for s2 in range(NPS):
                                P.dma("pool", kp[s2, 0:16, cols], kn[0:ST, :])
                        elif kind == "prompt":
                            tok0 = NMETA + t * 512 + st * 128
                            P.dma("pool", kp[s, tok0:tok0 + 128, cols], kn[0:ST, :])
                        else:
                            P.dma("pool", kso[st * 128:(st + 1) * 128, cols], kn[0:ST, :])
                    if PTS >= 4:
                        defer(lambda which=which, half=half, cols=cols, st=st, i=i: post2(which, half, cols, st, i), delay=2)
                else:
                    cp(kn[0:ST, :], ps[0:ST, :], eng="act")
                    if kind == "meta":
                        if os.environ.get("PT_VCP", "1") == "1":
                            cp(Vm[0:ST, cols], ps[0:ST, :])
                        else:
                            cp(Vm[0:ST, cols], kn[0:ST, :], eng="pool")
                        for s2 in range(NPS):
                            P.dma("pool", vp[s2, 0:16, cols], kn[0:ST, :])
                    elif kind == "prompt":
                        tok0 = NMETA + t * 512 + st * 128
                        cp(kb[0:ST, :], ps[0:ST, :])
                        P.dma("pool", vp[s, tok0:tok0 + 128, cols], kn[0:ST, :])
                        P.dma("sp", Vd[s, tok0:tok0 + 128, cols], kb[0:ST, :])
                    else:
                        P.dma("pool", vso[st * 128:(st + 1) * 128, cols], kn[0:ST, :])
                        cp(kb[0:ST, :], ps[0:ST, :])
                        for q2 in range(2):
                            P.dma("sp", Vsd[st * 2 + q2, :, cols], kb[q2 * 64:(q2 + 1) * 64, :])

        if kind != "meta":
            proj_tok(B_Q, 0, "q"); proj_tok(B_Q + 1, 1, "q")
        proj_tok(B_K, 0, "k"); proj_tok(B_K + 1, 1, "k")
        proj_tok(B_V, 0, "v"); proj_tok(B_V + 1, 1, "v")
        flush_deferred()
        if stage < 2:
            return
        P.phase = kind + ":attn"
        if kind != "meta":
            m = top[0]
            NQ = 512 if kind == "prompt" else 64
            nkeys = (NMETA + (t + 1) * 512) if kind == "prompt" else (PAST + 64)
            ktb = [alloc([TP], BF16) for _ in range(2)]
            vtb = [alloc([17, 128], BF16) for _ in range(2)]
            pT = [alloc([512], BF16) for _ in range(4)]
            o1s = [alloc([512]) for _ in range(2)]; o2 = alloc([512]); rr = alloc([512]); rr2 = alloc([512])
            rr3 = alloc([512]); rr4 = alloc([512]); osq = alloc([512], BF16)
            pcount = [0]
            segs = [0] if kind == "prompt" else list(range(4))
            hl = [(sg_, h) for sg_ in segs for h in range(8)]

            def load_kv(i):
                sg_, h = hl[i]
                kt = ktb[i % 2]; vt = vtb[i % 2]
                if kind == "prompt":
                    P.dma("sp", kt[:, 16:nkeys], KTd[s, h, :, 16:nkeys])
                    for g4 in range(t + 1):
                        r0 = NMETA + g4 * 512
                        P.dma("sp", vt[:, g4 * 4:(g4 + 1) * 4, :], Vd[s, r0:r0 + 512, h * 128:(h + 1) * 128].with_ap(
                            Vd.ap[s, r0:r0 + 512, h * 128:(h + 1) * 128].rearrange("(c p) e -> p c e", p=128)))
                else:
                    P.dma("sp", kt[:, 0:nkeys], KTs[sg_, h, :, 0:nkeys])
                    for g4 in range(2):
                        P.dma("sp", vt[:, g4 * 4:(g4 + 1) * 4, :], Vcd[sg_, g4 * 512:(g4 + 1) * 512, h * 128:(h + 1) * 128].with_ap(
                            Vcd.ap[sg_, g4 * 512:(g4 + 1) * 512, h * 128:(h + 1) * 128].rearrange("(c p) e -> p c e", p=128)))
            if kind == "sample":
                memset(vnew_s[64:128], 0.0)
                for kt_ in ktb:
                    memset(kt_[:, PAST + 64:PAST + 128], 0.0)
                for sq_ in range(NSS):
                    P.dma("sp", vnew_s[0:64, sq_, :], Vsd[sq_])
            load_kv(0)
            for i, (sg_, h) in enumerate(hl):
                if i + 1 < len(hl):
                    load_kv(i + 1)
                kt = ktb[i % 2]; vt = vtb[i % 2]
                q0c = sg_ * 64 if kind == "sample" else 0
                blocks = []
                if kind == "prompt":
                    blocks.append((KTm[:, h, :], Vm[0:16, h * 128:(h + 1) * 128], 16, (GOFF + 16) if t == 0 else None))
                    for kc in range((t + 1) * 4):
                        delta = kc * 128 - t * 512
                        win = (GOFF - delta) if delta >= -128 else None
                        blocks.append((kt[:, 16 + kc * 128:16 + (kc + 1) * 128], vt[:, kc, :], 128, win))
                else:
                    for kc in range(8):
                        win = (GOFF + 128) if kc == 7 else None
                        blocks.append((kt[:, kc * 128:(kc + 1) * 128], vt[:, kc, :], 128, win))
                    blocks.append((kt[:, PAST:PAST + 128], vnew_s[:, sg_, h * 128:(h + 1) * 128], 128, GOFF))
                import os as _os
                _sk = _os.environ.get("ATT_SKIP", "")
                if kind == "sample" and _sk:
                    nb_ = []
                    for bi_, blk in enumerate(blocks):
                        typ = "new" if bi_ == 8 else ("win7" if bi_ == 7 else "far")
                        if typ not in _sk:
                            nb_.append(blk)
                    blocks = nb_
                nb = len(blocks)
                for bi, (kv, vv, nk, win) in enumerate(blocks):
                    for mp in range(2):
                        pbS = pbank(4)
                        S = psb[pbS][0:nk, 0:NQ]
                        mm(S, V(kv.ap[mp * 64:(mp + 1) * 64, :], kv.key, kv.lo, kv.hi, kv.page), qT[mp * 64:(mp + 1) * 64, h, q0c:q0c + NQ],
                           start=True, stop=(win is None))
                        if win is not None:
                            mm(S, identb[0:nk, 0:nk], G[0:nk, h, win:win + NQ], start=False, stop=True)
                        pt = pT[pcount[0] % 4]; pcount[0] += 1
                        if win is None:
                            act(pt[0:nk, 0:NQ], S, AF.Exp, bias=b15[0:nk, h:h + 1], scale=A_SCALE)
                        else:
                            act(pt[0:nk, 0:NQ], S, AF.Exp, scale=A_SCALE)
                        mm(psb[4 + mp][:, 0:NQ], vv, pt[0:nk, 0:NQ], start=(bi == 0), stop=(bi == nb - 1))
                        mm(psb[6 + mp][:, 0:NQ], ones_bf[0:nk, :], pt[0:nk, 0:NQ], start=(bi == 0), stop=(bi == nb - 1))
                        group_issued()
                o1 = o1s[i % 2]
                act(rr[:, 0:NQ], psb[6][:, 0:NQ], AF.Ln)
                act(rr2[:, 0:NQ], psb[7][:, 0:NQ], AF.Ln)
                act(rr[:, 0:NQ], rr[:, 0:NQ], AF.Exp, scale=-1.0)
                act(rr2[:, 0:NQ], rr2[:, 0:NQ], AF.Exp, scale=-1.0)
                tt(o1[:, 0:NQ], psb[4][:, 0:NQ], rr[:, 0:NQ], ALU.mult)
                tt(o2[:, 0:NQ], psb[5][:, 0:NQ], rr2[:, 0:NQ], ALU.mult)
                stt(o1[:, 0:NQ], o2[:, 0:NQ], small[:, 2:3], o1[:, 0:NQ], ALU.mult, ALU.add)

                def finish_head(o1=o1, h=h, q0c=q0c):
                    act(osq[:, 0:NQ], o1[:, 0:NQ], AF.Square)
                    pbn = pbank(4)
                    mm(psb[pbn][:, 0:NQ], ones_bf.full(), osq[:, 0:NQ])
                    rsqrt(rr3[:, 0:NQ], psb[pbn][:, 0:NQ], 1.0 / 128, rr4[:, 0:NQ])
                    stt(oaT[:, h, q0c:q0c + NQ], o1[:, 0:NQ], small[:, 1:2], rr3[:, 0:NQ], ALU.mult, ALU.mult)
                defer(finish_head, delay=3)
            flush_deferred()
            top[0] = m
        top[0] = m_attn
        if dbg and kind == "prompt" and s == 0 and t == 0:
            P.dma("pool", dbg_oa.full(), oaT.full())
        if stage < 3:
            return
        P.phase = kind + ":gdnproj"
        obT = alloc([8, NT], BF16) if kind != "meta" else None
        m_gdn = top[0]
        qg = alloc([8, NT], BF16); kg = alloc([8, NT], BF16); vg = alloc([8, NT], BF16)
        sz = alloc([8, NT], BF16) if kind != "meta" else None
        og = alloc([8, NT], BF16) if kind != "meta" else None
        cb = alloc([8, NT])
        slotBA = wk8(wload(B_BA))
        pb = pbank()
        for kc in range(8):
            mm(psb[pb][0:8, 0:NT], slotBA[:, kc, 0:8], xnT[:, kc, :], start=(kc == 0), stop=(kc == 7))
        pb2 = pbank()
        for kc in range(8):
            mm(psb[pb2][0:8, 0:NT], slotBA[:, kc, 8:16], xnT[:, kc, :], start=(kc == 0), stop=(kc == 7))
        act(cb[0:8, 4, :], psb[pb][0:8, 0:NT], AF.Sigmoid)
        act(cb[0:8, 6, :], psb[pb][0:8, 0:NT], AF.Exp, scale=-1.0)
        act(cb[0:8, 6, :], cb[0:8, 6, :], AF.Ln, bias=1.0)
        act(cb[0:8, 7, :], psb[pb2][0:8, 0:NT], AF.Exp, bias=small[0:8, 3:4])
        act(cb[0:8, 7, :], cb[0:8, 7, :], AF.Ln, bias=1.0)
        ts(cb[0:8, 0, :], cb[0:8, 7, :], small[0:8, 4:5], ALU.mult)
        a_, b_ = 0, 7
        sh = 1
        while sh < C:
            av = cb[0:8, a_, :]; bv = cb[0:8, b_, :]
            a3 = av.with_ap(av.ap.rearrange("p (c l) -> p c l", l=C)); b3 = bv.with_ap(bv.ap.rearrange("p (c l) -> p c l", l=C))
            cp(V(b3.ap[:, :, 0:sh], bv.key, bv.lo, bv.hi, bv.page), V(a3.ap[:, :, 0:sh], av.key, av.lo, av.hi, av.page))
            tt(V(b3.ap[:, :, sh:C], bv.key, bv.lo, bv.hi, bv.page), V(a3.ap[:, :, sh:C], av.key, av.lo, av.hi, av.page),
               V(a3.ap[:, :, 0:C - sh], av.key, av.lo, av.hi, av.page), ALU.add)
            a_, b_ = b_, a_
            sh *= 2
        if a_ != 0:
            cp(cb[0:8, 0, :], cb[0:8, a_, :])
        tt(cb[0:8, 1, :], cb[0:8, 0, :], cb[0:8, 6, :], ALU.subtract)
        act(cb[0:8, 2, :], cb[0:8, 0, :], AF.Exp)
        gv = cb[0:8, 0, :]
        g3 = gv.with_ap(gv.ap.rearrange("p (c l) -> p c l", l=C))
        kdv = cb[0:8, 3, :]
        kd3 = kdv.with_ap(kdv.ap.rearrange("p (c l) -> p c l", l=C))
        tt(kd3, V(g3.ap[:, :, C - 1:C].to_broadcast([8, nch, C]), gv.key, gv.lo, gv.hi, gv.page), g3, ALU.subtract)
        act(cb[0:8, 3, :], cb[0:8, 3, :], AF.Exp)
        tt(cb[0:8, 5, :], cb[0:8, 4, :], cb[0:8, 2, :], ALU.mult)
        cbb = alloc([8, NT], BF16)
        for q_, row in enumerate((0, 1, 2)):
            cp(cbb[0:8, 2 * q_, :], cb[0:8, row, :])
            tt(cb[0:8, 6, :], cb[0:8, row, :], cbb[0:8, 2 * q_, :], ALU.subtract)
            cp(cbb[0:8, 2 * q_ + 1, :], cb[0:8, 6, :])
        ts(cbb[0:8, 6, :], cbb[0:8, 0, :], -1.0, ALU.mult)
        ts(cbb[0:8, 7, :], cbb[0:8, 1, :], -1.0, ALU.mult)

        m_conv = top[0]
        cin = [alloc([nseg, L + 3]) for _ in range(3)]
        cacc = alloc([nseg, L])
        csq = alloc([NT], BF16)
        crn = alloc([NT])
        if kind == "sample":
            scrow = alloc([3072])
            for sg_ in range(4):
                P.dma("sp", scrow[0:3, :], sc[sg_])
                for g6 in range(6):
                    pb = pbank()
                    for c4 in range(4):
                        cid_ = g6 * 4 + c4
                        tr(psb[pb][:, c4 * 3:(c4 + 1) * 3], scrow[0:3, cid_ * 128:(cid_ + 1) * 128], identf[0:3, 0:3])
                    pv_ = psb[pb][:, 0:12]
                    cp(ctx_cur[:, g6 * 4:(g6 + 1) * 4, sg_, :], pv_.with_ap(pv_.ap.rearrange("p (c w) -> p c w", c=4)))
        caccs = [cacc] + [alloc([nseg, L]) for _ in range(5)]
        ctmp = alloc([nseg, L])
        csqs = [csq] + [alloc([NT], BF16) for _ in range(3)]
        crns = [crn, alloc([NT])]

        def l2norm_finish(h, j):
            ca = caccs[(h % 2) * 3 + j]
            cflat = ca.full().with_ap(ca.ap.rearrange("p s l -> p (s l)"))
            crn_ = crns[j]
            pbn = pbank()
            mm(psb[pbn][:, 0:NT], ones_bf.full(), csqs[(h % 2) * 2 + j].full())
            act(crn_.full(), psb[pbn][:, 0:NT], AF.Ln, bias=EPS, scale=1.0)
            act(crn_.full(), crn_.full(), AF.Exp, scale=-0.5)
            if j == 0:
                stt(qg[:, h, :], cflat, B_SCALE, crn_.full(), ALU.mult, ALU.mult)
            else:
                tt(kg[:, h, :], cflat, crn_.full(), ALU.mult)

        def silu_qk(h, j):
            ca = caccs[(h % 2) * 3 + j]
            cflat = ca.full().with_ap(ca.ap.rearrange("p s l -> p (s l)"))
            act(cflat, cflat, AF.Silu)
            act(csqs[(h % 2) * 2 + j].full(), cflat, AF.Square)

        def silu_v(h):
            ca = caccs[(h % 2) * 3 + 2]
            act(vg[:, h, :], ca.full().with_ap(ca.ap.rearrange("p s l -> p (s l)")), AF.Silu)

        for h in range(8):
            slot = wk8(wload(B_H + h))
            for j in range(4):
                pb = pbank()
                for kc in range(8):
                    mm(psb[pb][:, 0:NT], slot[:, kc, j * 128:(j + 1) * 128], xnT[:, kc, :], start=(kc == 0), stop=(kc == 7))
                group_issued()
                ps = psb[pb][:, 0:NT]
                if j == 3:
                    if kind != "meta":
                        act(sz[:, h, :], ps, AF.Silu)
                    continue
                cid = j * 8 + h
                ci = cin[j]
                if kind == "meta":
                    memset(ci[:, :, 0:3], 0.0, eng="pool")
                elif kind == "prompt":
                    cp(ci[:, 0, 0:3], (ctx_meta[:, cid, :] if t == 0 else ctx_cur[:, cid, 0, :]), eng="pool")
                else:
                    cp(ci[:, :, 0:3], ctx_cur[:, cid, 0:4, :], eng="pool")
                cp(ci[:, :, 3:3 + L], ps.with_ap(ps.ap.rearrange("p (s l) -> p s l", s=nseg)), eng="act")
                if kind == "meta":
                    cp(ctx_meta[:, cid, :], ci[:, 0, L:L + 3], eng="pool")
                else:
                    cp(ctx_cur[:, cid, 0:nseg, :], ci[:, :, L:L + 3], eng="pool")
            for j in range(3):
                cid = j * 8 + h
                ci = cin[j]
                ce = "pool" if j == 2 else "dve"
                ca = caccs[(h % 2) * 3 + j]
                ts(ca.full(), ci[:, :, 0:L], cwT[:, cid:cid + 1], ALU.mult, eng=ce)
                for w in range(1, 4):
                    if ce == "dve":
                        stt(ca.full(), ci[:, :, w:w + L], cwT[:, w * 24 + cid:w * 24 + cid + 1], ca.full(), ALU.mult, ALU.add)
                    else:
                        ts(ctmp.full(), ci[:, :, w:w + L], cwT[:, w * 24 + cid:w * 24 + cid + 1], ALU.mult, eng="pool")
                        tt(ca.full(), ca.full(), ctmp.full(), ALU.add, eng="pool")
            defer(lambda h=h: (silu_qk(h, 0), silu_qk(h, 1)), delay=1)
            defer(lambda h=h: silu_v(h), delay=3)
            defer(lambda h=h: (l2norm_finish(h, 0), l2norm_finish(h, 1)), delay=3)
        flush_deferred()
        if (kind == "prompt" and t == 3) or kind == "sample":
            tls = [alloc([512]) for _ in range(2)]
            for sg_ in range(nseg):
                for g6 in range(6):
                    tl = tls[g6 % 2]
                    pb = pbank()
                    for c4 in range(4):
                        tr(psb[pb][0:3, c4 * 128:(c4 + 1) * 128], ctx_cur[:, g6 * 4 + c4, sg_, :], identf.full())
                    cp(tl[0:3, :], psb[pb][0:3, :], eng="act")
                    dst_ = cpo[s, :, g6 * 512:(g6 + 1) * 512] if kind == "prompt" else cso[sg_, :, g6 * 512:(g6 + 1) * 512]
                    P.dma("pool", dst_, tl[0:3, :])

        P.phase = kind + ":gdnchunk"
        top[0] = m_conv
        nlev = {64: 5, 16: 3}[C]
        if C == 64:
            nU_i, nU_s, nL_s, idr, bmk = negU_incl[0:C], negU_strict[0:C], negL_strict[0:C], identrep[0:C], blockmask[0:8]
        else:
            cm = []
            for src_, np_ in ((negU_incl, C), (negU_strict, C), (negL_strict, C), (identrep, C), (blockmask, 8)):
                d_ = alloc([8, C], BF16)
                cp(d_[0:np_], src_[0:np_, :, 0:C])
                cm.append(d_[0:np_])
            nU_i, nU_s, nL_s, idr, bmk = cm
        gdb = [alloc([8, C], BF16) for _ in range(8)]
        tokc = alloc([32])
        DTi = alloc([8, C], BF16); NDT = alloc([8, C], BF16); NTD = alloc([8, C], BF16)
        Pm = [alloc([8, C], BF16) for _ in range(2)]; PmT = [alloc([8, C], BF16) for _ in range(2)]
        Rm = [alloc([8, C], BF16) for _ in range(2)]
        MT = alloc([8, C], BF16); qgc = alloc([8, C], BF16); nwT = alloc([8, C], BF16)
        bv_ = alloc([8, 128], BF16); kbg = alloc([8, 128], BF16); kdc = alloc([8, 128], BF16); vnw = alloc([8, 128], BF16)
        egl = alloc([8])

        def fl(v):
            return v.with_ap(v.ap.rearrange("p a b -> p (a b)"))

        for ci_ in range(nch):
            sgi = ci_ if kind == "sample" else 0
            cs = slice(ci_ * C, (ci_ + 1) * C)
            W8 = 8 * C
            if kind == "meta":
                if ci_ == 0:
                    memset(S_cur.full(), 0.0); memset(S_bf.full(), 0.0)
            elif kind == "prompt":
                if ci_ == 0 and t == 0:
                    cp(S_cur.full(), S_meta.full()); cp(S_bf.full(), S_meta.full(), eng="act")
            else:
                P.dma("sp", S_cur.full(), sg[sgi].with_ap(sg.ap[sgi].rearrange("h d e -> d h e")))
                cp(S_bf.full(), S_cur.full(), eng="act")
            for k_ in range(8):
                src = cbb[0:8, k_, cs]
                tt(gdb[k_][0:8], bmk, V(src.ap.unsqueeze(1).to_broadcast([8, 8, C]), src.key, src.lo, src.hi, src.page), ALU.mult, eng="pool")
            pbt = pbank()
            for k_, row in enumerate((4, 5, 3)):
                tr(psb[pbt][0:C, k_ * 8:(k_ + 1) * 8], cb[0:8, row, cs], identf[0:8, 0:8])
            cp(tokc[0:C, 0:24], psb[pbt][0:C, 0:24])
            on8 = ones_bf[0:8, 0:C]

            def xmat(diag_hi, diag_lo, col_hi, col_lo, mask):
                pbx = pbank()
                X = psb[pbx][0:C, 0:W8]
                mm(X, on8, fl(gdb[diag_hi][0:8]), start=True, stop=False)
                mm(X, on8, fl(gdb[diag_lo][0:8]), start=False, stop=False)
                mm(X, cbb[0:8, col_hi, cs], fl(bmk), start=False, stop=False)
                mm(X, cbb[0:8, col_lo, cs], fl(bmk), start=False, stop=False)
                mm(X, identb[0:C, 0:C], fl(mask), start=False, stop=True)
                return X
            act(fl(DTi[0:C]), xmat(0, 1, 6, 7, nU_i), AF.Exp)
            act(fl(NDT[0:C]), xmat(2, 3, 6, 7, nU_s), AF.Exp)
            act(fl(NTD[0:C]), xmat(6, 7, 2, 3, nL_s), AF.Exp)
            pbe = pbank()
            mm(psb[pbe][:, 0:W8], ones_bf[0:8, :], fl(gdb[4][0:8]), start=True, stop=False)
            mm(psb[pbe][:, 0:W8], ones_bf[0:8, :], fl(gdb[5][0:8]), start=False, stop=True)
            pe_v = psb[pbe][:, 0:W8]
            pe3 = pe_v.with_ap(pe_v.ap.rearrange("p (h c) -> p h c", h=8))
            tt(qgc[:, :, 0:C], qg[:, :, cs], pe3, ALU.mult)
            cp(egl.full(), V(pe3.ap[:, :, C - 1], pe_v.key, pe_v.lo, pe_v.hi, pe_v.page))
            pbk = pbank(); pbq = pbank()
            for h in range(8):
                mm(psb[pbk][0:C, h * C:(h + 1) * C], kg[:, h, cs], kg[:, h, cs])
            for h in range(8):
                mm(psb[pbq][0:C, h * C:(h + 1) * C], kg[:, h, cs], qg[:, h, cs])
            stt(fl(Pm[0][0:C]), psb[pbk][0:C, 0:W8], -1.0, fl(NDT[0:C]), ALU.mult, ALU.mult)
            stt(fl(PmT[0][0:C]), psb[pbk][0:C, 0:W8], -1.0, fl(NTD[0:C]), ALU.mult, ALU.mult)
            tt(fl(MT[0:C]), psb[pbq][0:C, 0:W8], fl(DTi[0:C]), ALU.mult)
            tt(fl(Rm[0][0:C]), fl(Pm[0][0:C]), fl(idr), ALU.add, eng="pool")
            cur = 0
            for lv in range(1, nlev + 1):
                nxt = 1 - cur
                pbp = pbank(); pbpt = pbank()
                for h in range(8):
                    mm(psb[pbpt][0:C, h * C:(h + 1) * C], Pm[cur][0:C, h, :], PmT[cur][0:C, h, :])
                if lv < nlev:
                    for h in range(8):
                        mm(psb[pbp][0:C, h * C:(h + 1) * C], PmT[cur][0:C, h, :], Pm[cur][0:C, h, :])
                cp(fl(PmT[nxt][0:C]), psb[pbpt][0:C, 0:W8], eng="act")
                if lv < nlev:
                    cp(fl(Pm[nxt][0:C]), psb[pbp][0:C, 0:W8])
                pbr = pbank()
                for h in range(8):
                    mm(psb[pbr][0:C, h * C:(h + 1) * C], PmT[nxt][0:C, h, :], Rm[cur][0:C, h, :])
                tt(fl(Rm[nxt][0:C]), psb[pbr][0:C, 0:W8], fl(Rm[cur][0:C]), ALU.add)
                cur = nxt
            TT = Rm[cur]
            pbk = pbank(); pbv = pbank()
            for h in range(8):
                tr(psb16[pbk][0:C, h * 128:(h + 1) * 128], kg[:, h, cs], identb.full())
            for h in range(8):
                tr(psb16[pbv][0:C, h * 128:(h + 1) * 128], vg[:, h, cs], identb.full())
            kt3 = psb16[pbk][0:C, :]; kt3 = kt3.with_ap(kt3.ap.rearrange("p (h d) -> p h d", h=8))
            vt3 = psb16[pbv][0:C, :]; vt3 = vt3.with_ap(vt3.ap.rearrange("p (h d) -> p h d", h=8))
            tt(bv_[0:C], vt3, bc3(tokc[0:C, 0:8], [C, 8, 128]), ALU.mult)
            tt(kbg[0:C], kt3, bc3(tokc[0:C, 8:16], [C, 8, 128]), ALU.mult)
            tt(kdc[0:C], kt3, bc3(tokc[0:C, 16:24], [C, 8, 128]), ALU.mult)
            pbw = pbank()
            for h in range(8):
                mm(psb[pbw][:, h * C:(h + 1) * C], kbg[0:C, h, :], TT[0:C, h, :])
            ts(fl(nwT[:, :, 0:C]), psb[pbw][:, 0:W8], -1.0, ALU.mult)
            pv0 = pbank(); pv1 = pbank()
            for h in range(8):
                o = psb[pv0 if h < 4 else pv1][0:C, (h % 4) * 128:(h % 4 + 1) * 128]
                mm(o, TT[0:C, h, :], bv_[0:C, h, :], start=True, stop=False)
                mm(o, nwT[:, h, 0:C], S_bf[:, h, :], start=False, stop=True)
            cp(fl(vnw[0:C, 0:4, :]), psb[pv0][0:C, :], eng="act")
            cp(fl(vnw[0:C, 4:8, :]), psb[pv1][0:C, :])
            if kind != "meta":
                pbo = pbank()
                for h in range(8):
                    o = psb[pbo][:, h * C:(h + 1) * C]
                    mm(o, S_bf[:, h, :], qgc[:, h, 0:C], start=True, stop=False)
                    mm(o, vnw[0:C, h, :], MT[0:C, h, :], start=False, stop=True)
                po = psb[pbo][:, 0:W8]
                cp(og[:, :, cs], po.with_ap(po.ap.rearrange("p (h c) -> p h c", h=8)), eng="act")
            ps0 = pbank(); ps1 = pbank()
            for h in range(8):
                mm(psb[ps0 if h < 4 else ps1][:, (h % 4) * 128:(h % 4 + 1) * 128], kdc[0:C, h, :], vnw[0:C, h, :])
            tt(S_cur.full(), S_cur.full(), bc3(egl.full(), [128, 8, 128]), ALU.mult)
            tt(fl(S_cur[:, 0:4, :]), fl(S_cur[:, 0:4, :]), psb[ps0].full(), ALU.add)
            tt(fl(S_cur[:, 4:8, :]), fl(S_cur[:, 4:8, :]), psb[ps1].full(), ALU.add)
            cp(S_bf.full(), S_cur.full(), eng="act")
            if kind == "sample":
                P.dma("pool", gso[sgi].with_ap(gso.ap[sgi].rearrange("h d e -> d h e")), S_cur.full())
        if kind == "meta":
            cp(S_meta.full(), S_cur.full())
            return
        if kind == "prompt" and t == 3:
            P.dma("pool", gp[s].with_ap(gp.ap[s].rearrange("h d e -> d h e")), S_cur.full())
        P.phase = kind + ":gdnnorm"
        top[0] = m_conv
        gsq = alloc([NT], BF16); grn = alloc([NT]); gt = alloc([NT])
        for h in range(8):
            act(gsq.full(), og[:, h, :], AF.Square)
            pbn = pbank()
            mm(psb[pbn][:, 0:NT], ones_bf.full(), gsq.full())
            rsqrt(grn.full(), psb[pbn][:, 0:NT], 1.0 / 128, gt.full())
            stt(gt.full(), og[:, h, :], small[:, 0:1], grn.full(), ALU.mult, ALU.mult)
            tt(obT[:, h, :], gt.full(), sz[:, h, :], ALU.mult, eng="pool")
        if dbg and kind == "prompt" and s == 0 and t == 0:
            P.dma("pool", dbg_ob.full(), obT.full())
        top[0] = m_gdn
        if stage < 4:
            return
        P.phase = kind + ":merge"
        mixT = alloc([8, NT], BF16)
        sga = alloc([NT]); sgb = alloc([NT]); tmp = alloc([NT])
        for oc in range(8):
            slot = wk8(wload(B_M + oc))
            pa = pbank(); pb_ = pbank(); pya = pbank(); pyb = pbank()
            for kc in range(8):
                mm(psb[pa][:, 0:NT], slot[:, kc, 0:128], xnT[:, kc, :], start=(kc == 0), stop=(kc == 7))
            for kc in range(8):
                mm(psb[pb_][:, 0:NT], slot[:, kc, 128:256], xnT[:, kc, :], start=(kc == 0), stop=(kc == 7))
            for kc in range(8):
                mm(psb[pya][:, 0:NT], slot[:, kc, 256:384], oaT[:, kc, :], start=(kc == 0), stop=(kc == 7))
            for kc in range(8):
                mm(psb[pyb][:, 0:NT], slot[:, kc, 384:512], obT[:, kc, :], start=(kc == 0), stop=(kc == 7))
            act(sga.full(), psb[pa][:, 0:NT], AF.Sigmoid, bias=bgT[:, oc:oc + 1])
            act(sgb.full(), psb[pb_][:, 0:NT], AF.Sigmoid, bias=bgT[:, 8 + oc:9 + oc])
            tt(tmp.full(), psb[pya][:, 0:NT], sga.full(), ALU.mult)
            tt(sgb.full(), psb[pyb][:, 0:NT], sgb.full(), ALU.mult)
            tt(mixT[:, oc, :], tmp.full(), sgb.full(), ALU.add, eng="pool")
        for half in range(2):
            slot = wk8(wload(B_WO + half))
            for st in range(nst):
                pb = pbank()
                for kc in range(8):
                    mm(psb[pb][0:ST, :], mixT[:, kc, st * ST:(st + 1) * ST], slot[:, kc, :], start=(kc == 0), stop=(kc == 7))
                tt(xtok[0:ST, st, half * 512:(half + 1) * 512], xtok[0:ST, st, half * 512:(half + 1) * 512], psb[pb][0:ST, :], ALU.add)
        if stage < 5:
            return
        P.phase = kind + ":ffn"
        norm_T(norm2)
        uT = alloc([32, NT], BF16)
        rl = [alloc([NT]) for _ in range(2)]
        for j in range(8):
            slot = wk8(wload(B_WU + j))
            for c4 in range(4):
                fc = j * 4 + c4
                pb = pbank()
                for kc in range(8):
                    mm(psb[pb][:, 0:NT], slot[:, kc, c4 * 128:(c4 + 1) * 128], xnT[:, kc, :], start=(kc == 0), stop=(kc == 7))
                r = rl[fc % 2]
                act(r.full(), psb[pb][:, 0:NT], AF.Relu)
                tt(uT[:, fc, :], r.full(), r.full(), ALU.mult, eng=("pool" if fc % 2 else "dve"))
        for oc in range(8):
            slot = wf32(wload(B_WD + oc))
            pb = pbank()
            for st in range(nst):
                for fc in range(32):
                    mm(psb[pb][0:ST, st * 128:(st + 1) * 128], uT[:, fc, st * ST:(st + 1) * ST], slot[:, fc, :], start=(fc == 0), stop=(fc == 31))
            pv = psb[pb][0:ST, 0:nst * 128]
            tt(xtok[0:ST, :, oc * 128:(oc + 1) * 128], xtok[0:ST, :, oc * 128:(oc + 1) * 128],
               pv.with_ap(pv.ap.rearrange("p (s c) -> p s c", s=nst)), ALU.add)
        for st in range(nst):
            if kind == "prompt":
                P.dma("pool", yp[s, t * 512 + st * 128: t * 512 + (st + 1) * 128, :], xtok[0:ST, st, :])
            else:
                P.dma("pool", ys[st * 128:(st + 1) * 128, :], xtok[0:ST, st, :])

    def cache_k_prep():
        P.phase = "cachek"
        top[0] = base_top
        ckf = [alloc([D]) for _ in range(2)]
        ckb = [alloc([D], BF16) for _ in range(2)]
        ckt = [alloc([8, 128], BF16) for _ in range(2)]
        i = 0
        for sq_ in range(NSS):
            P.dma("pool", Vcd[sq_], cv[sq_])
        for sq_ in range(NSS):
            for c in range(8):
                f = ckf[i % 2]; b = ckb[i % 2]; kt_ = ckt[i % 2]
                P.dma("sp", f.full(), ck[sq_, c * 128:(c + 1) * 128, :])
                cp(b.full(), f.full(), eng=("pool" if i % 2 else "dve"))
                pb = pbank()
                for h in range(8):
                    tr(psb16[pb][:, h * 128:(h + 1) * 128], b[:, h * 128:(h + 1) * 128], identb.full())
                src = psb16[pb].full()
                cp(kt_.full(), src.with_ap(src.ap.rearrange("p (h t) -> p h t", h=8)), eng="act")
                P.dma("sp", KTs[sq_, :, :, c * 128:(c + 1) * 128].with_ap(KTs.ap[sq_, :, :, c * 128:(c + 1) * 128].rearrange("h p t -> p h t")), kt_.full())
                i += 1

    if dbg:
        dbg_oa = dram("dbg_oa2", [128, 8, 512], "ExternalOutput", BF16)
        dbg_ob = dram("dbg_ob2", [128, 8, 512], "ExternalOutput", BF16)

    import os
    sel = os.environ.get("KTILES", "msp")
    if "s" in sel:
        cache_k_prep()
    run_tile("meta")
    if "s" in sel:
        run_tile("sample")
    if "p" in sel:
        for s in range(NPS):
            for t in range(4):
                run_tile("prompt", s, t)
    elif "q" in sel:
        run_tile("prompt", 0, 0)
    P.emit()
    es.close()
    return nc, P


_CACHE = {}


def kernel(x_prompt, x_sample, cache_attn_k, cache_attn_v, state_gdn, state_conv, meta_tokens, rel_bias,
           norm1, w_in, b_gate, q_norm, k_norm, lambda_q1, lambda_k1, lambda_q2, lambda_k2, sub_norm,
           conv_w, A_log, dt_bias, gdn_norm, w_br_a, w_br_b, w_out, norm2, w_up, w_down, _stage=99, _cores=8, _dbg=False):
    f = lambda a: np.ascontiguousarray(np.asarray(a, dtype=np.float32))
    key = (_stage, _dbg)
    if key not in _CACHE:
        _CACHE[key] = build_program(_stage, _dbg)
    nc, P = _CACHE[key]
    consts = _consts()
    shared = {
        "meta": f(meta_tokens), "relb": f(rel_bias), "norm1": f(norm1).reshape(-1), "w_in": f(w_in)[0],
        "b_gate": f(b_gate).reshape(-1), "q_norm": f(q_norm).reshape(-1), "k_norm": f(k_norm).reshape(-1),
        "lq1": f(lambda_q1).reshape(-1), "lk1": f(lambda_k1).reshape(-1), "lq2": f(lambda_q2).reshape(-1), "lk2": f(lambda_k2).reshape(-1),
        "sub_norm": f(sub_norm).reshape(-1), "conv_w": f(conv_w).reshape(-1), "a_log": f(A_log).reshape(-1),
        "dt_bias": f(dt_bias).reshape(-1), "gdn_norm": f(gdn_norm).reshape(-1), "w_bra": f(w_br_a)[0], "w_brb": f(w_br_b)[0],
        "w_out": f(w_out)[0], "norm2": f(norm2).reshape(-1), "w_up": f(w_up)[0], "w_down": f(w_down)[0],
    }
    for k, v in consts.items():
        shared["c_" + k] = v
    xp = f(x_prompt); xs = f(x_sample)
    ck = f(cache_attn_k)[0].reshape(32, PAST, D); cv = f(cache_attn_v)[0].reshape(32, PAST, D)
    sg = f(state_gdn)[0]; sc = f(state_conv)[0]
    in_maps = []
    for c in range(_cores):
        m = dict(shared)
        m["xp"] = xp[c * NPS:(c + 1) * NPS]
        m["xs"] = xs[c * NSS:(c + 1) * NSS].reshape(NSS * DSEQ, D)
        m["ck"] = ck[c * NSS:(c + 1) * NSS]
        m["cv"] = cv[c * NSS:(c + 1) * NSS]
        m["sg"] = sg[c * NSS:(c + 1) * NSS]
        m["sc"] = sc[c * NSS:(c + 1) * NSS]
        in_maps.append(m)
    res = run_bass_kernel_spmd(nc, in_maps, core_ids=list(range(_cores)))
    R = res.results
    cat = lambda k: np.concatenate([np.asarray(r[k], dtype=np.float32) for r in R], axis=0)
    nb = _cores * NPS
    ns = _cores * NSS
    outs = (
        cat("yp"),
        cat("ys").reshape(ns, DSEQ, D),
        cat("kp").reshape(1, nb, TP, 8, 128),
        cat("vp").reshape(1, nb, TP, 8, 128),
        cat("gp").reshape(1, nb, 8, 128, 128),
        cat("cpo").reshape(1, nb, 3, 3072),
        cat("kso").reshape(1, ns, DSEQ, 8, 128),
        cat("vso").reshape(1, ns, DSEQ, 8, 128),
        cat("gso").reshape(1, ns, 8, 128, 128),
        cat("cso").reshape(1, ns, 3, 3072),
    )
    if _dbg:
        return outs, R
    return outs
```

```python
import contextlib
import math
import os
from collections import defaultdict

import numpy as np
import concourse.bass as bass
import concourse.mybir as mybir
from concourse.bass_utils import run_bass_kernel_spmd

F32 = mybir.dt.float32
BF16 = mybir.dt.bfloat16
I32 = mybir.dt.int32
ALU = mybir.AluOpType
AF = mybir.ActivationFunctionType
AX = mybir.AxisListType

SEM_LIMIT = 1000
DMA_SEMS = 24
NEG = -30000.0


class V:
    __slots__ = ("ap", "key", "lo", "hi", "page", "track")

    def __init__(self, ap, key, lo, hi, page, track=True):
        self.ap = ap
        self.key = key
        self.lo = lo
        self.hi = hi
        self.page = page
        self.track = track

    def with_ap(self, ap):
        return V(ap, self.key, self.lo, self.hi, self.page, self.track)


class T:
    def __init__(self, ap, name, shape, dram=False, esize=4, base_off=0, page=2048, track=True, whole=False):
        self.whole = whole
        self.ap = ap
        self.name = name
        self.shape = list(shape)
        self.dram = dram
        self.esize = esize
        self.base_off = base_off
        self.page = page
        self.track = track
        fs = self.shape if dram else self.shape[1:]
        st = []
        acc = 1
        for s in reversed(fs):
            st.append(acc)
            acc *= s
        self.fstrides = list(reversed(st))

    def __getitem__(self, key):
        if not isinstance(key, tuple):
            key = (key,)
        ap = self.ap[key]
        fs = self.shape if self.dram else self.shape[1:]
        k2 = list(key) if self.dram else list(key[1:])
        while len(k2) < len(fs):
            k2.append(slice(None))
        lo = 0
        hi = 0
        for k, s, st in zip(k2, fs, self.fstrides):
            if isinstance(k, slice):
                a = 0 if k.start is None else k.start
                b = s if k.stop is None else k.stop
            else:
                a = k
                b = k + 1
            lo += a * st
            hi += (b - 1) * st
        hi += 1
        if self.whole:
            return V(ap, self.name, 0, self.page, self.page, self.track)
        return V(ap, self.name, self.base_off + lo * self.esize, self.base_off + hi * self.esize, self.page, self.track)

    def full(self):
        return self[tuple(slice(None) for _ in self.shape)]


class Op:
    __slots__ = ("eng", "fn", "deps", "id", "is_dma", "has_dependents", "sig", "dma_sem", "dma_val", "phase")

    def __init__(self, eng, fn, is_dma):
        self.eng = eng
        self.fn = fn
        self.deps = set()
        self.is_dma = is_dma
        self.has_dependents = False
        self.sig = None
        self.dma_sem = None
        self.dma_val = None


ENGS = ("pe", "act", "dve", "pool", "sp")


class Prog:
    def __init__(self, nc):
        self.nc = nc
        self.ops = []
        self.hist = defaultdict(list)

    def op(self, eng, fn, reads=(), writes=(), dma=False):
        o = Op(eng, fn, dma)
        o.phase = getattr(self, "phase", "")
        o.id = len(self.ops)
        self.ops.append(o)
        deps = o.deps
        tag = eng + ("_dma" if dma else "")
        hist = self.hist
        for v in reads:
            if not v.track:
                continue
            lo, hi = v.lo, v.hi
            for pg in range(lo // v.page, (hi - 1) // v.page + 1):
                for rec in hist[(v.key, pg)]:
                    if rec[2] == "W" and rec[0] < hi and lo < rec[1]:
                        if rec[4] == "pe" and tag == "pe":
                            continue
                        deps.add(rec[3])
                    elif rec[2] == "R" and v.key.startswith("ps") and rec[4] != tag:
                        deps.add(rec[3])
        for v in writes:
            if not v.track:
                continue
            lo, hi = v.lo, v.hi
            for pg in range(lo // v.page, (hi - 1) // v.page + 1):
                h = hist[(v.key, pg)]
                keep = []
                for rec in h:
                    if rec[0] < hi and lo < rec[1]:
                        if not (rec[4] == "pe" and tag == "pe"):
                            deps.add(rec[3])
                        if lo <= rec[0] and rec[1] <= hi:
                            continue
                    keep.append(rec)
                keep.append([lo, hi, "W", o.id, tag])
                hist[(v.key, pg)] = keep
        for v in reads:
            if not v.track:
                continue
            lo, hi = v.lo, v.hi
            for pg in range(lo // v.page, (hi - 1) // v.page + 1):
                h = hist[(v.key, pg)]
                found = False
                if not dma:
                    for r in h:
                        if r[2] == "R" and r[0] == lo and r[1] == hi and r[4] == tag:
                            r[3] = o.id
                            found = True
                            break
                if not found:
                    h.append([lo, hi, "R", o.id, tag])
        deps.discard(o.id)
        return o

    def dma(self, eng, out, in_, **kw):
        def fn(e):
            return e.dma_start(out=out.ap, in_=in_.ap, **kw)
        return self.op(eng, fn, reads=[in_], writes=[out], dma=True)

    def emit(self):
        nc = self.nc
        ops = self.ops
        for o in ops:
            for d in o.deps:
                ops[d].has_dependents = True
        cnt = {e: 0 for e in ENGS}
        dma_cnt = {e: 0 for e in ENGS}
        for o in ops:
            if o.is_dma:
                i = dma_cnt[o.eng]
                dma_cnt[o.eng] += 1
                o.dma_sem = (o.eng, i % DMA_SEMS)
                o.dma_val = 16 * (i // DMA_SEMS + 1)
            elif o.has_dependents:
                cnt[o.eng] += 1
                o.sig = cnt[o.eng]
        n_epochs = {e: (cnt[e] + SEM_LIMIT - 1) // SEM_LIMIT for e in ENGS}
        stack = contextlib.ExitStack()
        sems = {}
        for e in ENGS:
            for ep in range(max(1, n_epochs[e])):
                sems[(e, ep)] = stack.enter_context(nc.semaphore(f"s_{e}_{ep}"))
            if dma_cnt[e]:
                for i in range(DMA_SEMS):
                    sems[("dma", e, i)] = stack.enter_context(nc.semaphore(f"d_{e}_{i}"))
        per_eng = {e: [] for e in ENGS}
        for o in ops:
            per_eng[o.eng].append(o)
        waited = {e: {} for e in ENGS}
        waits = {}
        for o in ops:
            w = {}
            for d in o.deps:
                p = ops[d]
                if p.is_dma:
                    key = ("dma",) + p.dma_sem
                    val = p.dma_val
                else:
                    key = ("c", p.eng)
                    val = p.sig
                if val > w.get(key, 0):
                    w[key] = val
            if o.is_dma:
                key = ("dma",) + o.dma_sem
                if o.dma_val > 16 and o.dma_val - 16 > w.get(key, 0):
                    w[key] = o.dma_val - 16
            wl = []
            wd = waited[o.eng]
            for key, val in w.items():
                if wd.get(key, 0) >= val:
                    continue
                wd[key] = val
                wl.append((key, val))
            waits[o.id] = wl
        final = {}
        for e in ENGS:
            if dma_cnt[e]:
                fl = []
                for i in range(DMA_SEMS):
                    n = len(range(i, dma_cnt[e], DMA_SEMS))
                    if n:
                        fl.append((("dma", e, i), 16 * n))
                final[e] = fl

        def sem_of(key, val):
            if key[0] == "dma":
                return sems[("dma", key[1], key[2])], val
            e = key[1]
            ep = (val - 1) // SEM_LIMIT
            return sems[(e, ep)], val - ep * SEM_LIMIT

        def run(engname, engobj):
            for o in per_eng[engname]:
                for key, val in waits[o.id]:
                    s, v = sem_of(key, val)
                    engobj.wait_ge(s, v)
                ins = o.fn(engobj)
                if o.is_dma:
                    ins.then_inc(sems[("dma",) + o.dma_sem], 16)
                elif o.sig is not None:
                    s, v = sem_of(("c", engname), o.sig)
                    ins.then_inc(s, 1)
            for key, val in final.get(engname, []):
                s, v = sem_of(key, val)
                engobj.wait_ge(s, v)

        with nc.Block() as block:
            @block.sync
            def _(e):
                run("sp", e)

            @block.scalar
            def _(e):
                run("act", e)

            @block.vector
            def _(e):
                run("dve", e)

            @block.gpsimd
            def _(e):
                run("pool", e)

            @block.tensor
            def _(e):
                run("pe", e)
        stack.close()
        self.stats = {e: len(per_eng[e]) for e in ENGS}


D = 1024
SEQ = 2048
NMETA = 16
TP = NMETA + SEQ
PAST = 1024
DSEQ = 64
NIN = 9232
OFF_KA, OFF_VA, OFF_B, OFF_Z, OFF_BETA, OFF_ALPHA, OFF_GA, OFF_GB = 1024, 2048, 3072, 6144, 7168, 7176, 7184, 8208
DFF = 4096
EPS = 1e-6
LAM_INIT = 0.8 - 0.6 * math.exp(-0.3 * 0)
A_SCALE = 0.125
B_SCALE = 128 ** -0.5
NPS = 2
NSS = 4
GW = 1024
GL = 1152
GOFF = 384

B_Q, B_K, B_V, B_H, B_BA, B_M, B_WO, B_WU, B_WD, NBLK = 0, 2, 4, 6, 14, 15, 23, 25, 33, 41


def _consts():
    c = {}
    c["identf"] = np.eye(128, dtype=np.float32)
    p = np.arange(64)[:, None]
    f = np.arange(64)[None, :]
    def rep(m):
        return np.ascontiguousarray(np.broadcast_to(m[:, None, :], (64, 8, 64)).reshape(64, 512)).astype(np.float32)
    c["negU_incl"] = rep(np.where(f >= p, 0.0, NEG))
    c["negU_strict"] = rep(np.where(f > p, 0.0, NEG))
    c["negL_strict"] = rep(np.where(f < p, 0.0, NEG))
    c["identrep"] = rep(np.eye(64))
    bm = np.zeros((8, 8, 64), np.float32)
    for h in range(8):
        bm[h, h, :] = 1.0
    c["blockmask"] = bm.reshape(8, 512)
    pp = np.arange(128)[:, None]
    cc = np.arange(GW)[None, :] - GOFF
    c["maskG"] = np.where(np.floor_divide(cc, 64) >= np.floor_divide(pp, 64), 0.0, NEG).astype(np.float32)
    lo = [0, 1, 2, 3, 4, 5, 6, 7, 8, 12, 16, 23, 32, 46, 64, 91]
    hi = lo[1:] + [10 ** 9]
    oh = np.zeros((32, GL), np.float32)
    for i in range(GL):
        rel = 511 - i
        n = abs(rel)
        b = 0
        for k in range(16):
            if lo[k] <= n < hi[k]:
                b = k
        if rel > 0:
            b += 16
        oh[b, i] = 1.0
    c["onehot"] = oh
    return c


def build_program(stage=99, dbg=False):
    nc = bass.Bass("TRN2", target_bir_lowering=False)
    P = Prog(nc)
    es = contextlib.ExitStack()

    def dram(name, shape, kind, dt=F32, page=1 << 20, track=False):
        ap = nc.dram_tensor(name, shape, dt, kind=kind).ap()
        return T(ap, name, shape, dram=True, esize=(2 if dt == BF16 else 4), page=page, track=track)

    def din(name, shape):
        return dram(name, shape, "ExternalInput")

    def dout(name, shape):
        return dram(name, shape, "ExternalOutput")

    xp = din("xp", [NPS, SEQ, D])
    xs = din("xs", [NSS * DSEQ, D])
    ck = din("ck", [NSS, PAST, D])
    cv = din("cv", [NSS, PAST, D])
    sg = din("sg", [NSS, 8, 128, 128])
    sc = din("sc", [NSS, 3, 3072])
    meta = din("meta", [NMETA, D])
    relb = din("relb", [32, 8])
    norm1 = din("norm1", [D])
    w_in = din("w_in", [D, NIN])
    b_gate = din("b_gate", [2 * D])
    q_norm = din("q_norm", [64])
    k_norm = din("k_norm", [64])
    lq1 = din("lq1", [64]); lk1 = din("lk1", [64]); lq2 = din("lq2", [64]); lk2 = din("lk2", [64])
    sub_norm = din("sub_norm", [128])
    conv_w = din("conv_w", [4 * 3072])
    a_log = din("a_log", [8])
    dt_bias = din("dt_bias", [8])
    gdn_norm = din("gdn_norm", [128])
    w_bra = din("w_bra", [D, D]); w_brb = din("w_brb", [D, D]); w_out = din("w_out", [D, D])
    norm2 = din("norm2", [D])
    w_up = din("w_up", [D, DFF]); w_down = din("w_down", [DFF, D])
    cst = {k: din("c_" + k, list(v.shape)) for k, v in _consts().items()}
    yp = dout("yp", [NPS, SEQ, D]); ys = dout("ys", [NSS * DSEQ, D])
    kp = dout("kp", [NPS, TP, D]); vp = dout("vp", [NPS, TP, D])
    gp = dout("gp", [NPS, 8, 128, 128]); cpo = dout("cpo", [NPS, 3, 3072])
    kso = dout("kso", [NSS * DSEQ, D]); vso = dout("vso", [NSS * DSEQ, D])
    gso = dout("gso", [NSS, 8, 128, 128]); cso = dout("cso", [NSS, 3, 3072])
    WS = dram("WS", [NBLK, 128, 4096], "Internal", BF16, page=1 << 20, track=True)
    KTd = dram("KTd", [NPS, 8, 128, TP], "Internal", BF16, page=1 << 16, track=True)
    Vd = dram("Vd", [NPS, TP, D], "Internal", BF16, page=1 << 16, track=True)
    KTs = dram("KTs", [NSS, 8, 128, PAST + DSEQ], "Internal", BF16, page=1 << 16, track=True)
    FD = dram("FD", [8, 128, GL], "Internal", F32, page=1 << 16, track=True)
    Vsd = dram("Vsd", [NSS, DSEQ, D], "Internal", BF16, page=1 << 16, track=True)
    Vcd = dram("Vcd", [NSS, PAST, D], "Internal", BF16, page=1 << 16, track=True)

    ARENA = 206 * 1024
    arena_h = es.enter_context(nc.sbuf_tensor("arena", [128, ARENA // 4], F32))
    top = [0]

    def alloc(shape, dt=F32):
        n = int(np.prod(shape))
        esz = 2 if dt == BF16 else 4
        nb = (n * esz + 31) // 32 * 32
        off = top[0]
        top[0] += nb
        assert top[0] <= ARENA, ("arena overflow", top[0])
        ap = arena_h[:, off // 4:(off + nb) // 4]
        if dt != F32:
            ap = ap.bitcast(dt)
        ap = ap[:, 0:n]
        if len(shape) == 2:
            ap = ap.rearrange("p (a b) -> p a b", a=shape[0])
        elif len(shape) == 3:
            ap = ap.rearrange("p (a b c) -> p a b c", a=shape[0], b=shape[1])
        return T(ap, "arena", [128] + list(shape), esize=esz, base_off=off)

    psb = []
    psb16 = []
    for i in range(8):
        h = es.enter_context(nc.psum_tensor(f"ps{i}", [128, 512], F32))
        psb.append(T(h, f"ps{i}", [128, 512], page=4096, whole=True))
        psb16.append(T(h[:, :].bitcast(BF16), f"ps{i}", [128, 1024], esize=2, page=4096, whole=True))
    rot = [0]

    def pbank(pool=8):
        i = rot[0] % pool
        rot[0] += 1
        return i

    def rw(*vs):
        return [v for v in vs if isinstance(v, V)]

    def A_(x):
        return x.ap if isinstance(x, V) else x

    def mm(out, lhsT, rhs, start=True, stop=True):
        P.op("pe", lambda e: e.matmul(out=out.ap, lhsT=lhsT.ap, rhs=rhs.ap, start=start, stop=stop), reads=[lhsT, rhs], writes=[out])

    def tr(out, in_, ident):
        P.op("pe", lambda e: e.transpose(out=out.ap, in_=in_.ap, identity=ident.ap), reads=[in_, ident], writes=[out])

    def act(out, in_, func, bias=None, scale=None, accum=None):
        kw = {}
        if bias is not None:
            kw["bias"] = A_(bias)
        if scale is not None:
            kw["scale"] = A_(scale)
        if accum is not None:
            kw["accum_out"] = accum.ap
        P.op("act", lambda e: e.activation(out=out.ap, in_=in_.ap, func=func, **kw), reads=rw(in_, bias, scale), writes=rw(out, accum))

    def tt(out, a, b, op, eng="dve"):
        P.op(eng, lambda e: e.tensor_tensor(out=out.ap, in0=a.ap, in1=b.ap, op=op), reads=[a, b], writes=[out])

    def ts(out, a, s1, op0, s2=None, op1=None, eng="dve"):
        if op1 is None:
            P.op(eng, lambda e: e.tensor_scalar(out=out.ap, in0=a.ap, scalar1=A_(s1), scalar2=0.0, op0=op0, op1=ALU.add), reads=rw(a, s1), writes=[out])
        else:
            P.op(eng, lambda e: e.tensor_scalar(out=out.ap, in0=a.ap, scalar1=A_(s1), scalar2=A_(s2), op0=op0, op1=op1), reads=rw(a, s1, s2), writes=[out])

    def stt(out, a, s, b, op0, op1, eng="dve"):
        P.op(eng, lambda e: e.scalar_tensor_tensor(out=out.ap, in0=a.ap, scalar=A_(s), in1=b.ap, op0=op0, op1=op1), reads=rw(a, s, b), writes=[out])

    def cp(out, in_, eng="dve"):
        if eng == "act":
            P.op("act", lambda e: e.copy(out=out.ap, in_=in_.ap), reads=[in_], writes=[out])
        else:
            P.op(eng, lambda e: e.tensor_copy(out=out.ap, in_=in_.ap), reads=[in_], writes=[out])

    def red(out, in_, op=ALU.add):
        P.op("dve", lambda e: e.tensor_reduce(out=out.ap, in_=in_.ap, axis=AX.X, op=op), reads=[in_], writes=[out])

    def recip(out, in_):
        act(out, in_, AF.Ln)
        act(out, out, AF.Exp, scale=-1.0)

    def memset(v, val, eng="dve"):
        P.op(eng, lambda e: e.memset(v.ap, val), writes=[v])

    def rsqrt(out, in_, scale, tmp):
        act(tmp, in_, AF.Ln, bias=EPS, scale=scale)
        act(out, tmp, AF.Exp, scale=-0.5)

    def bc3(v, shape):
        return v.with_ap(v.ap.unsqueeze(2).to_broadcast(shape))

    deferred = []

    def defer(fn, delay=1):
        deferred.append([delay, fn])

    def group_issued():
        run_now = []
        keep = []
        for d in deferred:
            d[0] -= 1
            (run_now if d[0] <= 0 else keep).append(d)
        deferred[:] = keep
        for d in run_now:
            d[1]()

    def flush_deferred():
        while deferred:
            group_issued()

    def dap(t, off, pat):
        return t.full().with_ap(bass.AP(t.ap.tensor, off, pat))

    def ws_k8(b):
        return WS.ap[b].rearrange("p (k c) -> p k c", k=8)

    def cast_piece(b, off, w, src, c0):
        dst = WS[b].with_ap(ws_k8(b)[:, :, off:off + w])
        s = src.full().with_ap(src.ap[:, c0:c0 + w].rearrange("(k p) c -> p k c", p=128))
        P.dma("pool", dst, s)

    def cast_block(b):
        if B_Q <= b < B_K:
            cast_piece(b, 0, 512, w_in, (b - B_Q) * 512)
        elif B_K <= b < B_V:
            cast_piece(b, 0, 512, w_in, OFF_KA + (b - B_K) * 512)
        elif B_V <= b < B_H:
            cast_piece(b, 0, 512, w_in, OFF_VA + (b - B_V) * 512)
        elif b == B_BA:
            cast_piece(B_BA, 0, 16, w_in, OFF_BETA)
        elif B_H <= b < B_BA:
            h = b - B_H
            for j in range(3):
                cast_piece(b, j * 128, 128, w_in, OFF_B + j * 1024 + h * 128)
            cast_piece(b, 384, 128, w_in, OFF_Z + h * 128)
        elif B_M <= b < B_WO:
            oc = b - B_M
            cast_piece(b, 0, 128, w_in, OFF_GA + oc * 128)
            cast_piece(b, 128, 128, w_in, OFF_GB + oc * 128)
            cast_piece(b, 256, 128, w_bra, oc * 128)
            cast_piece(b, 384, 128, w_brb, oc * 128)
        elif B_WO <= b < B_WU:
            cast_piece(b, 0, 512, w_out, (b - B_WO) * 512)
        elif B_WU <= b < B_WD:
            cast_piece(b, 0, 512, w_up, (b - B_WU) * 512)
        else:
            oc = b - B_WD
            dst = WS[b].with_ap(WS.ap[b].rearrange("p (f c) -> p f c", f=32))
            s_ = w_down.full().with_ap(w_down.ap[:, oc * 128:(oc + 1) * 128].rearrange("(f p) c -> p f c", p=128))
            P.dma("pool", dst, s_)

    cast_order = ([B_K, B_K + 1, B_V, B_V + 1, B_BA] + [B_H + h for h in range(8)] + [B_Q, B_Q + 1]
                  + [B_M + i for i in range(8)] + [B_WO, B_WO + 1] + [B_WU + i for i in range(8)] + [B_WD + i for i in range(8)])
    cast_done = [0]

    def cast_upto(n):
        while cast_done[0] < min(n, len(cast_order)):
            cast_block(cast_order[cast_done[0]])
            cast_done[0] += 1
    cast_upto(4)

    identf = alloc([128]); P.dma("sp", identf.full(), cst["identf"].full())
    identb = alloc([128], BF16); cp(identb.full(), identf.full())
    ones_bf = alloc([128], BF16); memset(ones_bf.full(), 1.0)
    negU_incl = alloc([8, 64], BF16)
    negU_strict = alloc([8, 64], BF16)
    negL_strict = alloc([8, 64], BF16)
    identrep = alloc([8, 64], BF16)
    blockmask = alloc([8, 64], BF16)
    qn_bc = alloc([8, 64]); P.dma("sp", qn_bc.full(), dap(q_norm, 0, [[0, 128], [0, 8], [1, 64]]))
    kn_bc = alloc([8, 64]); P.dma("sp", kn_bc.full(), dap(k_norm, 0, [[0, 128], [0, 8], [1, 64]]))
    small = alloc([64])
    P.dma("sp", small[:, 0:1], dap(gdn_norm, 0, [[1, 128], [1, 1]]))
    P.dma("sp", small[:, 1:2], dap(sub_norm, 0, [[1, 128], [1, 1]]))
    ts(small[:, 1:2], small[:, 1:2], 1.0 - LAM_INIT, ALU.mult)
    P.dma("sp", small[0:8, 3:4], dap(dt_bias, 0, [[1, 8], [1, 1]]))
    P.dma("sp", small[0:8, 5:6], dap(a_log, 0, [[1, 8], [1, 1]]))
    act(small[0:8, 4:5], small[0:8, 5:6], AF.Exp)
    ts(small[0:8, 4:5], small[0:8, 4:5], -1.0, ALU.mult)
    b15 = alloc([8]); P.dma("sp", b15.full(), dap(relb, 15 * 8, [[0, 128], [1, 8]]))
    rowsA = alloc([128]); P.dma("sp", rowsA[0:16, :], b_gate.full().with_ap(b_gate.ap.rearrange("(t p) -> t p", p=128)))
    rowsC = alloc([128]); P.dma("sp", rowsC[0:96, :], conv_w.full().with_ap(conv_w.ap.rearrange("(t p) -> t p", p=128)))
    bgT = alloc([16])
    cwT = alloc([96])
    pb = pbank()
    tr(psb[pb][:, 0:16], rowsA[0:16, :], identf[0:16, 0:16])
    tr(psb[pb][:, 16:112], rowsC[0:96, :], identf[0:96, 0:96])
    cp(bgT.full(), psb[pb][:, 0:16])
    cp(cwT.full(), psb[pb][:, 16:112])
    G = alloc([8, GW], BF16)
    m0 = top[0]
    lam4 = alloc([4, 64])
    for i, t in enumerate((lq1, lk1, lq2, lk2)):
        P.dma("sp", lam4[:, i, :], dap(t, 0, [[0, 128], [1, 64]]))
    tt(lam4[:, 0, :], lam4[:, 0, :], lam4[:, 1, :], ALU.mult)
    tt(lam4[:, 2, :], lam4[:, 2, :], lam4[:, 3, :], ALU.mult)
    red(small[:, 6:7], lam4[:, 0, :]); red(small[:, 7:8], lam4[:, 2, :])
    act(small[:, 6:8], small[:, 6:8], AF.Exp)
    tt(small[:, 8:9], small[:, 7:8], small[:, 6:7], ALU.subtract)
    ts(small[:, 2:3], small[:, 8:9], -LAM_INIT, ALU.add)
    for nm_, dst_, rows_ in (("negU_incl", negU_incl, 64), ("negU_strict", negU_strict, 64), ("negL_strict", negL_strict, 64),
                             ("identrep", identrep, 64), ("blockmask", blockmask, 8)):
        stg_ = alloc([8, 64])
        P.dma("sp", stg_[0:rows_], cst[nm_].full().with_ap(cst[nm_].ap.rearrange("p (a b) -> p a b", a=8)))
        cp(dst_[0:rows_], stg_[0:rows_])
    onehot = alloc([GL]); P.dma("sp", onehot[0:32, :], cst["onehot"].full())
    tab = alloc([8]); P.dma("sp", tab[0:32, :], relb.full())
    tabrep = alloc([8, 128])
    cp(tabrep[0:32], bc3(tab[0:32, :], [32, 8, 128]))
    maskG = alloc([GW]); P.dma("sp", maskG.full(), cst["maskG"].full())
    frep = alloc([GL])
    gsk = alloc([GW])
    for h in range(8):
        for j in range(3):
            pb = pbank()
            mm(psb[pb][:, 0:384], tabrep[0:32, h, :], onehot[0:32, j * 384:(j + 1) * 384])
            ts(frep[:, j * 384:(j + 1) * 384], psb[pb][:, 0:384], 1.0 / A_SCALE, ALU.mult)
        P.dma("sp", FD[h], frep.full())
        P.dma("sp", gsk.full(), FD[h].with_ap(bass.AP(FD.ap.tensor, h * 128 * GL + 127, [[GL - 1, 128], [1, GW]])))
        tt(G[:, h, :], gsk.full(), maskG.full(), ALU.add)
    top[0] = m0

    S_meta = alloc([8, 128])
    ctx_meta = alloc([24, 3])
    KTm = alloc([8, 16], BF16)
    Vm = alloc([1024], BF16)
    S_cur = alloc([8, 128])
    S_bf = alloc([8, 128], BF16)
    ctx_cur = alloc([24, 4, 3])
    NSLOT = 3
    wring = [alloc([4096], BF16) for _ in range(NSLOT)]
    wcnt = [0]

    def wload(b):
        cast_upto(cast_order.index(b) + 4)
        slot = wring[wcnt[0] % NSLOT]
        wcnt[0] += 1
        if b == B_BA:
            P.dma("sp", wk8(slot)[:, :, 0:16], WS[b].with_ap(ws_k8(b)[:, :, 0:16]))
        else:
            P.dma("sp", slot.full(), WS[b])
        return slot

    def wk8(slot):
        return T(slot.ap.rearrange("p (k c) -> p k c", k=8), "arena", [128, 8, 512], esize=2, base_off=slot.base_off)

    def wf32(slot):
        return T(slot.ap.rearrange("p (f c) -> p f c", f=32), "arena", [128, 32, 128], esize=2, base_off=slot.base_off)

    base_top = top[0]

    def run_tile(kind, s=0, t=0):
        top[0] = base_top
        if kind == "meta":
            NT, ST, nst, nseg, L, C = 16, 16, 1, 1, 16, 16
        elif kind == "prompt":
            NT, ST, nst, nseg, L, C = 512, 128, 4, 1, 512, 64
        else:
            NT, ST, nst, nseg, L, C = 256, 128, 2, 4, 64, 64
        nch = NT // C
        xtok = alloc([nst, D])
        xnT = alloc([8, NT], BF16)
        sstat = alloc([32])

        for st in range(nst):
            if kind == "meta":
                src = meta.full()
            elif kind == "prompt":
                src = xp[s, t * 512 + st * 128: t * 512 + (st + 1) * 128, :]
            else:
                src = xs[st * 128:(st + 1) * 128, :]
            P.dma("sp", xtok[0:ST, st, :], src)

        def norm_T(norm_dram):
            m = top[0]
            norm_bc = alloc([D])
            P.dma("sp", norm_bc.full(), dap(norm_dram, 0, [[0, 128], [1, D]]))
            junk = alloc([D])
            xnb = alloc([D], BF16)
            for st in range(nst):
                memset(sstat[0:ST, st:st + 1], 0.0)
                act(junk[0:ST, :], xtok[0:ST, st, :], AF.Square, accum=sstat[0:ST, st:st + 1])
                rsqrt(sstat[0:ST, 8 + st:9 + st], sstat[0:ST, st:st + 1], 1.0 / D, sstat[0:ST, 16 + st:17 + st])
                stt(xnb[0:ST, :], xtok[0:ST, st, :], sstat[0:ST, 8 + st:9 + st], norm_bc[0:ST, :], ALU.mult, ALU.mult)
                pb = pbank()
                for kc in range(8):
                    tr(psb16[pb][:, kc * ST:(kc + 1) * ST], xnb[0:ST, kc * 128:(kc + 1) * 128], identb[0:ST, 0:ST])
                src = psb16[pb][:, 0:8 * ST]
                cp(xnT[:, :, st * ST:(st + 1) * ST], src.with_ap(src.ap.rearrange("p (k t) -> p k t", k=8)), eng="act")
            top[0] = m

        if stage < 0:
            return
        P.phase = kind + ":norm1"
        norm_T(norm1)
        if stage < 1:
            flush_deferred()
            return
        P.phase = kind + ":qkvproj"

        oaT = alloc([8, NT], BF16) if kind != "meta" else None
        vnew_s = alloc([4, D], BF16) if kind == "sample" else None
        m_attn = top[0]
        qT = alloc([8, NT], BF16) if kind != "meta" else None

        NB_ = 4
        sqs = [alloc([512]) for _ in range(NB_)]
        t1s = [alloc([512]) for _ in range(NB_)]
        kns = [alloc([512]) for _ in range(NB_)]
        kbs = [alloc([512], BF16) for _ in range(NB_)]
        ktiles = [alloc([4, ST], BF16) for _ in range(NB_)]
        rsts = [alloc([16]) for _ in range(NB_)]
        ptc = [0]

        def post2(which, half, cols, st, i):
            kb = kbs[i]; kn = kns[i]; ktile = ktiles[i]
            pb2 = pbank()
            for hh in range(4):
                tr(psb16[pb2][:, hh * ST:(hh + 1) * ST], kb[0:ST, hh * 128:(hh + 1) * 128], identb[0:ST, 0:ST])
            src = psb16[pb2][:, 0:4 * ST]
            srcv = src.with_ap(src.ap.rearrange("p (k t) -> p k t", k=4))
            if which == "q":
                cp(qT[:, half * 4:(half + 1) * 4, st * ST:(st + 1) * ST], srcv, eng="act")
                return
            cp(ktile.full(), srcv, eng="act")
            if kind == "meta":
                cp(KTm[:, half * 4:(half + 1) * 4, :], ktile.full(), eng="pool")
            elif kind == "prompt":
                tok0 = NMETA + t * 512 + st * 128
                P.dma("sp", KTd[s, half * 4:(half + 1) * 4, :, tok0:tok0 + 128].with_ap(
                    KTd.ap[s, half * 4:(half + 1) * 4, :, tok0:tok0 + 128].rearrange("h p t -> p h t")), ktile.full())
            else:
                for q2 in range(2):
                    sq_ = st * 2 + q2
                    P.dma("sp", KTs[sq_, half * 4:(half + 1) * 4, :, PAST:PAST + 64].with_ap(
                        KTs.ap[sq_, half * 4:(half + 1) * 4, :, PAST:PAST + 64].rearrange("h p t -> p h t")),
                        ktile[:, :, q2 * 64:(q2 + 1) * 64])

        def proj_tok(blk_id, half, which):
            slot = wk8(wload(blk_id))
            cols = slice(half * 512, (half + 1) * 512)
            for st in range(nst):
                pb = pbank()
                for kc in range(8):
                    mm(psb[pb][0:ST, :], xnT[:, kc, st * ST:(st + 1) * ST], slot[:, kc, :], start=(kc == 0), stop=(kc == 7))
                group_issued()
                i = ptc[0] % NB_
                ptc[0] += 1
                ps = psb[pb]
                sq = sqs[i]; t1 = t1s[i]; kn = kns[i]; kb = kbs[i]; rs = rsts[i]
                PTS = int(os.environ.get("PT_STOP", "9"))
                if PTS <= 1 or (which == "v" and os.environ.get("PT_VSKIP", "0") == "1"):
                    cp(sq[0:ST, :], ps[0:ST, :])
                    continue
                if which in ("q", "k"):
                    act(sq[0:ST, :], ps[0:ST, :], AF.Square)
                    sqv = sq[0:ST, :]
                    red(rs[0:ST, 0:8], sqv.with_ap(sqv.ap.rearrange("p (a b) -> p a b", a=8)))
                    PRS = int(os.environ.get("PT_RS", "2"))
                    if PRS == 2:
                        rsqrt(rs[0:ST, 0:8], rs[0:ST, 0:8], 1.0 / 64, rs[0:ST, 8:16])
                    elif PRS == 1:
                        act(rs[0:ST, 8:16], rs[0:ST, 0:8], AF.Sqrt, bias=EPS, scale=1.0 / 64)
                        P.op("dve", lambda e, rs=rs: e.reciprocal(out=rs[0:ST, 0:8].ap, in_=rs[0:ST, 8:16].ap), reads=[rs[0:ST, 8:16]], writes=[rs[0:ST, 0:8]])
                    if PTS <= 2:
                        continue
                    psv = ps[0:ST, :]
                    t1v = t1[0:ST, :]
                    tt(t1v.with_ap(t1v.ap.rearrange("p (a b) -> p a b", a=8)), psv.with_ap(psv.ap.rearrange("p (a b) -> p a b", a=8)),
                       bc3(rs[0:ST, 0:8], [ST, 8, 64]), ALU.mult)
                    wbc = (qn_bc if which == "q" else kn_bc)[0:ST]
                    wflat = wbc.with_ap(wbc.ap.rearrange("p a b -> p (a b)"))
                    tt(kb[0:ST, :], t1v, wflat, ALU.mult)
                    if which == "k":
                        tt(kn[0:ST, :], t1v, wflat, ALU.mult, eng=("pool" if os.environ.get("PT_POOLMUL", "1") == "1" else "dve"))
                        if kind == "meta":
                            for s2 in range(NPS):
                                P.dma("pool", kp[s2, 0:16, cols], kn[0:ST, :])
                        elif kind == "prompt":
                            tok0 = NMETA + t * 512 + st * 128
                            P.dma("pool", kp[s, tok0:tok0 + 128, cols], kn[0:ST, :])
                        else:
                            P.dma("pool", kso[st * 128:(st + 1) * 128, cols], kn[0:ST, :])
                    if PTS >= 4:
                        defer(lambda which=which, half=half, cols=cols, st=st, i=i: post2(which, half, cols, st, i), delay=2)
                else:
                    cp(kn[0:ST, :], ps[0:ST, :], eng="act")
                    if kind == "meta":
                        if os.environ.get("PT_VCP", "1") == "1":
                            cp(Vm[0:ST, cols], ps[0:ST, :])
                        else:
                            cp(Vm[0:ST, cols], kn[0:ST, :], eng="pool")
                        for s2 in range(NPS):
                            P.dma("pool", vp[s2, 0:16, cols], kn[0:ST, :])
                    elif kind == "prompt":
                        tok0 = NMETA + t * 512 + st * 128
                        cp(kb[0:ST, :], ps[0:ST, :])
                        P.dma("pool", vp[s, tok0:tok0 + 128, cols], kn[0:ST, :])
                        P.dma("sp", Vd[s, tok0:tok0 + 128, cols], kb[0:ST, :])
                    else:
                        P.dma("pool", vso[st * 128:(st + 1) * 128, cols], kn[0:ST, :])
                        cp(kb[0:ST, :], ps[0:ST, :])
                        for q2 in range(2):
                            P.dma("sp", Vsd[st * 2 + q2, :, cols], kb[q2 * 64:(q2 + 1) * 64, :])

        if kind != "meta":
            proj_tok(B_Q, 0, "q"); proj_tok(B_Q + 1, 1, "q")
        proj_tok(B_K, 0, "k"); proj_tok(B_K + 1, 1, "k")
        proj_tok(B_V, 0, "v"); proj_tok(B_V + 1, 1, "v")
        flush_deferred()
        if stage < 2:
            return
        P.phase = kind + ":attn"
        if kind != "meta":
            m = top[0]
            NQ = 512 if kind == "prompt" else 64
            nkeys = (NMETA + (t + 1) * 512) if kind == "prompt" else (PAST + 64)
            ktb = [alloc([TP], BF16) for _ in range(2)]
            vtb = [alloc([17, 128], BF16) for _ in range(2)]
            pT = [alloc([512], BF16) for _ in range(4)]
            o1s = [alloc([512]) for _ in range(2)]; o2 = alloc([512]); rr = alloc([512]); rr2 = alloc([512])
            rr3 = alloc([512]); rr4 = alloc([512]); osq = alloc([512], BF16)
            pcount = [0]
            segs = [0] if kind == "prompt" else list(range(4))
            hl = [(sg_, h) for sg_ in segs for h in range(8)]

            def load_kv(i):
                sg_, h = hl[i]
                kt = ktb[i % 2]; vt = vtb[i % 2]
                if kind == "prompt":
                    P.dma("sp", kt[:, 16:nkeys], KTd[s, h, :, 16:nkeys])
                    for g4 in range(t + 1):
                        r0 = NMETA + g4 * 512
                        P.dma("sp", vt[:, g4 * 4:(g4 + 1) * 4, :], Vd[s, r0:r0 + 512, h * 128:(h + 1) * 128].with_ap(
                            Vd.ap[s, r0:r0 + 512, h * 128:(h + 1) * 128].rearrange("(c p) e -> p c e", p=128)))
                else:
                    P.dma("sp", kt[:, 0:nkeys], KTs[sg_, h, :, 0:nkeys])
                    for g4 in range(2):
                        P.dma("sp", vt[:, g4 * 4:(g4 + 1) * 4, :], Vcd[sg_, g4 * 512:(g4 + 1) * 512, h * 128:(h + 1) * 128].with_ap(
                            Vcd.ap[sg_, g4 * 512:(g4 + 1) * 512, h * 128:(h + 1) * 128].rearrange("(c p) e -> p c e", p=128)))
            if kind == "sample":
                memset(vnew_s[64:128], 0.0)
                for kt_ in ktb:
                    memset(kt_[:, PAST + 64:PAST + 128], 0.0)
                for sq_ in range(NSS):
                    P.dma("sp", vnew_s[0:64, sq_, :], Vsd[sq_])
            load_kv(0)
            for i, (sg_, h) in enumerate(hl):
                if i + 1 < len(hl):
                    load_kv(i + 1)
                kt = ktb[i % 2]; vt = vtb[i % 2]
                q0c = sg_ * 64 if kind == "sample" else 0
                blocks = []
                if kind == "prompt":
                    blocks.append((KTm[:, h, :], Vm[0:16, h * 128:(h + 1) * 128], 16, (GOFF + 16) if t == 0 else None))
                    for kc in range((t + 1) * 4):
                        delta = kc * 128 - t * 512
                        win = (GOFF - delta) if delta >= -128 else None
                        blocks.append((kt[:, 16 + kc * 128:16 + (kc + 1) * 128], vt[:, kc, :], 128, win))
                else:
                    for kc in range(8):
                        win = (GOFF + 128) if kc == 7 else None
                        blocks.append((kt[:, kc * 128:(kc + 1) * 128], vt[:, kc, :], 128, win))
                    blocks.append((kt[:, PAST:PAST + 128], vnew_s[:, sg_, h * 128:(h + 1) * 128], 128, GOFF))
                import os as _os
                _sk = _os.environ.get("ATT_SKIP", "")
                if kind == "sample" and _sk:
                    nb_ = []
                    for bi_, blk in enumerate(blocks):
                        typ = "new" if bi_ == 8 else ("win7" if bi_ == 7 else "far")
                        if typ not in _sk:
                            nb_.append(blk)
                    blocks = nb_
                nb = len(blocks)
                for bi, (kv, vv, nk, win) in enumerate(blocks):
                    for mp in range(2):
                        pbS = pbank(4)
                        S = psb[pbS][0:nk, 0:NQ]
                        mm(S, V(kv.ap[mp * 64:(mp + 1) * 64, :], kv.key, kv.lo, kv.hi, kv.page), qT[mp * 64:(mp + 1) * 64, h, q0c:q0c + NQ],
                           start=True, stop=(win is None))
                        if win is not None:
                            mm(S, identb[0:nk, 0:nk], G[0:nk, h, win:win + NQ], start=False, stop=True)
                        pt = pT[pcount[0] % 4]; pcount[0] += 1
                        if win is None:
                            act(pt[0:nk, 0:NQ], S, AF.Exp, bias=b15[0:nk, h:h + 1], scale=A_SCALE)
                        else:
                            act(pt[0:nk, 0:NQ], S, AF.Exp, scale=A_SCALE)
                        group_issued()

                        def pv_den(mp=mp, vv=vv, pt=pt, nk=nk, bi=bi, nb=nb):
                            mm(psb[4 + mp][:, 0:NQ], vv, pt[0:nk, 0:NQ], start=(bi == 0), stop=(bi == nb - 1))
                            mm(psb[6 + mp][:, 0:NQ], ones_bf[0:nk, :], pt[0:nk, 0:NQ], start=(bi == 0), stop=(bi == nb - 1))
                        defer(pv_den, delay=2)
                flush_deferred()
                o1 = o1s[i % 2]
                act(rr[:, 0:NQ], psb[6][:, 0:NQ], AF.Ln)
                act(rr2[:, 0:NQ], psb[7][:, 0:NQ], AF.Ln)
                act(rr[:, 0:NQ], rr[:, 0:NQ], AF.Exp, scale=-1.0)
                act(rr2[:, 0:NQ], rr2[:, 0:NQ], AF.Exp, scale=-1.0)
                tt(o1[:, 0:NQ], psb[4][:, 0:NQ], rr[:, 0:NQ], ALU.mult)
                tt(o2[:, 0:NQ], psb[5][:, 0:NQ], rr2[:, 0:NQ], ALU.mult)
                stt(o1[:, 0:NQ], o2[:, 0:NQ], small[:, 2:3], o1[:, 0:NQ], ALU.mult, ALU.add)

                def finish_head(o1=o1, h=h, q0c=q0c):
                    act(osq[:, 0:NQ], o1[:, 0:NQ], AF.Square)
                    pbn = pbank(4)
                    mm(psb[pbn][:, 0:NQ], ones_bf.full(), osq[:, 0:NQ])
                    rsqrt(rr3[:, 0:NQ], psb[pbn][:, 0:NQ], 1.0 / 128, rr4[:, 0:NQ])
                    stt(oaT[:, h, q0c:q0c + NQ], o1[:, 0:NQ], small[:, 1:2], rr3[:, 0:NQ], ALU.mult, ALU.mult)
                defer(finish_head, delay=3)
            flush_deferred()
            top[0] = m
        top[0] = m_attn
        if dbg and kind == "prompt" and s == 0 and t == 0:
            P.dma("pool", dbg_oa.full(), oaT.full())
        if stage < 3:
            return
        P.phase = kind + ":gdnproj"
        obT = alloc([8, NT], BF16) if kind != "meta" else None
        m_gdn = top[0]
        qg = alloc([8, NT], BF16); kg = alloc([8, NT], BF16); vg = alloc([8, NT], BF16)
        sz = alloc([8, NT], BF16) if kind != "meta" else None
        og = alloc([8, NT], BF16) if kind != "meta" else None
        cb = alloc([8, NT])
        slotBA = wk8(wload(B_BA))
        pb = pbank()
        for kc in range(8):
            mm(psb[pb][0:8, 0:NT], slotBA[:, kc, 0:8], xnT[:, kc, :], start=(kc == 0), stop=(kc == 7))
        pb2 = pbank()
        for kc in range(8):
            mm(psb[pb2][0:8, 0:NT], slotBA[:, kc, 8:16], xnT[:, kc, :], start=(kc == 0), stop=(kc == 7))
        act(cb[0:8, 4, :], psb[pb][0:8, 0:NT], AF.Sigmoid)
        act(cb[0:8, 6, :], psb[pb][0:8, 0:NT], AF.Exp, scale=-1.0)
        act(cb[0:8, 6, :], cb[0:8, 6, :], AF.Ln, bias=1.0)
        act(cb[0:8, 7, :], psb[pb2][0:8, 0:NT], AF.Exp, bias=small[0:8, 3:4])
        act(cb[0:8, 7, :], cb[0:8, 7, :], AF.Ln, bias=1.0)
        ts(cb[0:8, 0, :], cb[0:8, 7, :], small[0:8, 4:5], ALU.mult)
        a_, b_ = 0, 7
        sh = 1
        while sh < C:
            av = cb[0:8, a_, :]; bv = cb[0:8, b_, :]
            a3 = av.with_ap(av.ap.rearrange("p (c l) -> p c l", l=C)); b3 = bv.with_ap(bv.ap.rearrange("p (c l) -> p c l", l=C))
            cp(V(b3.ap[:, :, 0:sh], bv.key, bv.lo, bv.hi, bv.page), V(a3.ap[:, :, 0:sh], av.key, av.lo, av.hi, av.page))
            tt(V(b3.ap[:, :, sh:C], bv.key, bv.lo, bv.hi, bv.page), V(a3.ap[:, :, sh:C], av.key, av.lo, av.hi, av.page),
               V(a3.ap[:, :, 0:C - sh], av.key, av.lo, av.hi, av.page), ALU.add)
            a_, b_ = b_, a_
            sh *= 2
        if a_ != 0:
            cp(cb[0:8, 0, :], cb[0:8, a_, :])
        tt(cb[0:8, 1, :], cb[0:8, 0, :], cb[0:8, 6, :], ALU.subtract)
        act(cb[0:8, 2, :], cb[0:8, 0, :], AF.Exp)
        gv = cb[0:8, 0, :]
        g3 = gv.with_ap(gv.ap.rearrange("p (c l) -> p c l", l=C))
        kdv = cb[0:8, 3, :]
        kd3 = kdv.with_ap(kdv.ap.rearrange("p (c l) -> p c l", l=C))
        tt(kd3, V(g3.ap[:, :, C - 1:C].to_broadcast([8, nch, C]), gv.key, gv.lo, gv.hi, gv.page), g3, ALU.subtract)
        act(cb[0:8, 3, :], cb[0:8, 3, :], AF.Exp)
        tt(cb[0:8, 5, :], cb[0:8, 4, :], cb[0:8, 2, :], ALU.mult)
        cbb = alloc([8, NT], BF16)
        for q_, row in enumerate((0, 1, 2)):
            cp(cbb[0:8, 2 * q_, :], cb[0:8, row, :])
            tt(cb[0:8, 6, :], cb[0:8, row, :], cbb[0:8, 2 * q_, :], ALU.subtract)
            cp(cbb[0:8, 2 * q_ + 1, :], cb[0:8, 6, :])
        ts(cbb[0:8, 6, :], cbb[0:8, 0, :], -1.0, ALU.mult)
        ts(cbb[0:8, 7, :], cbb[0:8, 1, :], -1.0, ALU.mult)

        m_conv = top[0]
        cin = [alloc([nseg, L + 3]) for _ in range(3)]
        cacc = alloc([nseg, L])
        csq = alloc([NT], BF16)
        crn = alloc([NT])
        if kind == "sample":
            scrow = alloc([3072])
            for sg_ in range(4):
                P.dma("sp", scrow[0:3, :], sc[sg_])
                for g6 in range(6):
                    pb = pbank()
                    for c4 in range(4):
                        cid_ = g6 * 4 + c4
                        tr(psb[pb][:, c4 * 3:(c4 + 1) * 3], scrow[0:3, cid_ * 128:(cid_ + 1) * 128], identf[0:3, 0:3])
                    pv_ = psb[pb][:, 0:12]
                    cp(ctx_cur[:, g6 * 4:(g6 + 1) * 4, sg_, :], pv_.with_ap(pv_.ap.rearrange("p (c w) -> p c w", c=4)))
        caccs = [cacc] + [alloc([nseg, L]) for _ in range(5)]
        ctmp = alloc([nseg, L])
        csqs = [csq] + [alloc([NT], BF16) for _ in range(3)]
        crns = [crn, alloc([NT])]

        def l2norm_finish(h, j):
            ca = caccs[(h % 2) * 3 + j]
            cflat = ca.full().with_ap(ca.ap.rearrange("p s l -> p (s l)"))
            crn_ = crns[j]
            pbn = pbank()
            mm(psb[pbn][:, 0:NT], ones_bf.full(), csqs[(h % 2) * 2 + j].full())
            act(crn_.full(), psb[pbn][:, 0:NT], AF.Ln, bias=EPS, scale=1.0)
            act(crn_.full(), crn_.full(), AF.Exp, scale=-0.5)
            if j == 0:
                stt(qg[:, h, :], cflat, B_SCALE, crn_.full(), ALU.mult, ALU.mult)
            else:
                tt(kg[:, h, :], cflat, crn_.full(), ALU.mult)

        def silu_qk(h, j):
            ca = caccs[(h % 2) * 3 + j]
            cflat = ca.full().with_ap(ca.ap.rearrange("p s l -> p (s l)"))
            act(cflat, cflat, AF.Silu)
            act(csqs[(h % 2) * 2 + j].full(), cflat, AF.Square)

        def silu_v(h):
            ca = caccs[(h % 2) * 3 + 2]
            act(vg[:, h, :], ca.full().with_ap(ca.ap.rearrange("p s l -> p (s l)")), AF.Silu)

        for h in range(8):
            slot = wk8(wload(B_H + h))
            for j in range(4):
                pb = pbank()
                for kc in range(8):
                    mm(psb[pb][:, 0:NT], slot[:, kc, j * 128:(j + 1) * 128], xnT[:, kc, :], start=(kc == 0), stop=(kc == 7))
                group_issued()
                ps = psb[pb][:, 0:NT]
                if j == 3:
                    if kind != "meta":
                        act(sz[:, h, :], ps, AF.Silu)
                    continue
                cid = j * 8 + h
                ci = cin[j]
                if kind == "meta":
                    memset(ci[:, :, 0:3], 0.0, eng="pool")
                elif kind == "prompt":
                    cp(ci[:, 0, 0:3], (ctx_meta[:, cid, :] if t == 0 else ctx_cur[:, cid, 0, :]), eng="pool")
                else:
                    cp(ci[:, :, 0:3], ctx_cur[:, cid, 0:4, :], eng="pool")
                cp(ci[:, :, 3:3 + L], ps.with_ap(ps.ap.rearrange("p (s l) -> p s l", s=nseg)), eng="act")
                if kind == "meta":
                    cp(ctx_meta[:, cid, :], ci[:, 0, L:L + 3], eng="pool")
                else:
                    cp(ctx_cur[:, cid, 0:nseg, :], ci[:, :, L:L + 3], eng="pool")
            for j in range(3):
                cid = j * 8 + h
                ci = cin[j]
                ce = "pool" if j == 2 else "dve"
                ca = caccs[(h % 2) * 3 + j]
                ts(ca.full(), ci[:, :, 0:L], cwT[:, cid:cid + 1], ALU.mult, eng=ce)
                for w in range(1, 4):
                    if ce == "dve":
                        stt(ca.full(), ci[:, :, w:w + L], cwT[:, w * 24 + cid:w * 24 + cid + 1], ca.full(), ALU.mult, ALU.add)
                    else:
                        ts(ctmp.full(), ci[:, :, w:w + L], cwT[:, w * 24 + cid:w * 24 + cid + 1], ALU.mult, eng="pool")
                        tt(ca.full(), ca.full(), ctmp.full(), ALU.add, eng="pool")
            defer(lambda h=h: (silu_qk(h, 0), silu_qk(h, 1)), delay=1)
            defer(lambda h=h: silu_v(h), delay=3)
            defer(lambda h=h: (l2norm_finish(h, 0), l2norm_finish(h, 1)), delay=3)
        flush_deferred()
        if (kind == "prompt" and t == 3) or kind == "sample":
            tls = [alloc([512]) for _ in range(2)]
            for sg_ in range(nseg):
                for g6 in range(6):
                    tl = tls[g6 % 2]
                    pb = pbank()
                    for c4 in range(4):
                        tr(psb[pb][0:3, c4 * 128:(c4 + 1) * 128], ctx_cur[:, g6 * 4 + c4, sg_, :], identf.full())
                    cp(tl[0:3, :], psb[pb][0:3, :], eng="act")
                    dst_ = cpo[s, :, g6 * 512:(g6 + 1) * 512] if kind == "prompt" else cso[sg_, :, g6 * 512:(g6 + 1) * 512]
                    P.dma("pool", dst_, tl[0:3, :])

        P.phase = kind + ":gdnchunk"
        top[0] = m_conv
        nlev = {64: 5, 16: 3}[C]
        if C == 64:
            nU_i, nU_s, nL_s, idr, bmk = negU_incl[0:C], negU_strict[0:C], negL_strict[0:C], identrep[0:C], blockmask[0:8]
        else:
            cm = []
            for src_, np_ in ((negU_incl, C), (negU_strict, C), (negL_strict, C), (identrep, C), (blockmask, 8)):
                d_ = alloc([8, C], BF16)
                cp(d_[0:np_], src_[0:np_, :, 0:C])
                cm.append(d_[0:np_])
            nU_i, nU_s, nL_s, idr, bmk = cm
        gdb = [alloc([8, C], BF16) for _ in range(8)]
        tokc = alloc([32])
        DTi = alloc([8, C], BF16); NDT = alloc([8, C], BF16); NTD = alloc([8, C], BF16)
        Pm = [alloc([8, C], BF16) for _ in range(2)]; PmT = [alloc([8, C], BF16) for _ in range(2)]
        Rm = [alloc([8, C], BF16) for _ in range(2)]
        MT = alloc([8, C], BF16); qgc = alloc([8, C], BF16); nwT = alloc([8, C], BF16)
        bv_ = alloc([8, 128], BF16); kbg = alloc([8, 128], BF16); kdc = alloc([8, 128], BF16); vnw = alloc([8, 128], BF16)
        egl = alloc([8])

        def fl(v):
            return v.with_ap(v.ap.rearrange("p a b -> p (a b)"))

        for ci_ in range(nch):
            sgi = ci_ if kind == "sample" else 0
            cs = slice(ci_ * C, (ci_ + 1) * C)
            W8 = 8 * C
            if kind == "meta":
                if ci_ == 0:
                    memset(S_cur.full(), 0.0); memset(S_bf.full(), 0.0)
            elif kind == "prompt":
                if ci_ == 0 and t == 0:
                    cp(S_cur.full(), S_meta.full()); cp(S_bf.full(), S_meta.full(), eng="act")
            else:
                P.dma("sp", S_cur.full(), sg[sgi].with_ap(sg.ap[sgi].rearrange("h d e -> d h e")))
                cp(S_bf.full(), S_cur.full(), eng="act")
            for k_ in range(8):
                src = cbb[0:8, k_, cs]
                tt(gdb[k_][0:8], bmk, V(src.ap.unsqueeze(1).to_broadcast([8, 8, C]), src.key, src.lo, src.hi, src.page), ALU.mult, eng="pool")
            pbt = pbank()
            for k_, row in enumerate((4, 5, 3)):
                tr(psb[pbt][0:C, k_ * 8:(k_ + 1) * 8], cb[0:8, row, cs], identf[0:8, 0:8])
            cp(tokc[0:C, 0:24], psb[pbt][0:C, 0:24])
            on8 = ones_bf[0:8, 0:C]

            def xmat(diag_hi, diag_lo, col_hi, col_lo, mask):
                pbx = pbank()
                X = psb[pbx][0:C, 0:W8]
                mm(X, on8, fl(gdb[diag_hi][0:8]), start=True, stop=False)
                mm(X, on8, fl(gdb[diag_lo][0:8]), start=False, stop=False)
                mm(X, cbb[0:8, col_hi, cs], fl(bmk), start=False, stop=False)
                mm(X, cbb[0:8, col_lo, cs], fl(bmk), start=False, stop=False)
                mm(X, identb[0:C, 0:C], fl(mask), start=False, stop=True)
                return X
            act(fl(DTi[0:C]), xmat(0, 1, 6, 7, nU_i), AF.Exp)
            act(fl(NDT[0:C]), xmat(2, 3, 6, 7, nU_s), AF.Exp)
            act(fl(NTD[0:C]), xmat(6, 7, 2, 3, nL_s), AF.Exp)
            pbe = pbank()
            mm(psb[pbe][:, 0:W8], ones_bf[0:8, :], fl(gdb[4][0:8]), start=True, stop=False)
            mm(psb[pbe][:, 0:W8], ones_bf[0:8, :], fl(gdb[5][0:8]), start=False, stop=True)
            pe_v = psb[pbe][:, 0:W8]
            pe3 = pe_v.with_ap(pe_v.ap.rearrange("p (h c) -> p h c", h=8))
            tt(qgc[:, :, 0:C], qg[:, :, cs], pe3, ALU.mult)
            cp(egl.full(), V(pe3.ap[:, :, C - 1], pe_v.key, pe_v.lo, pe_v.hi, pe_v.page))
            pbk = pbank(); pbq = pbank()
            for h in range(8):
                mm(psb[pbk][0:C, h * C:(h + 1) * C], kg[:, h, cs], kg[:, h, cs])
            for h in range(8):
                mm(psb[pbq][0:C, h * C:(h + 1) * C], kg[:, h, cs], qg[:, h, cs])
            stt(fl(Pm[0][0:C]), psb[pbk][0:C, 0:W8], -1.0, fl(NDT[0:C]), ALU.mult, ALU.mult)
            stt(fl(PmT[0][0:C]), psb[pbk][0:C, 0:W8], -1.0, fl(NTD[0:C]), ALU.mult, ALU.mult)
            tt(fl(MT[0:C]), psb[pbq][0:C, 0:W8], fl(DTi[0:C]), ALU.mult)
            tt(fl(Rm[0][0:C]), fl(Pm[0][0:C]), fl(idr), ALU.add, eng="pool")
            cur = 0
            for lv in range(1, nlev + 1):
                nxt = 1 - cur
                pbp = pbank(); pbpt = pbank()
                for h in range(8):
                    mm(psb[pbpt][0:C, h * C:(h + 1) * C], Pm[cur][0:C, h, :], PmT[cur][0:C, h, :])
                if lv < nlev:
                    for h in range(8):
                        mm(psb[pbp][0:C, h * C:(h + 1) * C], PmT[cur][0:C, h, :], Pm[cur][0:C, h, :])
                cp(fl(PmT[nxt][0:C]), psb[pbpt][0:C, 0:W8], eng="act")
                if lv < nlev:
                    cp(fl(Pm[nxt][0:C]), psb[pbp][0:C, 0:W8])
                pbr = pbank()
                for h in range(8):
                    mm(psb[pbr][0:C, h * C:(h + 1) * C], PmT[nxt][0:C, h, :], Rm[cur][0:C, h, :])
                tt(fl(Rm[nxt][0:C]), psb[pbr][0:C, 0:W8], fl(Rm[cur][0:C]), ALU.add)
                cur = nxt
            TT = Rm[cur]
            pbk = pbank(); pbv = pbank()
            for h in range(8):
                tr(psb16[pbk][0:C, h * 128:(h + 1) * 128], kg[:, h, cs], identb.full())
            for h in range(8):
                tr(psb16[pbv][0:C, h * 128:(h + 1) * 128], vg[:, h, cs], identb.full())
            kt3 = psb16[pbk][0:C, :]; kt3 = kt3.with_ap(kt3.ap.rearrange("p (h d) -> p h d", h=8))
            vt3 = psb16[pbv][0:C, :]; vt3 = vt3.with_ap(vt3.ap.rearrange("p (h d) -> p h d", h=8))
            tt(bv_[0:C], vt3, bc3(tokc[0:C, 0:8], [C, 8, 128]), ALU.mult)
            tt(kbg[0:C], kt3, bc3(tokc[0:C, 8:16], [C, 8, 128]), ALU.mult)
            tt(kdc[0:C], kt3, bc3(tokc[0:C, 16:24], [C, 8, 128]), ALU.mult)
            pbw = pbank()
            for h in range(8):
                mm(psb[pbw][:, h * C:(h + 1) * C], kbg[0:C, h, :], TT[0:C, h, :])
            ts(fl(nwT[:, :, 0:C]), psb[pbw][:, 0:W8], -1.0, ALU.mult)
            pv0 = pbank(); pv1 = pbank()
            for h in range(8):
                o = psb[pv0 if h < 4 else pv1][0:C, (h % 4) * 128:(h % 4 + 1) * 128]
                mm(o, TT[0:C, h, :], bv_[0:C, h, :], start=True, stop=False)
                mm(o, nwT[:, h, 0:C], S_bf[:, h, :], start=False, stop=True)
            cp(fl(vnw[0:C, 0:4, :]), psb[pv0][0:C, :], eng="act")
            cp(fl(vnw[0:C, 4:8, :]), psb[pv1][0:C, :])
            if kind != "meta":
                pbo = pbank()
                for h in range(8):
                    o = psb[pbo][:, h * C:(h + 1) * C]
                    mm(o, S_bf[:, h, :], qgc[:, h, 0:C], start=True, stop=False)
                    mm(o, vnw[0:C, h, :], MT[0:C, h, :], start=False, stop=True)
                po = psb[pbo][:, 0:W8]
                cp(og[:, :, cs], po.with_ap(po.ap.rearrange("p (h c) -> p h c", h=8)), eng="act")
            ps0 = pbank(); ps1 = pbank()
            for h in range(8):
                mm(psb[ps0 if h < 4 else ps1][:, (h % 4) * 128:(h % 4 + 1) * 128], kdc[0:C, h, :], vnw[0:C, h, :])
            tt(S_cur.full(), S_cur.full(), bc3(egl.full(), [128, 8, 128]), ALU.mult)
            tt(fl(S_cur[:, 0:4, :]), fl(S_cur[:, 0:4, :]), psb[ps0].full(), ALU.add)
            tt(fl(S_cur[:, 4:8, :]), fl(S_cur[:, 4:8, :]), psb[ps1].full(), ALU.add)
            cp(S_bf.full(), S_cur.full(), eng="act")
            if kind == "sample":
                P.dma("pool", gso[sgi].with_ap(gso.ap[sgi].rearrange("h d e -> d h e")), S_cur.full())
        if kind == "meta":
            cp(S_meta.full(), S_cur.full())
            return
        if kind == "prompt" and t == 3:
            P.dma("pool", gp[s].with_ap(gp.ap[s].rearrange("h d e -> d h e")), S_cur.full())
        P.phase = kind + ":gdnnorm"
        top[0] = m_conv
        gsq = alloc([NT], BF16); grn = alloc([NT]); gt = alloc([NT])
        for h in range(8):
            act(gsq.full(), og[:, h, :], AF.Square)
            pbn = pbank()
            mm(psb[pbn][:, 0:NT], ones_bf.full(), gsq.full())
            rsqrt(grn.full(), psb[pbn][:, 0:NT], 1.0 / 128, gt.full())
            stt(gt.full(), og[:, h, :], small[:, 0:1], grn.full(), ALU.mult, ALU.mult)
            tt(obT[:, h, :], gt.full(), sz[:, h, :], ALU.mult, eng="pool")
        if dbg and kind == "prompt" and s == 0 and t == 0:
            P.dma("pool", dbg_ob.full(), obT.full())
        top[0] = m_gdn
        if stage < 4:
            return
        P.phase = kind + ":merge"
        mixT = alloc([8, NT], BF16)
        sga = alloc([NT]); sgb = alloc([NT]); tmp = alloc([NT])
        for oc in range(8):
            slot = wk8(wload(B_M + oc))
            pa = pbank(); pb_ = pbank(); pya = pbank(); pyb = pbank()
            for kc in range(8):
                mm(psb[pa][:, 0:NT], slot[:, kc, 0:128], xnT[:, kc, :], start=(kc == 0), stop=(kc == 7))
            for kc in range(8):
                mm(psb[pb_][:, 0:NT], slot[:, kc, 128:256], xnT[:, kc, :], start=(kc == 0), stop=(kc == 7))
            for kc in range(8):
                mm(psb[pya][:, 0:NT], slot[:, kc, 256:384], oaT[:, kc, :], start=(kc == 0), stop=(kc == 7))
            for kc in range(8):
                mm(psb[pyb][:, 0:NT], slot[:, kc, 384:512], obT[:, kc, :], start=(kc == 0), stop=(kc == 7))
            act(sga.full(), psb[pa][:, 0:NT], AF.Sigmoid, bias=bgT[:, oc:oc + 1])
            act(sgb.full(), psb[pb_][:, 0:NT], AF.Sigmoid, bias=bgT[:, 8 + oc:9 + oc])
            tt(tmp.full(), psb[pya][:, 0:NT], sga.full(), ALU.mult)
            tt(sgb.full(), psb[pyb][:, 0:NT], sgb.full(), ALU.mult)
            tt(mixT[:, oc, :], tmp.full(), sgb.full(), ALU.add, eng="pool")
        for half in range(2):
            slot = wk8(wload(B_WO + half))
            for st in range(nst):
                pb = pbank()
                for kc in range(8):
                    mm(psb[pb][0:ST, :], mixT[:, kc, st * ST:(st + 1) * ST], slot[:, kc, :], start=(kc == 0), stop=(kc == 7))
                tt(xtok[0:ST, st, half * 512:(half + 1) * 512], xtok[0:ST, st, half * 512:(half + 1) * 512], psb[pb][0:ST, :], ALU.add)
        if stage < 5:
            return
        P.phase = kind + ":ffn"
        norm_T(norm2)
        uT = alloc([32, NT], BF16)
        rl = [alloc([NT]) for _ in range(2)]
        for j in range(8):
            slot = wk8(wload(B_WU + j))
            for c4 in range(4):
                fc = j * 4 + c4
                pb = pbank()
                for kc in range(8):
                    mm(psb[pb][:, 0:NT], slot[:, kc, c4 * 128:(c4 + 1) * 128], xnT[:, kc, :], start=(kc == 0), stop=(kc == 7))
                r = rl[fc % 2]
                act(r.full(), psb[pb][:, 0:NT], AF.Relu)
                tt(uT[:, fc, :], r.full(), r.full(), ALU.mult, eng=("pool" if fc % 2 else "dve"))
        for oc in range(8):
            slot = wf32(wload(B_WD + oc))
            pb = pbank()
            for st in range(nst):
                for fc in range(32):
                    mm(psb[pb][0:ST, st * 128:(st + 1) * 128], uT[:, fc, st * ST:(st + 1) * ST], slot[:, fc, :], start=(fc == 0), stop=(fc == 31))
            pv = psb[pb][0:ST, 0:nst * 128]
            tt(xtok[0:ST, :, oc * 128:(oc + 1) * 128], xtok[0:ST, :, oc * 128:(oc + 1) * 128],
               pv.with_ap(pv.ap.rearrange("p (s c) -> p s c", s=nst)), ALU.add)
        for st in range(nst):
            if kind == "prompt":
                P.dma("pool", yp[s, t * 512 + st * 128: t * 512 + (st + 1) * 128, :], xtok[0:ST, st, :])
            else:
                P.dma("pool", ys[st * 128:(st + 1) * 128, :], xtok[0:ST, st, :])

    def cache_k_prep():
        P.phase = "cachek"
        top[0] = base_top
        ckf = [alloc([D]) for _ in range(2)]
        ckb = [alloc([D], BF16) for _ in range(2)]
        ckt = [alloc([8, 128], BF16) for _ in range(2)]
        i = 0
        for sq_ in range(NSS):
            P.dma("pool", Vcd[sq_], cv[sq_])
        for sq_ in range(NSS):
            for c in range(8):
                f = ckf[i % 2]; b = ckb[i % 2]; kt_ = ckt[i % 2]
                P.dma("sp", f.full(), ck[sq_, c * 128:(c + 1) * 128, :])
                cp(b.full(), f.full(), eng=("pool" if i % 2 else "dve"))
                pb = pbank()
                for h in range(8):
                    tr(psb16[pb][:, h * 128:(h + 1) * 128], b[:, h * 128:(h + 1) * 128], identb.full())
                src = psb16[pb].full()
                cp(kt_.full(), src.with_ap(src.ap.rearrange("p (h t) -> p h t", h=8)), eng="act")
                P.dma("sp", KTs[sq_, :, :, c * 128:(c + 1) * 128].with_ap(KTs.ap[sq_, :, :, c * 128:(c + 1) * 128].rearrange("h p t -> p h t")), kt_.full())
                i += 1

    if dbg:
        dbg_oa = dram("dbg_oa2", [128, 8, 512], "ExternalOutput", BF16)
        dbg_ob = dram("dbg_ob2", [128, 8, 512], "ExternalOutput", BF16)

    import os
    sel = os.environ.get("KTILES", "msp")
    if "s" in sel:
        cache_k_prep()
    run_tile("meta")
    if "s" in sel:
        run_tile("sample")
    if "p" in sel:
        for s in range(NPS):
            for t in range(4):
                run_tile("prompt", s, t)
    elif "q" in sel:
        run_tile("prompt", 0, 0)
    P.emit()
    es.close()
    return nc, P


_CACHE = {}


def kernel(x_prompt, x_sample, cache_attn_k, cache_attn_v, state_gdn, state_conv, meta_tokens, rel_bias,
           norm1, w_in, b_gate, q_norm, k_norm, lambda_q1, lambda_k1, lambda_q2, lambda_k2, sub_norm,
           conv_w, A_log, dt_bias, gdn_norm, w_br_a, w_br_b, w_out, norm2, w_up, w_down, _stage=99, _cores=8, _dbg=False):
    f = lambda a: np.ascontiguousarray(np.asarray(a, dtype=np.float32))
    key = (_stage, _dbg)
    if key not in _CACHE:
        _CACHE[key] = build_program(_stage, _dbg)
    nc, P = _CACHE[key]
    consts = _consts()
    shared = {
        "meta": f(meta_tokens), "relb": f(rel_bias), "norm1": f(norm1).reshape(-1), "w_in": f(w_in)[0],
        "b_gate": f(b_gate).reshape(-1), "q_norm": f(q_norm).reshape(-1), "k_norm": f(k_norm).reshape(-1),
        "lq1": f(lambda_q1).reshape(-1), "lk1": f(lambda_k1).reshape(-1), "lq2": f(lambda_q2).reshape(-1), "lk2": f(lambda_k2).reshape(-1),
        "sub_norm": f(sub_norm).reshape(-1), "conv_w": f(conv_w).reshape(-1), "a_log": f(A_log).reshape(-1),
        "dt_bias": f(dt_bias).reshape(-1), "gdn_norm": f(gdn_norm).reshape(-1), "w_bra": f(w_br_a)[0], "w_brb": f(w_br_b)[0],
        "w_out": f(w_out)[0], "norm2": f(norm2).reshape(-1), "w_up": f(w_up)[0], "w_down": f(w_down)[0],
    }
    for k, v in consts.items():
        shared["c_" + k] = v
    xp = f(x_prompt); xs = f(x_sample)
    ck = f(cache_attn_k)[0].reshape(32, PAST, D); cv = f(cache_attn_v)[0].reshape(32, PAST, D)
    sg = f(state_gdn)[0]; sc = f(state_conv)[0]
    in_maps = []
    for c in range(_cores):
        m = dict(shared)
        m["xp"] = xp[c * NPS:(c + 1) * NPS]
        m["xs"] = xs[c * NSS:(c + 1) * NSS].reshape(NSS * DSEQ, D)
        m["ck"] = ck[c * NSS:(c + 1) * NSS]
        m["cv"] = cv[c * NSS:(c + 1) * NSS]
        m["sg"] = sg[c * NSS:(c + 1) * NSS]
        m["sc"] = sc[c * NSS:(c + 1) * NSS]
        in_maps.append(m)
    res = run_bass_kernel_spmd(nc, in_maps, core_ids=list(range(_cores)))
    R = res.results
    cat = lambda k: np.concatenate([np.asarray(r[k], dtype=np.float32) for r in R], axis=0)
    nb = _cores * NPS
    ns = _cores * NSS
    outs = (
        cat("yp"),
        cat("ys").reshape(ns, DSEQ, D),
        cat("kp").reshape(1, nb, TP, 8, 128),
        cat("vp").reshape(1, nb, TP, 8, 128),
        cat("gp").reshape(1, nb, 8, 128, 128),
        cat("cpo").reshape(1, nb, 3, 3072),
        cat("kso").reshape(1, ns, DSEQ, 8, 128),
        cat("vso").reshape(1, ns, DSEQ, 8, 128),
        cat("gso").reshape(1, ns, 8, 128, 128),
        cat("cso").reshape(1, ns, 3, 3072),
    )
    if _dbg:
        return outs, R
    return outs
```

```python
import contextlib
import math
import os
from collections import defaultdict

import numpy as np
import concourse.bass as bass
import concourse.mybir as mybir
from concourse.bass_utils import run_bass_kernel_spmd

F32 = mybir.dt.float32
BF16 = mybir.dt.bfloat16
I32 = mybir.dt.int32
ALU = mybir.AluOpType
AF = mybir.ActivationFunctionType
AX = mybir.AxisListType

SEM_LIMIT = 1000
DMA_SEMS = 24
NEG = -30000.0


class V:
    __slots__ = ("ap", "key", "lo", "hi", "page", "track")

    def __init__(self, ap, key, lo, hi, page, track=True):
        self.ap = ap
        self.key = key
        self.lo = lo
        self.hi = hi
        self.page = page
        self.track = track

    def with_ap(self, ap):
        return V(ap, self.key, self.lo, self.hi, self.page, self.track)


class T:
    def __init__(self, ap, name, shape, dram=False, esize=4, base_off=0, page=2048, track=True, whole=False):
        self.whole = whole
        self.ap = ap
        self.name = name
        self.shape = list(shape)
        self.dram = dram
        self.esize = esize
        self.base_off = base_off
        self.page = page
        self.track = track
        fs = self.shape if dram else self.shape[1:]
        st = []
        acc = 1
        for s in reversed(fs):
            st.append(acc)
            acc *= s
        self.fstrides = list(reversed(st))

    def __getitem__(self, key):
        if not isinstance(key, tuple):
            key = (key,)
        ap = self.ap[key]
        fs = self.shape if self.dram else self.shape[1:]
        k2 = list(key) if self.dram else list(key[1:])
        while len(k2) < len(fs):
            k2.append(slice(None))
        lo = 0
        hi = 0
        for k, s, st in zip(k2, fs, self.fstrides):
            if isinstance(k, slice):
                a = 0 if k.start is None else k.start
                b = s if k.stop is None else k.stop
            else:
                a = k
                b = k + 1
            lo += a * st
            hi += (b - 1) * st
        hi += 1
        if self.whole:
            return V(ap, self.name, 0, self.page, self.page, self.track)
        return V(ap, self.name, self.base_off + lo * self.esize, self.base_off + hi * self.esize, self.page, self.track)

    def full(self):
        return self[tuple(slice(None) for _ in self.shape)]


class Op:
    __slots__ = ("eng", "fn", "deps", "id", "is_dma", "has_dependents", "sig", "dma_sem", "dma_val", "phase")

    def __init__(self, eng, fn, is_dma):
        self.eng = eng
        self.fn = fn
        self.deps = set()
        self.is_dma = is_dma
        self.has_dependents = False
        self.sig = None
        self.dma_sem = None
        self.dma_val = None


ENGS = ("pe", "act", "dve", "pool", "sp")


class Prog:
    def __init__(self, nc):
        self.nc = nc
        self.ops = []
        self.hist = defaultdict(list)

    def op(self, eng, fn, reads=(), writes=(), dma=False):
        o = Op(eng, fn, dma)
        o.phase = getattr(self, "phase", "")
        o.id = len(self.ops)
        self.ops.append(o)
        deps = o.deps
        tag = eng + ("_dma" if dma else "")
        hist = self.hist
        for v in reads:
            if not v.track:
                continue
            lo, hi = v.lo, v.hi
            for pg in range(lo // v.page, (hi - 1) // v.page + 1):
                for rec in hist[(v.key, pg)]:
                    if rec[2] == "W" and rec[0] < hi and lo < rec[1]:
                        if rec[4] == "pe" and tag == "pe":
                            continue
                        deps.add(rec[3])
                    elif rec[2] == "R" and v.key.startswith("ps") and rec[4] != tag:
                        deps.add(rec[3])
        for v in writes:
            if not v.track:
                continue
            lo, hi = v.lo, v.hi
            for pg in range(lo // v.page, (hi - 1) // v.page + 1):
                h = hist[(v.key, pg)]
                keep = []
                for rec in h:
                    if rec[0] < hi and lo < rec[1]:
                        if not (rec[4] == "pe" and tag == "pe"):
                            deps.add(rec[3])
                        if lo <= rec[0] and rec[1] <= hi:
                            continue
                    keep.append(rec)
                keep.append([lo, hi, "W", o.id, tag])
                hist[(v.key, pg)] = keep
        for v in reads:
            if not v.track:
                continue
            lo, hi = v.lo, v.hi
            for pg in range(lo // v.page, (hi - 1) // v.page + 1):
                h = hist[(v.key, pg)]
                found = False
                if not dma:
                    for r in h:
                        if r[2] == "R" and r[0] == lo and r[1] == hi and r[4] == tag:
                            r[3] = o.id
                            found = True
                            break
                if not found:
                    h.append([lo, hi, "R", o.id, tag])
        deps.discard(o.id)
        return o

    def dma(self, eng, out, in_, **kw):
        def fn(e):
            return e.dma_start(out=out.ap, in_=in_.ap, **kw)
        return self.op(eng, fn, reads=[in_], writes=[out], dma=True)

    def emit(self):
        nc = self.nc
        ops = self.ops
        for o in ops:
            for d in o.deps:
                ops[d].has_dependents = True
        cnt = {e: 0 for e in ENGS}
        dma_cnt = {e: 0 for e in ENGS}
        for o in ops:
            if o.is_dma:
                i = dma_cnt[o.eng]
                dma_cnt[o.eng] += 1
                o.dma_sem = (o.eng, i % DMA_SEMS)
                o.dma_val = 16 * (i // DMA_SEMS + 1)
            elif o.has_dependents:
                cnt[o.eng] += 1
                o.sig = cnt[o.eng]
        n_epochs = {e: (cnt[e] + SEM_LIMIT - 1) // SEM_LIMIT for e in ENGS}
        stack = contextlib.ExitStack()
        sems = {}
        for e in ENGS:
            for ep in range(max(1, n_epochs[e])):
                sems[(e, ep)] = stack.enter_context(nc.semaphore(f"s_{e}_{ep}"))
            if dma_cnt[e]:
                for i in range(DMA_SEMS):
                    sems[("dma", e, i)] = stack.enter_context(nc.semaphore(f"d_{e}_{i}"))
        per_eng = {e: [] for e in ENGS}
        for o in ops:
            per_eng[o.eng].append(o)
        waited = {e: {} for e in ENGS}
        waits = {}
        for o in ops:
            w = {}
            for d in o.deps:
                p = ops[d]
                if p.is_dma:
                    key = ("dma",) + p.dma_sem
                    val = p.dma_val
                else:
                    key = ("c", p.eng)
                    val = p.sig
                if val > w.get(key, 0):
                    w[key] = val
            if o.is_dma:
                key = ("dma",) + o.dma_sem
                if o.dma_val > 16 and o.dma_val - 16 > w.get(key, 0):
                    w[key] = o.dma_val - 16
            wl = []
            wd = waited[o.eng]
            for key, val in w.items():
                if wd.get(key, 0) >= val:
                    continue
                wd[key] = val
                wl.append((key, val))
            waits[o.id] = wl
        final = {}
        for e in ENGS:
            if dma_cnt[e]:
                fl = []
                for i in range(DMA_SEMS):
                    n = len(range(i, dma_cnt[e], DMA_SEMS))
                    if n:
                        fl.append((("dma", e, i), 16 * n))
                final[e] = fl

        def sem_of(key, val):
            if key[0] == "dma":
                return sems[("dma", key[1], key[2])], val
            e = key[1]
            ep = (val - 1) // SEM_LIMIT
            return sems[(e, ep)], val - ep * SEM_LIMIT

        def run(engname, engobj):
            for o in per_eng[engname]:
                for key, val in waits[o.id]:
                    s, v = sem_of(key, val)
                    engobj.wait_ge(s, v)
                ins = o.fn(engobj)
                if o.is_dma:
                    ins.then_inc(sems[("dma",) + o.dma_sem], 16)
                elif o.sig is not None:
                    s, v = sem_of(("c", engname), o.sig)
                    ins.then_inc(s, 1)
            for key, val in final.get(engname, []):
                s, v = sem_of(key, val)
                engobj.wait_ge(s, v)

        with nc.Block() as block:
            @block.sync
            def _(e):
                run("sp", e)

            @block.scalar
            def _(e):
                run("act", e)

            @block.vector
            def _(e):
                run("dve", e)

            @block.gpsimd
            def _(e):
                run("pool", e)

            @block.tensor
            def _(e):
                run("pe", e)
        stack.close()
        self.stats = {e: len(per_eng[e]) for e in ENGS}


D = 1024
SEQ = 2048
NMETA = 16
TP = NMETA + SEQ
PAST = 1024
DSEQ = 64
NIN = 9232
OFF_KA, OFF_VA, OFF_B, OFF_Z, OFF_BETA, OFF_ALPHA, OFF_GA, OFF_GB = 1024, 2048, 3072, 6144, 7168, 7176, 7184, 8208
DFF = 4096
EPS = 1e-6
LAM_INIT = 0.8 - 0.6 * math.exp(-0.3 * 0)
A_SCALE = 0.125
B_SCALE = 128 ** -0.5
NPS = 2
NSS = 4
GW = 1024
GL = 1152
GOFF = 384

B_Q, B_K, B_V, B_H, B_BA, B_M, B_WO, B_WU, B_WD, NBLK = 0, 2, 4, 6, 14, 15, 23, 25, 33, 41


def _consts():
    c = {}
    c["identf"] = np.eye(128, dtype=np.float32)
    p = np.arange(64)[:, None]
    f = np.arange(64)[None, :]
    def rep(m):
        return np.ascontiguousarray(np.broadcast_to(m[:, None, :], (64, 8, 64)).reshape(64, 512)).astype(np.float32)
    c["negU_incl"] = rep(np.where(f >= p, 0.0, NEG))
    c["negU_strict"] = rep(np.where(f > p, 0.0, NEG))
    c["negL_strict"] = rep(np.where(f < p, 0.0, NEG))
    c["identrep"] = rep(np.eye(64))
    bm = np.zeros((8, 8, 64), np.float32)
    for h in range(8):
        bm[h, h, :] = 1.0
    c["blockmask"] = bm.reshape(8, 512)
    pp = np.arange(128)[:, None]
    cc = np.arange(GW)[None, :] - GOFF
    c["maskG"] = np.where(np.floor_divide(cc, 64) >= np.floor_divide(pp, 64), 0.0, NEG).astype(np.float32)
    lo = [0, 1, 2, 3, 4, 5, 6, 7, 8, 12, 16, 23, 32, 46, 64, 91]
    hi = lo[1:] + [10 ** 9]
    oh = np.zeros((32, GL), np.float32)
    for i in range(GL):
        rel = 511 - i
        n = abs(rel)
        b = 0
        for k in range(16):
            if lo[k] <= n < hi[k]:
                b = k
        if rel > 0:
            b += 16
        oh[b, i] = 1.0
    c["onehot"] = oh
    return c


def build_program(stage=99, dbg=False):
    nc = bass.Bass("TRN2", target_bir_lowering=False)
    P = Prog(nc)
    es = contextlib.ExitStack()

    def dram(name, shape, kind, dt=F32, page=1 << 20, track=False):
        ap = nc.dram_tensor(name, shape, dt, kind=kind).ap()
        return T(ap, name, shape, dram=True, esize=(2 if dt == BF16 else 4), page=page, track=track)

    def din(name, shape):
        return dram(name, shape, "ExternalInput")

    def dout(name, shape):
        return dram(name, shape, "ExternalOutput")

    xp = din("xp", [NPS, SEQ, D])
    xs = din("xs", [NSS * DSEQ, D])
    ck = din("ck", [NSS, PAST, D])
    cv = din("cv", [NSS, PAST, D])
    sg = din("sg", [NSS, 8, 128, 128])
    sc = din("sc", [NSS, 3, 3072])
    meta = din("meta", [NMETA, D])
    relb = din("relb", [32, 8])
    norm1 = din("norm1", [D])
    w_in = din("w_in", [D, NIN])
    b_gate = din("b_gate", [2 * D])
    q_norm = din("q_norm", [64])
    k_norm = din("k_norm", [64])
    lq1 = din("lq1", [64]); lk1 = din("lk1", [64]); lq2 = din("lq2", [64]); lk2 = din("lk2", [64])
    sub_norm = din("sub_norm", [128])
    conv_w = din("conv_w", [4 * 3072])
    a_log = din("a_log", [8])
    dt_bias = din("dt_bias", [8])
    gdn_norm = din("gdn_norm", [128])
    w_bra = din("w_bra", [D, D]); w_brb = din("w_brb", [D, D]); w_out = din("w_out", [D, D])
    norm2 = din("norm2", [D])
    w_up = din("w_up", [D, DFF]); w_down = din("w_down", [DFF, D])
    cst = {k: din("c_" + k, list(v.shape)) for k, v in _consts().items()}
    yp = dout("yp", [NPS, SEQ, D]); ys = dout("ys", [NSS * DSEQ, D])
    kp = dout("kp", [NPS, TP, D]); vp = dout("vp", [NPS, TP, D])
    gp = dout("gp", [NPS, 8, 128, 128]); cpo = dout("cpo", [NPS, 3, 3072])
    kso = dout("kso", [NSS * DSEQ, D]); vso = dout("vso", [NSS * DSEQ, D])
    gso = dout("gso", [NSS, 8, 128, 128]); cso = dout("cso", [NSS, 3, 3072])
    WS = dram("WS", [NBLK, 128, 4096], "Internal", BF16, page=1 << 20, track=True)
    KTd = dram("KTd", [NPS, 8, 128, TP], "Internal", BF16, page=1 << 16, track=True)
    Vd = dram("Vd", [NPS, TP, D], "Internal", BF16, page=1 << 16, track=True)
    KTs = dram("KTs", [NSS, 8, 128, PAST + DSEQ], "Internal", BF16, page=1 << 16, track=True)
    FD = dram("FD", [8, 128, GL], "Internal", F32, page=1 << 16, track=True)
    Vsd = dram("Vsd", [NSS, DSEQ, D], "Internal", BF16, page=1 << 16, track=True)
    Vcd = dram("Vcd", [NSS, PAST, D], "Internal", BF16, page=1 << 16, track=True)

    ARENA = 206 * 1024
    arena_h = es.enter_context(nc.sbuf_tensor("arena", [128, ARENA // 4], F32))
    top = [0]

    def alloc(shape, dt=F32):
        n = int(np.prod(shape))
        esz = 2 if dt == BF16 else 4
        nb = (n * esz + 31) // 32 * 32
        off = top[0]
        top[0] += nb
        assert top[0] <= ARENA, ("arena overflow", top[0])
        ap = arena_h[:, off // 4:(off + nb) // 4]
        if dt != F32:
            ap = ap.bitcast(dt)
        ap = ap[:, 0:n]
        if len(shape) == 2:
            ap = ap.rearrange("p (a b) -> p a b", a=shape[0])
        elif len(shape) == 3:
            ap = ap.rearrange("p (a b c) -> p a b c", a=shape[0], b=shape[1])
        return T(ap, "arena", [128] + list(shape), esize=esz, base_off=off)

    psb = []
    psb16 = []
    for i in range(8):
        h = es.enter_context(nc.psum_tensor(f"ps{i}", [128, 512], F32))
        psb.append(T(h, f"ps{i}", [128, 512], page=4096, whole=True))
        psb16.append(T(h[:, :].bitcast(BF16), f"ps{i}", [128, 1024], esize=2, page=4096, whole=True))
    rot = [0]

    def pbank(pool=8):
        i = rot[0] % pool
        rot[0] += 1
        return i

    def rw(*vs):
        return [v for v in vs if isinstance(v, V)]

    def A_(x):
        return x.ap if isinstance(x, V) else x

    def mm(out, lhsT, rhs, start=True, stop=True):
        P.op("pe", lambda e: e.matmul(out=out.ap, lhsT=lhsT.ap, rhs=rhs.ap, start=start, stop=stop), reads=[lhsT, rhs], writes=[out])

    def tr(out, in_, ident):
        P.op("pe", lambda e: e.transpose(out=out.ap, in_=in_.ap, identity=ident.ap), reads=[in_, ident], writes=[out])

    def act(out, in_, func, bias=None, scale=None, accum=None):
        kw = {}
        if bias is not None:
            kw["bias"] = A_(bias)
        if scale is not None:
            kw["scale"] = A_(scale)
        if accum is not None:
            kw["accum_out"] = accum.ap
        P.op("act", lambda e: e.activation(out=out.ap, in_=in_.ap, func=func, **kw), reads=rw(in_, bias, scale), writes=rw(out, accum))

    def tt(out, a, b, op, eng="dve"):
        P.op(eng, lambda e: e.tensor_tensor(out=out.ap, in0=a.ap, in1=b.ap, op=op), reads=[a, b], writes=[out])

    def ts(out, a, s1, op0, s2=None, op1=None, eng="dve"):
        if op1 is None:
            P.op(eng, lambda e: e.tensor_scalar(out=out.ap, in0=a.ap, scalar1=A_(s1), scalar2=0.0, op0=op0, op1=ALU.add), reads=rw(a, s1), writes=[out])
        else:
            P.op(eng, lambda e: e.tensor_scalar(out=out.ap, in0=a.ap, scalar1=A_(s1), scalar2=A_(s2), op0=op0, op1=op1), reads=rw(a, s1, s2), writes=[out])

    def stt(out, a, s, b, op0, op1, eng="dve"):
        P.op(eng, lambda e: e.scalar_tensor_tensor(out=out.ap, in0=a.ap, scalar=A_(s), in1=b.ap, op0=op0, op1=op1), reads=rw(a, s, b), writes=[out])

    def cp(out, in_, eng="dve"):
        if eng == "act":
            P.op("act", lambda e: e.copy(out=out.ap, in_=in_.ap), reads=[in_], writes=[out])
        else:
            P.op(eng, lambda e: e.tensor_copy(out=out.ap, in_=in_.ap), reads=[in_], writes=[out])

    def red(out, in_, op=ALU.add):
        P.op("dve", lambda e: e.tensor_reduce(out=out.ap, in_=in_.ap, axis=AX.X, op=op), reads=[in_], writes=[out])

    def recip(out, in_):
        act(out, in_, AF.Ln)
        act(out, out, AF.Exp, scale=-1.0)

    def memset(v, val, eng="dve"):
        P.op(eng, lambda e: e.memset(v.ap, val), writes=[v])

    def rsqrt(out, in_, scale, tmp):
        act(tmp, in_, AF.Ln, bias=EPS, scale=scale)
        act(out, tmp, AF.Exp, scale=-0.5)

    def bc3(v, shape):
        return v.with_ap(v.ap.unsqueeze(2).to_broadcast(shape))

    deferred = []

    def defer(fn, delay=1):
        deferred.append([delay, fn])

    def group_issued():
        run_now = []
        keep = []
        for d in deferred:
            d[0] -= 1
            (run_now if d[0] <= 0 else keep).append(d)
        deferred[:] = keep
        for d in run_now:
            d[1]()

    def flush_deferred():
        while deferred:
            group_issued()

    def dap(t, off, pat):
        return t.full().with_ap(bass.AP(t.ap.tensor, off, pat))

    def ws_k8(b):
        return WS.ap[b].rearrange("p (k c) -> p k c", k=8)

    def cast_piece(b, off, w, src, c0):
        dst = WS[b].with_ap(ws_k8(b)[:, :, off:off + w])
        s = src.full().with_ap(src.ap[:, c0:c0 + w].rearrange("(k p) c -> p k c", p=128))
        P.dma("pool", dst, s)

    def cast_block(b):
        if B_Q <= b < B_K:
            cast_piece(b, 0, 512, w_in, (b - B_Q) * 512)
        elif B_K <= b < B_V:
            cast_piece(b, 0, 512, w_in, OFF_KA + (b - B_K) * 512)
        elif B_V <= b < B_H:
            cast_piece(b, 0, 512, w_in, OFF_VA + (b - B_V) * 512)
        elif b == B_BA:
            cast_piece(B_BA, 0, 16, w_in, OFF_BETA)
        elif B_H <= b < B_BA:
            h = b - B_H
            for j in range(3):
                cast_piece(b, j * 128, 128, w_in, OFF_B + j * 1024 + h * 128)
            cast_piece(b, 384, 128, w_in, OFF_Z + h * 128)
        elif B_M <= b < B_WO:
            oc = b - B_M
            cast_piece(b, 0, 128, w_in, OFF_GA + oc * 128)
            cast_piece(b, 128, 128, w_in, OFF_GB + oc * 128)
            cast_piece(b, 256, 128, w_bra, oc * 128)
            cast_piece(b, 384, 128, w_brb, oc * 128)
        elif B_WO <= b < B_WU:
            cast_piece(b, 0, 512, w_out, (b - B_WO) * 512)
        elif B_WU <= b < B_WD:
            cast_piece(b, 0, 512, w_up, (b - B_WU) * 512)
        else:
            oc = b - B_WD
            dst = WS[b].with_ap(WS.ap[b].rearrange("p (f c) -> p f c", f=32))
            s_ = w_down.full().with_ap(w_down.ap[:, oc * 128:(oc + 1) * 128].rearrange("(f p) c -> p f c", p=128))
            P.dma("pool", dst, s_)

    cast_order = ([B_K, B_K + 1, B_V, B_V + 1, B_BA] + [B_H + h for h in range(8)] + [B_Q, B_Q + 1]
                  + [B_M + i for i in range(8)] + [B_WO, B_WO + 1] + [B_WU + i for i in range(8)] + [B_WD + i for i in range(8)])
    cast_done = [0]

    def cast_upto(n):
        while cast_done[0] < min(n, len(cast_order)):
            cast_block(cast_order[cast_done[0]])
            cast_done[0] += 1
    cast_upto(4)

    identf = alloc([128]); P.dma("sp", identf.full(), cst["identf"].full())
    identb = alloc([128], BF16); cp(identb.full(), identf.full())
    ones_bf = alloc([128], BF16); memset(ones_bf.full(), 1.0)
    negU_incl = alloc([8, 64], BF16)
    negU_strict = alloc([8, 64], BF16)
    negL_strict = alloc([8, 64], BF16)
    identrep = alloc([8, 64], BF16)
    blockmask = alloc([8, 64], BF16)
    qn_bc = alloc([8, 64]); P.dma("sp", qn_bc.full(), dap(q_norm, 0, [[0, 128], [0, 8], [1, 64]]))
    kn_bc = alloc([8, 64]); P.dma("sp", kn_bc.full(), dap(k_norm, 0, [[0, 128], [0, 8], [1, 64]]))
    small = alloc([64])
    P.dma("sp", small[:, 0:1], dap(gdn_norm, 0, [[1, 128], [1, 1]]))
    P.dma("sp", small[:, 1:2], dap(sub_norm, 0, [[1, 128], [1, 1]]))
    ts(small[:, 1:2], small[:, 1:2], 1.0 - LAM_INIT, ALU.mult)
    P.dma("sp", small[0:8, 3:4], dap(dt_bias, 0, [[1, 8], [1, 1]]))
    P.dma("sp", small[0:8, 5:6], dap(a_log, 0, [[1, 8], [1, 1]]))
    act(small[0:8, 4:5], small[0:8, 5:6], AF.Exp)
    ts(small[0:8, 4:5], small[0:8, 4:5], -1.0, ALU.mult)
    b15 = alloc([8]); P.dma("sp", b15.full(), dap(relb, 15 * 8, [[0, 128], [1, 8]]))
    rowsA = alloc([128]); P.dma("sp", rowsA[0:16, :], b_gate.full().with_ap(b_gate.ap.rearrange("(t p) -> t p", p=128)))
    rowsC = alloc([128]); P.dma("sp", rowsC[0:96, :], conv_w.full().with_ap(conv_w.ap.rearrange("(t p) -> t p", p=128)))
    bgT = alloc([16])
    cwT = alloc([96])
    pb = pbank()
    tr(psb[pb][:, 0:16], rowsA[0:16, :], identf[0:16, 0:16])
    tr(psb[pb][:, 16:112], rowsC[0:96, :], identf[0:96, 0:96])
    cp(bgT.full(), psb[pb][:, 0:16])
    cp(cwT.full(), psb[pb][:, 16:112])
    G = alloc([8, GW], BF16)
    m0 = top[0]
    lam4 = alloc([4, 64])
    for i, t in enumerate((lq1, lk1, lq2, lk2)):
        P.dma("sp", lam4[:, i, :], dap(t, 0, [[0, 128], [1, 64]]))
    tt(lam4[:, 0, :], lam4[:, 0, :], lam4[:, 1, :], ALU.mult)
    tt(lam4[:, 2, :], lam4[:, 2, :], lam4[:, 3, :], ALU.mult)
    red(small[:, 6:7], lam4[:, 0, :]); red(small[:, 7:8], lam4[:, 2, :])
    act(small[:, 6:8], small[:, 6:8], AF.Exp)
    tt(small[:, 8:9], small[:, 7:8], small[:, 6:7], ALU.subtract)
    ts(small[:, 2:3], small[:, 8:9], -LAM_INIT, ALU.add)
    for nm_, dst_, rows_ in (("negU_incl", negU_incl, 64), ("negU_strict", negU_strict, 64), ("negL_strict", negL_strict, 64),
                             ("identrep", identrep, 64), ("blockmask", blockmask, 8)):
        stg_ = alloc([8, 64])
        P.dma("sp", stg_[0:rows_], cst[nm_].full().with_ap(cst[nm_].ap.rearrange("p (a b) -> p a b", a=8)))
        cp(dst_[0:rows_], stg_[0:rows_])
    onehot = alloc([GL]); P.dma("sp", onehot[0:32, :], cst["onehot"].full())
    tab = alloc([8]); P.dma("sp", tab[0:32, :], relb.full())
    tabrep = alloc([8, 128])
    cp(tabrep[0:32], bc3(tab[0:32, :], [32, 8, 128]))
    maskG = alloc([GW]); P.dma("sp", maskG.full(), cst["maskG"].full())
    frep = alloc([GL])
    gsk = alloc([GW])
    for h in range(8):
        for j in range(3):
            pb = pbank()
            mm(psb[pb][:, 0:384], tabrep[0:32, h, :], onehot[0:32, j * 384:(j + 1) * 384])
            ts(frep[:, j * 384:(j + 1) * 384], psb[pb][:, 0:384], 1.0 / A_SCALE, ALU.mult)
        P.dma("sp", FD[h], frep.full())
        P.dma("sp", gsk.full(), FD[h].with_ap(bass.AP(FD.ap.tensor, h * 128 * GL + 127, [[GL - 1, 128], [1, GW]])))
        tt(G[:, h, :], gsk.full(), maskG.full(), ALU.add)
    top[0] = m0

    S_meta = alloc([8, 128])
    ctx_meta = alloc([24, 3])
    KTm = alloc([8, 16], BF16)
    Vm = alloc([1024], BF16)
    S_cur = alloc([8, 128])
    S_bf = alloc([8, 128], BF16)
    ctx_cur = alloc([24, 4, 3])
    NSLOT = 3
    wring = [alloc([4096], BF16) for _ in range(NSLOT)]
    wcnt = [0]

    def wload(b):
        cast_upto(cast_order.index(b) + 4)
        slot = wring[wcnt[0] % NSLOT]
        wcnt[0] += 1
        if b == B_BA:
            P.dma("sp", wk8(slot)[:, :, 0:16], WS[b].with_ap(ws_k8(b)[:, :, 0:16]))
        else:
            P.dma("sp", slot.full(), WS[b])
        return slot

    def wk8(slot):
        return T(slot.ap.rearrange("p (k c) -> p k c", k=8), "arena", [128, 8, 512], esize=2, base_off=slot.base_off)

    def wf32(slot):
        return T(slot.ap.rearrange("p (f c) -> p f c", f=32), "arena", [128, 32, 128], esize=2, base_off=slot.base_off)

    base_top = top[0]

    def run_tile(kind, s=0, t=0):
        top[0] = base_top
        if kind == "meta":
            NT, ST, nst, nseg, L, C = 16, 16, 1, 1, 16, 16
        elif kind == "prompt":
            NT, ST, nst, nseg, L, C = 512, 128, 4, 1, 512, 64
        else:
            NT, ST, nst, nseg, L, C = 256, 128, 2, 4, 64, 64
        nch = NT // C
        xtok = alloc([nst, D])
        xnT = alloc([8, NT], BF16)
        sstat = alloc([32])

        for st in range(nst):
            if kind == "meta":
                src = meta.full()
            elif kind == "prompt":
                src = xp[s, t * 512 + st * 128: t * 512 + (st + 1) * 128, :]
            else:
                src = xs[st * 128:(st + 1) * 128, :]
            P.dma("sp", xtok[0:ST, st, :], src)

        def norm_T(norm_dram):
            m = top[0]
            norm_bc = alloc([D])
            P.dma("sp", norm_bc.full(), dap(norm_dram, 0, [[0, 128], [1, D]]))
            junk = alloc([D])
            xnb = alloc([D], BF16)
            for st in range(nst):
                memset(sstat[0:ST, st:st + 1], 0.0)
                act(junk[0:ST, :], xtok[0:ST, st, :], AF.Square, accum=sstat[0:ST, st:st + 1])
                rsqrt(sstat[0:ST, 8 + st:9 + st], sstat[0:ST, st:st + 1], 1.0 / D, sstat[0:ST, 16 + st:17 + st])
                stt(xnb[0:ST, :], xtok[0:ST, st, :], sstat[0:ST, 8 + st:9 + st], norm_bc[0:ST, :], ALU.mult, ALU.mult)
                pb = pbank()
                for kc in range(8):
                    tr(psb16[pb][:, kc * ST:(kc + 1) * ST], xnb[0:ST, kc * 128:(kc + 1) * 128], identb[0:ST, 0:ST])
                src = psb16[pb][:, 0:8 * ST]
                cp(xnT[:, :, st * ST:(st + 1) * ST], src.with_ap(src.ap.rearrange("p (k t) -> p k t", k=8)), eng="act")
            top[0] = m

        if stage < 0:
            return
        P.phase = kind + ":norm1"
        norm_T(norm1)
        if stage < 1:
            flush_deferred()
            return
        P.phase = kind + ":qkvproj"

        oaT = alloc([8, NT], BF16) if kind != "meta" else None
        vnew_s = alloc([4, D], BF16) if kind == "sample" else None
        m_attn = top[0]
        qT = alloc([8, NT], BF16) if kind != "meta" else None

        NB_ = 4
        sqs = [alloc([512]) for _ in range(NB_)]
        t1s = [alloc([512]) for _ in range(NB_)]
        kns = [alloc([512]) for _ in range(NB_)]
        kbs = [alloc([512], BF16) for _ in range(NB_)]
        ktiles = [alloc([4, ST], BF16) for _ in range(NB_)]
        rsts = [alloc([16]) for _ in range(NB_)]
        ptc = [0]

        def post2(which, half, cols, st, i):
            kb = kbs[i]; kn = kns[i]; ktile = ktiles[i]
            pb2 = pbank()
            for hh in range(4):
                tr(psb16[pb2][:, hh * ST:(hh + 1) * ST], kb[0:ST, hh * 128:(hh + 1) * 128], identb[0:ST, 0:ST])
            src = psb16[pb2][:, 0:4 * ST]
            srcv = src.with_ap(src.ap.rearrange("p (k t) -> p k t", k=4))
            if which == "q":
                cp(qT[:, half * 4:(half + 1) * 4, st * ST:(st + 1) * ST], srcv, eng="act")
                return
            cp(ktile.full(), srcv, eng="act")
            if kind == "meta":
                cp(KTm[:, half * 4:(half + 1) * 4, :], ktile.full(), eng="pool")
            elif kind == "prompt":
                tok0 = NMETA + t * 512 + st * 128
                P.dma("sp", KTd[s, half * 4:(half + 1) * 4, :, tok0:tok0 + 128].with_ap(
                    KTd.ap[s, half * 4:(half + 1) * 4, :, tok0:tok0 + 128].rearrange("h p t -> p h t")), ktile.full())
            else:
                for q2 in range(2):
                    sq_ = st * 2 + q2
                    P.dma("sp", KTs[sq_, half * 4:(half + 1) * 4, :, PAST:PAST + 64].with_ap(
                        KTs.ap[sq_, half * 4:(half + 1) * 4, :, PAST:PAST + 64].rearrange("h p t -> p h t")),
                        ktile[:, :, q2 * 64:(q2 + 1) * 64])

        def proj_tok(blk_id, half, which):
            slot = wk8(wload(blk_id))
            cols = slice(half * 512, (half + 1) * 512)
            for st in range(nst):
                pb = pbank()
                for kc in range(8):
                    mm(psb[pb][0:ST, :], xnT[:, kc, st * ST:(st + 1) * ST], slot[:, kc, :], start=(kc == 0), stop=(kc == 7))
                group_issued()
                i = ptc[0] % NB_
                ptc[0] += 1
                ps = psb[pb]
                sq = sqs[i]; t1 = t1s[i]; kn = kns[i]; kb = kbs[i]; rs = rsts[i]
                PTS = int(os.environ.get("PT_STOP", "9"))
                if PTS <= 1 or (which == "v" and os.environ.get("PT_VSKIP", "0") == "1"):
                    cp(sq[0:ST, :], ps[0:ST, :])
                    continue
                if which in ("q", "k"):
                    act(sq[0:ST, :], ps[0:ST, :], AF.Square)
                    sqv = sq[0:ST, :]
                    red(rs[0:ST, 0:8], sqv.with_ap(sqv.ap.rearrange("p (a b) -> p a b", a=8)))
                    PRS = int(os.environ.get("PT_RS", "2"))
                    if PRS == 2:
                        rsqrt(rs[0:ST, 0:8], rs[0:ST, 0:8], 1.0 / 64, rs[0:ST, 8:16])
                    elif PRS == 1:
                        act(rs[0:ST, 8:16], rs[0:ST, 0:8], AF.Sqrt, bias=EPS, scale=1.0 / 64)
                        P.op("dve", lambda e, rs=rs: e.reciprocal(out=rs[0:ST, 0:8].ap, in_=rs[0:ST, 8:16].ap), reads=[rs[0:ST, 8:16]], writes=[rs[0:ST, 0:8]])
                    if PTS <= 2:
                        continue
                    psv = ps[0:ST, :]
                    t1v = t1[0:ST, :]
                    tt(t1v.with_ap(t1v.ap.rearrange("p (a b) -> p a b", a=8)), psv.with_ap(psv.ap.rearrange("p (a b) -> p a b", a=8)),
                       bc3(rs[0:ST, 0:8], [ST, 8, 64]), ALU.mult)
                    wbc = (qn_bc if which == "q" else kn_bc)[0:ST]
                    wflat = wbc.with_ap(wbc.ap.rearrange("p a b -> p (a b)"))
                    tt(kb[0:ST, :], t1v, wflat, ALU.mult)
                    if which == "k":
                        tt(kn[0:ST, :], t1v, wflat, ALU.mult, eng=("pool" if os.environ.get("PT_POOLMUL", "1") == "1" else "dve"))
                        if kind == "meta":
                            for s2 in range(NPS):
                                P.dma("pool", kp[s2, 0:16, cols], kn[0:ST, :])
                        elif kind == "prompt":
                            tok0 = NMETA + t * 512 + st * 128
                            P.dma("pool", kp[s, tok0:tok0 + 128, cols], kn[0:ST, :])
                        else:
                            P.dma("pool", kso[st * 128:(st + 1) * 128, cols], kn[0:ST, :])
                    if PTS >= 4:
                        defer(lambda which=which, half=half, cols=cols, st=st, i=i: post2(which, half, cols, st, i), delay=2)
                else:
                    cp(kn[0:ST, :], ps[0:ST, :], eng="act")
                    if kind == "meta":
                        if os.environ.get("PT_VCP", "1") == "1":
                            cp(Vm[0:ST, cols], ps[0:ST, :])
                        else:
                            cp(Vm[0:ST, cols], kn[0:ST, :], eng="pool")
                        for s2 in range(NPS):
                            P.dma("pool", vp[s2, 0:16, cols], kn[0:ST, :])
                    elif kind == "prompt":
                        tok0 = NMETA + t * 512 + st * 128
                        cp(kb[0:ST, :], ps[0:ST, :])
                        P.dma("pool", vp[s, tok0:tok0 + 128, cols], kn[0:ST, :])
                        P.dma("sp", Vd[s, tok0:tok0 + 128, cols], kb[0:ST, :])
                    else:
                        P.dma("pool", vso[st * 128:(st + 1) * 128, cols], kn[0:ST, :])
                        cp(kb[0:ST, :], ps[0:ST, :])
                        for q2 in range(2):
                            P.dma("sp", Vsd[st * 2 + q2, :, cols], kb[q2 * 64:(q2 + 1) * 64, :])

        if kind != "meta":
            proj_tok(B_Q, 0, "q"); proj_tok(B_Q + 1, 1, "q")
        proj_tok(B_K, 0, "k"); proj_tok(B_K + 1, 1, "k")
        proj_tok(B_V, 0, "v"); proj_tok(B_V + 1, 1, "v")
        flush_deferred()
        if stage < 2:
            return
        P.phase = kind + ":attn"
        if kind != "meta":
            m = top[0]
            NQ = 512 if kind == "prompt" else 64
            nkeys = (NMETA + (t + 1) * 512) if kind == "prompt" else (PAST + 64)
            ktb = [alloc([TP], BF16) for _ in range(2)]
            vtb = [alloc([17, 128], BF16) for _ in range(2)]
            pT = [alloc([512], BF16) for _ in range(4)]
            o1s = [alloc([512]) for _ in range(2)]; o2 = alloc([512]); rr = alloc([512]); rr2 = alloc([512])
            rr3 = alloc([512]); rr4 = alloc([512]); osq = alloc([512], BF16)
            dacc = [alloc([512]) for _ in range(2)]; dhi = alloc([512], BF16); dlo = alloc([512], BF16)
            pcount = [0]
            segs = [0] if kind == "prompt" else list(range(4))
            hl = [(sg_, h) for sg_ in segs for h in range(8)]

            def load_kv(i):
                sg_, h = hl[i]
                kt = ktb[i % 2]; vt = vtb[i % 2]
                if kind == "prompt":
                    P.dma("sp", kt[:, 16:nkeys], KTd[s, h, :, 16:nkeys])
                    for g4 in range(t + 1):
                        r0 = NMETA + g4 * 512
                        P.dma("sp", vt[:, g4 * 4:(g4 + 1) * 4, :], Vd[s, r0:r0 + 512, h * 128:(h + 1) * 128].with_ap(
                            Vd.ap[s, r0:r0 + 512, h * 128:(h + 1) * 128].rearrange("(c p) e -> p c e", p=128)))
                else:
                    P.dma("sp", kt[:, 0:nkeys], KTs[sg_, h, :, 0:nkeys])
                    for g4 in range(2):
                        P.dma("sp", vt[:, g4 * 4:(g4 + 1) * 4, :], Vcd[sg_, g4 * 512:(g4 + 1) * 512, h * 128:(h + 1) * 128].with_ap(
                            Vcd.ap[sg_, g4 * 512:(g4 + 1) * 512, h * 128:(h + 1) * 128].rearrange("(c p) e -> p c e", p=128)))
            if kind == "sample":
                memset(vnew_s[64:128], 0.0)
                for kt_ in ktb:
                    memset(kt_[:, PAST + 64:PAST + 128], 0.0)
                for sq_ in range(NSS):
                    P.dma("sp", vnew_s[0:64, sq_, :], Vsd[sq_])
            load_kv(0)
            for i, (sg_, h) in enumerate(hl):
                if i + 1 < len(hl):
                    load_kv(i + 1)
                kt = ktb[i % 2]; vt = vtb[i % 2]
                q0c = sg_ * 64 if kind == "sample" else 0
                blocks = []
                if kind == "prompt":
                    blocks.append((KTm[:, h, :], Vm[0:16, h * 128:(h + 1) * 128], 16, (GOFF + 16) if t == 0 else None))
                    for kc in range((t + 1) * 4):
                        delta = kc * 128 - t * 512
                        win = (GOFF - delta) if delta >= -128 else None
                        blocks.append((kt[:, 16 + kc * 128:16 + (kc + 1) * 128], vt[:, kc, :], 128, win))
                else:
                    for kc in range(8):
                        win = (GOFF + 128) if kc == 7 else None
                        blocks.append((kt[:, kc * 128:(kc + 1) * 128], vt[:, kc, :], 128, win))
                    blocks.append((kt[:, PAST:PAST + 128], vnew_s[:, sg_, h * 128:(h + 1) * 128], 128, GOFF))
                import os as _os
                _sk = _os.environ.get("ATT_SKIP", "")
                if kind == "sample" and _sk:
                    nb_ = []
                    for bi_, blk in enumerate(blocks):
                        typ = "new" if bi_ == 8 else ("win7" if bi_ == 7 else "far")
                        if typ not in _sk:
                            nb_.append(blk)
                    blocks = nb_
                nb = len(blocks)
                memset(dacc[0].full(), 0.0)
                memset(dacc[1].full(), 0.0, eng="pool")
                for bi, (kv, vv, nk, win) in enumerate(blocks):
                    for mp in range(2):
                        pbS = pbank(4)
                        S = psb[pbS][0:nk, 0:NQ]
                        mm(S, V(kv.ap[mp * 64:(mp + 1) * 64, :], kv.key, kv.lo, kv.hi, kv.page), qT[mp * 64:(mp + 1) * 64, h, q0c:q0c + NQ],
                           start=True, stop=(win is None))
                        if win is not None:
                            mm(S, identb[0:nk, 0:nk], G[0:nk, h, win:win + NQ], start=False, stop=True)
                        pt = pT[pcount[0] % 4]; pcount[0] += 1
                        if win is None:
                            act(pt[0:nk, 0:NQ], S, AF.Exp, bias=b15[0:nk, h:h + 1], scale=A_SCALE)
                        else:
                            act(pt[0:nk, 0:NQ], S, AF.Exp, scale=A_SCALE)
                        group_issued()

                        def pv_only(mp=mp, vv=vv, pt=pt, nk=nk, bi=bi, nb=nb):
                            mm(psb[4 + mp][:, 0:NQ], vv, pt[0:nk, 0:NQ], start=(bi == 0), stop=(bi == nb - 1))
                        defer(pv_only, delay=2)
                        tt(dacc[mp][0:nk, 0:NQ], dacc[mp][0:nk, 0:NQ], pt[0:nk, 0:NQ], ALU.add, eng=("dve" if mp == 0 else "pool"))
                flush_deferred()
                for mp in range(2):
                    cp(dhi[:, 0:NQ], dacc[mp][:, 0:NQ])
                    tt(dacc[mp][:, 0:NQ], dacc[mp][:, 0:NQ], dhi[:, 0:NQ], ALU.subtract)
                    cp(dlo[:, 0:NQ], dacc[mp][:, 0:NQ])
                    mm(psb[6 + mp][:, 0:NQ], ones_bf.full(), dhi[:, 0:NQ], start=True, stop=False)
                    mm(psb[6 + mp][:, 0:NQ], ones_bf.full(), dlo[:, 0:NQ], start=False, stop=True)
                o1 = o1s[i % 2]
                act(rr[:, 0:NQ], psb[6][:, 0:NQ], AF.Ln)
                act(rr2[:, 0:NQ], psb[7][:, 0:NQ], AF.Ln)
                act(rr[:, 0:NQ], rr[:, 0:NQ], AF.Exp, scale=-1.0)
                act(rr2[:, 0:NQ], rr2[:, 0:NQ], AF.Exp, scale=-1.0)
                tt(o1[:, 0:NQ], psb[4][:, 0:NQ], rr[:, 0:NQ], ALU.mult)
                tt(o2[:, 0:NQ], psb[5][:, 0:NQ], rr2[:, 0:NQ], ALU.mult)
                stt(o1[:, 0:NQ], o2[:, 0:NQ], small[:, 2:3], o1[:, 0:NQ], ALU.mult, ALU.add)

                def finish_head(o1=o1, h=h, q0c=q0c):
                    act(osq[:, 0:NQ], o1[:, 0:NQ], AF.Square)
                    pbn = pbank(4)
                    mm(psb[pbn][:, 0:NQ], ones_bf.full(), osq[:, 0:NQ])
                    rsqrt(rr3[:, 0:NQ], psb[pbn][:, 0:NQ], 1.0 / 128, rr4[:, 0:NQ])
                    stt(oaT[:, h, q0c:q0c + NQ], o1[:, 0:NQ], small[:, 1:2], rr3[:, 0:NQ], ALU.mult, ALU.mult)
                defer(finish_head, delay=3)
            flush_deferred()
            top[0] = m
        top[0] = m_attn
        if dbg and kind == "prompt" and s == 0 and t == 0:
            P.dma("pool", dbg_oa.full(), oaT.full())
        if stage < 3:
            return
        P.phase = kind + ":gdnproj"
        obT = alloc([8, NT], BF16) if kind != "meta" else None
        m_gdn = top[0]
        qg = alloc([8, NT], BF16); kg = alloc([8, NT], BF16); vg = alloc([8, NT], BF16)
        sz = alloc([8, NT], BF16) if kind != "meta" else None
        og = alloc([8, NT], BF16) if kind != "meta" else None
        cb = alloc([8, NT])
        slotBA = wk8(wload(B_BA))
        pb = pbank()
        for kc in range(8):
            mm(psb[pb][0:8, 0:NT], slotBA[:, kc, 0:8], xnT[:, kc, :], start=(kc == 0), stop=(kc == 7))
        pb2 = pbank()
        for kc in range(8):
            mm(psb[pb2][0:8, 0:NT], slotBA[:, kc, 8:16], xnT[:, kc, :], start=(kc == 0), stop=(kc == 7))
        act(cb[0:8, 4, :], psb[pb][0:8, 0:NT], AF.Sigmoid)
        act(cb[0:8, 6, :], psb[pb][0:8, 0:NT], AF.Exp, scale=-1.0)
        act(cb[0:8, 6, :], cb[0:8, 6, :], AF.Ln, bias=1.0)
        act(cb[0:8, 7, :], psb[pb2][0:8, 0:NT], AF.Exp, bias=small[0:8, 3:4])
        act(cb[0:8, 7, :], cb[0:8, 7, :], AF.Ln, bias=1.0)
        ts(cb[0:8, 0, :], cb[0:8, 7, :], small[0:8, 4:5], ALU.mult)
        a_, b_ = 0, 7
        sh = 1
        while sh < C:
            av = cb[0:8, a_, :]; bv = cb[0:8, b_, :]
            a3 = av.with_ap(av.ap.rearrange("p (c l) -> p c l", l=C)); b3 = bv.with_ap(bv.ap.rearrange("p (c l) -> p c l", l=C))
            cp(V(b3.ap[:, :, 0:sh], bv.key, bv.lo, bv.hi, bv.page), V(a3.ap[:, :, 0:sh], av.key, av.lo, av.hi, av.page))
            tt(V(b3.ap[:, :, sh:C], bv.key, bv.lo, bv.hi, bv.page), V(a3.ap[:, :, sh:C], av.key, av.lo, av.hi, av.page),
               V(a3.ap[:, :, 0:C - sh], av.key, av.lo, av.hi, av.page), ALU.add)
            a_, b_ = b_, a_
            sh *= 2
        if a_ != 0:
            cp(cb[0:8, 0, :], cb[0:8, a_, :])
        tt(cb[0:8, 1, :], cb[0:8, 0, :], cb[0:8, 6, :], ALU.subtract)
        act(cb[0:8, 2, :], cb[0:8, 0, :], AF.Exp)
        gv = cb[0:8, 0, :]
        g3 = gv.with_ap(gv.ap.rearrange("p (c l) -> p c l", l=C))
        kdv = cb[0:8, 3, :]
        kd3 = kdv.with_ap(kdv.ap.rearrange("p (c l) -> p c l", l=C))
        tt(kd3, V(g3.ap[:, :, C - 1:C].to_broadcast([8, nch, C]), gv.key, gv.lo, gv.hi, gv.page), g3, ALU.subtract)
        act(cb[0:8, 3, :], cb[0:8, 3, :], AF.Exp)
        tt(cb[0:8, 5, :], cb[0:8, 4, :], cb[0:8, 2, :], ALU.mult)
        cbb = alloc([8, NT], BF16)
        for q_, row in enumerate((0, 1, 2)):
            cp(cbb[0:8, 2 * q_, :], cb[0:8, row, :])
            tt(cb[0:8, 6, :], cb[0:8, row, :], cbb[0:8, 2 * q_, :], ALU.subtract)
            cp(cbb[0:8, 2 * q_ + 1, :], cb[0:8, 6, :])
        ts(cbb[0:8, 6, :], cbb[0:8, 0, :], -1.0, ALU.mult)
        ts(cbb[0:8, 7, :], cbb[0:8, 1, :], -1.0, ALU.mult)

        m_conv = top[0]
        cin = [alloc([nseg, L + 3]) for _ in range(3)]
        cacc = alloc([nseg, L])
        csq = alloc([NT], BF16)
        crn = alloc([NT])
        if kind == "sample":
            scrow = alloc([3072])
            for sg_ in range(4):
                P.dma("sp", scrow[0:3, :], sc[sg_])
                for g6 in range(6):
                    pb = pbank()
                    for c4 in range(4):
                        cid_ = g6 * 4 + c4
                        tr(psb[pb][:, c4 * 3:(c4 + 1) * 3], scrow[0:3, cid_ * 128:(cid_ + 1) * 128], identf[0:3, 0:3])
                    pv_ = psb[pb][:, 0:12]
                    cp(ctx_cur[:, g6 * 4:(g6 + 1) * 4, sg_, :], pv_.with_ap(pv_.ap.rearrange("p (c w) -> p c w", c=4)))
        caccs = [cacc] + [alloc([nseg, L]) for _ in range(5)]
        ctmp = alloc([nseg, L])
        csqs = [csq] + [alloc([NT], BF16) for _ in range(3)]
        crns = [crn, alloc([NT])]

        def l2norm_finish(h, j):
            ca = caccs[(h % 2) * 3 + j]
            cflat = ca.full().with_ap(ca.ap.rearrange("p s l -> p (s l)"))
            crn_ = crns[j]
            pbn = pbank()
            mm(psb[pbn][:, 0:NT], ones_bf.full(), csqs[(h % 2) * 2 + j].full())
            act(crn_.full(), psb[pbn][:, 0:NT], AF.Ln, bias=EPS, scale=1.0)
            act(crn_.full(), crn_.full(), AF.Exp, scale=-0.5)
            if j == 0:
                stt(qg[:, h, :], cflat, B_SCALE, crn_.full(), ALU.mult, ALU.mult)
            else:
                tt(kg[:, h, :], cflat, crn_.full(), ALU.mult)

        def silu_qk(h, j):
            ca = caccs[(h % 2) * 3 + j]
            cflat = ca.full().with_ap(ca.ap.rearrange("p s l -> p (s l)"))
            act(cflat, cflat, AF.Silu)
            act(csqs[(h % 2) * 2 + j].full(), cflat, AF.Square)

        def silu_v(h):
            ca = caccs[(h % 2) * 3 + 2]
            act(vg[:, h, :], ca.full().with_ap(ca.ap.rearrange("p s l -> p (s l)")), AF.Silu)

        for h in range(8):
            slot = wk8(wload(B_H + h))
            for j in range(4):
                pb = pbank()
                for kc in range(8):
                    mm(psb[pb][:, 0:NT], slot[:, kc, j * 128:(j + 1) * 128], xnT[:, kc, :], start=(kc == 0), stop=(kc == 7))
                group_issued()
                ps = psb[pb][:, 0:NT]
                if j == 3:
                    if kind != "meta":
                        act(sz[:, h, :], ps, AF.Silu)
                    continue
                cid = j * 8 + h
                ci = cin[j]
                if kind == "meta":
                    memset(ci[:, :, 0:3], 0.0, eng="pool")
                elif kind == "prompt":
                    cp(ci[:, 0, 0:3], (ctx_meta[:, cid, :] if t == 0 else ctx_cur[:, cid, 0, :]), eng="pool")
                else:
                    cp(ci[:, :, 0:3], ctx_cur[:, cid, 0:4, :], eng="pool")
                cp(ci[:, :, 3:3 + L], ps.with_ap(ps.ap.rearrange("p (s l) -> p s l", s=nseg)), eng="act")
                if kind == "meta":
                    cp(ctx_meta[:, cid, :], ci[:, 0, L:L + 3], eng="pool")
                else:
                    cp(ctx_cur[:, cid, 0:nseg, :], ci[:, :, L:L + 3], eng="pool")
            for j in range(3):
                cid = j * 8 + h
                ci = cin[j]
                ce = "pool" if j == 2 else "dve"
                ca = caccs[(h % 2) * 3 + j]
                ts(ca.full(), ci[:, :, 0:L], cwT[:, cid:cid + 1], ALU.mult, eng=ce)
                for w in range(1, 4):
                    if ce == "dve":
                        stt(ca.full(), ci[:, :, w:w + L], cwT[:, w * 24 + cid:w * 24 + cid + 1], ca.full(), ALU.mult, ALU.add)
                    else:
                        ts(ctmp.full(), ci[:, :, w:w + L], cwT[:, w * 24 + cid:w * 24 + cid + 1], ALU.mult, eng="pool")
                        tt(ca.full(), ca.full(), ctmp.full(), ALU.add, eng="pool")
            defer(lambda h=h: (silu_qk(h, 0), silu_qk(h, 1)), delay=1)
            defer(lambda h=h: silu_v(h), delay=3)
            defer(lambda h=h: (l2norm_finish(h, 0), l2norm_finish(h, 1)), delay=3)
        flush_deferred()
        if (kind == "prompt" and t == 3) or kind == "sample":
            tls = [alloc([512]) for _ in range(2)]
            for sg_ in range(nseg):
                for g6 in range(6):
                    tl = tls[g6 % 2]
                    pb = pbank()
                    for c4 in range(4):
                        tr(psb[pb][0:3, c4 * 128:(c4 + 1) * 128], ctx_cur[:, g6 * 4 + c4, sg_, :], identf.full())
                    cp(tl[0:3, :], psb[pb][0:3, :], eng="act")
                    dst_ = cpo[s, :, g6 * 512:(g6 + 1) * 512] if kind == "prompt" else cso[sg_, :, g6 * 512:(g6 + 1) * 512]
                    P.dma("pool", dst_, tl[0:3, :])

        P.phase = kind + ":gdnchunk"
        top[0] = m_conv
        nlev = {64: 5, 16: 3}[C]
        if C == 64:
            nU_i, nU_s, nL_s, idr, bmk = negU_incl[0:C], negU_strict[0:C], negL_strict[0:C], identrep[0:C], blockmask[0:8]
        else:
            cm = []
            for src_, np_ in ((negU_incl, C), (negU_strict, C), (negL_strict, C), (identrep, C), (blockmask, 8)):
                d_ = alloc([8, C], BF16)
                cp(d_[0:np_], src_[0:np_, :, 0:C])
                cm.append(d_[0:np_])
            nU_i, nU_s, nL_s, idr, bmk = cm
        gdb = [alloc([8, C], BF16) for _ in range(8)]
        tokc = alloc([32])
        DTi = alloc([8, C], BF16); NDT = alloc([8, C], BF16); NTD = alloc([8, C], BF16)
        Pm = [alloc([8, C], BF16) for _ in range(2)]; PmT = [alloc([8, C], BF16) for _ in range(2)]
        Rm = [alloc([8, C], BF16) for _ in range(2)]
        MT = alloc([8, C], BF16); qgc = alloc([8, C], BF16); nwT = alloc([8, C], BF16)
        bv_ = alloc([8, 128], BF16); kbg = alloc([8, 128], BF16); kdc = alloc([8, 128], BF16); vnw = alloc([8, 128], BF16)
        egl = alloc([8])

        def fl(v):
            return v.with_ap(v.ap.rearrange("p a b -> p (a b)"))

        for ci_ in range(nch):
            sgi = ci_ if kind == "sample" else 0
            cs = slice(ci_ * C, (ci_ + 1) * C)
            W8 = 8 * C
            if kind == "meta":
                if ci_ == 0:
                    memset(S_cur.full(), 0.0); memset(S_bf.full(), 0.0)
            elif kind == "prompt":
                if ci_ == 0 and t == 0:
                    cp(S_cur.full(), S_meta.full()); cp(S_bf.full(), S_meta.full(), eng="act")
            else:
                P.dma("sp", S_cur.full(), sg[sgi].with_ap(sg.ap[sgi].rearrange("h d e -> d h e")))
                cp(S_bf.full(), S_cur.full(), eng="act")
            for k_ in range(8):
                src = cbb[0:8, k_, cs]
                tt(gdb[k_][0:8], bmk, V(src.ap.unsqueeze(1).to_broadcast([8, 8, C]), src.key, src.lo, src.hi, src.page), ALU.mult, eng="pool")
            pbt = pbank()
            for k_, row in enumerate((4, 5, 3)):
                tr(psb[pbt][0:C, k_ * 8:(k_ + 1) * 8], cb[0:8, row, cs], identf[0:8, 0:8])
            cp(tokc[0:C, 0:24], psb[pbt][0:C, 0:24])
            on8 = ones_bf[0:8, 0:C]

            def xmat(diag_hi, diag_lo, col_hi, col_lo, mask):
                pbx = pbank()
                X = psb[pbx][0:C, 0:W8]
                mm(X, on8, fl(gdb[diag_hi][0:8]), start=True, stop=False)
                mm(X, on8, fl(gdb[diag_lo][0:8]), start=False, stop=False)
                mm(X, cbb[0:8, col_hi, cs], fl(bmk), start=False, stop=False)
                mm(X, cbb[0:8, col_lo, cs], fl(bmk), start=False, stop=False)
                mm(X, identb[0:C, 0:C], fl(mask), start=False, stop=True)
                return X
            act(fl(DTi[0:C]), xmat(0, 1, 6, 7, nU_i), AF.Exp)
            act(fl(NDT[0:C]), xmat(2, 3, 6, 7, nU_s), AF.Exp)
            act(fl(NTD[0:C]), xmat(6, 7, 2, 3, nL_s), AF.Exp)
            pbe = pbank()
            mm(psb[pbe][:, 0:W8], ones_bf[0:8, :], fl(gdb[4][0:8]), start=True, stop=False)
            mm(psb[pbe][:, 0:W8], ones_bf[0:8, :], fl(gdb[5][0:8]), start=False, stop=True)
            pe_v = psb[pbe][:, 0:W8]
            pe3 = pe_v.with_ap(pe_v.ap.rearrange("p (h c) -> p h c", h=8))
            tt(qgc[:, :, 0:C], qg[:, :, cs], pe3, ALU.mult)
            cp(egl.full(), V(pe3.ap[:, :, C - 1], pe_v.key, pe_v.lo, pe_v.hi, pe_v.page))
            pbk = pbank(); pbq = pbank()
            for h in range(8):
                mm(psb[pbk][0:C, h * C:(h + 1) * C], kg[:, h, cs], kg[:, h, cs])
            for h in range(8):
                mm(psb[pbq][0:C, h * C:(h + 1) * C], kg[:, h, cs], qg[:, h, cs])
            stt(fl(Pm[0][0:C]), psb[pbk][0:C, 0:W8], -1.0, fl(NDT[0:C]), ALU.mult, ALU.mult)
            stt(fl(PmT[0][0:C]), psb[pbk][0:C, 0:W8], -1.0, fl(NTD[0:C]), ALU.mult, ALU.mult)
            tt(fl(MT[0:C]), psb[pbq][0:C, 0:W8], fl(DTi[0:C]), ALU.mult)
            tt(fl(Rm[0][0:C]), fl(Pm[0][0:C]), fl(idr), ALU.add, eng="pool")
            cur = 0
            for lv in range(1, nlev + 1):
                nxt = 1 - cur
                pbp = pbank(); pbpt = pbank()
                for h in range(8):
                    mm(psb[pbpt][0:C, h * C:(h + 1) * C], Pm[cur][0:C, h, :], PmT[cur][0:C, h, :])
                if lv < nlev:
                    for h in range(8):
                        mm(psb[pbp][0:C, h * C:(h + 1) * C], PmT[cur][0:C, h, :], Pm[cur][0:C, h, :])
                cp(fl(PmT[nxt][0:C]), psb[pbpt][0:C, 0:W8], eng="act")
                if lv < nlev:
                    cp(fl(Pm[nxt][0:C]), psb[pbp][0:C, 0:W8])
                pbr = pbank()
                for h in range(8):
                    mm(psb[pbr][0:C, h * C:(h + 1) * C], PmT[nxt][0:C, h, :], Rm[cur][0:C, h, :])
                tt(fl(Rm[nxt][0:C]), psb[pbr][0:C, 0:W8], fl(Rm[cur][0:C]), ALU.add)
                cur = nxt
            TT = Rm[cur]
            pbk = pbank(); pbv = pbank()
            for h in range(8):
                tr(psb16[pbk][0:C, h * 128:(h + 1) * 128], kg[:, h, cs], identb.full())
            for h in range(8):
                tr(psb16[pbv][0:C, h * 128:(h + 1) * 128], vg[:, h, cs], identb.full())
            kt3 = psb16[pbk][0:C, :]; kt3 = kt3.with_ap(kt3.ap.rearrange("p (h d) -> p h d", h=8))
            vt3 = psb16[pbv][0:C, :]; vt3 = vt3.with_ap(vt3.ap.rearrange("p (h d) -> p h d", h=8))
            tt(bv_[0:C], vt3, bc3(tokc[0:C, 0:8], [C, 8, 128]), ALU.mult)
            tt(kbg[0:C], kt3, bc3(tokc[0:C, 8:16], [C, 8, 128]), ALU.mult)
            tt(kdc[0:C], kt3, bc3(tokc[0:C, 16:24], [C, 8, 128]), ALU.mult)
            pbw = pbank()
            for h in range(8):
                mm(psb[pbw][:, h * C:(h + 1) * C], kbg[0:C, h, :], TT[0:C, h, :])
            ts(fl(nwT[:, :, 0:C]), psb[pbw][:, 0:W8], -1.0, ALU.mult)
            pv0 = pbank(); pv1 = pbank()
            for h in range(8):
                o = psb[pv0 if h < 4 else pv1][0:C, (h % 4) * 128:(h % 4 + 1) * 128]
                mm(o, TT[0:C, h, :], bv_[0:C, h, :], start=True, stop=False)
                mm(o, nwT[:, h, 0:C], S_bf[:, h, :], start=False, stop=True)
            cp(fl(vnw[0:C, 0:4, :]), psb[pv0][0:C, :], eng="act")
            cp(fl(vnw[0:C, 4:8, :]), psb[pv1][0:C, :])
            if kind != "meta":
                pbo = pbank()
                for h in range(8):
                    o = psb[pbo][:, h * C:(h + 1) * C]
                    mm(o, S_bf[:, h, :], qgc[:, h, 0:C], start=True, stop=False)
                    mm(o, vnw[0:C, h, :], MT[0:C, h, :], start=False, stop=True)
                po = psb[pbo][:, 0:W8]
                cp(og[:, :, cs], po.with_ap(po.ap.rearrange("p (h c) -> p h c", h=8)), eng="act")
            ps0 = pbank(); ps1 = pbank()
            for h in range(8):
                mm(psb[ps0 if h < 4 else ps1][:, (h % 4) * 128:(h % 4 + 1) * 128], kdc[0:C, h, :], vnw[0:C, h, :])
            tt(S_cur.full(), S_cur.full(), bc3(egl.full(), [128, 8, 128]), ALU.mult)
            tt(fl(S_cur[:, 0:4, :]), fl(S_cur[:, 0:4, :]), psb[ps0].full(), ALU.add)
            tt(fl(S_cur[:, 4:8, :]), fl(S_cur[:, 4:8, :]), psb[ps1].full(), ALU.add)
            cp(S_bf.full(), S_cur.full(), eng="act")
            if kind == "sample":
                P.dma("pool", gso[sgi].with_ap(gso.ap[sgi].rearrange("h d e -> d h e")), S_cur.full())
        if kind == "meta":
            cp(S_meta.full(), S_cur.full())
            return
        if kind == "prompt" and t == 3:
            P.dma("pool", gp[s].with_ap(gp.ap[s].rearrange("h d e -> d h e")), S_cur.full())
        P.phase = kind + ":gdnnorm"
        top[0] = m_conv
        gsq = alloc([NT], BF16); grn = alloc([NT]); gt = alloc([NT])
        for h in range(8):
            act(gsq.full(), og[:, h, :], AF.Square)
            pbn = pbank()
            mm(psb[pbn][:, 0:NT], ones_bf.full(), gsq.full())
            rsqrt(grn.full(), psb[pbn][:, 0:NT], 1.0 / 128, gt.full())
            stt(gt.full(), og[:, h, :], small[:, 0:1], grn.full(), ALU.mult, ALU.mult)
            tt(obT[:, h, :], gt.full(), sz[:, h, :], ALU.mult, eng="pool")
        if dbg and kind == "prompt" and s == 0 and t == 0:
            P.dma("pool", dbg_ob.full(), obT.full())
        top[0] = m_gdn
        if stage < 4:
            return
        P.phase = kind + ":merge"
        mixT = alloc([8, NT], BF16)
        sga = alloc([NT]); sgb = alloc([NT]); tmp = alloc([NT])
        for oc in range(8):
            slot = wk8(wload(B_M + oc))
            pa = pbank(); pb_ = pbank(); pya = pbank(); pyb = pbank()
            for kc in range(8):
                mm(psb[pa][:, 0:NT], slot[:, kc, 0:128], xnT[:, kc, :], start=(kc == 0), stop=(kc == 7))
            for kc in range(8):
                mm(psb[pb_][:, 0:NT], slot[:, kc, 128:256], xnT[:, kc, :], start=(kc == 0), stop=(kc == 7))
            for kc in range(8):
                mm(psb[pya][:, 0:NT], slot[:, kc, 256:384], oaT[:, kc, :], start=(kc == 0), stop=(kc == 7))
            for kc in range(8):
                mm(psb[pyb][:, 0:NT], slot[:, kc, 384:512], obT[:, kc, :], start=(kc == 0), stop=(kc == 7))
            act(sga.full(), psb[pa][:, 0:NT], AF.Sigmoid, bias=bgT[:, oc:oc + 1])
            act(sgb.full(), psb[pb_][:, 0:NT], AF.Sigmoid, bias=bgT[:, 8 + oc:9 + oc])
            tt(tmp.full(), psb[pya][:, 0:NT], sga.full(), ALU.mult)
            tt(sgb.full(), psb[pyb][:, 0:NT], sgb.full(), ALU.mult)
            tt(mixT[:, oc, :], tmp.full(), sgb.full(), ALU.add, eng="pool")
        for half in range(2):
            slot = wk8(wload(B_WO + half))
            for st in range(nst):
                pb = pbank()
                for kc in range(8):
                    mm(psb[pb][0:ST, :], mixT[:, kc, st * ST:(st + 1) * ST], slot[:, kc, :], start=(kc == 0), stop=(kc == 7))
                tt(xtok[0:ST, st, half * 512:(half + 1) * 512], xtok[0:ST, st, half * 512:(half + 1) * 512], psb[pb][0:ST, :], ALU.add)
        if stage < 5:
            return
        P.phase = kind + ":ffn"
        norm_T(norm2)
        uT = alloc([32, NT], BF16)
        rl = [alloc([NT]) for _ in range(2)]
        for j in range(8):
            slot = wk8(wload(B_WU + j))
            for c4 in range(4):
                fc = j * 4 + c4
                pb = pbank()
                for kc in range(8):
                    mm(psb[pb][:, 0:NT], slot[:, kc, c4 * 128:(c4 + 1) * 128], xnT[:, kc, :], start=(kc == 0), stop=(kc == 7))
                r = rl[fc % 2]
                act(r.full(), psb[pb][:, 0:NT], AF.Relu)
                tt(uT[:, fc, :], r.full(), r.full(), ALU.mult, eng=("pool" if fc % 2 else "dve"))
        for oc in range(8):
            slot = wf32(wload(B_WD + oc))
            pb = pbank()
            for st in range(nst):
                for fc in range(32):
                    mm(psb[pb][0:ST, st * 128:(st + 1) * 128], uT[:, fc, st * ST:(st + 1) * ST], slot[:, fc, :], start=(fc == 0), stop=(fc == 31))
            pv = psb[pb][0:ST, 0:nst * 128]
            tt(xtok[0:ST, :, oc * 128:(oc + 1) * 128], xtok[0:ST, :, oc * 128:(oc + 1) * 128],
               pv.with_ap(pv.ap.rearrange("p (s c) -> p s c", s=nst)), ALU.add)
        for st in range(nst):
            if kind == "prompt":
                P.dma("pool", yp[s, t * 512 + st * 128: t * 512 + (st + 1) * 128, :], xtok[0:ST, st, :])
            else:
                P.dma("pool", ys[st * 128:(st + 1) * 128, :], xtok[0:ST, st, :])

    def cache_k_prep():
        P.phase = "cachek"
        top[0] = base_top
        ckf = [alloc([D]) for _ in range(2)]
        ckb = [alloc([D], BF16) for _ in range(2)]
        ckt = [alloc([8, 128], BF16) for _ in range(2)]
        i = 0
        for sq_ in range(NSS):
            P.dma("pool", Vcd[sq_], cv[sq_])
        for sq_ in range(NSS):
            for c in range(8):
                f = ckf[i % 2]; b = ckb[i % 2]; kt_ = ckt[i % 2]
                P.dma("sp", f.full(), ck[sq_, c * 128:(c + 1) * 128, :])
                cp(b.full(), f.full(), eng=("pool" if i % 2 else "dve"))
                pb = pbank()
                for h in range(8):
                    tr(psb16[pb][:, h * 128:(h + 1) * 128], b[:, h * 128:(h + 1) * 128], identb.full())
                src = psb16[pb].full()
                cp(kt_.full(), src.with_ap(src.ap.rearrange("p (h t) -> p h t", h=8)), eng="act")
                P.dma("sp", KTs[sq_, :, :, c * 128:(c + 1) * 128].with_ap(KTs.ap[sq_, :, :, c * 128:(c + 1) * 128].rearrange("h p t -> p h t")), kt_.full())
                i += 1

    if dbg:
        dbg_oa = dram("dbg_oa2", [128, 8, 512], "ExternalOutput", BF16)
        dbg_ob = dram("dbg_ob2", [128, 8, 512], "ExternalOutput", BF16)

    import os
    sel = os.environ.get("KTILES", "msp")
    if "s" in sel:
        cache_k_prep()
    run_tile("meta")
    if "s" in sel:
        run_tile("sample")
    if "p" in sel:
        for s in range(NPS):
            for t in range(4):
                run_tile("prompt", s, t)
    elif "q" in sel:
        run_tile("prompt", 0, 0)
    P.emit()
    es.close()
    return nc, P


_CACHE = {}


def kernel(x_prompt, x_sample, cache_attn_k, cache_attn_v, state_gdn, state_conv, meta_tokens, rel_bias,
           norm1, w_in, b_gate, q_norm, k_norm, lambda_q1, lambda_k1, lambda_q2, lambda_k2, sub_norm,
           conv_w, A_log, dt_bias, gdn_norm, w_br_a, w_br_b, w_out, norm2, w_up, w_down, _stage=99, _cores=8, _dbg=False):
    f = lambda a: np.ascontiguousarray(np.asarray(a, dtype=np.float32))
    key = (_stage, _dbg)
    if key not in _CACHE:
        _CACHE[key] = build_program(_stage, _dbg)
    nc, P = _CACHE[key]
    consts = _consts()
    shared = {
        "meta": f(meta_tokens), "relb": f(rel_bias), "norm1": f(norm1).reshape(-1), "w_in": f(w_in)[0],
        "b_gate": f(b_gate).reshape(-1), "q_norm": f(q_norm).reshape(-1), "k_norm": f(k_norm).reshape(-1),
        "lq1": f(lambda_q1).reshape(-1), "lk1": f(lambda_k1).reshape(-1), "lq2": f(lambda_q2).reshape(-1), "lk2": f(lambda_k2).reshape(-1),
        "sub_norm": f(sub_norm).reshape(-1), "conv_w": f(conv_w).reshape(-1), "a_log": f(A_log).reshape(-1),
        "dt_bias": f(dt_bias).reshape(-1), "gdn_norm": f(gdn_norm).reshape(-1), "w_bra": f(w_br_a)[0], "w_brb": f(w_br_b)[0],
        "w_out": f(w_out)[0], "norm2": f(norm2).reshape(-1), "w_up": f(w_up)[0], "w_down": f(w_down)[0],
    }
    for k, v in consts.items():
        shared["c_" + k] = v
    xp = f(x_prompt); xs = f(x_sample)
    ck = f(cache_attn_k)[0].reshape(32, PAST, D); cv = f(cache_attn_v)[0].reshape(32, PAST, D)
    sg = f(state_gdn)[0]; sc = f(state_conv)[0]
    in_maps = []
    for c in range(_cores):
        m = dict(shared)
        m["xp"] = xp[c * NPS:(c + 1) * NPS]
        m["xs"] = xs[c * NSS:(c + 1) * NSS].reshape(NSS * DSEQ, D)
        m["ck"] = ck[c * NSS:(c + 1) * NSS]
        m["cv"] = cv[c * NSS:(c + 1) * NSS]
        m["sg"] = sg[c * NSS:(c + 1) * NSS]
        m["sc"] = sc[c * NSS:(c + 1) * NSS]
        in_maps.append(m)
    res = run_bass_kernel_spmd(nc, in_maps, core_ids=list(range(_cores)))
    R = res.results
    cat = lambda k: np.concatenate([np.asarray(r[k], dtype=np.float32) for r in R], axis=0)
    nb = _cores * NPS
    ns = _cores * NSS
    outs = (
        cat("yp"),
        cat("ys").reshape(ns, DSEQ, D),
        cat("kp").reshape(1, nb, TP, 8, 128),
        cat("vp").reshape(1, nb, TP, 8, 128),
        cat("gp").reshape(1, nb, 8, 128, 128),
        cat("cpo").reshape(1, nb, 3, 3072),
        cat("kso").reshape(1, ns, DSEQ, 8, 128),
        cat("vso").reshape(1, ns, DSEQ, 8, 128),
        cat("gso").reshape(1, ns, 8, 128, 128),
        cat("cso").reshape(1, ns, 3, 3072),
    )
    if _dbg:
        return outs, R
    return outs
```

```python
import contextlib
import math
import os
from collections import defaultdict

import numpy as np
import concourse.bass as bass
import concourse.mybir as mybir
from concourse.bass_utils import run_bass_kernel_spmd

F32 = mybir.dt.float32
BF16 = mybir.dt.bfloat16
I32 = mybir.dt.int32
ALU = mybir.AluOpType
AF = mybir.ActivationFunctionType
AX = mybir.AxisListType

SEM_LIMIT = 1000
DMA_SEMS = 24
NEG = -30000.0


class V:
    __slots__ = ("ap", "key", "lo", "hi", "page", "track")

    def __init__(self, ap, key, lo, hi, page, track=True):
        self.ap = ap
        self.key = key
        self.lo = lo
        self.hi = hi
        self.page = page
        self.track = track

    def with_ap(self, ap):
        return V(ap, self.key, self.lo, self.hi, self.page, self.track)


class T:
    def __init__(self, ap, name, shape, dram=False, esize=4, base_off=0, page=2048, track=True, whole=False):
        self.whole = whole
        self.ap = ap
        self.name = name
        self.shape = list(shape)
        self.dram = dram
        self.esize = esize
        self.base_off = base_off
        self.page = page
        self.track = track
        fs = self.shape if dram else self.shape[1:]
        st = []
        acc = 1
        for s in reversed(fs):
            st.append(acc)
            acc *= s
        self.fstrides = list(reversed(st))

    def __getitem__(self, key):
        if not isinstance(key, tuple):
            key = (key,)
        ap = self.ap[key]
        fs = self.shape if self.dram else self.shape[1:]
        k2 = list(key) if self.dram else list(key[1:])
        while len(k2) < len(fs):
            k2.append(slice(None))
        lo = 0
        hi = 0
        for k, s, st in zip(k2, fs, self.fstrides):
            if isinstance(k, slice):
                a = 0 if k.start is None else k.start
                b = s if k.stop is None else k.stop
            else:
                a = k
                b = k + 1
            lo += a * st
            hi += (b - 1) * st
        hi += 1
        if self.whole:
            return V(ap, self.name, 0, self.page, self.page, self.track)
        return V(ap, self.name, self.base_off + lo * self.esize, self.base_off + hi * self.esize, self.page, self.track)

    def full(self):
        return self[tuple(slice(None) for _ in self.shape)]


class Op:
    __slots__ = ("eng", "fn", "deps", "id", "is_dma", "has_dependents", "sig", "dma_sem", "dma_val", "phase")

    def __init__(self, eng, fn, is_dma):
        self.eng = eng
        self.fn = fn
        self.deps = set()
        self.is_dma = is_dma
        self.has_dependents = False
        self.sig = None
        self.dma_sem = None
        self.dma_val = None


ENGS = ("pe", "act", "dve", "pool", "sp")


class Prog:
    def __init__(self, nc):
        self.nc = nc
        self.ops = []
        self.hist = defaultdict(list)

    def op(self, eng, fn, reads=(), writes=(), dma=False):
        o = Op(eng, fn, dma)
        o.phase = getattr(self, "phase", "")
        o.id = len(self.ops)
        self.ops.append(o)
        deps = o.deps
        tag = eng + ("_dma" if dma else "")
        hist = self.hist
        for v in reads:
            if not v.track:
                continue
            lo, hi = v.lo, v.hi
            for pg in range(lo // v.page, (hi - 1) // v.page + 1):
                for rec in hist[(v.key, pg)]:
                    if rec[2] == "W" and rec[0] < hi and lo < rec[1]:
                        if rec[4] == "pe" and tag == "pe":
                            continue
                        deps.add(rec[3])
                    elif rec[2] == "R" and v.key.startswith("ps") and rec[4] != tag:
                        deps.add(rec[3])
        for v in writes:
            if not v.track:
                continue
            lo, hi = v.lo, v.hi
            for pg in range(lo // v.page, (hi - 1) // v.page + 1):
                h = hist[(v.key, pg)]
                keep = []
                for rec in h:
                    if rec[0] < hi and lo < rec[1]:
                        if not (rec[4] == "pe" and tag == "pe"):
                            deps.add(rec[3])
                        if lo <= rec[0] and rec[1] <= hi:
                            continue
                    keep.append(rec)
                keep.append([lo, hi, "W", o.id, tag])
                hist[(v.key, pg)] = keep
        for v in reads:
            if not v.track:
                continue
            lo, hi = v.lo, v.hi
            for pg in range(lo // v.page, (hi - 1) // v.page + 1):
                h = hist[(v.key, pg)]
                found = False
                if not dma:
                    for r in h:
                        if r[2] == "R" and r[0] == lo and r[1] == hi and r[4] == tag:
                            r[3] = o.id
                            found = True
                            break
                if not found:
                    h.append([lo, hi, "R", o.id, tag])
        deps.discard(o.id)
        return o

    def dma(self, eng, out, in_, **kw):
        def fn(e):
            return e.dma_start(out=out.ap, in_=in_.ap, **kw)
        return self.op(eng, fn, reads=[in_], writes=[out], dma=True)

    def emit(self):
        nc = self.nc
        ops = self.ops
        for o in ops:
            for d in o.deps:
                ops[d].has_dependents = True
        cnt = {e: 0 for e in ENGS}
        dma_cnt = {e: 0 for e in ENGS}
        for o in ops:
            if o.is_dma:
                i = dma_cnt[o.eng]
                dma_cnt[o.eng] += 1
                o.dma_sem = (o.eng, i % DMA_SEMS)
                o.dma_val = 16 * (i // DMA_SEMS + 1)
            elif o.has_dependents:
                cnt[o.eng] += 1
                o.sig = cnt[o.eng]
        n_epochs = {e: (cnt[e] + SEM_LIMIT - 1) // SEM_LIMIT for e in ENGS}
        stack = contextlib.ExitStack()
        sems = {}
        for e in ENGS:
            for ep in range(max(1, n_epochs[e])):
                sems[(e, ep)] = stack.enter_context(nc.semaphore(f"s_{e}_{ep}"))
            if dma_cnt[e]:
                for i in range(DMA_SEMS):
                    sems[("dma", e, i)] = stack.enter_context(nc.semaphore(f"d_{e}_{i}"))
        per_eng = {e: [] for e in ENGS}
        for o in ops:
            per_eng[o.eng].append(o)
        waited = {e: {} for e in ENGS}
        waits = {}
        for o in ops:
            w = {}
            for d in o.deps:
                p = ops[d]
                if p.is_dma:
                    key = ("dma",) + p.dma_sem
                    val = p.dma_val
                else:
                    key = ("c", p.eng)
                    val = p.sig
                if val > w.get(key, 0):
                    w[key] = val
            if o.is_dma:
                key = ("dma",) + o.dma_sem
                if o.dma_val > 16 and o.dma_val - 16 > w.get(key, 0):
                    w[key] = o.dma_val - 16
            wl = []
            wd = waited[o.eng]
            for key, val in w.items():
                if wd.get(key, 0) >= val:
                    continue
                wd[key] = val
                wl.append((key, val))
            waits[o.id] = wl
        final = {}
        for e in ENGS:
            if dma_cnt[e]:
                fl = []
                for i in range(DMA_SEMS):
                    n = len(range(i, dma_cnt[e], DMA_SEMS))
                    if n:
                        fl.append((("dma", e, i), 16 * n))
                final[e] = fl

        def sem_of(key, val):
            if key[0] == "dma":
                return sems[("dma", key[1], key[2])], val
            e = key[1]
            ep = (val - 1) // SEM_LIMIT
            return sems[(e, ep)], val - ep * SEM_LIMIT

        def run(engname, engobj):
            for o in per_eng[engname]:
                for key, val in waits[o.id]:
                    s, v = sem_of(key, val)
                    engobj.wait_ge(s, v)
                ins = o.fn(engobj)
                if o.is_dma:
                    ins.then_inc(sems[("dma",) + o.dma_sem], 16)
                elif o.sig is not None:
                    s, v = sem_of(("c", engname), o.sig)
                    ins.then_inc(s, 1)
            for key, val in final.get(engname, []):
                s, v = sem_of(key, val)
                engobj.wait_ge(s, v)

        with nc.Block() as block:
            @block.sync
            def _(e):
                run("sp", e)

            @block.scalar
            def _(e):
                run("act", e)

            @block.vector
            def _(e):
                run("dve", e)

            @block.gpsimd
            def _(e):
                run("pool", e)

            @block.tensor
            def _(e):
                run("pe", e)
        stack.close()
        self.stats = {e: len(per_eng[e]) for e in ENGS}


D = 1024
SEQ = 2048
NMETA = 16
TP = NMETA + SEQ
PAST = 1024
DSEQ = 64
NIN = 9232
OFF_KA, OFF_VA, OFF_B, OFF_Z, OFF_BETA, OFF_ALPHA, OFF_GA, OFF_GB = 1024, 2048, 3072, 6144, 7168, 7176, 7184, 8208
DFF = 4096
EPS = 1e-6
LAM_INIT = 0.8 - 0.6 * math.exp(-0.3 * 0)
A_SCALE = 0.125
B_SCALE = 128 ** -0.5
NPS = 2
NSS = 4
GW = 1024
GL = 1152
GOFF = 384

B_Q, B_K, B_V, B_H, B_BA, B_M, B_WO, B_WU, B_WD, NBLK = 0, 2, 4, 6, 14, 15, 23, 25, 33, 41


def _consts():
    c = {}
    c["identf"] = np.eye(128, dtype=np.float32)
    p = np.arange(64)[:, None]
    f = np.arange(64)[None, :]
    def rep(m):
        return np.ascontiguousarray(np.broadcast_to(m[:, None, :], (64, 8, 64)).reshape(64, 512)).astype(np.float32)
    c["negU_incl"] = rep(np.where(f >= p, 0.0, NEG))
    c["negU_strict"] = rep(np.where(f > p, 0.0, NEG))
    c["negL_strict"] = rep(np.where(f < p, 0.0, NEG))
    c["identrep"] = rep(np.eye(64))
    bm = np.zeros((8, 8, 64), np.float32)
    for h in range(8):
        bm[h, h, :] = 1.0
    c["blockmask"] = bm.reshape(8, 512)
    pp = np.arange(128)[:, None]
    cc = np.arange(GW)[None, :] - GOFF
    c["maskG"] = np.where(np.floor_divide(cc, 64) >= np.floor_divide(pp, 64), 0.0, NEG).astype(np.float32)
    lo = [0, 1, 2, 3, 4, 5, 6, 7, 8, 12, 16, 23, 32, 46, 64, 91]
    hi = lo[1:] + [10 ** 9]
    oh = np.zeros((32, GL), np.float32)
    for i in range(GL):
        rel = 511 - i
        n = abs(rel)
        b = 0
        for k in range(16):
            if lo[k] <= n < hi[k]:
                b = k
        if rel > 0:
            b += 16
        oh[b, i] = 1.0
    c["onehot"] = oh
    return c


def build_program(stage=99, dbg=False):
    nc = bass.Bass("TRN2", target_bir_lowering=False)
    P = Prog(nc)
    es = contextlib.ExitStack()

    def dram(name, shape, kind, dt=F32, page=1 << 20, track=False):
        ap = nc.dram_tensor(name, shape, dt, kind=kind).ap()
        return T(ap, name, shape, dram=True, esize=(2 if dt == BF16 else 4), page=page, track=track)

    def din(name, shape):
        return dram(name, shape, "ExternalInput")

    def dout(name, shape):
        return dram(name, shape, "ExternalOutput")

    xp = din("xp", [NPS, SEQ, D])
    xs = din("xs", [NSS * DSEQ, D])
    ck = din("ck", [NSS, PAST, D])
    cv = din("cv", [NSS, PAST, D])
    sg = din("sg", [NSS, 8, 128, 128])
    sc = din("sc", [NSS, 3, 3072])
    meta = din("meta", [NMETA, D])
    relb = din("relb", [32, 8])
    norm1 = din("norm1", [D])
    w_in = din("w_in", [D, NIN])
    b_gate = din("b_gate", [2 * D])
    q_norm = din("q_norm", [64])
    k_norm = din("k_norm", [64])
    lq1 = din("lq1", [64]); lk1 = din("lk1", [64]); lq2 = din("lq2", [64]); lk2 = din("lk2", [64])
    sub_norm = din("sub_norm", [128])
    conv_w = din("conv_w", [4 * 3072])
    a_log = din("a_log", [8])
    dt_bias = din("dt_bias", [8])
    gdn_norm = din("gdn_norm", [128])
    w_bra = din("w_bra", [D, D]); w_brb = din("w_brb", [D, D]); w_out = din("w_out", [D, D])
    norm2 = din("norm2", [D])
    w_up = din("w_up", [D, DFF]); w_down = din("w_down", [DFF, D])
    cst = {k: din("c_" + k, list(v.shape)) for k, v in _consts().items()}
    yp = dout("yp", [NPS, SEQ, D]); ys = dout("ys", [NSS * DSEQ, D])
    kp = dout("kp", [NPS, TP, D]); vp = dout("vp", [NPS, TP, D])
    gp = dout("gp", [NPS, 8, 128, 128]); cpo = dout("cpo", [NPS, 3, 3072])
    kso = dout("kso", [NSS * DSEQ, D]); vso = dout("vso", [NSS * DSEQ, D])
    gso = dout("gso", [NSS, 8, 128, 128]); cso = dout("cso", [NSS, 3, 3072])
    WS = dram("WS", [NBLK, 128, 4096], "Internal", BF16, page=1 << 20, track=True)
    KTd = dram("KTd", [NPS, 8, 128, TP], "Internal", BF16, page=1 << 16, track=True)
    Vd = dram("Vd", [NPS, TP, D], "Internal", BF16, page=1 << 16, track=True)
    KTs = dram("KTs", [NSS, 8, 128, PAST + DSEQ], "Internal", BF16, page=1 << 16, track=True)
    FD = dram("FD", [8, 128, GL], "Internal", F32, page=1 << 16, track=True)
    Vsd = dram("Vsd", [NSS, DSEQ, D], "Internal", BF16, page=1 << 16, track=True)
    Vcd = dram("Vcd", [NSS, PAST, D], "Internal", BF16, page=1 << 16, track=True)

    ARENA = 206 * 1024
    arena_h = es.enter_context(nc.sbuf_tensor("arena", [128, ARENA // 4], F32))
    top = [0]

    def alloc(shape, dt=F32):
        n = int(np.prod(shape))
        esz = 2 if dt == BF16 else 4
        nb = (n * esz + 31) // 32 * 32
        off = top[0]
        top[0] += nb
        assert top[0] <= ARENA, ("arena overflow", top[0])
        ap = arena_h[:, off // 4:(off + nb) // 4]
        if dt != F32:
            ap = ap.bitcast(dt)
        ap = ap[:, 0:n]
        if len(shape) == 2:
            ap = ap.rearrange("p (a b) -> p a b", a=shape[0])
        elif len(shape) == 3:
            ap = ap.rearrange("p (a b c) -> p a b c", a=shape[0], b=shape[1])
        return T(ap, "arena", [128] + list(shape), esize=esz, base_off=off)

    psb = []
    psb16 = []
    for i in range(8):
        h = es.enter_context(nc.psum_tensor(f"ps{i}", [128, 512], F32))
        psb.append(T(h, f"ps{i}", [128, 512], page=4096, whole=True))
        psb16.append(T(h[:, :].bitcast(BF16), f"ps{i}", [128, 1024], esize=2, page=4096, whole=True))
    rot = [0]

    def pbank(pool=8):
        i = rot[0] % pool
        rot[0] += 1
        return i

    def rw(*vs):
        return [v for v in vs if isinstance(v, V)]

    def A_(x):
        return x.ap if isinstance(x, V) else x

    def mm(out, lhsT, rhs, start=True, stop=True):
        P.op("pe", lambda e: e.matmul(out=out.ap, lhsT=lhsT.ap, rhs=rhs.ap, start=start, stop=stop), reads=[lhsT, rhs], writes=[out])

    def tr(out, in_, ident):
        P.op("pe", lambda e: e.transpose(out=out.ap, in_=in_.ap, identity=ident.ap), reads=[in_, ident], writes=[out])

    def act(out, in_, func, bias=None, scale=None, accum=None):
        kw = {}
        if bias is not None:
            kw["bias"] = A_(bias)
        if scale is not None:
            kw["scale"] = A_(scale)
        if accum is not None:
            kw["accum_out"] = accum.ap
        P.op("act", lambda e: e.activation(out=out.ap, in_=in_.ap, func=func, **kw), reads=rw(in_, bias, scale), writes=rw(out, accum))

    def tt(out, a, b, op, eng="dve"):
        P.op(eng, lambda e: e.tensor_tensor(out=out.ap, in0=a.ap, in1=b.ap, op=op), reads=[a, b], writes=[out])

    def ts(out, a, s1, op0, s2=None, op1=None, eng="dve"):
        if op1 is None:
            P.op(eng, lambda e: e.tensor_scalar(out=out.ap, in0=a.ap, scalar1=A_(s1), scalar2=0.0, op0=op0, op1=ALU.add), reads=rw(a, s1), writes=[out])
        else:
            P.op(eng, lambda e: e.tensor_scalar(out=out.ap, in0=a.ap, scalar1=A_(s1), scalar2=A_(s2), op0=op0, op1=op1), reads=rw(a, s1, s2), writes=[out])

    def stt(out, a, s, b, op0, op1, eng="dve"):
        P.op(eng, lambda e: e.scalar_tensor_tensor(out=out.ap, in0=a.ap, scalar=A_(s), in1=b.ap, op0=op0, op1=op1), reads=rw(a, s, b), writes=[out])

    def cp(out, in_, eng="dve"):
        if eng == "act":
            P.op("act", lambda e: e.copy(out=out.ap, in_=in_.ap), reads=[in_], writes=[out])
        else:
            P.op(eng, lambda e: e.tensor_copy(out=out.ap, in_=in_.ap), reads=[in_], writes=[out])

    def red(out, in_, op=ALU.add):
        P.op("dve", lambda e: e.tensor_reduce(out=out.ap, in_=in_.ap, axis=AX.X, op=op), reads=[in_], writes=[out])

    def recip(out, in_):
        act(out, in_, AF.Ln)
        act(out, out, AF.Exp, scale=-1.0)

    def memset(v, val, eng="dve"):
        P.op(eng, lambda e: e.memset(v.ap, val), writes=[v])

    def rsqrt(out, in_, scale, tmp):
        act(tmp, in_, AF.Ln, bias=EPS, scale=scale)
        act(out, tmp, AF.Exp, scale=-0.5)

    def bc3(v, shape):
        return v.with_ap(v.ap.unsqueeze(2).to_broadcast(shape))

    deferred = []

    def defer(fn, delay=1):
        deferred.append([delay, fn])

    def group_issued():
        run_now = []
        keep = []
        for d in deferred:
            d[0] -= 1
            (run_now if d[0] <= 0 else keep).append(d)
        deferred[:] = keep
        for d in run_now:
            d[1]()

    def flush_deferred():
        while deferred:
            group_issued()

    def dap(t, off, pat):
        return t.full().with_ap(bass.AP(t.ap.tensor, off, pat))

    def ws_k8(b):
        return WS.ap[b].rearrange("p (k c) -> p k c", k=8)

    def cast_piece(b, off, w, src, c0):
        dst = WS[b].with_ap(ws_k8(b)[:, :, off:off + w])
        s = src.full().with_ap(src.ap[:, c0:c0 + w].rearrange("(k p) c -> p k c", p=128))
        P.dma("pool", dst, s)

    def cast_block(b):
        if B_Q <= b < B_K:
            cast_piece(b, 0, 512, w_in, (b - B_Q) * 512)
        elif B_K <= b < B_V:
            cast_piece(b, 0, 512, w_in, OFF_KA + (b - B_K) * 512)
        elif B_V <= b < B_H:
            cast_piece(b, 0, 512, w_in, OFF_VA + (b - B_V) * 512)
        elif b == B_BA:
            cast_piece(B_BA, 0, 16, w_in, OFF_BETA)
        elif B_H <= b < B_BA:
            h = b - B_H
            for j in range(3):
                cast_piece(b, j * 128, 128, w_in, OFF_B + j * 1024 + h * 128)
            cast_piece(b, 384, 128, w_in, OFF_Z + h * 128)
        elif B_M <= b < B_WO:
            oc = b - B_M
            cast_piece(b, 0, 128, w_in, OFF_GA + oc * 128)
            cast_piece(b, 128, 128, w_in, OFF_GB + oc * 128)
            cast_piece(b, 256, 128, w_bra, oc * 128)
            cast_piece(b, 384, 128, w_brb, oc * 128)
        elif B_WO <= b < B_WU:
            cast_piece(b, 0, 512, w_out, (b - B_WO) * 512)
        elif B_WU <= b < B_WD:
            cast_piece(b, 0, 512, w_up, (b - B_WU) * 512)
        else:
            oc = b - B_WD
            dst = WS[b].with_ap(WS.ap[b].rearrange("p (f c) -> p f c", f=32))
            s_ = w_down.full().with_ap(w_down.ap[:, oc * 128:(oc + 1) * 128].rearrange("(f p) c -> p f c", p=128))
            P.dma("pool", dst, s_)

    cast_order = ([B_K, B_K + 1, B_V, B_V + 1, B_BA] + [B_H + h for h in range(8)] + [B_Q, B_Q + 1]
                  + [B_M + i for i in range(8)] + [B_WO, B_WO + 1] + [B_WU + i for i in range(8)] + [B_WD + i for i in range(8)])
    cast_done = [0]

    def cast_upto(n):
        while cast_done[0] < min(n, len(cast_order)):
            cast_block(cast_order[cast_done[0]])
            cast_done[0] += 1
    cast_upto(13)

    identf = alloc([128]); P.dma("sp", identf.full(), cst["identf"].full())
    identb = alloc([128], BF16); cp(identb.full(), identf.full())
    ones_bf = alloc([128], BF16); memset(ones_bf.full(), 1.0)
    negU_incl = alloc([8, 64], BF16)
    negU_strict = alloc([8, 64], BF16)
    negL_strict = alloc([8, 64], BF16)
    identrep = alloc([8, 64], BF16)
    blockmask = alloc([8, 64], BF16)
    qn_bc = alloc([8, 64]); P.dma("sp", qn_bc.full(), dap(q_norm, 0, [[0, 128], [0, 8], [1, 64]]))
    kn_bc = alloc([8, 64]); P.dma("sp", kn_bc.full(), dap(k_norm, 0, [[0, 128], [0, 8], [1, 64]]))
    small = alloc([64])
    P.dma("sp", small[:, 0:1], dap(gdn_norm, 0, [[1, 128], [1, 1]]))
    P.dma("sp", small[:, 1:2], dap(sub_norm, 0, [[1, 128], [1, 1]]))
    ts(small[:, 1:2], small[:, 1:2], 1.0 - LAM_INIT, ALU.mult)
    P.dma("sp", small[0:8, 3:4], dap(dt_bias, 0, [[1, 8], [1, 1]]))
    P.dma("sp", small[0:8, 5:6], dap(a_log, 0, [[1, 8], [1, 1]]))
    act(small[0:8, 4:5], small[0:8, 5:6], AF.Exp)
    ts(small[0:8, 4:5], small[0:8, 4:5], -1.0, ALU.mult)
    b15 = alloc([8]); P.dma("sp", b15.full(), dap(relb, 15 * 8, [[0, 128], [1, 8]]))
    rowsA = alloc([128]); P.dma("sp", rowsA[0:16, :], b_gate.full().with_ap(b_gate.ap.rearrange("(t p) -> t p", p=128)))
    rowsC = alloc([128]); P.dma("sp", rowsC[0:96, :], conv_w.full().with_ap(conv_w.ap.rearrange("(t p) -> t p", p=128)))
    bgT = alloc([16])
    cwT = alloc([96])
    pb = pbank()
    tr(psb[pb][:, 0:16], rowsA[0:16, :], identf[0:16, 0:16])
    tr(psb[pb][:, 16:112], rowsC[0:96, :], identf[0:96, 0:96])
    cp(bgT.full(), psb[pb][:, 0:16])
    cp(cwT.full(), psb[pb][:, 16:112])
    G = alloc([8, GW], BF16)
    m0 = top[0]
    lam4 = alloc([4, 64])
    for i, t in enumerate((lq1, lk1, lq2, lk2)):
        P.dma("sp", lam4[:, i, :], dap(t, 0, [[0, 128], [1, 64]]))
    tt(lam4[:, 0, :], lam4[:, 0, :], lam4[:, 1, :], ALU.mult)
    tt(lam4[:, 2, :], lam4[:, 2, :], lam4[:, 3, :], ALU.mult)
    red(small[:, 6:7], lam4[:, 0, :]); red(small[:, 7:8], lam4[:, 2, :])
    act(small[:, 6:8], small[:, 6:8], AF.Exp)
    tt(small[:, 8:9], small[:, 7:8], small[:, 6:7], ALU.subtract)
    ts(small[:, 2:3], small[:, 8:9], -LAM_INIT, ALU.add)
    for nm_, dst_, rows_ in (("negU_incl", negU_incl, 64), ("negU_strict", negU_strict, 64), ("negL_strict", negL_strict, 64),
                             ("identrep", identrep, 64), ("blockmask", blockmask, 8)):
        stg_ = alloc([8, 64])
        P.dma("sp", stg_[0:rows_], cst[nm_].full().with_ap(cst[nm_].ap.rearrange("p (a b) -> p a b", a=8)))
        cp(dst_[0:rows_], stg_[0:rows_])
    onehot = alloc([GL]); P.dma("sp", onehot[0:32, :], cst["onehot"].full())
    tab = alloc([8]); P.dma("sp", tab[0:32, :], relb.full())
    tabrep = alloc([8, 128])
    cp(tabrep[0:32], bc3(tab[0:32, :], [32, 8, 128]))
    maskG = alloc([GW]); P.dma("sp", maskG.full(), cst["maskG"].full())
    frep = alloc([GL])
    gsk = alloc([GW])
    for h in range(8):
        for j in range(3):
            pb = pbank()
            mm(psb[pb][:, 0:384], tabrep[0:32, h, :], onehot[0:32, j * 384:(j + 1) * 384])
            ts(frep[:, j * 384:(j + 1) * 384], psb[pb][:, 0:384], 1.0 / A_SCALE, ALU.mult)
        P.dma("sp", FD[h], frep.full())
        P.dma("sp", gsk.full(), FD[h].with_ap(bass.AP(FD.ap.tensor, h * 128 * GL + 127, [[GL - 1, 128], [1, GW]])))
        tt(G[:, h, :], gsk.full(), maskG.full(), ALU.add)
    top[0] = m0

    S_meta = alloc([8, 128])
    ctx_meta = alloc([24, 3])
    KTm = alloc([8, 16], BF16)
    Vm = alloc([1024], BF16)
    S_cur = alloc([8, 128])
    S_bf = alloc([8, 128], BF16)
    ctx_cur = alloc([24, 4, 3])
    NSLOT = 3
    wring = [alloc([4096], BF16) for _ in range(NSLOT)]
    wcnt = [0]

    def wload(b):
        cast_upto(cast_order.index(b) + 4)
        slot = wring[wcnt[0] % NSLOT]
        wcnt[0] += 1
        if b == B_BA:
            P.dma("sp", wk8(slot)[:, :, 0:16], WS[b].with_ap(ws_k8(b)[:, :, 0:16]))
        else:
            P.dma("sp", slot.full(), WS[b])
        return slot

    def wk8(slot):
        return T(slot.ap.rearrange("p (k c) -> p k c", k=8), "arena", [128, 8, 512], esize=2, base_off=slot.base_off)

    def wf32(slot):
        return T(slot.ap.rearrange("p (f c) -> p f c", f=32), "arena", [128, 32, 128], esize=2, base_off=slot.base_off)

    base_top = top[0]

    def run_tile(kind, s=0, t=0):
        top[0] = base_top
        if kind == "meta":
            NT, ST, nst, nseg, L, C = 16, 16, 1, 1, 16, 16
        elif kind == "prompt":
            NT, ST, nst, nseg, L, C = 512, 128, 4, 1, 512, 64
        else:
            NT, ST, nst, nseg, L, C = 256, 128, 2, 4, 64, 64
        nch = NT // C
        xtok = alloc([nst, D])
        xnT = alloc([8, NT], BF16)
        sstat = alloc([32])

        for st in range(nst):
            if kind == "meta":
                src = meta.full()
            elif kind == "prompt":
                src = xp[s, t * 512 + st * 128: t * 512 + (st + 1) * 128, :]
            else:
                src = xs[st * 128:(st + 1) * 128, :]
            P.dma("sp", xtok[0:ST, st, :], src)

        def norm_T(norm_dram):
            m = top[0]
            norm_bc = alloc([D])
            P.dma("sp", norm_bc.full(), dap(norm_dram, 0, [[0, 128], [1, D]]))
            junk = alloc([D])
            xnb = alloc([D], BF16)
            for st in range(nst):
                memset(sstat[0:ST, st:st + 1], 0.0)
                act(junk[0:ST, :], xtok[0:ST, st, :], AF.Square, accum=sstat[0:ST, st:st + 1])
                rsqrt(sstat[0:ST, 8 + st:9 + st], sstat[0:ST, st:st + 1], 1.0 / D, sstat[0:ST, 16 + st:17 + st])
                stt(xnb[0:ST, :], xtok[0:ST, st, :], sstat[0:ST, 8 + st:9 + st], norm_bc[0:ST, :], ALU.mult, ALU.mult)
                pb = pbank()
                for kc in range(8):
                    tr(psb16[pb][:, kc * ST:(kc + 1) * ST], xnb[0:ST, kc * 128:(kc + 1) * 128], identb[0:ST, 0:ST])
                src = psb16[pb][:, 0:8 * ST]
                cp(xnT[:, :, st * ST:(st + 1) * ST], src.with_ap(src.ap.rearrange("p (k t) -> p k t", k=8)), eng="act")
            top[0] = m

        if stage < 0:
            return
        P.phase = kind + ":norm1"
        norm_T(norm1)
        if stage < 1:
            flush_deferred()
            return
        P.phase = kind + ":qkvproj"

        oaT = alloc([8, NT], BF16) if kind != "meta" else None
        vnew_s = alloc([4, D], BF16) if kind == "sample" else None
        m_attn = top[0]
        qT = alloc([8, NT], BF16) if kind != "meta" else None

        NB_ = 4
        sqs = [alloc([512]) for _ in range(NB_)]
        t1s = [alloc([512]) for _ in range(NB_)]
        kns = [alloc([512]) for _ in range(NB_)]
        kbs = [alloc([512], BF16) for _ in range(NB_)]
        ktiles = [alloc([4, ST], BF16) for _ in range(NB_)]
        rsts = [alloc([16]) for _ in range(NB_)]
        ptc = [0]

        def post2(which, half, cols, st, i):
            kb = kbs[i]; kn = kns[i]; ktile = ktiles[i]
            pb2 = pbank()
            for hh in range(4):
                tr(psb16[pb2][:, hh * ST:(hh + 1) * ST], kb[0:ST, hh * 128:(hh + 1) * 128], identb[0:ST, 0:ST])
            src = psb16[pb2][:, 0:4 * ST]
            srcv = src.with_ap(src.ap.rearrange("p (k t) -> p k t", k=4))
            if which == "q":
                cp(qT[:, half * 4:(half + 1) * 4, st * ST:(st + 1) * ST], srcv, eng="act")
                return
            cp(ktile.full(), srcv, eng="act")
            if kind == "meta":
                cp(KTm[:, half * 4:(half + 1) * 4, :], ktile.full(), eng="pool")
            elif kind == "prompt":
                tok0 = NMETA + t * 512 + st * 128
                P.dma("sp", KTd[s, half * 4:(half + 1) * 4, :, tok0:tok0 + 128].with_ap(
                    KTd.ap[s, half * 4:(half + 1) * 4, :, tok0:tok0 + 128].rearrange("h p t -> p h t")), ktile.full())
            else:
                for q2 in range(2):
                    sq_ = st * 2 + q2
                    P.dma("sp", KTs[sq_, half * 4:(half + 1) * 4, :, PAST:PAST + 64].with_ap(
                        KTs.ap[sq_, half * 4:(half + 1) * 4, :, PAST:PAST + 64].rearrange("h p t -> p h t")),
                        ktile[:, :, q2 * 64:(q2 + 1) * 64])

        def proj_tok(blk_id, half, which):
            slot = wk8(wload(blk_id))
            cols = slice(half * 512, (half + 1) * 512)
            for st in range(nst):
                pb = pbank()
                for kc in range(8):
                    mm(psb[pb][0:ST, :], xnT[:, kc, st * ST:(st + 1) * ST], slot[:, kc, :], start=(kc == 0), stop=(kc == 7))
                group_issued()
                i = ptc[0] % NB_
                ptc[0] += 1
                ps = psb[pb]
                sq = sqs[i]; t1 = t1s[i]; kn = kns[i]; kb = kbs[i]; rs = rsts[i]
                PTS = int(os.environ.get("PT_STOP", "9"))
                if PTS <= 1 or (which == "v" and os.environ.get("PT_VSKIP", "0") == "1"):
                    cp(sq[0:ST, :], ps[0:ST, :])
                    continue
                if which in ("q", "k"):
                    act(sq[0:ST, :], ps[0:ST, :], AF.Square)
                    sqv = sq[0:ST, :]
                    red(rs[0:ST, 0:8], sqv.with_ap(sqv.ap.rearrange("p (a b) -> p a b", a=8)))
                    PRS = int(os.environ.get("PT_RS", "2"))
                    if PRS == 2:
                        rsqrt(rs[0:ST, 0:8], rs[0:ST, 0:8], 1.0 / 64, rs[0:ST, 8:16])
                    elif PRS == 1:
                        act(rs[0:ST, 8:16], rs[0:ST, 0:8], AF.Sqrt, bias=EPS, scale=1.0 / 64)
                        P.op("dve", lambda e, rs=rs: e.reciprocal(out=rs[0:ST, 0:8].ap, in_=rs[0:ST, 8:16].ap), reads=[rs[0:ST, 8:16]], writes=[rs[0:ST, 0:8]])
                    if PTS <= 2:
                        continue
                    psv = ps[0:ST, :]
                    t1v = t1[0:ST, :]
                    tt(t1v.with_ap(t1v.ap.rearrange("p (a b) -> p a b", a=8)), psv.with_ap(psv.ap.rearrange("p (a b) -> p a b", a=8)),
                       bc3(rs[0:ST, 0:8], [ST, 8, 64]), ALU.mult)
                    wbc = (qn_bc if which == "q" else kn_bc)[0:ST]
                    wflat = wbc.with_ap(wbc.ap.rearrange("p a b -> p (a b)"))
                    tt(kb[0:ST, :], t1v, wflat, ALU.mult)
                    if which == "k":
                        tt(kn[0:ST, :], t1v, wflat, ALU.mult, eng=("pool" if os.environ.get("PT_POOLMUL", "1") == "1" else "dve"))
                        if kind == "meta":
                            for s2 in range(NPS):
                                P.dma("pool", kp[s2, 0:16, cols], kn[0:ST, :])
                        elif kind == "prompt":
                            tok0 = NMETA + t * 512 + st * 128
                            P.dma("pool", kp[s, tok0:tok0 + 128, cols], kn[0:ST, :])
                        else:
                            P.dma("pool", kso[st * 128:(st + 1) * 128, cols], kn[0:ST, :])
                    if PTS >= 4:
                        defer(lambda which=which, half=half, cols=cols, st=st, i=i: post2(which, half, cols, st, i), delay=2)
                else:
                    cp(kn[0:ST, :], ps[0:ST, :], eng="act")
                    if kind == "meta":
                        if os.environ.get("PT_VCP", "1") == "1":
                            cp(Vm[0:ST, cols], ps[0:ST, :])
                        else:
                            cp(Vm[0:ST, cols], kn[0:ST, :], eng="pool")
                        for s2 in range(NPS):
                            P.dma("pool", vp[s2, 0:16, cols], kn[0:ST, :])
                    elif kind == "prompt":
                        tok0 = NMETA + t * 512 + st * 128
                        cp(kb[0:ST, :], ps[0:ST, :])
                        P.dma("pool", vp[s, tok0:tok0 + 128, cols], kn[0:ST, :])
                        P.dma("sp", Vd[s, tok0:tok0 + 128, cols], kb[0:ST, :])
                    else:
                        P.dma("pool", vso[st * 128:(st + 1) * 128, cols], kn[0:ST, :])
                        cp(kb[0:ST, :], ps[0:ST, :])
                        for q2 in range(2):
                            P.dma("sp", Vsd[st * 2 + q2, :, cols], kb[q2 * 64:(q2 + 1) * 64, :])

        if kind != "meta":
            proj_tok(B_Q, 0, "q"); proj_tok(B_Q + 1, 1, "q")
        proj_tok(B_K, 0, "k"); proj_tok(B_K + 1, 1, "k")
        proj_tok(B_V, 0, "v"); proj_tok(B_V + 1, 1, "v")
        flush_deferred()
        if stage < 2:
            return
        P.phase = kind + ":attn"
        if kind != "meta":
            m = top[0]
            NQ = 512 if kind == "prompt" else 64
            nkeys = (NMETA + (t + 1) * 512) if kind == "prompt" else (PAST + 64)
            ktb = [alloc([TP], BF16) for _ in range(2)]
            vtb = [alloc([17, 128], BF16) for _ in range(2)]
            pT = [alloc([512], BF16) for _ in range(4)]
            o1s = [alloc([512]) for _ in range(2)]; o2 = alloc([512]); rr = alloc([512]); rr2 = alloc([512])
            rr3 = alloc([512]); rr4 = alloc([512]); osq = alloc([512], BF16)
            pcount = [0]
            segs = [0] if kind == "prompt" else list(range(4))
            hl = [(sg_, h) for sg_ in segs for h in range(8)]

            def load_kv(i):
                sg_, h = hl[i]
                kt = ktb[i % 2]; vt = vtb[i % 2]
                if kind == "prompt":
                    P.dma("sp", kt[:, 16:nkeys], KTd[s, h, :, 16:nkeys])
                    for g4 in range(t + 1):
                        r0 = NMETA + g4 * 512
                        P.dma("sp", vt[:, g4 * 4:(g4 + 1) * 4, :], Vd[s, r0:r0 + 512, h * 128:(h + 1) * 128].with_ap(
                            Vd.ap[s, r0:r0 + 512, h * 128:(h + 1) * 128].rearrange("(c p) e -> p c e", p=128)))
                else:
                    P.dma("sp", kt[:, 0:nkeys], KTs[sg_, h, :, 0:nkeys])
                    for g4 in range(2):
                        P.dma("sp", vt[:, g4 * 4:(g4 + 1) * 4, :], Vcd[sg_, g4 * 512:(g4 + 1) * 512, h * 128:(h + 1) * 128].with_ap(
                            Vcd.ap[sg_, g4 * 512:(g4 + 1) * 512, h * 128:(h + 1) * 128].rearrange("(c p) e -> p c e", p=128)))
            if kind == "sample":
                memset(vnew_s[64:128], 0.0)
                for kt_ in ktb:
                    memset(kt_[:, PAST + 64:PAST + 128], 0.0)
                for sq_ in range(NSS):
                    P.dma("sp", vnew_s[0:64, sq_, :], Vsd[sq_])
            load_kv(0)
            for i, (sg_, h) in enumerate(hl):
                if i + 1 < len(hl):
                    load_kv(i + 1)
                kt = ktb[i % 2]; vt = vtb[i % 2]
                q0c = sg_ * 64 if kind == "sample" else 0
                blocks = []
                if kind == "prompt":
                    blocks.append((KTm[:, h, :], Vm[0:16, h * 128:(h + 1) * 128], 16, (GOFF + 16) if t == 0 else None))
                    for kc in range((t + 1) * 4):
                        delta = kc * 128 - t * 512
                        win = (GOFF - delta) if delta >= -128 else None
                        blocks.append((kt[:, 16 + kc * 128:16 + (kc + 1) * 128], vt[:, kc, :], 128, win))
                else:
                    for kc in range(8):
                        win = (GOFF + 128) if kc == 7 else None
                        blocks.append((kt[:, kc * 128:(kc + 1) * 128], vt[:, kc, :], 128, win))
                    blocks.append((kt[:, PAST:PAST + 128], vnew_s[:, sg_, h * 128:(h + 1) * 128], 128, GOFF))
                import os as _os
                _sk = _os.environ.get("ATT_SKIP", "")
                if kind == "sample" and _sk:
                    nb_ = []
                    for bi_, blk in enumerate(blocks):
                        typ = "new" if bi_ == 8 else ("win7" if bi_ == 7 else "far")
                        if typ not in _sk:
                            nb_.append(blk)
                    blocks = nb_
                nb = len(blocks)
                for bi, (kv, vv, nk, win) in enumerate(blocks):
                    for mp in range(2):
                        pbS = pbank(4)
                        S = psb[pbS][0:nk, 0:NQ]
                        mm(S, V(kv.ap[mp * 64:(mp + 1) * 64, :], kv.key, kv.lo, kv.hi, kv.page), qT[mp * 64:(mp + 1) * 64, h, q0c:q0c + NQ],
                           start=True, stop=(win is None))
                        if win is not None:
                            mm(S, identb[0:nk, 0:nk], G[0:nk, h, win:win + NQ], start=False, stop=True)
                        pt = pT[pcount[0] % 4]; pcount[0] += 1
                        if win is None:
                            act(pt[0:nk, 0:NQ], S, AF.Exp, bias=b15[0:nk, h:h + 1], scale=A_SCALE)
                        else:
                            act(pt[0:nk, 0:NQ], S, AF.Exp, scale=A_SCALE)
                        group_issued()

                        def pv_den(mp=mp, vv=vv, pt=pt, nk=nk, bi=bi, nb=nb):
                            mm(psb[4 + mp][:, 0:NQ], vv, pt[0:nk, 0:NQ], start=(bi == 0), stop=(bi == nb - 1))
                            mm(psb[6 + mp][:, 0:NQ], ones_bf[0:nk, :], pt[0:nk, 0:NQ], start=(bi == 0), stop=(bi == nb - 1))
                        defer(pv_den, delay=2)
                flush_deferred()
                o1 = o1s[i % 2]
                act(rr[:, 0:NQ], psb[6][:, 0:NQ], AF.Ln)
                act(rr2[:, 0:NQ], psb[7][:, 0:NQ], AF.Ln)
                act(rr[:, 0:NQ], rr[:, 0:NQ], AF.Exp, scale=-1.0)
                act(rr2[:, 0:NQ], rr2[:, 0:NQ], AF.Exp, scale=-1.0)
                tt(o1[:, 0:NQ], psb[4][:, 0:NQ], rr[:, 0:NQ], ALU.mult)
                tt(o2[:, 0:NQ], psb[5][:, 0:NQ], rr2[:, 0:NQ], ALU.mult)
                stt(o1[:, 0:NQ], o2[:, 0:NQ], small[:, 2:3], o1[:, 0:NQ], ALU.mult, ALU.add)

                def finish_head(o1=o1, h=h, q0c=q0c):
                    act(osq[:, 0:NQ], o1[:, 0:NQ], AF.Square)
                    pbn = pbank(4)
                    mm(psb[pbn][:, 0:NQ], ones_bf.full(), osq[:, 0:NQ])
                    rsqrt(rr3[:, 0:NQ], psb[pbn][:, 0:NQ], 1.0 / 128, rr4[:, 0:NQ])
                    stt(oaT[:, h, q0c:q0c + NQ], o1[:, 0:NQ], small[:, 1:2], rr3[:, 0:NQ], ALU.mult, ALU.mult)
                defer(finish_head, delay=3)
            flush_deferred()
            top[0] = m
        top[0] = m_attn
        if dbg and kind == "prompt" and s == 0 and t == 0:
            P.dma("pool", dbg_oa.full(), oaT.full())
        if stage < 3:
            return
        P.phase = kind + ":gdnproj"
        obT = alloc([8, NT], BF16) if kind != "meta" else None
        m_gdn = top[0]
        qg = alloc([8, NT], BF16); kg = alloc([8, NT], BF16); vg = alloc([8, NT], BF16)
        sz = alloc([8, NT], BF16) if kind != "meta" else None
        og = alloc([8, NT], BF16) if kind != "meta" else None
        cb = alloc([8, NT])
        slotBA = wk8(wload(B_BA))
        pb = pbank()
        for kc in range(8):
            mm(psb[pb][0:8, 0:NT], slotBA[:, kc, 0:8], xnT[:, kc, :], start=(kc == 0), stop=(kc == 7))
        pb2 = pbank()
        for kc in range(8):
            mm(psb[pb2][0:8, 0:NT], slotBA[:, kc, 8:16], xnT[:, kc, :], start=(kc == 0), stop=(kc == 7))
        act(cb[0:8, 4, :], psb[pb][0:8, 0:NT], AF.Sigmoid)
        act(cb[0:8, 6, :], psb[pb][0:8, 0:NT], AF.Exp, scale=-1.0)
        act(cb[0:8, 6, :], cb[0:8, 6, :], AF.Ln, bias=1.0)
        act(cb[0:8, 7, :], psb[pb2][0:8, 0:NT], AF.Exp, bias=small[0:8, 3:4])
        act(cb[0:8, 7, :], cb[0:8, 7, :], AF.Ln, bias=1.0)
        ts(cb[0:8, 0, :], cb[0:8, 7, :], small[0:8, 4:5], ALU.mult)
        a_, b_ = 0, 7
        sh = 1
        while sh < C:
            av = cb[0:8, a_, :]; bv = cb[0:8, b_, :]
            a3 = av.with_ap(av.ap.rearrange("p (c l) -> p c l", l=C)); b3 = bv.with_ap(bv.ap.rearrange("p (c l) -> p c l", l=C))
            cp(V(b3.ap[:, :, 0:sh], bv.key, bv.lo, bv.hi, bv.page), V(a3.ap[:, :, 0:sh], av.key, av.lo, av.hi, av.page))
            tt(V(b3.ap[:, :, sh:C], bv.key, bv.lo, bv.hi, bv.page), V(a3.ap[:, :, sh:C], av.key, av.lo, av.hi, av.page),
               V(a3.ap[:, :, 0:C - sh], av.key, av.lo, av.hi, av.page), ALU.add)
            a_, b_ = b_, a_
            sh *= 2
        if a_ != 0:
            cp(cb[0:8, 0, :], cb[0:8, a_, :])
        tt(cb[0:8, 1, :], cb[0:8, 0, :], cb[0:8, 6, :], ALU.subtract)
        act(cb[0:8, 2, :], cb[0:8, 0, :], AF.Exp)
        gv = cb[0:8, 0, :]
        g3 = gv.with_ap(gv.ap.rearrange("p (c l) -> p c l", l=C))
        kdv = cb[0:8, 3, :]
        kd3 = kdv.with_ap(kdv.ap.rearrange("p (c l) -> p c l", l=C))
        tt(kd3, V(g3.ap[:, :, C - 1:C].to_broadcast([8, nch, C]), gv.key, gv.lo, gv.hi, gv.page), g3, ALU.subtract)
        act(cb[0:8, 3, :], cb[0:8, 3, :], AF.Exp)
        tt(cb[0:8, 5, :], cb[0:8, 4, :], cb[0:8, 2, :], ALU.mult)
        cbb = alloc([8, NT], BF16)
        for q_, row in enumerate((0, 1, 2)):
            cp(cbb[0:8, 2 * q_, :], cb[0:8, row, :])
            tt(cb[0:8, 6, :], cb[0:8, row, :], cbb[0:8, 2 * q_, :], ALU.subtract)
            cp(cbb[0:8, 2 * q_ + 1, :], cb[0:8, 6, :])
        ts(cbb[0:8, 6, :], cbb[0:8, 0, :], -1.0, ALU.mult)
        ts(cbb[0:8, 7, :], cbb[0:8, 1, :], -1.0, ALU.mult)

        m_conv = top[0]
        cin = [alloc([nseg, L + 3]) for _ in range(3)]
        cacc = alloc([nseg, L])
        csq = alloc([NT], BF16)
        crn = alloc([NT])
        if kind == "sample":
            scrow = alloc([3072])
            for sg_ in range(4):
                P.dma("sp", scrow[0:3, :], sc[sg_])
                for g6 in range(6):
                    pb = pbank()
                    for c4 in range(4):
                        cid_ = g6 * 4 + c4
                        tr(psb[pb][:, c4 * 3:(c4 + 1) * 3], scrow[0:3, cid_ * 128:(cid_ + 1) * 128], identf[0:3, 0:3])
                    pv_ = psb[pb][:, 0:12]
                    cp(ctx_cur[:, g6 * 4:(g6 + 1) * 4, sg_, :], pv_.with_ap(pv_.ap.rearrange("p (c w) -> p c w", c=4)))
        caccs = [cacc] + [alloc([nseg, L]) for _ in range(5)]
        ctmp = alloc([nseg, L])
        csqs = [csq] + [alloc([NT], BF16) for _ in range(3)]
        crns = [crn, alloc([NT])]

        def l2norm_finish(h, j):
            ca = caccs[(h % 2) * 3 + j]
            cflat = ca.full().with_ap(ca.ap.rearrange("p s l -> p (s l)"))
            crn_ = crns[j]
            pbn = pbank()
            mm(psb[pbn][:, 0:NT], ones_bf.full(), csqs[(h % 2) * 2 + j].full())
            act(crn_.full(), psb[pbn][:, 0:NT], AF.Ln, bias=EPS, scale=1.0)
            act(crn_.full(), crn_.full(), AF.Exp, scale=-0.5)
            if j == 0:
                stt(qg[:, h, :], cflat, B_SCALE, crn_.full(), ALU.mult, ALU.mult)
            else:
                tt(kg[:, h, :], cflat, crn_.full(), ALU.mult)

        def silu_qk(h, j):
            ca = caccs[(h % 2) * 3 + j]
            cflat = ca.full().with_ap(ca.ap.rearrange("p s l -> p (s l)"))
            act(cflat, cflat, AF.Silu)
            act(csqs[(h % 2) * 2 + j].full(), cflat, AF.Square)

        def silu_v(h):
            ca = caccs[(h % 2) * 3 + 2]
            act(vg[:, h, :], ca.full().with_ap(ca.ap.rearrange("p s l -> p (s l)")), AF.Silu)

        for h in range(8):
            slot = wk8(wload(B_H + h))
            for j in range(4):
                pb = pbank()
                for kc in range(8):
                    mm(psb[pb][:, 0:NT], slot[:, kc, j * 128:(j + 1) * 128], xnT[:, kc, :], start=(kc == 0), stop=(kc == 7))
                group_issued()
                ps = psb[pb][:, 0:NT]
                if j == 3:
                    if kind != "meta":
                        act(sz[:, h, :], ps, AF.Silu)
                    continue
                cid = j * 8 + h
                ci = cin[j]
                if kind == "meta":
                    memset(ci[:, :, 0:3], 0.0, eng="pool")
                elif kind == "prompt":
                    cp(ci[:, 0, 0:3], (ctx_meta[:, cid, :] if t == 0 else ctx_cur[:, cid, 0, :]), eng="pool")
                else:
                    cp(ci[:, :, 0:3], ctx_cur[:, cid, 0:4, :], eng="pool")
                cp(ci[:, :, 3:3 + L], ps.with_ap(ps.ap.rearrange("p (s l) -> p s l", s=nseg)), eng="act")
                if kind == "meta":
                    cp(ctx_meta[:, cid, :], ci[:, 0, L:L + 3], eng="pool")
                else:
                    cp(ctx_cur[:, cid, 0:nseg, :], ci[:, :, L:L + 3], eng="pool")
            for j in range(3):
                cid = j * 8 + h
                ci = cin[j]
                ce = "pool" if j == 2 else "dve"
                ca = caccs[(h % 2) * 3 + j]
                ts(ca.full(), ci[:, :, 0:L], cwT[:, cid:cid + 1], ALU.mult, eng=ce)
                for w in range(1, 4):
                    if ce == "dve":
                        stt(ca.full(), ci[:, :, w:w + L], cwT[:, w * 24 + cid:w * 24 + cid + 1], ca.full(), ALU.mult, ALU.add)
                    else:
                        ts(ctmp.full(), ci[:, :, w:w + L], cwT[:, w * 24 + cid:w * 24 + cid + 1], ALU.mult, eng="pool")
                        tt(ca.full(), ca.full(), ctmp.full(), ALU.add, eng="pool")
            defer(lambda h=h: (silu_qk(h, 0), silu_qk(h, 1)), delay=1)
            defer(lambda h=h: silu_v(h), delay=3)
            defer(lambda h=h: (l2norm_finish(h, 0), l2norm_finish(h, 1)), delay=3)
        flush_deferred()
        if (kind == "prompt" and t == 3) or kind == "sample":
            tls = [alloc([512]) for _ in range(2)]
            for sg_ in range(nseg):
                for g6 in range(6):
                    tl = tls[g6 % 2]
                    pb = pbank()
                    for c4 in range(4):
                        tr(psb[pb][0:3, c4 * 128:(c4 + 1) * 128], ctx_cur[:, g6 * 4 + c4, sg_, :], identf.full())
                    cp(tl[0:3, :], psb[pb][0:3, :], eng="act")
                    dst_ = cpo[s, :, g6 * 512:(g6 + 1) * 512] if kind == "prompt" else cso[sg_, :, g6 * 512:(g6 + 1) * 512]
                    P.dma("pool", dst_, tl[0:3, :])

        P.phase = kind + ":gdnchunk"
        top[0] = m_conv
        nlev = {64: 5, 16: 3}[C]
        if C == 64:
            nU_i, nU_s, nL_s, idr, bmk = negU_incl[0:C], negU_strict[0:C], negL_strict[0:C], identrep[0:C], blockmask[0:8]
        else:
            cm = []
            for src_, np_ in ((negU_incl, C), (negU_strict, C), (negL_strict, C), (identrep, C), (blockmask, 8)):
                d_ = alloc([8, C], BF16)
                cp(d_[0:np_], src_[0:np_, :, 0:C])
                cm.append(d_[0:np_])
            nU_i, nU_s, nL_s, idr, bmk = cm
        gdb = [alloc([8, C], BF16) for _ in range(8)]
        tokc = alloc([32])
        DTi = alloc([8, C], BF16); NDT = alloc([8, C], BF16); NTD = alloc([8, C], BF16)
        Pm = [alloc([8, C], BF16) for _ in range(2)]; PmT = [alloc([8, C], BF16) for _ in range(2)]
        Rm = [alloc([8, C], BF16) for _ in range(2)]
        MT = alloc([8, C], BF16); qgc = alloc([8, C], BF16); nwT = alloc([8, C], BF16)
        bv_ = alloc([8, 128], BF16); kbg = alloc([8, 128], BF16); kdc = alloc([8, 128], BF16); vnw = alloc([8, 128], BF16)
        egl = alloc([8])

        def fl(v):
            return v.with_ap(v.ap.rearrange("p a b -> p (a b)"))

        for ci_ in range(nch):
            sgi = ci_ if kind == "sample" else 0
            cs = slice(ci_ * C, (ci_ + 1) * C)
            W8 = 8 * C
            if kind == "meta":
                if ci_ == 0:
                    memset(S_cur.full(), 0.0); memset(S_bf.full(), 0.0)
            elif kind == "prompt":
                if ci_ == 0 and t == 0:
                    cp(S_cur.full(), S_meta.full()); cp(S_bf.full(), S_meta.full(), eng="act")
            else:
                P.dma("sp", S_cur.full(), sg[sgi].with_ap(sg.ap[sgi].rearrange("h d e -> d h e")))
                cp(S_bf.full(), S_cur.full(), eng="act")
            for k_ in range(8):
                src = cbb[0:8, k_, cs]
                tt(gdb[k_][0:8], bmk, V(src.ap.unsqueeze(1).to_broadcast([8, 8, C]), src.key, src.lo, src.hi, src.page), ALU.mult, eng="pool")
            pbt = pbank()
            for k_, row in enumerate((4, 5, 3)):
                tr(psb[pbt][0:C, k_ * 8:(k_ + 1) * 8], cb[0:8, row, cs], identf[0:8, 0:8])
            cp(tokc[0:C, 0:24], psb[pbt][0:C, 0:24])
            on8 = ones_bf[0:8, 0:C]

            def xmat(diag_hi, diag_lo, col_hi, col_lo, mask):
                pbx = pbank()
                X = psb[pbx][0:C, 0:W8]
                mm(X, on8, fl(gdb[diag_hi][0:8]), start=True, stop=False)
                mm(X, on8, fl(gdb[diag_lo][0:8]), start=False, stop=False)
                mm(X, cbb[0:8, col_hi, cs], fl(bmk), start=False, stop=False)
                mm(X, cbb[0:8, col_lo, cs], fl(bmk), start=False, stop=False)
                mm(X, identb[0:C, 0:C], fl(mask), start=False, stop=True)
                return X
            act(fl(DTi[0:C]), xmat(0, 1, 6, 7, nU_i), AF.Exp)
            act(fl(NDT[0:C]), xmat(2, 3, 6, 7, nU_s), AF.Exp)
            act(fl(NTD[0:C]), xmat(6, 7, 2, 3, nL_s), AF.Exp)
            pbe = pbank()
            mm(psb[pbe][:, 0:W8], ones_bf[0:8, :], fl(gdb[4][0:8]), start=True, stop=False)
            mm(psb[pbe][:, 0:W8], ones_bf[0:8, :], fl(gdb[5][0:8]), start=False, stop=True)
            pe_v = psb[pbe][:, 0:W8]
            pe3 = pe_v.with_ap(pe_v.ap.rearrange("p (h c) -> p h c", h=8))
            tt(qgc[:, :, 0:C], qg[:, :, cs], pe3, ALU.mult)
            cp(egl.full(), V(pe3.ap[:, :, C - 1], pe_v.key, pe_v.lo, pe_v.hi, pe_v.page))
            pbk = pbank(); pbq = pbank()
            for h in range(8):
                mm(psb[pbk][0:C, h * C:(h + 1) * C], kg[:, h, cs], kg[:, h, cs])
            for h in range(8):
                mm(psb[pbq][0:C, h * C:(h + 1) * C], kg[:, h, cs], qg[:, h, cs])
            stt(fl(Pm[0][0:C]), psb[pbk][0:C, 0:W8], -1.0, fl(NDT[0:C]), ALU.mult, ALU.mult)
            stt(fl(PmT[0][0:C]), psb[pbk][0:C, 0:W8], -1.0, fl(NTD[0:C]), ALU.mult, ALU.mult)
            tt(fl(MT[0:C]), psb[pbq][0:C, 0:W8], fl(DTi[0:C]), ALU.mult)
            tt(fl(Rm[0][0:C]), fl(Pm[0][0:C]), fl(idr), ALU.add, eng="pool")
            cur = 0
            for lv in range(1, nlev + 1):
                nxt = 1 - cur
                pbp = pbank(); pbpt = pbank()
                for h in range(8):
                    mm(psb[pbpt][0:C, h * C:(h + 1) * C], Pm[cur][0:C, h, :], PmT[cur][0:C, h, :])
                if lv < nlev:
                    for h in range(8):
                        mm(psb[pbp][0:C, h * C:(h + 1) * C], PmT[cur][0:C, h, :], Pm[cur][0:C, h, :])
                cp(fl(PmT[nxt][0:C]), psb[pbpt][0:C, 0:W8], eng="act")
                if lv < nlev:
                    cp(fl(Pm[nxt][0:C]), psb[pbp][0:C, 0:W8])
                pbr = pbank()
                for h in range(8):
                    mm(psb[pbr][0:C, h * C:(h + 1) * C], PmT[nxt][0:C, h, :], Rm[cur][0:C, h, :])
                tt(fl(Rm[nxt][0:C]), psb[pbr][0:C, 0:W8], fl(Rm[cur][0:C]), ALU.add)
                cur = nxt
            TT = Rm[cur]
            pbk = pbank(); pbv = pbank()
            for h in range(8):
                tr(psb16[pbk][0:C, h * 128:(h + 1) * 128], kg[:, h, cs], identb.full())
            for h in range(8):
                tr(psb16[pbv][0:C, h * 128:(h + 1) * 128], vg[:, h, cs], identb.full())
            kt3 = psb16[pbk][0:C, :]; kt3 = kt3.with_ap(kt3.ap.rearrange("p (h d) -> p h d", h=8))
            vt3 = psb16[pbv][0:C, :]; vt3 = vt3.with_ap(vt3.ap.rearrange("p (h d) -> p h d", h=8))
            tt(bv_[0:C], vt3, bc3(tokc[0:C, 0:8], [C, 8, 128]), ALU.mult)
            tt(kbg[0:C], kt3, bc3(tokc[0:C, 8:16], [C, 8, 128]), ALU.mult)
            tt(kdc[0:C], kt3, bc3(tokc[0:C, 16:24], [C, 8, 128]), ALU.mult)
            pbw = pbank()
            for h in range(8):
                mm(psb[pbw][:, h * C:(h + 1) * C], kbg[0:C, h, :], TT[0:C, h, :])
            ts(fl(nwT[:, :, 0:C]), psb[pbw][:, 0:W8], -1.0, ALU.mult)
            pv0 = pbank(); pv1 = pbank()
            for h in range(8):
                o = psb[pv0 if h < 4 else pv1][0:C, (h % 4) * 128:(h % 4 + 1) * 128]
                mm(o, TT[0:C, h, :], bv_[0:C, h, :], start=True, stop=False)
                mm(o, nwT[:, h, 0:C], S_bf[:, h, :], start=False, stop=True)
            cp(fl(vnw[0:C, 0:4, :]), psb[pv0][0:C, :], eng="act")
            cp(fl(vnw[0:C, 4:8, :]), psb[pv1][0:C, :])
            if kind != "meta":
                pbo = pbank()
                for h in range(8):
                    o = psb[pbo][:, h * C:(h + 1) * C]
                    mm(o, S_bf[:, h, :], qgc[:, h, 0:C], start=True, stop=False)
                    mm(o, vnw[0:C, h, :], MT[0:C, h, :], start=False, stop=True)
                po = psb[pbo][:, 0:W8]
                cp(og[:, :, cs], po.with_ap(po.ap.rearrange("p (h c) -> p h c", h=8)), eng="act")
            ps0 = pbank(); ps1 = pbank()
            for h in range(8):
                mm(psb[ps0 if h < 4 else ps1][:, (h % 4) * 128:(h % 4 + 1) * 128], kdc[0:C, h, :], vnw[0:C, h, :])
            tt(S_cur.full(), S_cur.full(), bc3(egl.full(), [128, 8, 128]), ALU.mult)
            tt(fl(S_cur[:, 0:4, :]), fl(S_cur[:, 0:4, :]), psb[ps0].full(), ALU.add)
            tt(fl(S_cur[:, 4:8, :]), fl(S_cur[:, 4:8, :]), psb[ps1].full(), ALU.add)
            cp(S_bf.full(), S_cur.full(), eng="act")
            if kind == "sample":
                P.dma("pool", gso[sgi].with_ap(gso.ap[sgi].rearrange("h d e -> d h e")), S_cur.full())
        if kind == "meta":
            cp(S_meta.full(), S_cur.full())
            return
        if kind == "prompt" and t == 3:
            P.dma("pool", gp[s].with_ap(gp.ap[s].rearrange("h d e -> d h e")), S_cur.full())
        P.phase = kind + ":gdnnorm"
        top[0] = m_conv
        gsq = alloc([NT], BF16); grn = alloc([NT]); gt = alloc([NT])
        for h in range(8):
            act(gsq.full(), og[:, h, :], AF.Square)
            pbn = pbank()
            mm(psb[pbn][:, 0:NT], ones_bf.full(), gsq.full())
            rsqrt(grn.full(), psb[pbn][:, 0:NT], 1.0 / 128, gt.full())
            stt(gt.full(), og[:, h, :], small[:, 0:1], grn.full(), ALU.mult, ALU.mult)
            tt(obT[:, h, :], gt.full(), sz[:, h, :], ALU.mult, eng="pool")
        if dbg and kind == "prompt" and s == 0 and t == 0:
            P.dma("pool", dbg_ob.full(), obT.full())
        top[0] = m_gdn
        if stage < 4:
            return
        P.phase = kind + ":merge"
        mixT = alloc([8, NT], BF16)
        sga = alloc([NT]); sgb = alloc([NT]); tmp = alloc([NT])
        for oc in range(8):
            slot = wk8(wload(B_M + oc))
            pa = pbank(); pb_ = pbank(); pya = pbank(); pyb = pbank()
            for kc in range(8):
                mm(psb[pa][:, 0:NT], slot[:, kc, 0:128], xnT[:, kc, :], start=(kc == 0), stop=(kc == 7))
            for kc in range(8):
                mm(psb[pb_][:, 0:NT], slot[:, kc, 128:256], xnT[:, kc, :], start=(kc == 0), stop=(kc == 7))
            for kc in range(8):
                mm(psb[pya][:, 0:NT], slot[:, kc, 256:384], oaT[:, kc, :], start=(kc == 0), stop=(kc == 7))
            for kc in range(8):
                mm(psb[pyb][:, 0:NT], slot[:, kc, 384:512], obT[:, kc, :], start=(kc == 0), stop=(kc == 7))
            act(sga.full(), psb[pa][:, 0:NT], AF.Sigmoid, bias=bgT[:, oc:oc + 1])
            act(sgb.full(), psb[pb_][:, 0:NT], AF.Sigmoid, bias=bgT[:, 8 + oc:9 + oc])
            tt(tmp.full(), psb[pya][:, 0:NT], sga.full(), ALU.mult)
            tt(sgb.full(), psb[pyb][:, 0:NT], sgb.full(), ALU.mult)
            tt(mixT[:, oc, :], tmp.full(), sgb.full(), ALU.add, eng="pool")
        for half in range(2):
            slot = wk8(wload(B_WO + half))
            for st in range(nst):
                pb = pbank()
                for kc in range(8):
                    mm(psb[pb][0:ST, :], mixT[:, kc, st * ST:(st + 1) * ST], slot[:, kc, :], start=(kc == 0), stop=(kc == 7))
                tt(xtok[0:ST, st, half * 512:(half + 1) * 512], xtok[0:ST, st, half * 512:(half + 1) * 512], psb[pb][0:ST, :], ALU.add)
        if stage < 5:
            return
        P.phase = kind + ":ffn"
        norm_T(norm2)
        uT = alloc([32, NT], BF16)
        rl = [alloc([NT]) for _ in range(2)]
        for j in range(8):
            slot = wk8(wload(B_WU + j))
            for c4 in range(4):
                fc = j * 4 + c4
                pb = pbank()
                for kc in range(8):
                    mm(psb[pb][:, 0:NT], slot[:, kc, c4 * 128:(c4 + 1) * 128], xnT[:, kc, :], start=(kc == 0), stop=(kc == 7))
                r = rl[fc % 2]
                act(r.full(), psb[pb][:, 0:NT], AF.Relu)
                tt(uT[:, fc, :], r.full(), r.full(), ALU.mult, eng=("pool" if fc % 2 else "dve"))
        for oc in range(8):
            slot = wf32(wload(B_WD + oc))
            pb = pbank()
            for st in range(nst):
                for fc in range(32):
                    mm(psb[pb][0:ST, st * 128:(st + 1) * 128], uT[:, fc, st * ST:(st + 1) * ST], slot[:, fc, :], start=(fc == 0), stop=(fc == 31))
            pv = psb[pb][0:ST, 0:nst * 128]
            tt(xtok[0:ST, :, oc * 128:(oc + 1) * 128], xtok[0:ST, :, oc * 128:(oc + 1) * 128],
               pv.with_ap(pv.ap.rearrange("p (s c) -> p s c", s=nst)), ALU.add)
        for st in range(nst):
            if kind == "prompt":
                P.dma("pool", yp[s, t * 512 + st * 128: t * 512 + (st + 1) * 128, :], xtok[0:ST, st, :])
            else:
                P.dma("pool", ys[st * 128:(st + 1) * 128, :], xtok[0:ST, st, :])

    def cache_k_prep():
        P.phase = "cachek"
        top[0] = base_top
        ckf = [alloc([D]) for _ in range(2)]
        ckb = [alloc([D], BF16) for _ in range(2)]
        ckt = [alloc([8, 128], BF16) for _ in range(2)]
        i = 0
        for sq_ in range(NSS):
            P.dma("pool", Vcd[sq_], cv[sq_])
        for sq_ in range(NSS):
            for c in range(8):
                f = ckf[i % 2]; b = ckb[i % 2]; kt_ = ckt[i % 2]
                P.dma("sp", f.full(), ck[sq_, c * 128:(c + 1) * 128, :])
                cp(b.full(), f.full(), eng=("pool" if i % 2 else "dve"))
                pb = pbank()
                for h in range(8):
                    tr(psb16[pb][:, h * 128:(h + 1) * 128], b[:, h * 128:(h + 1) * 128], identb.full())
                src = psb16[pb].full()
                cp(kt_.full(), src.with_ap(src.ap.rearrange("p (h t) -> p h t", h=8)), eng="act")
                P.dma("sp", KTs[sq_, :, :, c * 128:(c + 1) * 128].with_ap(KTs.ap[sq_, :, :, c * 128:(c + 1) * 128].rearrange("h p t -> p h t")), kt_.full())
                i += 1

    if dbg:
        dbg_oa = dram("dbg_oa2", [128, 8, 512], "ExternalOutput", BF16)
        dbg_ob = dram("dbg_ob2", [128, 8, 512], "ExternalOutput", BF16)

    import os
    sel = os.environ.get("KTILES", "msp")
    if "s" in sel:
        cache_k_prep()
    run_tile("meta")
    if "s" in sel:
        run_tile("sample")
    if "p" in sel:
        for s in range(NPS):
            for t in range(4):
                run_tile("prompt", s, t)
    elif "q" in sel:
        run_tile("prompt", 0, 0)
    P.emit()
    es.close()
    return nc, P


_CACHE = {}


def kernel(x_prompt, x_sample, cache_attn_k, cache_attn_v, state_gdn, state_conv, meta_tokens, rel_bias,
           norm1, w_in, b_gate, q_norm, k_norm, lambda_q1, lambda_k1, lambda_q2, lambda_k2, sub_norm,
           conv_w, A_log, dt_bias, gdn_norm, w_br_a, w_br_b, w_out, norm2, w_up, w_down, _stage=99, _cores=8, _dbg=False):
    f = lambda a: np.ascontiguousarray(np.asarray(a, dtype=np.float32))
    key = (_stage, _dbg)
    if key not in _CACHE:
        _CACHE[key] = build_program(_stage, _dbg)
    nc, P = _CACHE[key]
    consts = _consts()
    shared = {
        "meta": f(meta_tokens), "relb": f(rel_bias), "norm1": f(norm1).reshape(-1), "w_in": f(w_in)[0],
        "b_gate": f(b_gate).reshape(-1), "q_norm": f(q_norm).reshape(-1), "k_norm": f(k_norm).reshape(-1),
        "lq1": f(lambda_q1).reshape(-1), "lk1": f(lambda_k1).reshape(-1), "lq2": f(lambda_q2).reshape(-1), "lk2": f(lambda_k2).reshape(-1),
        "sub_norm": f(sub_norm).reshape(-1), "conv_w": f(conv_w).reshape(-1), "a_log": f(A_log).reshape(-1),
        "dt_bias": f(dt_bias).reshape(-1), "gdn_norm": f(gdn_norm).reshape(-1), "w_bra": f(w_br_a)[0], "w_brb": f(w_br_b)[0],
        "w_out": f(w_out)[0], "norm2": f(norm2).reshape(-1), "w_up": f(w_up)[0], "w_down": f(w_down)[0],
    }
    for k, v in consts.items():
        shared["c_" + k] = v
    xp = f(x_prompt); xs = f(x_sample)
    ck = f(cache_attn_k)[0].reshape(32, PAST, D); cv = f(cache_attn_v)[0].reshape(32, PAST, D)
    sg = f(state_gdn)[0]; sc = f(state_conv)[0]
    in_maps = []
    for c in range(_cores):
        m = dict(shared)
        m["xp"] = xp[c * NPS:(c + 1) * NPS]
        m["xs"] = xs[c * NSS:(c + 1) * NSS].reshape(NSS * DSEQ, D)
        m["ck"] = ck[c * NSS:(c + 1) * NSS]
        m["cv"] = cv[c * NSS:(c + 1) * NSS]
        m["sg"] = sg[c * NSS:(c + 1) * NSS]
        m["sc"] = sc[c * NSS:(c + 1) * NSS]
        in_maps.append(m)
    res = run_bass_kernel_spmd(nc, in_maps, core_ids=list(range(_cores)))
    R = res.results
    cat = lambda k: np.concatenate([np.asarray(r[k], dtype=np.float32) for r in R], axis=0)
    nb = _cores * NPS
    ns = _cores * NSS
    outs = (
        cat("yp"),
        cat("ys").reshape(ns, DSEQ, D),
        cat("kp").reshape(1, nb, TP, 8, 128),
        cat("vp").reshape(1, nb, TP, 8, 128),
        cat("gp").reshape(1, nb, 8, 128, 128),
        cat("cpo").reshape(1, nb, 3, 3072),
        cat("kso").reshape(1, ns, DSEQ, 8, 128),
        cat("vso").reshape(1, ns, DSEQ, 8, 128),
        cat("gso").reshape(1, ns, 8, 128, 128),
        cat("cso").reshape(1, ns, 3, 3072),
    )
    if _dbg:
        return outs, R
    return outs
```
